# Optimizing a Trainium2 kernel written in Bass

```python
import math
import jax, jax.numpy as jnp
from jax import lax
import numpy as np

D_MODEL = 1024
BATCH = 32
SEQ = 2048
DEPTH = 1

CHUNK = 64
Q_BLOCK = 128
PLE_DIM = 256
GDN_HEADS = 4
GDN_DK = 128
GDN_DV = 128
GDN_CONV = 4
MLA_HEADS = 4
MLA_NOPE = 128
MLA_ROPE = 64
MLA_V = 128
MLA_Q_LORA = 384
MLA_KV_LORA = 256
ROPE_THETA = 10000.0
D_FF = 2816
FFN_CONV = 3
ALPHA = (2.0 * DEPTH) ** 0.25
BETA = (8.0 * DEPTH) ** -0.25
NORM_EPS = 1e-6
GDN_QK = GDN_HEADS * GDN_DK
GDN_VW = GDN_HEADS * GDN_DV
D_IN = 2 * GDN_QK + 2 * GDN_VW + 2 * GDN_HEADS + MLA_Q_LORA + MLA_KV_LORA + MLA_ROPE
D_MIX = GDN_VW + MLA_HEADS * MLA_V

kernel_name = "hybrid_gdn_mla_convffn_deepnorm"


def _rmsnorm(x, g):
    xf = x.astype(jnp.float32)
    y = xf * lax.rsqrt(jnp.mean(xf * xf, axis=-1, keepdims=True) + NORM_EPS)
    return (y * g.astype(jnp.float32)).astype(x.dtype)


def _layernorm(x, g, b):
    xf = x.astype(jnp.float32)
    mu = jnp.mean(xf, axis=-1, keepdims=True)
    xc = xf - mu
    var = jnp.mean(xc * xc, axis=-1, keepdims=True)
    y = xc * lax.rsqrt(var + NORM_EPS) * g.astype(jnp.float32) + b.astype(jnp.float32)
    return y.astype(x.dtype)


def _l2norm(x):
    return x * lax.rsqrt(jnp.sum(x * x, axis=-1, keepdims=True) + NORM_EPS)


def _causal_dwconv(x, w):
    k = w.shape[0]
    return lax.conv_general_dilated(
        x, w[:, None, :].astype(x.dtype), window_strides=(1,), padding=[(k - 1, 0)],
        dimension_numbers=("NWC", "WIO", "NWC"), feature_group_count=x.shape[-1])


def _rope_tables(seq):
    inv = ROPE_THETA ** (-jnp.arange(0, MLA_ROPE, 2, dtype=jnp.float32) / MLA_ROPE)
    ang = jnp.arange(seq, dtype=jnp.float32)[:, None] * inv[None, :]
    return jnp.cos(ang), jnp.sin(ang)


def _apply_rope(x, cos, sin):
    xf = x.astype(jnp.float32)
    x1, x2 = jnp.split(xf, 2, axis=-1)
    return jnp.concatenate([x1 * cos - x2 * sin, x1 * sin + x2 * cos], axis=-1).astype(x.dtype)


def _chunk_gated_delta_rule(q, k, v, g, beta):
    bsz, seq, nh, dk = q.shape
    dv = v.shape[-1]
    n = seq // CHUNK

    def blocks(t):
        t = t.reshape((bsz, n, CHUNK, nh) + t.shape[3:])
        return jnp.moveaxis(jnp.swapaxes(t, 2, 3), 1, 0)

    qc, kc, vc = blocks(q), blocks(k), blocks(v)
    bc = blocks(beta)
    gc = jnp.cumsum(blocks(g), axis=-1)
    idx = jnp.arange(CHUNK)
    incl = idx[:, None] >= idx[None, :]
    strict = idx[:, None] > idx[None, :]
    decay = jnp.exp(jnp.where(incl, gc[..., :, None] - gc[..., None, :], -jnp.inf))
    kb = kc * bc[..., None]
    lower = jnp.where(strict, jnp.einsum("nbhcd,nbhsd->nbhcs", kb, kc) * decay, 0.0)
    tri = lower + jnp.eye(CHUNK, dtype=jnp.float32)
    w = lax.linalg.triangular_solve(tri, kb * jnp.exp(gc)[..., None], left_side=True, lower=True,
                                    unit_diagonal=True)
    u = lax.linalg.triangular_solve(tri, vc * bc[..., None], left_side=True, lower=True,
                                    unit_diagonal=True)
    qk = jnp.einsum("nbhcd,nbhsd->nbhcs", qc, kc) * decay
    qg = qc * jnp.exp(gc)[..., None]
    kd = kc * jnp.exp(gc[..., -1:] - gc)[..., None]
    glast = jnp.exp(gc[..., -1])

    def step(state, xs):
        w_n, u_n, qk_n, qg_n, kd_n, gl_n = xs
        v_new = u_n - jnp.einsum("bhck,bhkv->bhcv", w_n, state)
        o_n = jnp.einsum("bhck,bhkv->bhcv", qg_n, state) + jnp.einsum("bhcs,bhsv->bhcv", qk_n, v_new)
        state = state * gl_n[..., None, None] + jnp.einsum("bhck,bhcv->bhkv", kd_n, v_new)
        return state, o_n

    s0 = jnp.zeros((bsz, nh, dk, dv), jnp.float32)
    _, o = lax.scan(step, s0, (w, u, qk, qg, kd, glast))
    return jnp.swapaxes(jnp.moveaxis(o, 0, 1), 2, 3).reshape(bsz, seq, nh, dv)


def _gated_deltanet(qkv, z, a, b, conv_w, a_log, dt_bias, norm_g):
    bsz, seq, _ = qkv.shape
    h = jax.nn.silu(_causal_dwconv(qkv, conv_w)).astype(jnp.float32)
    q, k, v = jnp.split(h, [GDN_QK, 2 * GDN_QK], axis=-1)
    q = _l2norm(q.reshape(bsz, seq, GDN_HEADS, GDN_DK)) * (GDN_DK ** -0.5)
    k = _l2norm(k.reshape(bsz, seq, GDN_HEADS, GDN_DK))
    v = v.reshape(bsz, seq, GDN_HEADS, GDN_DV)
    beta = jax.nn.sigmoid(b.astype(jnp.float32))
    g = -jnp.exp(a_log.astype(jnp.float32)) * jax.nn.softplus(
        a.astype(jnp.float32) + dt_bias.astype(jnp.float32))
    o = _chunk_gated_delta_rule(q, k, v, g, beta)
    o = _rmsnorm(o, norm_g) * jax.nn.silu(z.astype(jnp.float32).reshape(bsz, seq, GDN_HEADS, GDN_DV))
    return o.reshape(bsz, seq, GDN_VW).astype(z.dtype)


def _mla(cq, ckv, k_rope, q_norm_g, w_q_up, kv_norm_g, w_kv_up):
    bsz, seq, _ = cq.shape
    q = (_rmsnorm(cq, q_norm_g) @ w_q_up).reshape(bsz, seq, MLA_HEADS, MLA_NOPE + MLA_ROPE)
    q_nope, q_rope = q[..., :MLA_NOPE], q[..., MLA_NOPE:]
    kv = (_rmsnorm(ckv, kv_norm_g) @ w_kv_up).reshape(bsz, seq, MLA_HEADS, MLA_NOPE + MLA_V)
    k_nope, v = kv[..., :MLA_NOPE], kv[..., MLA_NOPE:]
    cos, sin = _rope_tables(seq)
    q_rope = _apply_rope(q_rope, cos[:, None, :], sin[:, None, :])
    k_rope = _apply_rope(k_rope, cos, sin)
    nqb = seq // Q_BLOCK

    def qblocks(t):
        return jnp.moveaxis(t.reshape(bsz, nqb, Q_BLOCK, MLA_HEADS, t.shape[-1]), 1, 0)

    key_chunk = jnp.arange(seq) // CHUNK
    scale = (MLA_NOPE + MLA_ROPE) ** -0.5

    def attend(xs):
        qn, qr, blk = xs
        s = jnp.einsum("bqhd,bkhd->bhqk", qn, k_nope) + jnp.einsum("bqhd,bkd->bhqk", qr, k_rope)
        q_chunk = (blk * Q_BLOCK + jnp.arange(Q_BLOCK)) // CHUNK
        allowed = key_chunk[None, :] <= q_chunk[:, None]
        s = jnp.where(allowed, s.astype(jnp.float32) * scale, -jnp.inf)
        pr = jax.nn.softmax(s, axis=-1).astype(v.dtype)
        return jnp.einsum("bhqk,bkhd->bqhd", pr, v)

    o = lax.map(attend, (qblocks(q_nope), qblocks(q_rope), jnp.arange(nqb)))
    return jnp.moveaxis(o, 0, 1).reshape(bsz, seq, MLA_HEADS * MLA_V)


def _conv_ffn(h, w_up, conv_w, conv_b, w_down):
    u = _causal_dwconv(h @ w_up, conv_w) + conv_b
    gate, up = jnp.split(u, 2, axis=-1)
    return (jax.nn.silu(gate) * up) @ w_down


def setup_inputs(seed: int = 0) -> dict:
    key = jax.random.key(seed)
    ks = jax.random.split(key, 24)
    f32 = jnp.float32
    L = DEPTH

    def nrm(k, shape, scale):
        return jax.random.normal(k, shape, f32) * scale

    dt = jnp.exp(jax.random.uniform(ks[5], (L, GDN_HEADS), f32, math.log(1e-3), math.log(1e-1)))
    return {
        "x": nrm(ks[0], (BATCH, SEQ, D_MODEL), 1.0),
        "p": nrm(ks[1], (L, BATCH, SEQ, PLE_DIM), 1.0),
        "w_in": nrm(ks[2], (L, D_MODEL, D_IN), D_MODEL ** -0.5),
        "gdn_conv_w": nrm(ks[3], (L, GDN_CONV, 2 * GDN_QK + GDN_VW), GDN_CONV ** -0.5),
        "gdn_a_log": jnp.log(jax.random.uniform(ks[4], (L, GDN_HEADS), f32, 1.0, 16.0)),
        "gdn_dt_bias": dt + jnp.log(-jnp.expm1(-dt)),
        "gdn_norm_g": 1.0 + nrm(ks[6], (L, GDN_DV), 0.02),
        "mla_q_norm_g": 1.0 + nrm(ks[7], (L, MLA_Q_LORA), 0.02),
        "mla_w_q_up": nrm(ks[8], (L, MLA_Q_LORA, MLA_HEADS * (MLA_NOPE + MLA_ROPE)), MLA_Q_LORA ** -0.5),
        "mla_kv_norm_g": 1.0 + nrm(ks[9], (L, MLA_KV_LORA), 0.02),
        "mla_w_kv_up": nrm(ks[10], (L, MLA_KV_LORA, MLA_HEADS * (MLA_NOPE + MLA_V)), MLA_KV_LORA ** -0.5),
        "w_out": nrm(ks[11], (L, D_MIX, D_MODEL), BETA * D_MIX ** -0.5),
        "ln1_g": 1.0 + nrm(ks[12], (L, D_MODEL), 0.02),
        "ln1_b": nrm(ks[13], (L, D_MODEL), 0.02),
        "ffn_w_up": nrm(ks[14], (L, D_MODEL, 2 * D_FF), D_MODEL ** -0.5),
        "ffn_conv_w": nrm(ks[15], (L, FFN_CONV, 2 * D_FF), FFN_CONV ** -0.5),
        "ffn_conv_b": nrm(ks[16], (L, 2 * D_FF), 0.01),
        "ffn_w_down": nrm(ks[17], (L, D_FF, D_MODEL), BETA * D_FF ** -0.5),
        "ple_w_gate": nrm(ks[18], (L, D_MODEL, D_MODEL), D_MODEL ** -0.5),
        "ple_b_gate": nrm(ks[19], (L, D_MODEL), 0.01),
        "ple_w_proj": nrm(ks[20], (L, PLE_DIM, D_MODEL), BETA * PLE_DIM ** -0.5),
        "ln2_g": 1.0 + nrm(ks[21], (L, D_MODEL), 0.02),
        "ln2_b": nrm(ks[22], (L, D_MODEL), 0.02),
    }


def reference(x, p, w_in, gdn_conv_w, gdn_a_log, gdn_dt_bias, gdn_norm_g, mla_q_norm_g, mla_w_q_up,
              mla_kv_norm_g, mla_w_kv_up, w_out, ln1_g, ln1_b, ffn_w_up, ffn_conv_w, ffn_conv_b,
              ffn_w_down, ple_w_gate, ple_b_gate, ple_w_proj, ln2_g, ln2_b):
    o_qkv = 2 * GDN_QK + GDN_VW
    o_z = o_qkv + GDN_VW
    o_a = o_z + GDN_HEADS
    o_b = o_a + GDN_HEADS
    o_cq = o_b + MLA_Q_LORA
    o_ckv = o_cq + MLA_KV_LORA
    h = x
    for i in range(DEPTH):
        proj = h @ w_in[i]
        qkv, z, a, b, cq, ckv, k_rope = jnp.split(proj, [o_qkv, o_z, o_a, o_b, o_cq, o_ckv], axis=-1)
        out_a = _gated_deltanet(qkv, z, a, b, gdn_conv_w[i], gdn_a_log[i], gdn_dt_bias[i], gdn_norm_g[i])
        out_b = _mla(cq, ckv, k_rope, mla_q_norm_g[i], mla_w_q_up[i], mla_kv_norm_g[i], mla_w_kv_up[i])
        mix = jnp.concatenate([out_a, out_b], axis=-1) @ w_out[i]
        h = _layernorm(ALPHA * h + mix, ln1_g[i], ln1_b[i])
        ffn = _conv_ffn(h, ffn_w_up[i], ffn_conv_w[i], ffn_conv_b[i], ffn_w_down[i])
        ple = jax.nn.sigmoid(h @ ple_w_gate[i] + ple_b_gate[i]) * (p[i] @ ple_w_proj[i])
        h = _layernorm(ALPHA * h + ffn + ple, ln2_g[i], ln2_b[i])
    return h
```

```python
import numpy as np
from contextlib import ExitStack
import concourse.bass as bass
import concourse.mybir as mybir
from concourse.bass_utils import run_bass_kernel_spmd

F32, BF16 = mybir.dt.float32, mybir.dt.bfloat16
AF = mybir.ActivationFunctionType
ALU = mybir.AluOpType

S = 2048
NT = 16
D = 1024
KC = 8
DFF = 2816
NJ = 22
ALPHA = float(2.0 ** 0.25)
EPS = 1e-6
BLK2 = 512
HS = 1024
NBLK = 2
NTH = 8
EPOCH = 16000
NCORES = 8
TWO_CHAINS = True
OVERLAP_A = False
SEQ_GENS = True


class Trk:
    def __init__(self, nc, es):
        self.nc, self.es = nc, es
        self.engs = {'pe': nc.tensor, 'act': nc.scalar, 'dve': nc.vector, 'pool': nc.gpsimd, 'sp': nc.sync}
        self.cnt = {e: 0 for e in self.engs}
        self.esems = {e: [] for e in self.engs}
        self.seen = {e: {} for e in self.engs}
        self.lastw = {}
        self.rd = {}
        self.dsem = {}
        self.pend = {e: ([], []) for e in self.engs}
        self.latest = {}
        self.log = {e: [] for e in self.engs}

    def newsem(self, name):
        return self.es.enter_context(self.nc.semaphore(name))

    def _wait(self, e, ev):
        sem, val, src = ev
        if src == 'pe' and e == 'pe':
            return
        k = id(sem)
        if self.seen[e].get(k, 0) >= val:
            return
        self.engs[e].wait_ge(sem, val)
        self.log[e].append(('w', id(sem), val))
        self.seen[e][k] = val

    def _deps(self, e, reads, writes):
        for k in reads:
            ev = self.lastw.get(k)
            if ev is not None:
                self._wait(e, ev)
        for k in writes:
            ev = self.lastw.get(k)
            if ev is not None:
                self._wait(e, ev)
            for ev in self.rd.get(k, {}).values():
                self._wait(e, ev)

    def _reg(self, ev, reads, writes):
        sem, val, src = ev
        self.latest[id(sem)] = (sem, val)
        for k in writes:
            self.lastw[k] = ev
            self.rd[k] = {}
        for k in reads:
            self.rd.setdefault(k, {})[id(sem)] = ev

    def issue(self, e, fn, reads=(), writes=(), inc=True):
        writes = list(writes) + [k for k in reads if k[0] == 'ps' and k not in writes]
        reads = [k for k in reads if k[0] != 'ps']
        self._deps(e, reads, writes)
        ins = fn(self.engs[e])
        pr, pw = self.pend[e]
        pr.extend(reads)
        pw.extend(writes)
        if inc:
            n = self.cnt[e]
            ep, off = divmod(n, EPOCH)
            if ep >= len(self.esems[e]):
                self.esems[e].append(self.newsem(f"s_{e}_{ep}"))
            sem = self.esems[e][ep]
            ins.then_inc(sem, 1)
            self.log[e].append(('i', id(sem), 1))
            self.cnt[e] = n + 1
            self._reg((sem, off + 1, e), pr, pw)
            self.pend[e] = ([], [])
        return ins

    def dma(self, q, out, in_, reads=(), writes=(), key=None):
        self._deps(q, reads, writes)
        ins = self.engs[q].dma_start(out=out, in_=in_)
        if key not in self.dsem:
            self.dsem[key] = [self.newsem("d_" + "_".join(str(x) for x in key)), 0]
        d = self.dsem[key]
        d[1] += 16
        ins.then_inc(d[0], 16)
        self.log[q].append(('i', id(d[0]), 16))
        self._reg((d[0], d[1], 'dma'), list(reads), list(writes))
        return ins

    def barrier(self):
        for e in self.engs:
            assert not self.pend[e][0] and not self.pend[e][1], e
        for sem, val in list(self.latest.values()):
            self._wait('sp', (sem, val, 'x'))
        n = self.cnt['sp']
        ep, off = divmod(n, EPOCH)
        if ep >= len(self.esems['sp']):
            self.esems['sp'].append(self.newsem(f"s_sp_{ep}"))
        sem = self.esems['sp'][ep]
        self.engs['sp'].sem_inc(sem, 1)
        self.log['sp'].append(('i', id(sem), 1))
        self.cnt['sp'] = n + 1
        ev = (sem, off + 1, 'sp')
        self.latest[id(sem)] = (sem, off + 1)
        self.seen['sp'][id(sem)] = off + 1
        for e in self.engs:
            if e != 'sp':
                self._wait(e, ev)
        self.lastw.clear()
        self.rd.clear()

    def final_wait(self):
        for sem, val in list(self.latest.values()):
            self._wait('sp', (sem, val, 'x'))


class _Stop(Exception):
    pass


def build(NSEQ, dbg=None, stop=None):
    dbg = dbg or {}
    nc = bass.Bass("TRN2", target_bir_lowering=False)

    def din(name, shape, dt=F32):
        return nc.dram_tensor(name, list(shape), dt, kind="ExternalInput").ap()

    x = din("x", [NSEQ * S, D])
    p = din("p", [NSEQ * S, 256])
    w_in = din("w_in", [D, 2760])
    wq_up = din("wq_up", [384, 768])
    wkv_up = din("wkv_up", [256, 1024])
    w_out = din("w_out", [1024, 1024])
    w_up = din("w_up", [D, 2 * DFF])
    w_down = din("w_down", [DFF, D])
    w_gate = din("w_gate", [D, D])
    w_proj = din("w_proj", [256, D])
    cpp = din("cpp", [128, 512])
    cbc = din("cbc", [128, 5 * 1024 + 128])
    cmat = din("cmat", [128, 14 * 128])
    ctab = din("ctab", [64, 2 * S])
    out = nc.dram_tensor("out", [NSEQ * S, D], F32, kind="ExternalOutput").ap()
    hscr = nc.dram_tensor("hscr", [S, D], F32).ap()
    dbg_t = {k: nc.dram_tensor("dbg_" + k, list(v[0]), v[1], kind="ExternalOutput").ap() for k, v in dbg.items()}

    w_in_v = w_in.rearrange("(kc p) n -> p kc n", p=128)
    wq_v = wq_up.rearrange("(kc p) n -> p kc n", p=128)
    wkv_v = wkv_up.rearrange("(kc p) n -> p kc n", p=128)
    w_out_v = w_out.rearrange("(kc p) n -> p kc n", p=128)
    w_up_v = w_up.rearrange("(kc p) n -> p kc n", p=128)
    w_down_v = w_down.rearrange("(kc p) n -> p kc n", p=128)
    w_gate_v = w_gate.rearrange("(kc p) n -> p kc n", p=128)
    w_proj_v = w_proj.rearrange("(kc p) n -> p kc n", p=128)

    with ExitStack() as es:
        T = Trk(nc, es)

        uid = [0]

        def sb(scope, name, shape, dt):
            uid[0] += 1
            return scope.enter_context(nc.sbuf_tensor(f"{name}_{uid[0]}", list(shape), dt))

        ps_all = es.enter_context(nc.psum_tensor("ps_all", [128, 8 * 512], F32))
        ps = [ps_all[:, b * 512:(b + 1) * 512] for b in range(8)]

        cpp_t = sb(es, "cpp_t", [128, 512], F32)
        cbc_t = sb(es, "cbc_t", [128, 128], F32)
        cmat_t = sb(es, "cmat_t", [128, 1792], F32)
        identb = sb(es, "identb", [128, 128], BF16)
        onesb = sb(es, "onesb", [128, 128], BF16)
        c128b = sb(es, "c128b", [128, 128], BF16)
        c256b = sb(es, "c256b", [128, 128], BF16)
        onesf = sb(es, "onesf", [128, 128], F32)
        w_ab = sb(es, "w_ab", [128, 8, 8], BF16)
        xhT = sb(es, "xhT", [128, 8, S], BF16)

        T.dma('sp', cpp_t[:], cpp[:, :], writes=[('cpp',)], key=('cpp',))
        T.dma('sp', cbc_t[:], cbc[:, 5120:5248], writes=[('cbc',)], key=('cbc',))
        T.dma('sp', cmat_t[:], cmat[:, :], writes=[('cmat',)], key=('cmat',))
        T.dma('pool', w_ab[:], w_in_v[:, :, 2048:2056], writes=[('w_ab',)], key=('w_ab',))
        ident = cmat_t[:, 0:128]
        Umat = cmat_t[:, 128:256]
        NEGM = cmat_t[:, 256:384]
        NEGM4 = cmat_t[:, 256:768].rearrange("p (i c) -> p i c", i=4)
        NEGMS4 = cmat_t[:, 768:1280].rearrange("p (i c) -> p i c", i=4)
        ident4 = cmat_t[:, 1280:1792].rearrange("p (i c) -> p i c", i=4)
        T.issue('dve', lambda e: e.tensor_copy(out=identb[:], in_=ident), reads=[('cmat',)], writes=[('identb',)])
        T.issue('pool', lambda e: e.memset(onesb[:], 1.0), writes=[('onesb',)])
        T.issue('pool', lambda e: e.memset(c128b[:], 1.0 / 128), writes=[('c128b',)])
        T.issue('pool', lambda e: e.memset(c256b[:], 1.0 / 256), writes=[('c256b',)])
        T.issue('pool', lambda e: e.memset(onesf[:], 1.0), writes=[('onesf',)])
        epst = sb(es, "epst", [128, 2], F32)
        T.issue('pool', lambda e: e.memset(epst[:, 0:1], EPS), writes=[('epst',)])
        T.issue('pool', lambda e: e.memset(epst[:, 1:2], 384 * EPS), writes=[('epst',)])
        eps1 = epst[:, 0:1]
        eps384 = epst[:, 1:2]
        gcw = cpp_t[:, 0:48].rearrange("p (c j) -> p c j", j=4)
        fcw = cpp_t[:, 48:180].rearrange("p (c j) -> p c j", j=3)
        fcb = cpp_t[:, 180:224]
        normg = cpp_t[:, 224:225]
        qg = cpp_t[:, 225:228]
        kvg = cpp_t[:, 228:230]
        ALOGB = cbc_t[:, 0:64]
        DTBB = cbc_t[:, 64:128]
        CONST = [('cpp',), ('cbc',), ('cmat',)]

        def act(out_, in_, func, reads, writes, **kw):
            return T.issue('act', lambda e: e.activation(out=out_, in_=in_, func=func, **kw), reads, writes)

        def tt(eng, out_, in0, in1, op, reads, writes):
            return T.issue(eng, lambda e: e.tensor_tensor(out=out_, in0=in0, in1=in1, op=op), reads, writes)

        def ts(eng, out_, in0, s1, s2, op0, op1, reads, writes):
            if s2 is None:
                return T.issue(eng, lambda e: e.tensor_scalar(out=out_, in0=in0, scalar1=s1, scalar2=None, op0=op0), reads, writes)
            return T.issue(eng, lambda e: e.tensor_scalar(out=out_, in0=in0, scalar1=s1, scalar2=s2, op0=op0, op1=op1), reads, writes)

        def stt(out_, in0, sc, in1, op0, op1, reads, writes):
            return T.issue('dve', lambda e: e.scalar_tensor_tensor(out=out_, in0=in0, scalar=sc, in1=in1, op0=op0, op1=op1), reads, writes)

        def cp(eng, out_, in_, reads, writes):
            if eng == 'act':
                return T.issue('act', lambda e: e.copy(out=out_, in_=in_), reads, writes)
            return T.issue(eng, lambda e: e.tensor_copy(out=out_, in_=in_), reads, writes)

        def rsqrt(out_, in_, eps_ap, reads, writes):
            act(out_, in_, AF.Ln, list(reads) + [('epst',)], writes, bias=eps_ap)
            act(out_, out_, AF.Exp, writes, writes, scale=-0.5)

        def amul(out_, in_, m, reads, writes):
            return T.issue('act', lambda e: e.mul(out=out_, in_=in_, mul=m), reads, writes)

        def mm(out_, lhsT, rhs, start, stop, reads, writes, inc=None):
            if inc is None:
                inc = stop
            return T.issue('pe', lambda e: e.matmul(out_, lhsT, rhs, start=start, stop=stop), reads, writes, inc=inc)

        def tr(out_, in_, reads, writes, inc=True):
            return T.issue('pe', lambda e: e.transpose(out_, in_, ident), list(reads) + [('cmat',)], writes, inc=inc)

        def dump(name, src, reads):
            if name in dbg_t:
                T.dma('sp', dbg_t[name], src, reads=reads, writes=[('dbg', name)], key=('dbg', name))

        def ln_tile(r, G, B, scope_tiles, key_r, out_tile, key_out, kc_=('cbcL',), sfx=''):
            stats, mv, rs = scope_tiles
            for hb in range(2):
                T.issue('dve', lambda e: e.bn_stats(out=stats[:, hb * 6:(hb + 1) * 6], in_=r[:, hb * 512:(hb + 1) * 512]),
                        reads=[key_r], writes=[('lnst', sfx)])
            T.issue('dve', lambda e: e.bn_aggr(out=mv[:], in_=stats[:]), reads=[('lnst', sfx)], writes=[('lnmv', sfx)])
            rsqrt(rs[:], mv[:, 1:2], eps1, [('lnmv', sfx)], [('lnrs', sfx)])
            ts('dve', r[:], r[:], mv[:, 0:1], rs[:, 0:1], ALU.subtract, ALU.mult, [key_r, ('lnmv', sfx), ('lnrs', sfx)], [key_r])
            tt('dve', r[:], r[:], G, ALU.mult, [key_r, kc_], [key_r])
            tt('dve', out_tile, r[:], B, ALU.add, [key_r, kc_], [key_out])

        if True:
          def seq_body(sq):
            row0 = sq * S
            with ExitStack() as p1:
                mixT = sb(p1, "mixT", [128, 8, S], BF16)
                xin = [sb(p1, f"xin{i}", [128, D], F32) for i in range(2)]
                if True:
                    for t in range(NT):
                        sl = t % 2
                        T.dma('sp', xin[sl][:], x[row0 + t * 128: row0 + (t + 1) * 128, :], writes=[('xin', sl)], key=('xin', sl))
                        pb = (t % 2) * 2
                        for kc in range(8):
                            bank = pb + kc // 4
                            col = (kc % 4) * 128
                            tr(ps[bank][:, col:col + 128], xin[sl][:, kc * 128:(kc + 1) * 128], [('xin', sl)], [('ps', bank)], inc=(kc % 4 == 3))
                        for hb in range(2):
                            bank = pb + hb
                            cp('act' if hb == 0 else 'dve', xhT[:, hb * 4:(hb + 1) * 4, t * 128:(t + 1) * 128],
                               ps[bank][:, :].rearrange("p (k c) -> p k c", k=4), [('ps', bank)], [('xhT', t)])
                if stop == 'xT':
                    T.barrier()
                dump('xT', xhT[:], [('xhT', t) for t in range(NT)])
                if stop == 'xT':
                    return True

                with ExitStack() as sc:
                    g_ab = sb(sc, "g_ab", [128, 128], F32)
                    g_beta = sb(sc, "g_beta", [128, 64], F32)
                    g_g = sb(sc, "g_g", [128, 64], F32)
                    g_tmp = sb(sc, "g_tmp", [128, 64], F32)
                    g_eal = sb(sc, "g_eal", [128, 64], F32)
                    g_gc = sb(sc, "g_gc", [128, 64], F32)
                    g_ngc = sb(sc, "g_ngc", [128, 64], F32)
                    g_eg = sb(sc, "g_eg", [128, 64], F32)
                    g_egl = sb(sc, "g_egl", [128, 64], F32)
                    g_egla = sb(sc, "g_egla", [128, 64], F32)
                    raw2 = [sb(sc, f"raw{i}", [128, 3 + HS], F32) for i in range(2)]
                    acc = sb(sc, "acc", [128, HS], F32)
                    sil2 = [sb(sc, f"sil{i}", [128, HS], F32) for i in range(2)]
                    sqb = sb(sc, "sqb", [128, HS], BF16)
                    rstd = [sb(sc, f"rstd{i}", [128, 512], F32) for i in range(2)]
                    halo_g = sb(sc, "halo_g", [128, 12, 3], F32)
                    zero3 = sb(sc, "zero3", [128, 3], F32)
                    wst = [sb(sc, f"wst{i}", [128, 8, 128], BF16) for i in range(3)]
                    hq = sb(sc, "hq", [128, 4, HS], BF16)
                    hk = sb(sc, "hk", [128, 4, HS], BF16)
                    hkg = sb(sc, "hkg", [128, 4, NTH, 128], BF16)
                    hkd = sb(sc, "hkd", [128, 4, NTH, 128], BF16)
                    hv = sb(sc, "hv", [128, 4, NTH, 128], BF16)
                    hz = sb(sc, "hz", [128, 4, HS], BF16)
                    S32 = sb(sc, "S32", [128, 4, 128], F32)
                    Sbf = sb(sc, "Sbf", [128, 4, 128], BF16)

                    def tmpp(name, dt):
                        return sb(sc, name, [128, 4, 128], dt)
                    Ug = tmpp("Ug", F32)
                    EGb = tmpp("EGb", F32)
                    ARG = tmpp("ARG", F32)
                    ARG2 = tmpp("ARG2", F32)
                    DT = tmpp("DT", F32)
                    DTs = tmpp("DTs", F32)
                    Nf = tmpp("Nf", F32)
                    Pb = [tmpp("Pb0_", BF16), tmpp("Pb1_", BF16)]
                    PTb = [tmpp("PTb0_", BF16), tmpp("PTb1_", BF16)]
                    Xb = [tmpp("Xb0_", BF16), tmpp("Xb1_", BF16)]
                    QKD = tmpp("QKD", BF16)
                    nw2T = tmpp("nw2T", BF16)
                    vnew = tmpp("vnew", BF16)
                    qgT = tmpp("qgT", BF16)
                    sqo = tmpp("sqo", BF16)
                    rso = tmpp("rso", F32)
                    o1 = tmpp("o1", F32)

                    T.issue('pool', lambda e: e.memset(zero3[:], 0.0), writes=[('zero3',)])
                    cur_half = [0]

                    for t in range(NT):
                        for kc in range(8):
                            mm(ps[7][:, t * 8:(t + 1) * 8], xhT[:, kc, t * 128:(t + 1) * 128], w_ab[:, kc, :], kc == 0, kc == 7,
                               [('xhT', t), ('w_ab',)], [('ps', 7)], inc=(kc == 7 and t == NT - 1))
                    cp('dve', g_ab[:], ps[7][:, 0:128], [('ps', 7)], [('g_ab',)])
                    abv = g_ab[:].rearrange("p (t c) -> p t c", c=8)
                    v64 = lambda tl: tl[:].rearrange("p (t c) -> p t c", c=4)
                    act(v64(g_beta), abv[:, :, 4:8], AF.Sigmoid, [('g_ab',)], [('g_beta',)])
                    tt('dve', v64(g_tmp), abv[:, :, 0:4], DTBB.rearrange("p (t c) -> p t c", c=4), ALU.add, [('g_ab',), ('cbc',)], [('g_tmp',)])
                    act(g_tmp[:], g_tmp[:], AF.Exp, [('g_tmp',)], [('g_tmp',)])
                    ts('dve', g_tmp[:], g_tmp[:], 1.0, None, ALU.add, None, [('g_tmp',)], [('g_tmp',)])
                    act(g_tmp[:], g_tmp[:], AF.Ln, [('g_tmp',)], [('g_tmp',)])
                    act(g_eal[:], ALOGB, AF.Exp, [('cbc',)], [('g_eal',)])
                    stt(g_g[:], g_tmp[:], -1.0, g_eal[:], ALU.mult, ALU.mult, [('g_tmp',), ('g_eal',)], [('g_g',)])
                    mm(ps[7][:, 128:192], Umat, g_g[:], True, True, [('cmat',), ('g_g',)], [('ps', 7)])
                    mm(ps[7][:, 192:256], onesf[:], g_g[:], True, True, [('onesf',), ('g_g',)], [('ps', 7)])
                    cp('dve', g_gc[:], ps[7][:, 128:192], [('ps', 7)], [('g_gc',)])
                    ts('dve', g_ngc[:], g_gc[:], -1.0, None, ALU.mult, None, [('g_gc',)], [('g_ngc',)])
                    act(g_eg[:], g_gc[:], AF.Exp, [('g_gc',)], [('g_eg',)])
                    tt('dve', g_egl[:], ps[7][:, 192:256], g_gc[:], ALU.subtract, [('ps', 7), ('g_gc',)], [('g_egl',)])
                    act(g_egl[:], g_egl[:], AF.Exp, [('g_egl',)], [('g_egl',)])
                    act(g_egla[:], ps[7][:, 192:256], AF.Exp, [('ps', 7)], [('g_egla',)])
                    GS = [('g_beta',), ('g_gc',), ('g_ngc',), ('g_eg',), ('g_egl',), ('g_egla',), ('g_g',)]

                    wcnt = [0]

                    def load_wchunk(c0):
                        sl = wcnt[0] % 3
                        wcnt[0] += 1
                        T.dma('pool', wst[sl][:], w_in_v[:, :, c0:c0 + 128], writes=[('wst', sl)], key=('wst', sl))
                        return sl

                    def proj_block(sl, tb, bank):
                        gtb = cur_half[0] * NBLK + tb
                        for kc in range(8):
                            mm(ps[bank][:, :], wst[sl][:, kc, :], xhT[:, kc, gtb * 512:(gtb + 1) * 512], kc == 0, kc == 7,
                               [('wst', sl)] + [('xhT', gtb * 4 + i) for i in range(4)], [('ps', bank)])

                    trc = [0]

                    def stage_P(ch, tb):
                        kind, h, cidx, sl, ci = ch
                        par = h
                        rp = ci % 2
                        raw = raw2[rp]
                        bank = 6 + tb % 2
                        cs = slice(tb * 512, (tb + 1) * 512)
                        proj_block(sl, tb, bank)
                        if kind == 'z':
                            act(hz[:, par, cs], ps[bank][:, :], AF.Silu, [('ps', bank)], [('hz', par)])
                            return
                        if tb == 0:
                            if cur_half[0] == 0:
                                cp('act', raw[:, 0:3], zero3[:], [('zero3',)], [('raw', rp, -1)])
                            else:
                                cp('act', raw[:, 0:3], halo_g[:, cidx, :], [('halo_g', cidx)], [('raw', rp, -1)])
                        cp('act', raw[:, 3 + tb * 512: 3 + (tb + 1) * 512], ps[bank][:, :], [('ps', bank)], [('raw', rp, tb)])
                        if tb == NBLK - 1 and cur_half[0] == 0:
                            cp('act', halo_g[:, cidx, :], raw[:, HS:HS + 3], [('raw', rp, tb)], [('halo_g', cidx)])

                    def stage_C(ch, tb):
                        kind, h, cidx, sl, ci = ch
                        if kind == 'z':
                            return
                        rp = ci % 2
                        raw = raw2[rp]
                        cs = slice(tb * 512, (tb + 1) * 512)
                        RK = [('raw', rp, tb), ('raw', rp, tb - 1), ('cpp',)]
                        ka = ('acc', tb)
                        ts('dve', acc[:, cs], raw[:, 3 + tb * 512: 3 + (tb + 1) * 512], gcw[:, cidx, 3:4], None, ALU.mult, None, RK, [ka])
                        for j in (2, 1, 0):
                            stt(acc[:, cs], raw[:, j + tb * 512: j + (tb + 1) * 512], gcw[:, cidx, j:j + 1], acc[:, cs], ALU.mult, ALU.add,
                                RK + [ka], [ka])

                    def stage_S(ch, tb):
                        kind, h, cidx, sl, ci = ch
                        if kind == 'z':
                            return
                        sp_ = ci % 2
                        cs = slice(tb * 512, (tb + 1) * 512)
                        act(sil2[sp_][:, cs], acc[:, cs], AF.Silu, [('acc', tb)], [('sil', sp_, tb)])

                    def stage_N(ch):
                        kind, h, cidx, sl, ci = ch
                        if kind not in ('q', 'k'):
                            return
                        par = h
                        sp_ = ci % 2
                        sl_ = sil2[sp_]
                        for tb in range(NBLK):
                            cs = slice(tb * 512, (tb + 1) * 512)
                            act(sqb[:, cs], sl_[:, cs], AF.Square, [('sil', sp_, tb)], [('sqb', tb)])
                            mm(ps[2 + tb][:, :], onesb[:], sqb[:, cs], True, True, [('onesb',), ('sqb', tb)], [('ps', 2 + tb)])
                        for tb in range(NBLK):
                            act(rstd[tb][:], ps[2 + tb][:, :], AF.Ln, [('ps', 2 + tb), ('epst',)], [('rstd', tb)], bias=eps1)
                        for tb in range(NBLK):
                            act(rstd[tb][:], rstd[tb][:], AF.Exp, [('rstd', tb)], [('rstd', tb)], scale=-0.5)
                        for tb in range(NBLK):
                            cs = slice(tb * 512, (tb + 1) * 512)
                            ks = ('sil', sp_, tb)
                            if kind == 'q':
                                stt(hq[:, par, cs], sl_[:, cs], float(128 ** -0.5), rstd[tb][:], ALU.mult, ALU.mult,
                                    [ks, ('rstd', tb)], [('hq', par)])
                            else:
                                tt('dve', sl_[:, cs], sl_[:, cs], rstd[tb][:], ALU.mult, [ks, ('rstd', tb)], [ks])
                                cp('act', hk[:, par, cs], sl_[:, cs], [ks], [('hk', par)])

                    def stage_T(ch, tb):
                        kind, h, cidx, sl, ci = ch
                        if kind not in ('k', 'v'):
                            return
                        par = h
                        sp_ = ci % 2
                        ks = ('sil', sp_, tb)
                        for tl in range(4):
                            n = tb * 4 + tl
                            c = (cur_half[0] * NTH + n) * 4 + h
                            b3 = trc[0] % 2
                            trc[0] += 1
                            tr(ps[b3][:, 0:128], sil2[sp_][:, n * 128:(n + 1) * 128], [ks], [('ps', b3)])
                            if kind == 'k':
                                amul(hkg[:, par, n, :], ps[b3][:, 0:128], g_eg[:, c:c + 1], [('ps', b3), ('g_eg',)], [('hkg', par)])
                                ts('dve', hkd[:, par, n, :], ps[b3][:, 0:128], g_egl[:, c:c + 1], None, ALU.mult, None,
                                   [('ps', b3), ('g_egl',)], [('hkd', par)])
                            else:
                                cp('act' if n % 2 else 'dve', hv[:, par, n, :], ps[b3][:, 0:128], [('ps', b3)], [('hv', par)])

                    def run_A_quad():
                        chs = []
                        for h in range(4):
                            for kind, c0, cidx in (('q', h * 128, h), ('k', 512 + h * 128, 4 + h), ('v', 1024 + h * 128, 8 + h), ('z', 1536 + h * 128, 0)):
                                chs.append([kind, h, cidx, None, len(chs), c0])
                        nch = len(chs)
                        loaded = [0]

                        def ensure_loaded(upto):
                            while loaded[0] <= min(upto, nch - 1):
                                ch_ = chs[loaded[0]]
                                ch_[3] = load_wchunk(ch_[5])
                                loaded[0] += 1
                        nit = NBLK * nch
                        for tau in range(nit + NBLK + 5):
                            if tau < nit:
                                i, tb = divmod(tau, NBLK)
                                if tb == 0:
                                    ensure_loaded(i + 1)
                                stage_P(tuple(chs[i][:5]), tb)
                            if 0 <= tau - 1 < nit:
                                i, tb = divmod(tau - 1, NBLK)
                                stage_C(tuple(chs[i][:5]), tb)
                            if 0 <= tau - 2 < nit:
                                i, tb = divmod(tau - 2, NBLK)
                                stage_S(tuple(chs[i][:5]), tb)
                            tn = tau - (NBLK + 2)
                            if tn >= 0 and tn % NBLK == 0 and tn // NBLK < nch:
                                stage_N(tuple(chs[tn // NBLK][:5]))
                            if 0 <= tau - (NBLK + 3) < nit:
                                i, tb = divmod(tau - (NBLK + 3), NBLK)
                                stage_T(tuple(chs[i][:5]), tb)

                    def H4(b):
                        return ps[b][:, :].rearrange("p (i c) -> p i c", i=4), ('ps', b)

                    def rec_quad(half):
                        if half == 0:
                            T.issue('pool', lambda e: e.memset(S32[:], 0.0), writes=[('S32',)])
                            T.issue('pool', lambda e: e.memset(Sbf[:], 0.0), writes=[('Sbf',)])
                        pGb, kGb = H4(0)
                        pX, kX = H4(6)
                        pKK, kKK = H4(1)
                        pV, kV = H4(1)
                        pQK, kQK = H4(2)
                        pO, kO = H4(2)
                        pNT, kNT = H4(3)
                        pS, kS = H4(3)
                        pP, kP = H4(4)
                        pR, kR = H4(4)
                        pPT, kPT = H4(5)
                        pW, kW = H4(0)
                        for n in range(NTH):
                            tok = slice(n * 128, (n + 1) * 128)
                            gtok = slice((half * NTH + n) * 128, (half * NTH + n + 1) * 128)
                            cc = [(half * NTH + n) * 4 + i for i in range(4)]
                            for i in range(4):
                                amul(Ug[:, i, :], Umat, g_g[:, cc[i]:cc[i] + 1], [('cmat',), ('g_g',)], [('Ug',)])
                            for i in range(4):
                                mm(pGb[:, i, :], onesf[:], Ug[:, i, :], True, True, [('onesf',), ('Ug',)], [kGb], inc=(i == 3))
                            tt('dve', ARG2[:], pGb, NEGMS4, ALU.add, [kGb, ('cmat',)], [('ARG2',)])
                            tt('dve', ARG[:], pGb, NEGM4, ALU.add, [kGb, ('cmat',)], [('ARG',)])
                            act(EGb[:], pGb, AF.Exp, [kGb], [('EGb',)])
                            for i in range(4):
                                mm(pKK[:, i, :], hk[:, i, tok], hk[:, i, tok], True, True, [('hk', i)], [kKK], inc=(i == 3))
                            for i in range(4):
                                mm(pQK[:, i, :], hk[:, i, tok], hq[:, i, tok], True, True, [('hk', i), ('hq', i)], [kQK], inc=(i == 3))
                            for i in range(4):
                                act(DTs[:, i, :], ARG2[:, i, :], AF.Exp, [('ARG2',), ('g_ngc',)], [('DTs',)], bias=g_ngc[:, cc[i]:cc[i] + 1])
                            for i in range(4):
                                act(DT[:, i, :], ARG[:, i, :], AF.Exp, [('ARG',), ('g_ngc',)], [('DT',)], bias=g_ngc[:, cc[i]:cc[i] + 1])
                            for i in range(4):
                                stt(Nf[:, i, :], pKK[:, i, :], g_beta[:, cc[i]:cc[i] + 1], DTs[:, i, :], ALU.mult, ALU.mult,
                                    [kKK, ('g_beta',), ('DTs',)], [('Nf',)])
                            for i in range(4):
                                tr(pNT[:, i, :], Nf[:, i, :], [('Nf',)], [kNT], inc=(i == 3))
                            cur = 0
                            cp('act', Pb[cur][:], Nf[:], [('Nf',)], [('Pb0',)])
                            cp('dve', PTb[cur][:], pNT, [kNT], [('PTb0',)])
                            tt('dve', Xb[cur][:], ident4, Nf[:], ALU.subtract, [('cmat',), ('Nf',)], [('Xb0',)])
                            tt('dve', QKD[:], pQK, DT[:], ALU.mult, [kQK, ('DT',)], [('QKD',)])
                            tt('pool', qgT[:], hq[:, :, tok], EGb[:], ALU.mult, [('hq', i_) for i_ in range(4)] + [('EGb',)], [('qgT',)])
                            xc = 0

                            def x_update(ptb_idx, step_):
                                nonlocal_xc = x_state[0]
                                xn = 1 - nonlocal_xc
                                for i in range(4):
                                    mm(pX[:, i, :], identb[:], Xb[nonlocal_xc][:, i, :], True, False, [('identb',), (f'Xb{nonlocal_xc}',)], [kX], inc=False)
                                    mm(pX[:, i, :], PTb[ptb_idx][:, i, :], Xb[nonlocal_xc][:, i, :], False, True,
                                       [(f'PTb{ptb_idx}',), (f'Xb{nonlocal_xc}',)], [kX], inc=(i == 3))
                                x_state[0] = xn
                                return xn
                            x_state = [0]
                            pend = None
                            for step in range(6):
                                nx = 1 - cur
                                kPc, kPTc = (f'Pb{cur}',), (f'PTb{cur}',)
                                kPn, kPTn = (f'Pb{nx}',), (f'PTb{nx}',)
                                for i in range(4):
                                    mm(pPT[:, i, :], Pb[cur][:, i, :], PTb[cur][:, i, :], True, True, [kPc, kPTc], [kPT], inc=(i == 3))
                                if step < 5:
                                    for i in range(4):
                                        mm(pP[:, i, :], PTb[cur][:, i, :], Pb[cur][:, i, :], True, True, [kPc, kPTc], [kP], inc=(i == 3))
                                if pend is not None:
                                    xn = x_update(pend, step)
                                cp('dve', PTb[nx][:], pPT, [kPT], [kPTn])
                                if step < 5:
                                    cp('act', Pb[nx][:], pP, [kP], [kPn])
                                if pend is not None:
                                    cp('act' if step % 2 else 'dve', Xb[xn][:], pX, [kX], [(f'Xb{xn}',)])
                                pend = nx
                                cur = nx
                            xn = x_update(pend, 6)
                            cp('act', Xb[xn][:], pX, [kX], [(f'Xb{xn}',)])
                            cur = xn
                            kT2 = (f'Xb{cur}',)
                            T2T = Xb[cur]
                            for i in range(4):
                                mm(pW[:, i, :], hkg[:, i, n, :], T2T[:, i, :], True, True, [('hkg', i), kT2], [kW], inc=(i == 3))
                            amul(nw2T[:], pW, -1.0, [kW], [('nw2T',)])
                            for i in range(4):
                                mm(pV[:, i, :], T2T[:, i, :], hv[:, i, n, :], True, False, [kT2, ('hv', i)], [kV], inc=False)
                                mm(pV[:, i, :], nw2T[:, i, :], Sbf[:, i, :], False, True, [('nw2T',), ('Sbf',)], [kV], inc=(i == 3))
                            for i in range(4):
                                ts('dve', vnew[:, i, :], pV[:, i, :], g_beta[:, cc[i]:cc[i] + 1], None, ALU.mult, None, [kV, ('g_beta',)], [('vnew',)])
                            for i in range(4):
                                mm(pS[:, i, :], hkd[:, i, n, :], vnew[:, i, :], True, True, [('hkd', i), ('vnew',)], [kS], inc=(i == 3))
                            for i in range(4):
                                mm(pO[:, i, :], Sbf[:, i, :], qgT[:, i, :], True, False, [('Sbf',), ('qgT',)], [kO], inc=False)
                                mm(pO[:, i, :], vnew[:, i, :], QKD[:, i, :], False, True, [('vnew',), ('QKD',)], [kO], inc=(i == 3))
                            for i in range(4):
                                stt(S32[:, i, :], S32[:, i, :], g_egla[:, cc[i]:cc[i] + 1], pS[:, i, :], ALU.mult, ALU.add,
                                    [('S32',), ('g_egla',), kS], [('S32',)])
                            cp('act', Sbf[:], S32[:], [('S32',)], [('Sbf',)])
                            act(sqo[:], pO, AF.Square, [kO], [('sqo',)])
                            for i in range(4):
                                mm(pR[:, i, :], c128b[:], sqo[:, i, :], True, True, [('c128b',), ('sqo',)], [kR], inc=(i == 3))
                            rsqrt(rso[:], pR, eps1, [kR], [('rso',)])
                            stt(o1[:], pO, normg, rso[:], ALU.mult, ALU.mult, [kO, ('cpp',), ('rso',)], [('o1',)])
                            tt('pool', mixT[:, 0:4, gtok], o1[:], hz[:, :, tok], ALU.mult, [('o1',)] + [('hz', i_) for i_ in range(4)],
                               [('mixT', i_) for i_ in range(4)])

                    def gen_rec_pair(half, pr):
                        ia = 2 * pr
                        ii = (ia, ia + 1)
                        sl_ = slice(ia, ia + 2)
                        B = 4 * pr

                        def HB(b, hf):
                            return ps[b][:, hf * 256:(hf + 1) * 256].rearrange("p (i c) -> p i c", i=2), ('ps', b)
                        pGb, kGb = HB(B, 0)
                        pW, kW = HB(B, 0)
                        pP, kP = HB(B, 1)
                        pR, kR = HB(B, 1)
                        pKK, kKK = HB(B + 1, 0)
                        pV, kV = HB(B + 1, 0)
                        pPT, kPT = HB(B + 1, 1)
                        pQK, kQK = HB(B + 2, 0)
                        pO, kO = HB(B + 2, 0)
                        pX, kX = HB(B + 2, 1)
                        pNT, kNT = HB(B + 3, 0)
                        pS, kS = HB(B + 3, 0)
                        K = lambda nm: (nm, pr)
                        last = ia + 1
                        for n in range(NTH):
                            tok = slice(n * 128, (n + 1) * 128)
                            gtok = slice((half * NTH + n) * 128, (half * NTH + n + 1) * 128)
                            cc = {i: (half * NTH + n) * 4 + i for i in ii}
                            for i in ii:
                                amul(Ug[:, i, :], Umat, g_g[:, cc[i]:cc[i] + 1], [('cmat',), ('g_g',)], [K('Ug')])
                            yield
                            for i in ii:
                                mm(pGb[:, i - ia, :], onesf[:], Ug[:, i, :], True, True, [('onesf',), K('Ug')], [kGb], inc=(i == last))
                            yield
                            tt('dve', ARG2[:, sl_, :], pGb, NEGMS4[:, 0:2, :], ALU.add, [kGb, ('cmat',)], [K('ARG2')])
                            tt('dve', ARG[:, sl_, :], pGb, NEGM4[:, 0:2, :], ALU.add, [kGb, ('cmat',)], [K('ARG')])
                            act(EGb[:, sl_, :], pGb, AF.Exp, [kGb], [K('EGb')])
                            for i in ii:
                                mm(pKK[:, i - ia, :], hk[:, i, tok], hk[:, i, tok], True, True, [('hk', i)], [kKK], inc=(i == last))
                            for i in ii:
                                mm(pQK[:, i - ia, :], hk[:, i, tok], hq[:, i, tok], True, True, [('hk', i), ('hq', i)], [kQK], inc=(i == last))
                            yield
                            for i in ii:
                                act(DTs[:, i, :], ARG2[:, i, :], AF.Exp, [K('ARG2'), ('g_ngc',)], [K('DTs')], bias=g_ngc[:, cc[i]:cc[i] + 1])
                            for i in ii:
                                act(DT[:, i, :], ARG[:, i, :], AF.Exp, [K('ARG'), ('g_ngc',)], [K('DT')], bias=g_ngc[:, cc[i]:cc[i] + 1])
                            yield
                            for i in ii:
                                stt(Nf[:, i, :], pKK[:, i - ia, :], g_beta[:, cc[i]:cc[i] + 1], DTs[:, i, :], ALU.mult, ALU.mult,
                                    [kKK, ('g_beta',), K('DTs')], [K('Nf')])
                            yield
                            for i in ii:
                                tr(pNT[:, i - ia, :], Nf[:, i, :], [K('Nf')], [kNT], inc=(i == last))
                            cur = 0
                            cp('act', Pb[cur][:, sl_, :], Nf[:, sl_, :], [K('Nf')], [K('Pb0')])
                            yield
                            cp('dve', PTb[cur][:, sl_, :], pNT, [kNT], [K('PTb0')])
                            tt('dve', Xb[cur][:, sl_, :], ident4[:, 0:2, :], Nf[:, sl_, :], ALU.subtract, [('cmat',), K('Nf')], [K('Xb0')])
                            tt('dve', QKD[:, sl_, :], pQK, DT[:, sl_, :], ALU.mult, [kQK, K('DT')], [K('QKD')])
                            tt('pool', qgT[:, sl_, :], hq[:, sl_, tok], EGb[:, sl_, :], ALU.mult, [('hq', i_) for i_ in ii] + [K('EGb')], [K('qgT')])
                            yield
                            xs = [0]

                            def x_update(ptb_idx):
                                xc_ = xs[0]
                                xn_ = 1 - xc_
                                for i in ii:
                                    mm(pX[:, i - ia, :], identb[:], Xb[xc_][:, i, :], True, False, [('identb',), K(f'Xb{xc_}')], [kX], inc=False)
                                    mm(pX[:, i - ia, :], PTb[ptb_idx][:, i, :], Xb[xc_][:, i, :], False, True,
                                       [K(f'PTb{ptb_idx}'), K(f'Xb{xc_}')], [kX], inc=(i == last))
                                xs[0] = xn_
                                return xn_
                            pend = None
                            for step in range(6):
                                nx = 1 - cur
                                kPc, kPTc = K(f'Pb{cur}'), K(f'PTb{cur}')
                                kPn, kPTn = K(f'Pb{nx}'), K(f'PTb{nx}')
                                for i in ii:
                                    mm(pPT[:, i - ia, :], Pb[cur][:, i, :], PTb[cur][:, i, :], True, True, [kPc, kPTc], [kPT], inc=(i == last))
                                if step < 5:
                                    for i in ii:
                                        mm(pP[:, i - ia, :], PTb[cur][:, i, :], Pb[cur][:, i, :], True, True, [kPc, kPTc], [kP], inc=(i == last))
                                if pend is not None:
                                    xn = x_update(pend)
                                yield
                                cp('dve', PTb[nx][:, sl_, :], pPT, [kPT], [kPTn])
                                if step < 5:
                                    cp('act', Pb[nx][:, sl_, :], pP, [kP], [kPn])
                                if pend is not None:
                                    cp('act' if step % 2 else 'dve', Xb[xn][:, sl_, :], pX, [kX], [K(f'Xb{xn}')])
                                yield
                                pend = nx
                                cur = nx
                            xn = x_update(pend)
                            yield
                            cp('act', Xb[xn][:, sl_, :], pX, [kX], [K(f'Xb{xn}')])
                            yield
                            cur = xn
                            kT2 = K(f'Xb{cur}')
                            T2T = Xb[cur]
                            for i in ii:
                                mm(pW[:, i - ia, :], hkg[:, i, n, :], T2T[:, i, :], True, True, [('hkg', i), kT2], [kW], inc=(i == last))
                            yield
                            amul(nw2T[:, sl_, :], pW, -1.0, [kW], [K('nw2T')])
                            yield
                            for i in ii:
                                mm(pV[:, i - ia, :], T2T[:, i, :], hv[:, i, n, :], True, False, [kT2, ('hv', i)], [kV], inc=False)
                                mm(pV[:, i - ia, :], nw2T[:, i, :], Sbf[:, i, :], False, True, [K('nw2T'), K('Sbf')], [kV], inc=(i == last))
                            yield
                            for i in ii:
                                ts('dve', vnew[:, i, :], pV[:, i - ia, :], g_beta[:, cc[i]:cc[i] + 1], None, ALU.mult, None, [kV, ('g_beta',)], [K('vnew')])
                            yield
                            for i in ii:
                                mm(pS[:, i - ia, :], hkd[:, i, n, :], vnew[:, i, :], True, True, [('hkd', i), K('vnew')], [kS], inc=(i == last))
                            for i in ii:
                                mm(pO[:, i - ia, :], Sbf[:, i, :], qgT[:, i, :], True, False, [K('Sbf'), K('qgT')], [kO], inc=False)
                                mm(pO[:, i - ia, :], vnew[:, i, :], QKD[:, i, :], False, True, [K('vnew'), K('QKD')], [kO], inc=(i == last))
                            yield
                            for i in ii:
                                stt(S32[:, i, :], S32[:, i, :], g_egla[:, cc[i]:cc[i] + 1], pS[:, i - ia, :], ALU.mult, ALU.add,
                                    [K('S32'), ('g_egla',), kS], [K('S32')])
                            act(sqo[:, sl_, :], pO, AF.Square, [kO], [K('sqo')])
                            yield
                            cp('act', Sbf[:, sl_, :], S32[:, sl_, :], [K('S32')], [K('Sbf')])
                            for i in ii:
                                mm(pR[:, i - ia, :], c128b[:], sqo[:, i, :], True, True, [('c128b',), K('sqo')], [kR], inc=(i == last))
                            yield
                            rsqrt(rso[:, sl_, :], pR, eps1, [kR], [K('rso')])
                            yield
                            stt(o1[:, sl_, :], pO, normg, rso[:, sl_, :], ALU.mult, ALU.mult, [kO, ('cpp',), K('rso')], [K('o1')])
                            yield
                            tt('pool', mixT[:, sl_, gtok], o1[:, sl_, :], hz[:, sl_, tok], ALU.mult, [K('o1')] + [('hz', i_) for i_ in ii],
                               [('mixT', i_) for i_ in ii])
                            yield

                    def rec_two_chains(half):
                        if half == 0:
                            for pr in range(2):
                                T.issue('pool', lambda e: e.memset(S32[:, 2 * pr:2 * pr + 2, :], 0.0), writes=[('S32', pr)])
                                T.issue('pool', lambda e: e.memset(Sbf[:, 2 * pr:2 * pr + 2, :], 0.0), writes=[('Sbf', pr)])
                        gens = [gen_rec_pair(half, 0), gen_rec_pair(half, 1)]
                        while gens:
                            for g_ in list(gens):
                                try:
                                    next(g_)
                                except StopIteration:
                                    gens.remove(g_)

                    for half in range(2):
                        cur_half[0] = half
                        run_A_quad()
                        if TWO_CHAINS:
                            rec_two_chains(half)
                        else:
                            rec_quad(half)
                    T.barrier()
                dump('mixA', mixT[:, 0:4, :], [('mixT', h) for h in range(4)])
                if stop == 'mixA':
                    return True

                with ExitStack() as sc:
                    ctab_t = sb(sc, "ctab_t", [64, 2 * S], F32)
                    T.dma('sp', ctab_t[:], ctab[:, :], writes=[('ctab',)], key=('ctab',))
                    cos2 = ctab_t[:, 0:S]
                    sin2 = ctab_t[:, S:2 * S]
                    wq_t = sb(sc, "wq_t", [128, 3, 768], BF16)
                    wqr_t = sb(sc, "wqr_t", [128, 3, 4, 64], BF16)
                    wkv_t = sb(sc, "wkv_t", [128, 2, 1024], BF16)
                    wkr_t = sb(sc, "wkr_t", [128, 8, 128], BF16)
                    wst = [sb(sc, f"wstm{i}", [128, 8, 128], BF16) for i in range(3)]
                    cqg = sb(sc, "cqg", [128, 3, S], BF16)
                    ckvg = sb(sc, "ckvg", [128, 2, S], BF16)
                    sqr = [sb(sc, f"sqr{i}", [128, 512], BF16) for i in range(2)]
                    rsq = sb(sc, "rsq", [128, S], F32)
                    rskv = sb(sc, "rskv", [128, S], F32)
                    rskvt = sb(sc, "rskvt", [128, NT], F32)
                    krT = sb(sc, "krT", [64, S], BF16)
                    t1 = [sb(sc, f"rt1_{i}", [64, 512], F32) for i in range(2)]
                    t2 = [sb(sc, f"rt2_{i}", [64, 512], F32) for i in range(2)]
                    qn = [sb(sc, f"qn{i}", [128, S], BF16) for i in range(1)]
                    qr = [sb(sc, f"qr{i}", [64, S], BF16) for i in range(1)]
                    kn = [sb(sc, f"kn{i}", [128, S], BF16) for i in range(1)]
                    vh = [sb(sc, f"vh{i}", [128, NT, 128], BF16) for i in range(1)]
                    PT = [sb(sc, f"PTt{i}", [128, 512], BF16) for i in range(3)]
                    den = [sb(sc, f"den{i}", [128, 512], F32) for i in range(2)]

                    T.dma('pool', wq_t[:], wq_v[:, :, :], writes=[('wq',)], key=('wq',))
                    T.dma('pool', wkv_t[:], wkv_v[:, :, :], writes=[('wkv',)], key=('wkv',))
                    T.dma('pool', wkr_t[:, :, 0:64], w_in_v[:, :, 2696:2760], writes=[('wkr',)], key=('wkr',))
                    ts('dve', wkr_t[:, :, 64:96], wkr_t[:, :, 32:64], -1.0, None, ALU.mult, None, [('wkr',)], [('wkr2',)])
                    cp('dve', wkr_t[:, :, 96:128], wkr_t[:, :, 0:32], [('wkr',)], [('wkr2',)])
                    for h in range(4):
                        ts('dve', wqr_t[:, :, h, 0:32], wq_t[:, :, h * 192 + 160:h * 192 + 192], -1.0, None, ALU.mult, None, [('wq',)], [('wqr',)])
                        cp('dve', wqr_t[:, :, h, 32:64], wq_t[:, :, h * 192 + 128:h * 192 + 160], [('wq',)], [('wqr',)])

                    wcnt = [0]

                    def load_wchunk2(c0):
                        sl = wcnt[0] % 3
                        wcnt[0] += 1
                        T.dma('pool', wst[sl][:], w_in_v[:, :, c0:c0 + 128], writes=[('wst', sl)], key=('wstm', sl))
                        return sl

                    XH = lambda tb: [('xhT', tb * 4 + i) for i in range(4)]
                    sqcnt = [0]
                    for ci in range(5):
                        isq = ci < 3
                        cc = ci if isq else ci - 3
                        last = cc == (2 if isq else 1)
                        c0 = 2056 + ci * 128
                        sl = load_wchunk2(c0)
                        nrm = onesb if isq else c256b
                        knrm = ('onesb',) if isq else ('c256b',)
                        for tb in range(4):
                            bank = tb % 2
                            cs = slice(tb * 512, (tb + 1) * 512)
                            for kc in range(8):
                                mm(ps[bank][:, :], wst[sl][:, kc, :], xhT[:, kc, cs], kc == 0, kc == 7,
                                   [('wst', sl)] + XH(tb), [('ps', bank)])
                            if isq:
                                ts('dve', cqg[:, cc, cs], ps[bank][:, :], qg[:, cc:cc + 1], float(384 ** 0.5), ALU.mult, ALU.mult,
                                   [('ps', bank), ('cpp',)], [('cqg',)])
                            else:
                                ts('dve', ckvg[:, cc, cs], ps[bank][:, :], kvg[:, cc:cc + 1], None, ALU.mult, None,
                                   [('ps', bank), ('cpp',)], [('ckvg',)])
                            sqi = sqcnt[0] % 2
                            sqcnt[0] += 1
                            act(sqr[sqi][:], ps[bank][:, :], AF.Square, [('ps', bank)], [('sqr', sqi)])
                            mm(ps[2 + tb][:, :], nrm[:], sqr[sqi][:], cc == 0, last, [knrm, ('sqr', sqi)], [('ps', 2 + tb)], inc=True)
                            if last:
                                if isq:
                                    rsqrt(rsq[:, cs], ps[2 + tb][:, :], eps384, [('ps', 2 + tb)], [('rsq',)])
                                else:
                                    rsqrt(rskv[:, cs], ps[2 + tb][:, :], eps1, [('ps', 2 + tb)], [('rskv',)])
                        if ci == 4:
                            for t in range(NT):
                                bank = 6 + t % 2
                                tr(ps[bank][:, 0:128], rskv[:, t * 128:(t + 1) * 128], [('rskv',)], [('ps', bank)])
                                cp('dve', rskvt[:, t:t + 1], ps[bank][:, 0:1], [('ps', bank)], [('rskvt',)])
                    for tb in range(4):
                        cs = slice(tb * 512, (tb + 1) * 512)
                        for kc in range(8):
                            mm(ps[5][0:64, :], wkr_t[:, kc, 0:64], xhT[:, kc, cs], kc == 0, kc == 7, [('wkr',)] + XH(tb), [('ps', 5)])
                        for kc in range(8):
                            mm(ps[6][0:64, :], wkr_t[:, kc, 64:128], xhT[:, kc, cs], kc == 0, kc == 7, [('wkr2',)] + XH(tb), [('ps', 6)])
                        tt('dve', t1[tb % 2][:], ps[5][0:64, :], cos2[:, cs], ALU.mult, [('ps', 5), ('ctab',)], [('t1', tb % 2)])
                        tt('dve', t2[tb % 2][:], ps[6][0:64, :], sin2[:, cs], ALU.mult, [('ps', 6), ('ctab',)], [('t2', tb % 2)])
                        tt('pool', krT[:, cs], t1[tb % 2][:], t2[tb % 2][:], ALU.add, [('t1', tb % 2), ('t2', tb % 2)], [('krT',)])
                    scale = float(192 ** -0.5)
                    ptc = [0]
                    for h in range(4):
                        par = 0
                        for tb in range(4):
                            cs = slice(tb * 512, (tb + 1) * 512)
                            for kc in range(3):
                                mm(ps[0][:, :], wq_t[:, kc, h * 192:h * 192 + 128], cqg[:, kc, cs], kc == 0, kc == 2, [('wq',), ('cqg',)], [('ps', 0)])
                            tt('dve', qn[par][:, cs], ps[0][:, :], rsq[:, cs], ALU.mult, [('ps', 0), ('rsq',)], [('qn', par)])
                            for kc in range(3):
                                mm(ps[5][0:64, :], wq_t[:, kc, h * 192 + 128:h * 192 + 192], cqg[:, kc, cs], kc == 0, kc == 2, [('wq',), ('cqg',)], [('ps', 5)])
                            for kc in range(3):
                                mm(ps[6][0:64, :], wqr_t[:, kc, h, :], cqg[:, kc, cs], kc == 0, kc == 2, [('wqr',), ('cqg',)], [('ps', 6)])
                            tt('dve', t1[tb % 2][:], ps[5][0:64, :], cos2[:, cs], ALU.mult, [('ps', 5), ('ctab',)], [('t1', tb % 2)])
                            tt('dve', t2[tb % 2][:], ps[6][0:64, :], sin2[:, cs], ALU.mult, [('ps', 6), ('ctab',)], [('t2', tb % 2)])
                            tt('pool', t1[tb % 2][:], t1[tb % 2][:], t2[tb % 2][:], ALU.add, [('t1', tb % 2), ('t2', tb % 2)], [('t1', tb % 2)])
                            tt('pool', qr[par][:, cs], t1[tb % 2][:], rsq[0:64, cs], ALU.mult, [('t1', tb % 2), ('rsq',)], [('qr', par)])
                            for kc in range(2):
                                mm(ps[1][:, :], wkv_t[:, kc, h * 256:h * 256 + 128], ckvg[:, kc, cs], kc == 0, kc == 1, [('wkv',), ('ckvg',)], [('ps', 1)])
                            tt('dve', kn[par][:, cs], ps[1][:, :], rskv[:, cs], ALU.mult, [('ps', 1), ('rskv',)], [('kn', par)])
                        for t in range(NT):
                            bank = 2 + t % 2
                            for kc in range(2):
                                mm(ps[bank][:, 0:128], ckvg[:, kc, t * 128:(t + 1) * 128], wkv_t[:, kc, h * 256 + 128:h * 256 + 256], kc == 0, kc == 1,
                                   [('wkv',), ('ckvg',)], [('ps', bank)])
                            amul(vh[par][:, t, :], ps[bank][:, 0:128], rskvt[:, t:t + 1], [('ps', bank), ('rskvt',)], [('vh', par)])
                        items = [(qb, kt) for qb in range(4) for kt in range(4 * qb + 4)]

                        def att_S(idx):
                            qb, kt = items[idx]
                            r = max(0, kt - 4 * qb)
                            c0 = qb * 512 + r * 128
                            ncol = 512 - r * 128
                            sb_ = ps[idx % 2]
                            ksb = ('ps', idx % 2)
                            mm(sb_[:, 0:ncol], kn[par][:, kt * 128:(kt + 1) * 128], qn[par][:, c0:c0 + ncol], True, False,
                               [('kn', par), ('qn', par)], [ksb], inc=False)
                            mm(sb_[:, 0:ncol], krT[:, kt * 128:(kt + 1) * 128], qr[par][:, c0:c0 + ncol], False, True,
                               [('krT',), ('qr', par)], [ksb])

                        def att_EV(idx):
                            qb, kt = items[idx]
                            nk = 4 * qb + 4
                            r = max(0, kt - 4 * qb)
                            ncol = 512 - r * 128
                            sb_ = ps[idx % 2]
                            ksb = ('ps', idx % 2)
                            pO_, pD_ = ps[4 + (qb % 2) * 2], ps[5 + (qb % 2) * 2]
                            kO, kD = ('ps', 4 + (qb % 2) * 2), ('ps', 5 + (qb % 2) * 2)
                            pi = ptc[0] % 3
                            ptc[0] += 1
                            act(PT[pi][:, 0:ncol], sb_[:, 0:ncol], AF.Exp, [ksb], [('PT', pi)], scale=scale)
                            if kt >= 4 * qb:
                                T.issue('pool', lambda e: e.memset(PT[pi][64:128, 0:64], 0.0), [], [('PT', pi)])
                            mm(pO_[:, r * 128:512], vh[par][:, kt, :], PT[pi][:, 0:ncol], kt == 0, kt == nk - 1, [('vh', par), ('PT', pi)], [kO])
                            mm(pD_[:, r * 128:512], onesb[:], PT[pi][:, 0:ncol], kt == 0, kt == nk - 1, [('onesb',), ('PT', pi)], [kD])
                            if kt == nk - 1:
                                dq = den[qb % 2]
                                T.issue('dve', lambda e: e.reciprocal(out=dq[:], in_=pD_[:, :]), [kD], [('den', qb % 2)])
                                tt('dve', mixT[:, 4 + h, qb * 512:(qb + 1) * 512], pO_[:, :], dq[:], ALU.mult, [kO, ('den', qb % 2)], [('mixT', 4 + h)])

                        att_S(0)
                        for idx in range(len(items)):
                            if idx + 1 < len(items):
                                att_S(idx + 1)
                            att_EV(idx)
                    T.barrier()
                dump('mixB', mixT[:, 4:8, :], [('mixT', 4 + h) for h in range(4)])
                if stop == 'mixB':
                    return True

                with ExitStack() as sc:
                    wo_t = sb(sc, "wo_t", [128, 8, 1024], BF16)
                    ln1_t = sb(sc, "ln1_t", [128, 2048], F32)
                    T.dma('sp', ln1_t[:], cbc[:, 0:2048], writes=[('cbcL',)], key=('cbcL',))
                    G1 = ln1_t[:, 0:1024]
                    B1 = ln1_t[:, 1024:2048]
                    T.dma('pool', wo_t[:], w_out_v[:, :, :], writes=[('wo',)], key=('wo',))
                    xr = [sb(sc, f"xr{i}", [128, D], F32) for i in range(2)]
                    rr = [sb(sc, f"rr{i}", [128, D], F32) for i in range(2)]
                    hh = [sb(sc, f"hh{i}", [128, D], F32) for i in range(2)]
                    stats = sb(sc, "stats", [128, 12], F32)
                    mv = sb(sc, "mv", [128, 2], F32)
                    rs = sb(sc, "rs", [128, 1], F32)
                    def d1_mm(t):
                        sl = t % 2
                        tok = slice(t * 128, (t + 1) * 128)
                        for hb in range(2):
                            bank = sl * 2 + hb
                            for kc in range(8):
                                mm(ps[bank][:, :], mixT[:, kc, tok], wo_t[:, kc, hb * 512:(hb + 1) * 512], kc == 0, kc == 7,
                                   [('mixT', kc), ('wo',)], [('ps', bank)])

                    def d1_res(t):
                        sl = t % 2
                        for hb in range(2):
                            bank = sl * 2 + hb
                            stt(rr[sl][:, hb * 512:(hb + 1) * 512], xr[sl][:, hb * 512:(hb + 1) * 512], ALPHA, ps[bank][:, :], ALU.mult, ALU.add,
                                [('xr', sl), ('ps', bank)], [('rr', sl)])

                    def d1_ln(t):
                        sl = t % 2
                        ln_tile(rr[sl], G1, B1, (stats, mv, rs), ('rr', sl), hh[sl][:], ('hh', sl))
                        T.dma('sp', hscr[t * 128:(t + 1) * 128, :], hh[sl][:], reads=[('hh', sl)], writes=[('hscr', t)], key=('hh', sl))

                    def d1_tr(t):
                        sl = t % 2
                        tok = slice(t * 128, (t + 1) * 128)
                        for kc in range(8):
                            bank = 4 + sl * 2 + kc // 4
                            col = (kc % 4) * 128
                            tr(ps[bank][:, col:col + 128], hh[sl][:, kc * 128:(kc + 1) * 128], [('hh', sl)], [('ps', bank)], inc=(kc % 4 == 3))
                        for hb in range(2):
                            bank = 4 + sl * 2 + hb
                            cp('act', xhT[:, hb * 4:(hb + 1) * 4, tok], ps[bank][:, :].rearrange("p (k c) -> p k c", k=4), [('ps', bank)], [('xhT', t)])

                    T.dma('sp', xr[0][:], x[row0: row0 + 128, :], writes=[('xr', 0)], key=('xr', 0))
                    T.dma('sp', xr[1][:], x[row0 + 128: row0 + 256, :], writes=[('xr', 1)], key=('xr', 1))
                    d1_mm(0)
                    d1_res(0)
                    for t in range(NT):
                        if t + 1 < NT:
                            d1_mm(t + 1)
                        d1_ln(t)
                        if t + 2 < NT:
                            T.dma('sp', xr[t % 2][:], x[row0 + (t + 2) * 128: row0 + (t + 3) * 128, :], writes=[('xr', t % 2)], key=('xr', t % 2))
                        if t + 1 < NT:
                            d1_res(t + 1)
                        d1_tr(t)
                    T.barrier()
            dump('hT', xhT[:], [('xhT', t) for t in range(NT)])
            if stop == 'hT':
                return True

            with ExitStack() as p2:
                wd_t = sb(p2, "wd_t", [128, NJ, 1024], BF16)
                ln2_t = sb(p2, "ln2_t", [128, 3072], F32)
                T.dma('sp', ln2_t[:], cbc[:, 2048:5120], writes=[('cbcL',)], key=('cbcL2',))
                G2 = ln2_t[:, 0:1024]
                B2 = ln2_t[:, 1024:2048]
                BG = ln2_t[:, 2048:3072]
                wg_t = sb(p2, "wg_t", [128, 8, 1024], BF16)
                wp_t = sb(p2, "wp_t", [128, 2, 1024], BF16)
                actT = sb(p2, "actT", [128, NJ, BLK2], BF16)
                wup = [sb(p2, f"wup{i}", [128, 2, 8, 128], BF16) for i in range(3)]
                rawgu = [sb(p2, f"rawgu{i}", [128, 2, 2 + BLK2], F32) for i in range(2)]
                accg = [sb(p2, f"accg{i}", [128, BLK2], F32) for i in range(2)]
                accu = [sb(p2, f"accu{i}", [128, BLK2], F32) for i in range(2)]
                halo = sb(p2, "halo", [128, 2, NJ, 2], F32)
                hr_ = [sb(p2, f"hr{i}", [128, D], F32) for i in range(2)]
                r2_ = [sb(p2, f"r2{i}", [128, D], F32) for i in range(2)]
                sg_ = [sb(p2, f"sgt{i}", [128, D], F32) for i in range(2)]
                pin_ = [sb(p2, f"pin{i}", [128, 256], F32) for i in range(2)]
                pTb_ = [sb(p2, f"pTb{i}", [128, 2, 128], BF16) for i in range(2)]
                stats_ = [sb(p2, f"stats2{i}", [128, 12], F32) for i in range(2)]
                mv_ = [sb(p2, f"mv2{i}", [128, 2], F32) for i in range(2)]
                rs_ = [sb(p2, f"rs2{i}", [128, 1], F32) for i in range(2)]
                T.issue('pool', lambda e: e.memset(halo[:], 0.0), writes=[('halo', c_) for c_ in range(NJ)])
                def ld2(t_):
                    q_ = t_ % 2
                    T.dma('sp', hr_[q_][:], hscr[t_ * 128:(t_ + 1) * 128, :], reads=[('hscr', t_)], writes=[('hr', q_)], key=('hr', q_))
                    T.dma('sp', pin_[q_][:], p[row0 + t_ * 128:row0 + (t_ + 1) * 128, :], writes=[('pin', q_)], key=('pin', q_))

                NB = S // BLK2
                TPB = BLK2 // 128
                wc = [0]

                def prefetch_wup(upto):
                    while wc[0] < min(upto, NB * NJ):
                        jj = wc[0] % NJ
                        s_ = wc[0] % 3
                        wc[0] += 1
                        T.dma('pool', wup[s_][:, 0, :, :], w_up_v[:, :, jj * 128:(jj + 1) * 128], writes=[('wup', s_, 0)], key=('wup', s_, 0))
                        T.dma('pool', wup[s_][:, 1, :, :], w_up_v[:, :, DFF + jj * 128:DFF + (jj + 1) * 128], writes=[('wup', s_, 1)], key=('wup', s_, 1))
                prefetch_wup(3)
                T.dma('pool', wg_t[:], w_gate_v[:, :, :], writes=[('wg',)], key=('wg',))
                T.dma('pool', wp_t[:], w_proj_v[:, :, :], writes=[('wp',)], key=('wp',))
                T.dma('pool', wd_t[:], w_down_v[:, :, :], writes=[('wd',)], key=('wd',))

                def stage_B2(j_):
                    q_ = j_ % 2
                    kg_, ku_ = ('acc2', 0, q_), ('acc2', 1, q_)
                    act(accg[q_][:], accg[q_][:], AF.Silu, [kg_], [kg_])
                    tt('dve', actT[:, j_, :], accg[q_][:], accu[q_][:], ALU.mult, [kg_, ku_], [('actT', j_)])

                for blk in range(NB):
                    bs = slice(blk * BLK2, (blk + 1) * BLK2)
                    XB = [('xhT', blk * TPB + i) for i in range(TPB)]
                    for j in range(NJ):
                        step = blk * NJ + j
                        prefetch_wup(step + 3)
                        sl = step % 3
                        jp = j % 2
                        rg = rawgu[jp]
                        kr_ = ('raw2', jp)
                        for gu in range(2):
                            bank = jp * 2 + gu
                            for kc in range(8):
                                mm(ps[bank][:, :], wup[sl][:, gu, kc, :], xhT[:, kc, bs], kc == 0, kc == 7, [('wup', sl, gu)] + XB, [('ps', bank)])
                        cp('act', rg[:, :, 0:2], halo[:, :, j, :], [('halo', j)], [kr_ + ('h',)])
                        cp('act', rg[:, :, 2:2 + BLK2], ps_all[:, jp * 1024:(jp + 1) * 1024].rearrange("p (g c) -> p g c", g=2),
                           [('ps', jp * 2), ('ps', jp * 2 + 1)], [kr_])
                        cp('act', halo[:, :, j, :], rg[:, :, BLK2:BLK2 + 2], [kr_], [('halo', j)])
                        acs = (accg[jp], accu[jp])
                        kas = (('acc2', 0, jp), ('acc2', 1, jp))
                        act(acs[0][:], ps[jp * 2][:, :], AF.Identity, [('ps', jp * 2), ('cpp',)], [kas[0]], bias=fcb[:, j:j + 1], scale=fcw[:, j, 2:3])
                        ts('dve', acs[1][:], rg[:, 1, 2:2 + BLK2], fcw[:, NJ + j, 2:3], fcb[:, NJ + j:NJ + j + 1], ALU.mult, ALU.add, [kr_, ('cpp',)], [kas[1]])
                        for tap in (1, 0):
                            for gu in range(2):
                                cidx = gu * NJ + j
                                stt(acs[gu][:], rg[:, gu, tap:tap + BLK2], fcw[:, cidx, tap:tap + 1], acs[gu][:], ALU.mult, ALU.add,
                                    [kr_, kr_ + ('h',), kas[gu], ('cpp',)], [kas[gu]])
                        if j >= 1:
                            stage_B2(j - 1)
                    stage_B2(NJ - 1)
                    AK = [('actT', j) for j in range(NJ)]
                    if stop == 'p2a' and blk == 0:
                        T.barrier()
                        return True
                    for tl in range(TPB):
                        t = blk * TPB + tl
                        tok = slice(t * 128, (t + 1) * 128)
                        ltok = slice(tl * 128, (tl + 1) * 128)
                        grow = row0 + t * 128
                        tp_ = t % 2
                        hr, r2, sg, pin, pTb = hr_[tp_], r2_[tp_], sg_[tp_], pin_[tp_], pTb_[tp_]
                        kh, kr2, kpin, kpt = ('hr', tp_), ('r2', tp_), ('pin', tp_), ('pTb', tp_)
                        if t == 0:
                            ld2(0)
                        if t + 1 < NT:
                            ld2(t + 1)
                        for kc in range(2):
                            tr(ps[6 + tp_][:, kc * 128:(kc + 1) * 128], pin[:, kc * 128:(kc + 1) * 128], [kpin], [('ps', 6 + tp_)], inc=(kc == 1))
                        cp('act', pTb[:], ps[6 + tp_][:, 0:256].rearrange("p (k c) -> p k c", k=2), [('ps', 6 + tp_)], [kpt])
                        for hb in range(2):
                            hs = slice(hb * 512, (hb + 1) * 512)
                            ksg = ('sg', hb, tp_)
                            for kc in range(8):
                                mm(ps[2 + hb][:, :], xhT[:, kc, tok], wg_t[:, kc, hs], kc == 0, kc == 7, [('xhT', t), ('wg',)], [('ps', 2 + hb)])
                            for kc in range(2):
                                mm(ps[4 + hb][:, :], pTb[:, kc, :], wp_t[:, kc, hs], kc == 0, kc == 1, [kpt, ('wp',)], [('ps', 4 + hb)])
                            for j in range(NJ):
                                mm(ps[hb][:, :], actT[:, j, ltok], wd_t[:, j, hs], j == 0, j == NJ - 1, AK + [('wd',)], [('ps', hb)])
                            tt('dve', sg[:, hs], ps[2 + hb][:, :], BG[:, hs], ALU.add, [('ps', 2 + hb), ('cbcL',)], [ksg])
                            act(sg[:, hs], sg[:, hs], AF.Sigmoid, [ksg], [ksg])
                            tt('dve', sg[:, hs], sg[:, hs], ps[4 + hb][:, :], ALU.mult, [ksg, ('ps', 4 + hb)], [ksg])
                            stt(r2[:, hs], hr[:, hs], ALPHA, ps[hb][:, :], ALU.mult, ALU.add, [kh, ('ps', hb)], [kr2])
                            tt('dve', r2[:, hs], r2[:, hs], sg[:, hs], ALU.add, [kr2, ksg], [kr2])
                        ln_tile(r2, G2, B2, (stats_[tp_], mv_[tp_], rs_[tp_]), kr2, r2[:], kr2, sfx=tp_)
                        T.dma('sp', out[grow:grow + 128, :], r2[:], reads=[kr2], writes=[('out', sq, t)], key=('r2o', tp_))
                    if stop == 'p2b' and blk == 0:
                        T.barrier()
                        return True
                T.barrier()
          for sq in range(NSEQ):
            if seq_body(sq):
                break
        T.final_wait()
    nc._trk_log = T.log
    return nc


def _prep_common(inp):
    f = np.float32
    g = lambda k: np.asarray(inp[k], dtype=f)[0]
    cw = g("gdn_conv_w")
    fw = g("ffn_conv_w")
    fb = g("ffn_conv_b")
    cpp = np.zeros((128, 512), f)
    cpp[:, 0:48] = cw.reshape(4, 12, 128).transpose(2, 1, 0).reshape(128, 48)
    cpp[:, 48:180] = fw.reshape(3, 44, 128).transpose(2, 1, 0).reshape(128, 132)
    cpp[:, 180:224] = fb.reshape(44, 128).T
    cpp[:, 224] = g("gdn_norm_g")
    cpp[:, 225:228] = g("mla_q_norm_g").reshape(3, 128).T
    cpp[:, 228:230] = g("mla_kv_norm_g").reshape(2, 128).T
    cbc = np.zeros((128, 5 * 1024 + 128), f)
    for i, k in enumerate(["ln1_g", "ln1_b", "ln2_g", "ln2_b", "ple_b_gate"]):
        cbc[:, i * 1024:(i + 1) * 1024] = g(k)[None, :]
    cbc[:, 5120:5184] = np.tile(g("gdn_a_log"), 16)[None, :]
    cbc[:, 5184:5248] = np.tile(g("gdn_dt_bias"), 16)[None, :]
    j = np.arange(128)[:, None]
    i = np.arange(128)[None, :]
    cmat = np.zeros((128, 1792), f)
    cmat[:, 0:128] = np.eye(128, dtype=f)
    cmat[:, 128:256] = (j <= i).astype(f)
    for q_ in range(4):
        cmat[:, 256 + q_ * 128:384 + q_ * 128] = np.where(j <= i, 0.0, -30000.0).astype(f)
        cmat[:, 768 + q_ * 128:896 + q_ * 128] = np.where(j < i, 0.0, -30000.0).astype(f)
        cmat[:, 1280 + q_ * 128:1408 + q_ * 128] = np.eye(128, dtype=f)
    inv = (np.float32(10000.0) ** (-(np.arange(0, 64, 2, dtype=f)) / np.float32(64))).astype(f)
    ang = (np.arange(S, dtype=f)[:, None] * inv[None, :]).astype(f)
    cos = np.cos(ang.astype(np.float64)).astype(f).T
    sin = np.sin(ang.astype(np.float64)).astype(f).T
    ctab = np.zeros((64, 2 * S), f)
    ctab[0:32, 0:S] = cos
    ctab[32:64, 0:S] = cos
    ctab[0:32, S:] = sin
    ctab[32:64, S:] = sin
    return {
        "w_in": np.ascontiguousarray(g("w_in")), "wq_up": np.ascontiguousarray(g("mla_w_q_up")),
        "wkv_up": np.ascontiguousarray(g("mla_w_kv_up")), "w_out": np.ascontiguousarray(g("w_out")),
        "w_up": np.ascontiguousarray(g("ffn_w_up")), "w_down": np.ascontiguousarray(g("ffn_w_down")),
        "w_gate": np.ascontiguousarray(g("ple_w_gate")), "w_proj": np.ascontiguousarray(g("ple_w_proj")),
        "cpp": cpp, "cbc": cbc, "cmat": cmat, "ctab": ctab,
    }


def kernel(**inputs):
    common = _prep_common(inputs)
    x = np.asarray(inputs["x"], dtype=np.float32)
    p = np.asarray(inputs["p"], dtype=np.float32)[0]
    B = x.shape[0]
    nseq = B // NCORES
    nc = build(nseq)
    in_maps = []
    for c in range(NCORES):
        m = dict(common)
        m["x"] = np.ascontiguousarray(x[c * nseq:(c + 1) * nseq].reshape(nseq * S, D))
        m["p"] = np.ascontiguousarray(p[c * nseq:(c + 1) * nseq].reshape(nseq * S, 256))
        in_maps.append(m)
    res = run_bass_kernel_spmd(nc, in_maps, core_ids=list(range(NCORES)))
    outs = [np.asarray(r["out"]).reshape(nseq, S, D) for r in res.results]
    return np.concatenate(outs, axis=0).astype(np.float32)
```

```python
import numpy as np
from contextlib import ExitStack
import concourse.bass as bass
import concourse.mybir as mybir
from concourse.bass_utils import run_bass_kernel_spmd

F32, BF16 = mybir.dt.float32, mybir.dt.bfloat16
AF = mybir.ActivationFunctionType
ALU = mybir.AluOpType

S = 2048
NT = 16
D = 1024
KC = 8
DFF = 2816
NJ = 22
ALPHA = float(2.0 ** 0.25)
EPS = 1e-6
BLK2 = 512
HS = 1024
NBLK = 2
NTH = 8
EPOCH = 16000
NCORES = 8
TWO_CHAINS = True
OVERLAP_A = False
SEQ_GENS = True


class Trk:
    def __init__(self, nc, es):
        self.nc, self.es = nc, es
        self.engs = {'pe': nc.tensor, 'act': nc.scalar, 'dve': nc.vector, 'pool': nc.gpsimd, 'sp': nc.sync}
        self.cnt = {e: 0 for e in self.engs}
        self.esems = {e: [] for e in self.engs}
        self.seen = {e: {} for e in self.engs}
        self.lastw = {}
        self.rd = {}
        self.dsem = {}
        self.pend = {e: ([], []) for e in self.engs}
        self.latest = {}
        self.log = {e: [] for e in self.engs}

    def newsem(self, name):
        return self.es.enter_context(self.nc.semaphore(name))

    def _wait(self, e, ev):
        sem, val, src = ev
        if src == 'pe' and e == 'pe':
            return
        k = id(sem)
        if self.seen[e].get(k, 0) >= val:
            return
        self.engs[e].wait_ge(sem, val)
        self.log[e].append(('w', id(sem), val))
        self.seen[e][k] = val

    def _deps(self, e, reads, writes):
        for k in reads:
            ev = self.lastw.get(k)
            if ev is not None:
                self._wait(e, ev)
        for k in writes:
            ev = self.lastw.get(k)
            if ev is not None:
                self._wait(e, ev)
            for ev in self.rd.get(k, {}).values():
                self._wait(e, ev)

    def _reg(self, ev, reads, writes):
        sem, val, src = ev
        self.latest[id(sem)] = (sem, val)
        for k in writes:
            self.lastw[k] = ev
            self.rd[k] = {}
        for k in reads:
            self.rd.setdefault(k, {})[id(sem)] = ev

    def issue(self, e, fn, reads=(), writes=(), inc=True):
        writes = list(writes) + [k for k in reads if k[0] == 'ps' and k not in writes]
        reads = [k for k in reads if k[0] != 'ps']
        self._deps(e, reads, writes)
        ins = fn(self.engs[e])
        pr, pw = self.pend[e]
        pr.extend(reads)
        pw.extend(writes)
        if inc:
            n = self.cnt[e]
            ep, off = divmod(n, EPOCH)
            if ep >= len(self.esems[e]):
                self.esems[e].append(self.newsem(f"s_{e}_{ep}"))
            sem = self.esems[e][ep]
            ins.then_inc(sem, 1)
            self.log[e].append(('i', id(sem), 1))
            self.cnt[e] = n + 1
            self._reg((sem, off + 1, e), pr, pw)
            self.pend[e] = ([], [])
        return ins

    def dma(self, q, out, in_, reads=(), writes=(), key=None):
        self._deps(q, reads, writes)
        ins = self.engs[q].dma_start(out=out, in_=in_)
        if key not in self.dsem:
            self.dsem[key] = [self.newsem("d_" + "_".join(str(x) for x in key)), 0]
        d = self.dsem[key]
        d[1] += 16
        ins.then_inc(d[0], 16)
        self.log[q].append(('i', id(d[0]), 16))
        self._reg((d[0], d[1], 'dma'), list(reads), list(writes))
        return ins

    def barrier(self):
        for e in self.engs:
            assert not self.pend[e][0] and not self.pend[e][1], e
        for sem, val in list(self.latest.values()):
            self._wait('sp', (sem, val, 'x'))
        n = self.cnt['sp']
        ep, off = divmod(n, EPOCH)
        if ep >= len(self.esems['sp']):
            self.esems['sp'].append(self.newsem(f"s_sp_{ep}"))
        sem = self.esems['sp'][ep]
        self.engs['sp'].sem_inc(sem, 1)
        self.log['sp'].append(('i', id(sem), 1))
        self.cnt['sp'] = n + 1
        ev = (sem, off + 1, 'sp')
        self.latest[id(sem)] = (sem, off + 1)
        self.seen['sp'][id(sem)] = off + 1
        for e in self.engs:
            if e != 'sp':
                self._wait(e, ev)
        self.lastw.clear()
        self.rd.clear()

    def final_wait(self):
        for sem, val in list(self.latest.values()):
            self._wait('sp', (sem, val, 'x'))


class _Stop(Exception):
    pass


def build(NSEQ, dbg=None, stop=None):
    dbg = dbg or {}
    nc = bass.Bass("TRN2", target_bir_lowering=False)

    def din(name, shape, dt=F32):
        return nc.dram_tensor(name, list(shape), dt, kind="ExternalInput").ap()

    x = din("x", [NSEQ * S, D])
    p = din("p", [NSEQ * S, 256])
    w_in = din("w_in", [D, 2760])
    wq_up = din("wq_up", [384, 768])
    wkv_up = din("wkv_up", [256, 1024])
    w_out = din("w_out", [1024, 1024])
    w_up = din("w_up", [D, 2 * DFF])
    w_down = din("w_down", [DFF, D])
    w_gate = din("w_gate", [D, D])
    w_proj = din("w_proj", [256, D])
    cpp = din("cpp", [128, 512])
    cbc = din("cbc", [128, 5 * 1024 + 128])
    cmat = din("cmat", [128, 14 * 128])
    ctab = din("ctab", [64, 2 * S])
    out = nc.dram_tensor("out", [NSEQ * S, D], F32, kind="ExternalOutput").ap()
    hscr = nc.dram_tensor("hscr", [S, D], F32).ap()
    dbg_t = {k: nc.dram_tensor("dbg_" + k, list(v[0]), v[1], kind="ExternalOutput").ap() for k, v in dbg.items()}

    w_in_v = w_in.rearrange("(kc p) n -> p kc n", p=128)
    wq_v = wq_up.rearrange("(kc p) n -> p kc n", p=128)
    wkv_v = wkv_up.rearrange("(kc p) n -> p kc n", p=128)
    w_out_v = w_out.rearrange("(kc p) n -> p kc n", p=128)
    w_up_v = w_up.rearrange("(kc p) n -> p kc n", p=128)
    w_down_v = w_down.rearrange("(kc p) n -> p kc n", p=128)
    w_gate_v = w_gate.rearrange("(kc p) n -> p kc n", p=128)
    w_proj_v = w_proj.rearrange("(kc p) n -> p kc n", p=128)

    with ExitStack() as es:
        T = Trk(nc, es)

        uid = [0]

        def sb(scope, name, shape, dt):
            uid[0] += 1
            return scope.enter_context(nc.sbuf_tensor(f"{name}_{uid[0]}", list(shape), dt))

        ps_all = es.enter_context(nc.psum_tensor("ps_all", [128, 8 * 512], F32))
        ps = [ps_all[:, b * 512:(b + 1) * 512] for b in range(8)]

        cpp_t = sb(es, "cpp_t", [128, 512], F32)
        cbc_t = sb(es, "cbc_t", [128, 128], F32)
        cmat_t = sb(es, "cmat_t", [128, 1792], F32)
        identb = sb(es, "identb", [128, 128], BF16)
        onesb = sb(es, "onesb", [128, 128], BF16)
        c128b = sb(es, "c128b", [128, 128], BF16)
        c256b = sb(es, "c256b", [128, 128], BF16)
        onesf = sb(es, "onesf", [128, 128], F32)
        w_ab = sb(es, "w_ab", [128, 8, 8], BF16)
        xhT = sb(es, "xhT", [128, 8, S], BF16)

        T.dma('sp', cpp_t[:], cpp[:, :], writes=[('cpp',)], key=('cpp',))
        T.dma('sp', cbc_t[:], cbc[:, 5120:5248], writes=[('cbc',)], key=('cbc',))
        T.dma('sp', cmat_t[:], cmat[:, :], writes=[('cmat',)], key=('cmat',))
        T.dma('pool', w_ab[:], w_in_v[:, :, 2048:2056], writes=[('w_ab',)], key=('w_ab',))
        ident = cmat_t[:, 0:128]
        Umat = cmat_t[:, 128:256]
        NEGM = cmat_t[:, 256:384]
        NEGM4 = cmat_t[:, 256:768].rearrange("p (i c) -> p i c", i=4)
        NEGMS4 = cmat_t[:, 768:1280].rearrange("p (i c) -> p i c", i=4)
        ident4 = cmat_t[:, 1280:1792].rearrange("p (i c) -> p i c", i=4)
        T.issue('dve', lambda e: e.tensor_copy(out=identb[:], in_=ident), reads=[('cmat',)], writes=[('identb',)])
        T.issue('pool', lambda e: e.memset(onesb[:], 1.0), writes=[('onesb',)])
        T.issue('pool', lambda e: e.memset(c128b[:], 1.0 / 128), writes=[('c128b',)])
        T.issue('pool', lambda e: e.memset(c256b[:], 1.0 / 256), writes=[('c256b',)])
        T.issue('pool', lambda e: e.memset(onesf[:], 1.0), writes=[('onesf',)])
        epst = sb(es, "epst", [128, 2], F32)
        T.issue('pool', lambda e: e.memset(epst[:, 0:1], EPS), writes=[('epst',)])
        T.issue('pool', lambda e: e.memset(epst[:, 1:2], 384 * EPS), writes=[('epst',)])
        eps1 = epst[:, 0:1]
        eps384 = epst[:, 1:2]
        gcw = cpp_t[:, 0:48].rearrange("p (c j) -> p c j", j=4)
        fcw = cpp_t[:, 48:180].rearrange("p (c j) -> p c j", j=3)
        fcb = cpp_t[:, 180:224]
        normg = cpp_t[:, 224:225]
        qg = cpp_t[:, 225:228]
        kvg = cpp_t[:, 228:230]
        ALOGB = cbc_t[:, 0:64]
        DTBB = cbc_t[:, 64:128]
        CONST = [('cpp',), ('cbc',), ('cmat',)]

        def act(out_, in_, func, reads, writes, **kw):
            return T.issue('act', lambda e: e.activation(out=out_, in_=in_, func=func, **kw), reads, writes)

        def tt(eng, out_, in0, in1, op, reads, writes):
            return T.issue(eng, lambda e: e.tensor_tensor(out=out_, in0=in0, in1=in1, op=op), reads, writes)

        def ts(eng, out_, in0, s1, s2, op0, op1, reads, writes):
            if s2 is None:
                return T.issue(eng, lambda e: e.tensor_scalar(out=out_, in0=in0, scalar1=s1, scalar2=None, op0=op0), reads, writes)
            return T.issue(eng, lambda e: e.tensor_scalar(out=out_, in0=in0, scalar1=s1, scalar2=s2, op0=op0, op1=op1), reads, writes)

        def stt(out_, in0, sc, in1, op0, op1, reads, writes):
            return T.issue('dve', lambda e: e.scalar_tensor_tensor(out=out_, in0=in0, scalar=sc, in1=in1, op0=op0, op1=op1), reads, writes)

        def cp(eng, out_, in_, reads, writes):
            if eng == 'act':
                return T.issue('act', lambda e: e.copy(out=out_, in_=in_), reads, writes)
            return T.issue(eng, lambda e: e.tensor_copy(out=out_, in_=in_), reads, writes)

        def rsqrt(out_, in_, eps_ap, reads, writes):
            act(out_, in_, AF.Ln, list(reads) + [('epst',)], writes, bias=eps_ap)
            act(out_, out_, AF.Exp, writes, writes, scale=-0.5)

        def amul(out_, in_, m, reads, writes):
            return T.issue('act', lambda e: e.mul(out=out_, in_=in_, mul=m), reads, writes)

        def mm(out_, lhsT, rhs, start, stop, reads, writes, inc=None):
            if inc is None:
                inc = stop
            return T.issue('pe', lambda e: e.matmul(out_, lhsT, rhs, start=start, stop=stop), reads, writes, inc=inc)

        def tr(out_, in_, reads, writes, inc=True):
            return T.issue('pe', lambda e: e.transpose(out_, in_, ident), list(reads) + [('cmat',)], writes, inc=inc)

        def dump(name, src, reads):
            if name in dbg_t:
                T.dma('sp', dbg_t[name], src, reads=reads, writes=[('dbg', name)], key=('dbg', name))

        def ln_tile(r, G, B, scope_tiles, key_r, out_tile, key_out, kc_=('cbcL',), sfx=''):
            stats, mv, rs = scope_tiles
            for hb in range(2):
                T.issue('dve', lambda e: e.bn_stats(out=stats[:, hb * 6:(hb + 1) * 6], in_=r[:, hb * 512:(hb + 1) * 512]),
                        reads=[key_r], writes=[('lnst', sfx)])
            T.issue('dve', lambda e: e.bn_aggr(out=mv[:], in_=stats[:]), reads=[('lnst', sfx)], writes=[('lnmv', sfx)])
            rsqrt(rs[:], mv[:, 1:2], eps1, [('lnmv', sfx)], [('lnrs', sfx)])
            ts('dve', r[:], r[:], mv[:, 0:1], rs[:, 0:1], ALU.subtract, ALU.mult, [key_r, ('lnmv', sfx), ('lnrs', sfx)], [key_r])
            tt('dve', r[:], r[:], G, ALU.mult, [key_r, kc_], [key_r])
            tt('dve', out_tile, r[:], B, ALU.add, [key_r, kc_], [key_out])

        if True:
          def seq_body(sq):
            row0 = sq * S
            with ExitStack() as p1:
                mixT = sb(p1, "mixT", [128, 8, S], BF16)
                with ExitStack() as sc:
                    xin = [sb(sc, f"xin{i}", [128, D], F32) for i in range(2)]
                    for t in range(NT):
                        sl = t % 2
                        T.dma('sp', xin[sl][:], x[row0 + t * 128: row0 + (t + 1) * 128, :], writes=[('xin', sl)], key=('xin', sl))
                        pb = (t % 2) * 2
                        for kc in range(8):
                            bank = pb + kc // 4
                            col = (kc % 4) * 128
                            tr(ps[bank][:, col:col + 128], xin[sl][:, kc * 128:(kc + 1) * 128], [('xin', sl)], [('ps', bank)], inc=(kc % 4 == 3))
                        for hb in range(2):
                            bank = pb + hb
                            cp('act' if hb == 0 else 'dve', xhT[:, hb * 4:(hb + 1) * 4, t * 128:(t + 1) * 128],
                               ps[bank][:, :].rearrange("p (k c) -> p k c", k=4), [('ps', bank)], [('xhT', t)])
                    T.barrier()
                dump('xT', xhT[:], [('xhT', t) for t in range(NT)])
                if stop == 'xT':
                    return True

                with ExitStack() as sc:
                    g_ab = sb(sc, "g_ab", [128, 128], F32)
                    g_beta = sb(sc, "g_beta", [128, 64], F32)
                    g_g = sb(sc, "g_g", [128, 64], F32)
                    g_tmp = sb(sc, "g_tmp", [128, 64], F32)
                    g_eal = sb(sc, "g_eal", [128, 64], F32)
                    g_gc = sb(sc, "g_gc", [128, 64], F32)
                    g_ngc = sb(sc, "g_ngc", [128, 64], F32)
                    g_eg = sb(sc, "g_eg", [128, 64], F32)
                    g_egl = sb(sc, "g_egl", [128, 64], F32)
                    g_egla = sb(sc, "g_egla", [128, 64], F32)
                    raw2 = [sb(sc, f"raw{i}", [128, 3 + HS], F32) for i in range(2)]
                    acc = sb(sc, "acc", [128, HS], F32)
                    sil2 = [sb(sc, f"sil{i}", [128, HS], F32) for i in range(2)]
                    sqb = sb(sc, "sqb", [128, HS], BF16)
                    rstd = [sb(sc, f"rstd{i}", [128, 512], F32) for i in range(2)]
                    halo_g = sb(sc, "halo_g", [128, 12, 3], F32)
                    zero3 = sb(sc, "zero3", [128, 3], F32)
                    wst = [sb(sc, f"wst{i}", [128, 8, 128], BF16) for i in range(3)]
                    hq = sb(sc, "hq", [128, 4, HS], BF16)
                    hk = sb(sc, "hk", [128, 4, HS], BF16)
                    hkg = sb(sc, "hkg", [128, 4, NTH, 128], BF16)
                    hkd = sb(sc, "hkd", [128, 4, NTH, 128], BF16)
                    hv = sb(sc, "hv", [128, 4, NTH, 128], BF16)
                    hz = sb(sc, "hz", [128, 4, HS], BF16)
                    S32 = sb(sc, "S32", [128, 4, 128], F32)
                    Sbf = sb(sc, "Sbf", [128, 4, 128], BF16)

                    def tmpp(name, dt):
                        return sb(sc, name, [128, 4, 128], dt)
                    Ug = tmpp("Ug", F32)
                    EGb = tmpp("EGb", F32)
                    ARG = tmpp("ARG", F32)
                    ARG2 = tmpp("ARG2", F32)
                    DT = tmpp("DT", F32)
                    DTs = tmpp("DTs", F32)
                    Nf = tmpp("Nf", F32)
                    Pb = [tmpp("Pb0_", BF16), tmpp("Pb1_", BF16)]
                    PTb = [tmpp("PTb0_", BF16), tmpp("PTb1_", BF16)]
                    Xb = [tmpp("Xb0_", BF16), tmpp("Xb1_", BF16)]
                    QKD = tmpp("QKD", BF16)
                    nw2T = tmpp("nw2T", BF16)
                    vnew = tmpp("vnew", BF16)
                    qgT = tmpp("qgT", BF16)
                    sqo = tmpp("sqo", BF16)
                    rso = tmpp("rso", F32)
                    o1 = tmpp("o1", F32)

                    T.issue('pool', lambda e: e.memset(zero3[:], 0.0), writes=[('zero3',)])
                    cur_half = [0]

                    for t in range(NT):
                        for kc in range(8):
                            mm(ps[7][:, t * 8:(t + 1) * 8], xhT[:, kc, t * 128:(t + 1) * 128], w_ab[:, kc, :], kc == 0, kc == 7,
                               [('xhT', t), ('w_ab',)], [('ps', 7)], inc=(kc == 7 and t == NT - 1))
                    cp('dve', g_ab[:], ps[7][:, 0:128], [('ps', 7)], [('g_ab',)])
                    abv = g_ab[:].rearrange("p (t c) -> p t c", c=8)
                    v64 = lambda tl: tl[:].rearrange("p (t c) -> p t c", c=4)
                    act(v64(g_beta), abv[:, :, 4:8], AF.Sigmoid, [('g_ab',)], [('g_beta',)])
                    tt('dve', v64(g_tmp), abv[:, :, 0:4], DTBB.rearrange("p (t c) -> p t c", c=4), ALU.add, [('g_ab',), ('cbc',)], [('g_tmp',)])
                    act(g_tmp[:], g_tmp[:], AF.Exp, [('g_tmp',)], [('g_tmp',)])
                    ts('dve', g_tmp[:], g_tmp[:], 1.0, None, ALU.add, None, [('g_tmp',)], [('g_tmp',)])
                    act(g_tmp[:], g_tmp[:], AF.Ln, [('g_tmp',)], [('g_tmp',)])
                    act(g_eal[:], ALOGB, AF.Exp, [('cbc',)], [('g_eal',)])
                    stt(g_g[:], g_tmp[:], -1.0, g_eal[:], ALU.mult, ALU.mult, [('g_tmp',), ('g_eal',)], [('g_g',)])
                    mm(ps[7][:, 128:192], Umat, g_g[:], True, True, [('cmat',), ('g_g',)], [('ps', 7)])
                    mm(ps[7][:, 192:256], onesf[:], g_g[:], True, True, [('onesf',), ('g_g',)], [('ps', 7)])
                    cp('dve', g_gc[:], ps[7][:, 128:192], [('ps', 7)], [('g_gc',)])
                    ts('dve', g_ngc[:], g_gc[:], -1.0, None, ALU.mult, None, [('g_gc',)], [('g_ngc',)])
                    act(g_eg[:], g_gc[:], AF.Exp, [('g_gc',)], [('g_eg',)])
                    tt('dve', g_egl[:], ps[7][:, 192:256], g_gc[:], ALU.subtract, [('ps', 7), ('g_gc',)], [('g_egl',)])
                    act(g_egl[:], g_egl[:], AF.Exp, [('g_egl',)], [('g_egl',)])
                    act(g_egla[:], ps[7][:, 192:256], AF.Exp, [('ps', 7)], [('g_egla',)])
                    GS = [('g_beta',), ('g_gc',), ('g_ngc',), ('g_eg',), ('g_egl',), ('g_egla',), ('g_g',)]

                    wcnt = [0]

                    def load_wchunk(c0):
                        sl = wcnt[0] % 3
                        wcnt[0] += 1
                        T.dma('pool', wst[sl][:], w_in_v[:, :, c0:c0 + 128], writes=[('wst', sl)], key=('wst', sl))
                        return sl

                    def proj_block(sl, tb, bank):
                        gtb = cur_half[0] * NBLK + tb
                        for kc in range(8):
                            mm(ps[bank][:, :], wst[sl][:, kc, :], xhT[:, kc, gtb * 512:(gtb + 1) * 512], kc == 0, kc == 7,
                               [('wst', sl)] + [('xhT', gtb * 4 + i) for i in range(4)], [('ps', bank)])

                    trc = [0]

                    def stage_P(ch, tb):
                        kind, h, cidx, sl, ci = ch
                        par = h
                        rp = ci % 2
                        raw = raw2[rp]
                        bank = 6 + tb % 2
                        cs = slice(tb * 512, (tb + 1) * 512)
                        proj_block(sl, tb, bank)
                        if kind == 'z':
                            act(hz[:, par, cs], ps[bank][:, :], AF.Silu, [('ps', bank)], [('hz', par)])
                            return
                        if tb == 0:
                            if cur_half[0] == 0:
                                cp('act', raw[:, 0:3], zero3[:], [('zero3',)], [('raw', rp, -1)])
                            else:
                                cp('act', raw[:, 0:3], halo_g[:, cidx, :], [('halo_g', cidx)], [('raw', rp, -1)])
                        cp('act', raw[:, 3 + tb * 512: 3 + (tb + 1) * 512], ps[bank][:, :], [('ps', bank)], [('raw', rp, tb)])
                        if tb == NBLK - 1 and cur_half[0] == 0:
                            cp('act', halo_g[:, cidx, :], raw[:, HS:HS + 3], [('raw', rp, tb)], [('halo_g', cidx)])

                    def stage_C(ch, tb):
                        kind, h, cidx, sl, ci = ch
                        if kind == 'z':
                            return
                        rp = ci % 2
                        raw = raw2[rp]
                        cs = slice(tb * 512, (tb + 1) * 512)
                        RK = [('raw', rp, tb), ('raw', rp, tb - 1), ('cpp',)]
                        ka = ('acc', tb)
                        ts('dve', acc[:, cs], raw[:, 3 + tb * 512: 3 + (tb + 1) * 512], gcw[:, cidx, 3:4], None, ALU.mult, None, RK, [ka])
                        for j in (2, 1, 0):
                            stt(acc[:, cs], raw[:, j + tb * 512: j + (tb + 1) * 512], gcw[:, cidx, j:j + 1], acc[:, cs], ALU.mult, ALU.add,
                                RK + [ka], [ka])

                    def stage_S(ch, tb):
                        kind, h, cidx, sl, ci = ch
                        if kind == 'z':
                            return
                        sp_ = ci % 2
                        cs = slice(tb * 512, (tb + 1) * 512)
                        act(sil2[sp_][:, cs], acc[:, cs], AF.Silu, [('acc', tb)], [('sil', sp_, tb)])

                    def stage_N(ch):
                        kind, h, cidx, sl, ci = ch
                        if kind not in ('q', 'k'):
                            return
                        par = h
                        sp_ = ci % 2
                        sl_ = sil2[sp_]
                        for tb in range(NBLK):
                            cs = slice(tb * 512, (tb + 1) * 512)
                            tt('pool', sqb[:, cs], sl_[:, cs], sl_[:, cs], ALU.mult, [('sil', sp_, tb)], [('sqb', tb)])
                            mm(ps[2 + tb][:, :], onesb[:], sqb[:, cs], True, True, [('onesb',), ('sqb', tb)], [('ps', 2 + tb)])
                        for tb in range(NBLK):
                            act(rstd[tb][:], ps[2 + tb][:, :], AF.Ln, [('ps', 2 + tb), ('epst',)], [('rstd', tb)], bias=eps1)
                        for tb in range(NBLK):
                            act(rstd[tb][:], rstd[tb][:], AF.Exp, [('rstd', tb)], [('rstd', tb)], scale=-0.5)
                        for tb in range(NBLK):
                            cs = slice(tb * 512, (tb + 1) * 512)
                            ks = ('sil', sp_, tb)
                            if kind == 'q':
                                stt(hq[:, par, cs], sl_[:, cs], float(128 ** -0.5), rstd[tb][:], ALU.mult, ALU.mult,
                                    [ks, ('rstd', tb)], [('hq', par)])
                            else:
                                tt('dve', sl_[:, cs], sl_[:, cs], rstd[tb][:], ALU.mult, [ks, ('rstd', tb)], [ks])
                                cp('act', hk[:, par, cs], sl_[:, cs], [ks], [('hk', par)])

                    def stage_T(ch, tb):
                        kind, h, cidx, sl, ci = ch
                        if kind not in ('k', 'v'):
                            return
                        par = h
                        sp_ = ci % 2
                        ks = ('sil', sp_, tb)
                        for tl in range(4):
                            n = tb * 4 + tl
                            c = (cur_half[0] * NTH + n) * 4 + h
                            b3 = trc[0] % 2
                            trc[0] += 1
                            tr(ps[b3][:, 0:128], sil2[sp_][:, n * 128:(n + 1) * 128], [ks], [('ps', b3)])
                            if kind == 'k':
                                amul(hkg[:, par, n, :], ps[b3][:, 0:128], g_eg[:, c:c + 1], [('ps', b3), ('g_eg',)], [('hkg', par)])
                                ts('dve', hkd[:, par, n, :], ps[b3][:, 0:128], g_egl[:, c:c + 1], None, ALU.mult, None,
                                   [('ps', b3), ('g_egl',)], [('hkd', par)])
                            else:
                                cp('act' if n % 2 else 'dve', hv[:, par, n, :], ps[b3][:, 0:128], [('ps', b3)], [('hv', par)])

                    def run_A_quad():
                        chs = []
                        for h in range(4):
                            for kind, c0, cidx in (('q', h * 128, h), ('k', 512 + h * 128, 4 + h), ('v', 1024 + h * 128, 8 + h), ('z', 1536 + h * 128, 0)):
                                chs.append([kind, h, cidx, None, len(chs), c0])
                        nch = len(chs)
                        loaded = [0]

                        def ensure_loaded(upto):
                            while loaded[0] <= min(upto, nch - 1):
                                ch_ = chs[loaded[0]]
                                ch_[3] = load_wchunk(ch_[5])
                                loaded[0] += 1
                        nit = NBLK * nch
                        for tau in range(nit + NBLK + 5):
                            if tau < nit:
                                i, tb = divmod(tau, NBLK)
                                if tb == 0:
                                    ensure_loaded(i + 1)
                                stage_P(tuple(chs[i][:5]), tb)
                            if 0 <= tau - 1 < nit:
                                i, tb = divmod(tau - 1, NBLK)
                                stage_C(tuple(chs[i][:5]), tb)
                            if 0 <= tau - 2 < nit:
                                i, tb = divmod(tau - 2, NBLK)
                                stage_S(tuple(chs[i][:5]), tb)
                            tn = tau - (NBLK + 2)
                            if tn >= 0 and tn % NBLK == 0 and tn // NBLK < nch:
                                stage_N(tuple(chs[tn // NBLK][:5]))
                            if 0 <= tau - (NBLK + 3) < nit:
                                i, tb = divmod(tau - (NBLK + 3), NBLK)
                                stage_T(tuple(chs[i][:5]), tb)

                    def H4(b):
                        return ps[b][:, :].rearrange("p (i c) -> p i c", i=4), ('ps', b)

                    def rec_quad(half):
                        if half == 0:
                            T.issue('pool', lambda e: e.memset(S32[:], 0.0), writes=[('S32',)])
                            T.issue('pool', lambda e: e.memset(Sbf[:], 0.0), writes=[('Sbf',)])
                        pGb, kGb = H4(0)
                        pX, kX = H4(6)
                        pKK, kKK = H4(1)
                        pV, kV = H4(1)
                        pQK, kQK = H4(2)
                        pO, kO = H4(2)
                        pNT, kNT = H4(3)
                        pS, kS = H4(3)
                        pP, kP = H4(4)
                        pR, kR = H4(4)
                        pPT, kPT = H4(5)
                        pW, kW = H4(0)
                        for n in range(NTH):
                            tok = slice(n * 128, (n + 1) * 128)
                            gtok = slice((half * NTH + n) * 128, (half * NTH + n + 1) * 128)
                            cc = [(half * NTH + n) * 4 + i for i in range(4)]
                            for i in range(4):
                                amul(Ug[:, i, :], Umat, g_g[:, cc[i]:cc[i] + 1], [('cmat',), ('g_g',)], [('Ug',)])
                            for i in range(4):
                                mm(pGb[:, i, :], onesf[:], Ug[:, i, :], True, True, [('onesf',), ('Ug',)], [kGb], inc=(i == 3))
                            tt('dve', ARG2[:], pGb, NEGMS4, ALU.add, [kGb, ('cmat',)], [('ARG2',)])
                            tt('dve', ARG[:], pGb, NEGM4, ALU.add, [kGb, ('cmat',)], [('ARG',)])
                            act(EGb[:], pGb, AF.Exp, [kGb], [('EGb',)])
                            for i in range(4):
                                mm(pKK[:, i, :], hk[:, i, tok], hk[:, i, tok], True, True, [('hk', i)], [kKK], inc=(i == 3))
                            for i in range(4):
                                mm(pQK[:, i, :], hk[:, i, tok], hq[:, i, tok], True, True, [('hk', i), ('hq', i)], [kQK], inc=(i == 3))
                            for i in range(4):
                                act(DTs[:, i, :], ARG2[:, i, :], AF.Exp, [('ARG2',), ('g_ngc',)], [('DTs',)], bias=g_ngc[:, cc[i]:cc[i] + 1])
                            for i in range(4):
                                act(DT[:, i, :], ARG[:, i, :], AF.Exp, [('ARG',), ('g_ngc',)], [('DT',)], bias=g_ngc[:, cc[i]:cc[i] + 1])
                            for i in range(4):
                                stt(Nf[:, i, :], pKK[:, i, :], g_beta[:, cc[i]:cc[i] + 1], DTs[:, i, :], ALU.mult, ALU.mult,
                                    [kKK, ('g_beta',), ('DTs',)], [('Nf',)])
                            for i in range(4):
                                tr(pNT[:, i, :], Nf[:, i, :], [('Nf',)], [kNT], inc=(i == 3))
                            cur = 0
                            cp('act', Pb[cur][:], Nf[:], [('Nf',)], [('Pb0',)])
                            cp('dve', PTb[cur][:], pNT, [kNT], [('PTb0',)])
                            tt('dve', Xb[cur][:], ident4, Nf[:], ALU.subtract, [('cmat',), ('Nf',)], [('Xb0',)])
                            tt('dve', QKD[:], pQK, DT[:], ALU.mult, [kQK, ('DT',)], [('QKD',)])
                            tt('pool', qgT[:], hq[:, :, tok], EGb[:], ALU.mult, [('hq', i_) for i_ in range(4)] + [('EGb',)], [('qgT',)])
                            xc = 0

                            def x_update(ptb_idx, step_):
                                nonlocal_xc = x_state[0]
                                xn = 1 - nonlocal_xc
                                for i in range(4):
                                    mm(pX[:, i, :], identb[:], Xb[nonlocal_xc][:, i, :], True, False, [('identb',), (f'Xb{nonlocal_xc}',)], [kX], inc=False)
                                    mm(pX[:, i, :], PTb[ptb_idx][:, i, :], Xb[nonlocal_xc][:, i, :], False, True,
                                       [(f'PTb{ptb_idx}',), (f'Xb{nonlocal_xc}',)], [kX], inc=(i == 3))
                                x_state[0] = xn
                                return xn
                            x_state = [0]
                            pend = None
                            for step in range(6):
                                nx = 1 - cur
                                kPc, kPTc = (f'Pb{cur}',), (f'PTb{cur}',)
                                kPn, kPTn = (f'Pb{nx}',), (f'PTb{nx}',)
                                for i in range(4):
                                    mm(pPT[:, i, :], Pb[cur][:, i, :], PTb[cur][:, i, :], True, True, [kPc, kPTc], [kPT], inc=(i == 3))
                                if step < 5:
                                    for i in range(4):
                                        mm(pP[:, i, :], PTb[cur][:, i, :], Pb[cur][:, i, :], True, True, [kPc, kPTc], [kP], inc=(i == 3))
                                if pend is not None:
                                    xn = x_update(pend, step)
                                cp('dve', PTb[nx][:], pPT, [kPT], [kPTn])
                                if step < 5:
                                    cp('act', Pb[nx][:], pP, [kP], [kPn])
                                if pend is not None:
                                    cp('act' if step % 2 else 'dve', Xb[xn][:], pX, [kX], [(f'Xb{xn}',)])
                                pend = nx
                                cur = nx
                            xn = x_update(pend, 6)
                            cp('act', Xb[xn][:], pX, [kX], [(f'Xb{xn}',)])
                            cur = xn
                            kT2 = (f'Xb{cur}',)
                            T2T = Xb[cur]
                            for i in range(4):
                                mm(pW[:, i, :], hkg[:, i, n, :], T2T[:, i, :], True, True, [('hkg', i), kT2], [kW], inc=(i == 3))
                            amul(nw2T[:], pW, -1.0, [kW], [('nw2T',)])
                            for i in range(4):
                                mm(pV[:, i, :], T2T[:, i, :], hv[:, i, n, :], True, False, [kT2, ('hv', i)], [kV], inc=False)
                                mm(pV[:, i, :], nw2T[:, i, :], Sbf[:, i, :], False, True, [('nw2T',), ('Sbf',)], [kV], inc=(i == 3))
                            for i in range(4):
                                ts('dve', vnew[:, i, :], pV[:, i, :], g_beta[:, cc[i]:cc[i] + 1], None, ALU.mult, None, [kV, ('g_beta',)], [('vnew',)])
                            for i in range(4):
                                mm(pS[:, i, :], hkd[:, i, n, :], vnew[:, i, :], True, True, [('hkd', i), ('vnew',)], [kS], inc=(i == 3))
                            for i in range(4):
                                mm(pO[:, i, :], Sbf[:, i, :], qgT[:, i, :], True, False, [('Sbf',), ('qgT',)], [kO], inc=False)
                                mm(pO[:, i, :], vnew[:, i, :], QKD[:, i, :], False, True, [('vnew',), ('QKD',)], [kO], inc=(i == 3))
                            for i in range(4):
                                stt(S32[:, i, :], S32[:, i, :], g_egla[:, cc[i]:cc[i] + 1], pS[:, i, :], ALU.mult, ALU.add,
                                    [('S32',), ('g_egla',), kS], [('S32',)])
                            cp('act', Sbf[:], S32[:], [('S32',)], [('Sbf',)])
                            act(sqo[:], pO, AF.Square, [kO], [('sqo',)])
                            for i in range(4):
                                mm(pR[:, i, :], c128b[:], sqo[:, i, :], True, True, [('c128b',), ('sqo',)], [kR], inc=(i == 3))
                            rsqrt(rso[:], pR, eps1, [kR], [('rso',)])
                            stt(o1[:], pO, normg, rso[:], ALU.mult, ALU.mult, [kO, ('cpp',), ('rso',)], [('o1',)])
                            tt('pool', mixT[:, 0:4, gtok], o1[:], hz[:, :, tok], ALU.mult, [('o1',)] + [('hz', i_) for i_ in range(4)],
                               [('mixT', i_) for i_ in range(4)])

                    def gen_rec_pair(half, pr):
                        ia = 2 * pr
                        ii = (ia, ia + 1)
                        sl_ = slice(ia, ia + 2)
                        B = 4 * pr

                        def HB(b, hf):
                            return ps[b][:, hf * 256:(hf + 1) * 256].rearrange("p (i c) -> p i c", i=2), ('ps', b)
                        pGb, kGb = HB(B, 0)
                        pW, kW = HB(B, 0)
                        pP, kP = HB(B, 1)
                        pR, kR = HB(B, 1)
                        pKK, kKK = HB(B + 1, 0)
                        pV, kV = HB(B + 1, 0)
                        pPT, kPT = HB(B + 1, 1)
                        pQK, kQK = HB(B + 2, 0)
                        pO, kO = HB(B + 2, 0)
                        pX, kX = HB(B + 2, 1)
                        pNT, kNT = HB(B + 3, 0)
                        pS, kS = HB(B + 3, 0)
                        K = lambda nm: (nm, pr)
                        last = ia + 1
                        for n in range(NTH):
                            tok = slice(n * 128, (n + 1) * 128)
                            gtok = slice((half * NTH + n) * 128, (half * NTH + n + 1) * 128)
                            cc = {i: (half * NTH + n) * 4 + i for i in ii}
                            for i in ii:
                                amul(Ug[:, i, :], Umat, g_g[:, cc[i]:cc[i] + 1], [('cmat',), ('g_g',)], [K('Ug')])
                            yield
                            for i in ii:
                                mm(pGb[:, i - ia, :], onesf[:], Ug[:, i, :], True, True, [('onesf',), K('Ug')], [kGb], inc=(i == last))
                            yield
                            tt('dve', ARG2[:, sl_, :], pGb, NEGMS4[:, 0:2, :], ALU.add, [kGb, ('cmat',)], [K('ARG2')])
                            tt('dve', ARG[:, sl_, :], pGb, NEGM4[:, 0:2, :], ALU.add, [kGb, ('cmat',)], [K('ARG')])
                            act(EGb[:, sl_, :], pGb, AF.Exp, [kGb], [K('EGb')])
                            for i in ii:
                                mm(pKK[:, i - ia, :], hk[:, i, tok], hk[:, i, tok], True, True, [('hk', i)], [kKK], inc=(i == last))
                            for i in ii:
                                mm(pQK[:, i - ia, :], hk[:, i, tok], hq[:, i, tok], True, True, [('hk', i), ('hq', i)], [kQK], inc=(i == last))
                            yield
                            for i in ii:
                                act(DTs[:, i, :], ARG2[:, i, :], AF.Exp, [K('ARG2'), ('g_ngc',)], [K('DTs')], bias=g_ngc[:, cc[i]:cc[i] + 1])
                            for i in ii:
                                act(DT[:, i, :], ARG[:, i, :], AF.Exp, [K('ARG'), ('g_ngc',)], [K('DT')], bias=g_ngc[:, cc[i]:cc[i] + 1])
                            yield
                            for i in ii:
                                stt(Nf[:, i, :], pKK[:, i - ia, :], g_beta[:, cc[i]:cc[i] + 1], DTs[:, i, :], ALU.mult, ALU.mult,
                                    [kKK, ('g_beta',), K('DTs')], [K('Nf')])
                            yield
                            for i in ii:
                                tr(pNT[:, i - ia, :], Nf[:, i, :], [K('Nf')], [kNT], inc=(i == last))
                            cur = 0
                            cp('act', Pb[cur][:, sl_, :], Nf[:, sl_, :], [K('Nf')], [K('Pb0')])
                            yield
                            cp('dve', PTb[cur][:, sl_, :], pNT, [kNT], [K('PTb0')])
                            tt('dve', Xb[cur][:, sl_, :], ident4[:, 0:2, :], Nf[:, sl_, :], ALU.subtract, [('cmat',), K('Nf')], [K('Xb0')])
                            tt('dve', QKD[:, sl_, :], pQK, DT[:, sl_, :], ALU.mult, [kQK, K('DT')], [K('QKD')])
                            tt('pool', qgT[:, sl_, :], hq[:, sl_, tok], EGb[:, sl_, :], ALU.mult, [('hq', i_) for i_ in ii] + [K('EGb')], [K('qgT')])
                            yield
                            xs = [0]

                            def x_update(ptb_idx):
                                xc_ = xs[0]
                                xn_ = 1 - xc_
                                for i in ii:
                                    mm(pX[:, i - ia, :], identb[:], Xb[xc_][:, i, :], True, False, [('identb',), K(f'Xb{xc_}')], [kX], inc=False)
                                    mm(pX[:, i - ia, :], PTb[ptb_idx][:, i, :], Xb[xc_][:, i, :], False, True,
                                       [K(f'PTb{ptb_idx}'), K(f'Xb{xc_}')], [kX], inc=(i == last))
                                xs[0] = xn_
                                return xn_
                            pend = None
                            for step in range(6):
                                nx = 1 - cur
                                kPc, kPTc = K(f'Pb{cur}'), K(f'PTb{cur}')
                                kPn, kPTn = K(f'Pb{nx}'), K(f'PTb{nx}')
                                for i in ii:
                                    mm(pPT[:, i - ia, :], Pb[cur][:, i, :], PTb[cur][:, i, :], True, True, [kPc, kPTc], [kPT], inc=(i == last))
                                if step < 5:
                                    for i in ii:
                                        mm(pP[:, i - ia, :], PTb[cur][:, i, :], Pb[cur][:, i, :], True, True, [kPc, kPTc], [kP], inc=(i == last))
                                if pend is not None:
                                    xn = x_update(pend)
                                yield
                                cp('dve', PTb[nx][:, sl_, :], pPT, [kPT], [kPTn])
                                if step < 5:
                                    cp('act', Pb[nx][:, sl_, :], pP, [kP], [kPn])
                                if pend is not None:
                                    cp('act' if step % 2 else 'dve', Xb[xn][:, sl_, :], pX, [kX], [K(f'Xb{xn}')])
                                yield
                                pend = nx
                                cur = nx
                            xn = x_update(pend)
                            yield
                            cp('act', Xb[xn][:, sl_, :], pX, [kX], [K(f'Xb{xn}')])
                            yield
                            cur = xn
                            kT2 = K(f'Xb{cur}')
                            T2T = Xb[cur]
                            for i in ii:
                                mm(pW[:, i - ia, :], hkg[:, i, n, :], T2T[:, i, :], True, True, [('hkg', i), kT2], [kW], inc=(i == last))
                            yield
                            amul(nw2T[:, sl_, :], pW, -1.0, [kW], [K('nw2T')])
                            yield
                            for i in ii:
                                mm(pV[:, i - ia, :], T2T[:, i, :], hv[:, i, n, :], True, False, [kT2, ('hv', i)], [kV], inc=False)
                                mm(pV[:, i - ia, :], nw2T[:, i, :], Sbf[:, i, :], False, True, [K('nw2T'), K('Sbf')], [kV], inc=(i == last))
                            yield
                            for i in ii:
                                ts('dve', vnew[:, i, :], pV[:, i - ia, :], g_beta[:, cc[i]:cc[i] + 1], None, ALU.mult, None, [kV, ('g_beta',)], [K('vnew')])
                            yield
                            for i in ii:
                                mm(pS[:, i - ia, :], hkd[:, i, n, :], vnew[:, i, :], True, True, [('hkd', i), K('vnew')], [kS], inc=(i == last))
                            for i in ii:
                                mm(pO[:, i - ia, :], Sbf[:, i, :], qgT[:, i, :], True, False, [K('Sbf'), K('qgT')], [kO], inc=False)
                                mm(pO[:, i - ia, :], vnew[:, i, :], QKD[:, i, :], False, True, [K('vnew'), K('QKD')], [kO], inc=(i == last))
                            yield
                            for i in ii:
                                stt(S32[:, i, :], S32[:, i, :], g_egla[:, cc[i]:cc[i] + 1], pS[:, i - ia, :], ALU.mult, ALU.add,
                                    [K('S32'), ('g_egla',), kS], [K('S32')])
                            act(sqo[:, sl_, :], pO, AF.Square, [kO], [K('sqo')])
                            yield
                            cp('act', Sbf[:, sl_, :], S32[:, sl_, :], [K('S32')], [K('Sbf')])
                            for i in ii:
                                mm(pR[:, i - ia, :], c128b[:], sqo[:, i, :], True, True, [('c128b',), K('sqo')], [kR], inc=(i == last))
                            yield
                            rsqrt(rso[:, sl_, :], pR, eps1, [kR], [K('rso')])
                            yield
                            stt(o1[:, sl_, :], pO, normg, rso[:, sl_, :], ALU.mult, ALU.mult, [kO, ('cpp',), K('rso')], [K('o1')])
                            yield
                            tt('pool', mixT[:, sl_, gtok], o1[:, sl_, :], hz[:, sl_, tok], ALU.mult, [K('o1')] + [('hz', i_) for i_ in ii],
                               [('mixT', i_) for i_ in ii])
                            yield

                    def rec_two_chains(half):
                        if half == 0:
                            for pr in range(2):
                                T.issue('pool', lambda e: e.memset(S32[:, 2 * pr:2 * pr + 2, :], 0.0), writes=[('S32', pr)])
                                T.issue('pool', lambda e: e.memset(Sbf[:, 2 * pr:2 * pr + 2, :], 0.0), writes=[('Sbf', pr)])
                        gens = [gen_rec_pair(half, 0), gen_rec_pair(half, 1)]
                        while gens:
                            for g_ in list(gens):
                                try:
                                    next(g_)
                                except StopIteration:
                                    gens.remove(g_)

                    for half in range(2):
                        cur_half[0] = half
                        run_A_quad()
                        if TWO_CHAINS:
                            rec_two_chains(half)
                        else:
                            rec_quad(half)
                    T.barrier()
                dump('mixA', mixT[:, 0:4, :], [('mixT', h) for h in range(4)])
                if stop == 'mixA':
                    return True

                wo_t = sb(p1, "wo_t", [128, 8, 1024], BF16)
                with ExitStack() as sc:
                    ctab_t = sb(sc, "ctab_t", [64, 2 * S], F32)
                    T.dma('sp', ctab_t[:], ctab[:, :], writes=[('ctab',)], key=('ctab',))
                    cos2 = ctab_t[:, 0:S]
                    sin2 = ctab_t[:, S:2 * S]
                    wq_t = sb(sc, "wq_t", [128, 3, 768], BF16)
                    wqr_t = sb(sc, "wqr_t", [128, 3, 4, 64], BF16)
                    wkv_t = sb(sc, "wkv_t", [128, 2, 1024], BF16)
                    wkr_t = sb(sc, "wkr_t", [128, 8, 128], BF16)
                    wst = [sb(sc, f"wstm{i}", [128, 8, 128], BF16) for i in range(3)]
                    cqg = sb(sc, "cqg", [128, 3, S], BF16)
                    ckvg = sb(sc, "ckvg", [128, 2, S], BF16)
                    sqr = [sb(sc, f"sqr{i}", [128, 512], BF16) for i in range(2)]
                    rsq = sb(sc, "rsq", [128, S], F32)
                    rskv = sb(sc, "rskv", [128, S], F32)
                    rskvt = sb(sc, "rskvt", [128, NT], F32)
                    krT = sb(sc, "krT", [64, S], BF16)
                    t1 = [sb(sc, f"rt1_{i}", [64, 512], F32) for i in range(2)]
                    t2 = [sb(sc, f"rt2_{i}", [64, 512], F32) for i in range(2)]
                    qn = [sb(sc, f"qn{i}", [128, S], BF16) for i in range(1)]
                    qr = [sb(sc, f"qr{i}", [64, S], BF16) for i in range(1)]
                    kn = [sb(sc, f"kn{i}", [128, S], BF16) for i in range(1)]
                    vh = [sb(sc, f"vh{i}", [128, NT, 128], BF16) for i in range(1)]
                    PT = [sb(sc, f"PTt{i}", [128, 512], BF16) for i in range(3)]
                    den = [sb(sc, f"den{i}", [128, 512], F32) for i in range(2)]

                    wcnt = [0]

                    def load_wchunk2(c0):
                        sl = wcnt[0] % 3
                        wcnt[0] += 1
                        T.dma('pool', wst[sl][:], w_in_v[:, :, c0:c0 + 128], writes=[('wst', sl)], key=('wstm', sl))
                        return sl
                    pre_sl = {0: load_wchunk2(2056), 1: load_wchunk2(2056 + 128)}
                    T.dma('pool', wkr_t[:, :, 0:64], w_in_v[:, :, 2696:2760], writes=[('wkr',)], key=('wkr',))
                    T.dma('pool', wq_t[:], wq_v[:, :, :], writes=[('wq',)], key=('wq',))
                    T.dma('pool', wkv_t[:], wkv_v[:, :, :], writes=[('wkv',)], key=('wkv',))
                    ts('dve', wkr_t[:, :, 64:96], wkr_t[:, :, 32:64], -1.0, None, ALU.mult, None, [('wkr',)], [('wkr2',)])
                    cp('dve', wkr_t[:, :, 96:128], wkr_t[:, :, 0:32], [('wkr',)], [('wkr2',)])
                    for h in range(4):
                        ts('dve', wqr_t[:, :, h, 0:32], wq_t[:, :, h * 192 + 160:h * 192 + 192], -1.0, None, ALU.mult, None, [('wq',)], [('wqr',)])
                        cp('dve', wqr_t[:, :, h, 32:64], wq_t[:, :, h * 192 + 128:h * 192 + 160], [('wq',)], [('wqr',)])

                    XH = lambda tb: [('xhT', tb * 4 + i) for i in range(4)]
                    sqcnt = [0]
                    pend_n = [None]
                    for ci in range(5):
                        isq = ci < 3
                        cc = ci if isq else ci - 3
                        last = cc == (2 if isq else 1)
                        c0 = 2056 + ci * 128
                        if ci + 2 < 5:
                            pre_sl[ci + 2] = load_wchunk2(c0 + 256)
                        if ci == 0:
                            T.dma('pool', wo_t[:], w_out_v[:, :, :], writes=[('wo',)], key=('wo',))
                        sl = pre_sl[ci]
                        nrm = onesb if isq else c256b
                        knrm = ('onesb',) if isq else ('c256b',)
                        for tb in range(4):
                            bank = tb % 2
                            cs = slice(tb * 512, (tb + 1) * 512)
                            for kc in range(8):
                                mm(ps[bank][:, :], wst[sl][:, kc, :], xhT[:, kc, cs], kc == 0, kc == 7,
                                   [('wst', sl)] + XH(tb), [('ps', bank)])
                            if isq:
                                ts('dve', cqg[:, cc, cs], ps[bank][:, :], qg[:, cc:cc + 1], float(384 ** 0.5), ALU.mult, ALU.mult,
                                   [('ps', bank), ('cpp',)], [('cqg',)])
                            else:
                                ts('dve', ckvg[:, cc, cs], ps[bank][:, :], kvg[:, cc:cc + 1], None, ALU.mult, None,
                                   [('ps', bank), ('cpp',)], [('ckvg',)])
                            sqi = sqcnt[0] % 2
                            sqcnt[0] += 1
                            act(sqr[sqi][:], ps[bank][:, :], AF.Square, [('ps', bank)], [('sqr', sqi)])
                            if pend_n[0] is not None:
                                pend_n[0]()

                            def _norm(tb=tb, cs=cs, nrm=nrm, knrm=knrm, sqi=sqi, cc=cc, last=last, isq=isq):
                                mm(ps[2 + tb][:, :], nrm[:], sqr[sqi][:], cc == 0, last, [knrm, ('sqr', sqi)], [('ps', 2 + tb)], inc=True)
                                if last:
                                    if isq:
                                        rsqrt(rsq[:, cs], ps[2 + tb][:, :], eps384, [('ps', 2 + tb)], [('rsq',)])
                                    else:
                                        rsqrt(rskv[:, cs], ps[2 + tb][:, :], eps1, [('ps', 2 + tb)], [('rskv',)])
                            pend_n[0] = _norm
                        if ci == 4:
                            pend_n[0]()
                            pend_n[0] = None
                            for t in range(NT):
                                bank = 6 + t % 2
                                tr(ps[bank][:, 0:128], rskv[:, t * 128:(t + 1) * 128], [('rskv',)], [('ps', bank)])
                                cp('dve', rskvt[:, t:t + 1], ps[bank][:, 0:1], [('ps', bank)], [('rskvt',)])
                    for tb in range(4):
                        cs = slice(tb * 512, (tb + 1) * 512)
                        bA, bB = (5, 6) if tb % 2 == 0 else (0, 1)
                        for kc in range(8):
                            mm(ps[bA][0:64, :], wkr_t[:, kc, 0:64], xhT[:, kc, cs], kc == 0, kc == 7, [('wkr',)] + XH(tb), [('ps', bA)])
                        for kc in range(8):
                            mm(ps[bB][0:64, :], wkr_t[:, kc, 64:128], xhT[:, kc, cs], kc == 0, kc == 7, [('wkr2',)] + XH(tb), [('ps', bB)])
                        tt('dve', t1[tb % 2][:], ps[bA][0:64, :], cos2[:, cs], ALU.mult, [('ps', bA), ('ctab',)], [('t1', tb % 2)])
                        tt('dve', t2[tb % 2][:], ps[bB][0:64, :], sin2[:, cs], ALU.mult, [('ps', bB), ('ctab',)], [('t2', tb % 2)])
                        tt('pool', krT[:, cs], t1[tb % 2][:], t2[tb % 2][:], ALU.add, [('t1', tb % 2), ('t2', tb % 2)], [('krT',)])
                    scale = float(192 ** -0.5)
                    ptc = [0]
                    for h in range(4):
                        par = 0
                        for tb in range(4):
                            cs = slice(tb * 512, (tb + 1) * 512)
                            b0, b1, b5, b6 = (0, 1, 5, 6) if tb % 2 == 0 else (2, 3, 4, 7)
                            for kc in range(3):
                                mm(ps[b0][:, :], wq_t[:, kc, h * 192:h * 192 + 128], cqg[:, kc, cs], kc == 0, kc == 2, [('wq',), ('cqg',)], [('ps', b0)])
                            for kc in range(3):
                                mm(ps[b5][0:64, :], wq_t[:, kc, h * 192 + 128:h * 192 + 192], cqg[:, kc, cs], kc == 0, kc == 2, [('wq',), ('cqg',)], [('ps', b5)])
                            for kc in range(3):
                                mm(ps[b6][0:64, :], wqr_t[:, kc, h, :], cqg[:, kc, cs], kc == 0, kc == 2, [('wqr',), ('cqg',)], [('ps', b6)])
                            for kc in range(2):
                                mm(ps[b1][:, :], wkv_t[:, kc, h * 256:h * 256 + 128], ckvg[:, kc, cs], kc == 0, kc == 1, [('wkv',), ('ckvg',)], [('ps', b1)])
                            tt('dve', qn[par][:, cs], ps[b0][:, :], rsq[:, cs], ALU.mult, [('ps', b0), ('rsq',)], [('qn', par)])
                            tt('dve', t1[tb % 2][:], ps[b5][0:64, :], cos2[:, cs], ALU.mult, [('ps', b5), ('ctab',)], [('t1', tb % 2)])
                            tt('dve', t2[tb % 2][:], ps[b6][0:64, :], sin2[:, cs], ALU.mult, [('ps', b6), ('ctab',)], [('t2', tb % 2)])
                            tt('dve', kn[par][:, cs], ps[b1][:, :], rskv[:, cs], ALU.mult, [('ps', b1), ('rskv',)], [('kn', par)])
                            tt('pool', t1[tb % 2][:], t1[tb % 2][:], t2[tb % 2][:], ALU.add, [('t1', tb % 2), ('t2', tb % 2)], [('t1', tb % 2)])
                            tt('pool', qr[par][:, cs], t1[tb % 2][:], rsq[0:64, cs], ALU.mult, [('t1', tb % 2), ('rsq',)], [('qr', par)])
                        for t in range(NT):
                            bank = 2 + t % 2
                            for kc in range(2):
                                mm(ps[bank][:, 0:128], ckvg[:, kc, t * 128:(t + 1) * 128], wkv_t[:, kc, h * 256 + 128:h * 256 + 256], kc == 0, kc == 1,
                                   [('wkv',), ('ckvg',)], [('ps', bank)])
                            amul(vh[par][:, t, :], ps[bank][:, 0:128], rskvt[:, t:t + 1], [('ps', bank), ('rskvt',)], [('vh', par)])
                        items = [(qb, kt) for qb in range(4) for kt in range(4 * qb + 4)]

                        def att_S(idx):
                            qb, kt = items[idx]
                            r = max(0, kt - 4 * qb)
                            c0 = qb * 512 + r * 128
                            ncol = 512 - r * 128
                            sb_ = ps[idx % 2]
                            ksb = ('ps', idx % 2)
                            mm(sb_[:, 0:ncol], kn[par][:, kt * 128:(kt + 1) * 128], qn[par][:, c0:c0 + ncol], True, False,
                               [('kn', par), ('qn', par)], [ksb], inc=False)
                            mm(sb_[:, 0:ncol], krT[:, kt * 128:(kt + 1) * 128], qr[par][:, c0:c0 + ncol], False, True,
                               [('krT',), ('qr', par)], [ksb])

                        def att_EV(idx):
                            qb, kt = items[idx]
                            nk = 4 * qb + 4
                            r = max(0, kt - 4 * qb)
                            ncol = 512 - r * 128
                            sb_ = ps[idx % 2]
                            ksb = ('ps', idx % 2)
                            pO_, pD_ = ps[4 + (qb % 2) * 2], ps[5 + (qb % 2) * 2]
                            kO, kD = ('ps', 4 + (qb % 2) * 2), ('ps', 5 + (qb % 2) * 2)
                            pi = ptc[0] % 3
                            ptc[0] += 1
                            act(PT[pi][:, 0:ncol], sb_[:, 0:ncol], AF.Exp, [ksb], [('PT', pi)], scale=scale)
                            if kt >= 4 * qb:
                                T.issue('pool', lambda e: e.memset(PT[pi][64:128, 0:64], 0.0), [], [('PT', pi)])
                            mm(pO_[:, r * 128:512], vh[par][:, kt, :], PT[pi][:, 0:ncol], kt == 0, kt == nk - 1, [('vh', par), ('PT', pi)], [kO])
                            mm(pD_[:, r * 128:512], onesb[:], PT[pi][:, 0:ncol], kt == 0, kt == nk - 1, [('onesb',), ('PT', pi)], [kD])
                            if kt == nk - 1:
                                dq = den[qb % 2]
                                T.issue('dve', lambda e: e.reciprocal(out=dq[:], in_=pD_[:, :]), [kD], [('den', qb % 2)])
                                tt('dve', mixT[:, 4 + h, qb * 512:(qb + 1) * 512], pO_[:, :], dq[:], ALU.mult, [kO, ('den', qb % 2)], [('mixT', 4 + h)])

                        att_S(0)
                        for idx in range(len(items)):
                            if idx + 1 < len(items):
                                att_S(idx + 1)
                            att_EV(idx)
                    T.barrier()
                dump('mixB', mixT[:, 4:8, :], [('mixT', 4 + h) for h in range(4)])
                if stop == 'mixB':
                    return True

                with ExitStack() as sc:
                    ln1_t = sb(sc, "ln1_t", [128, 2048], F32)
                    T.dma('sp', ln1_t[:], cbc[:, 0:2048], writes=[('cbcL',)], key=('cbcL',))
                    G1 = ln1_t[:, 0:1024]
                    B1 = ln1_t[:, 1024:2048]
                    xr = [sb(sc, f"xr{i}", [128, D], F32) for i in range(2)]
                    rr = [sb(sc, f"rr{i}", [128, D], F32) for i in range(2)]
                    hh = [sb(sc, f"hh{i}", [128, D], F32) for i in range(2)]
                    stats = sb(sc, "stats", [128, 12], F32)
                    mv = sb(sc, "mv", [128, 2], F32)
                    rs = sb(sc, "rs", [128, 1], F32)
                    def d1_mm(t):
                        sl = t % 2
                        tok = slice(t * 128, (t + 1) * 128)
                        for hb in range(2):
                            bank = sl * 2 + hb
                            for kc in range(8):
                                mm(ps[bank][:, :], mixT[:, kc, tok], wo_t[:, kc, hb * 512:(hb + 1) * 512], kc == 0, kc == 7,
                                   [('mixT', kc), ('wo',)], [('ps', bank)])

                    def d1_res(t):
                        sl = t % 2
                        for hb in range(2):
                            bank = sl * 2 + hb
                            stt(rr[sl][:, hb * 512:(hb + 1) * 512], xr[sl][:, hb * 512:(hb + 1) * 512], ALPHA, ps[bank][:, :], ALU.mult, ALU.add,
                                [('xr', sl), ('ps', bank)], [('rr', sl)])

                    def d1_ln(t):
                        sl = t % 2
                        ln_tile(rr[sl], G1, B1, (stats, mv, rs), ('rr', sl), hh[sl][:], ('hh', sl))
                        T.dma('sp', hscr[t * 128:(t + 1) * 128, :], hh[sl][:], reads=[('hh', sl)], writes=[('hscr', t)], key=('hh', sl))

                    def d1_tr(t):
                        sl = t % 2
                        tok = slice(t * 128, (t + 1) * 128)
                        for kc in range(8):
                            bank = 4 + sl * 2 + kc // 4
                            col = (kc % 4) * 128
                            tr(ps[bank][:, col:col + 128], hh[sl][:, kc * 128:(kc + 1) * 128], [('hh', sl)], [('ps', bank)], inc=(kc % 4 == 3))
                        for hb in range(2):
                            bank = 4 + sl * 2 + hb
                            cp('act', xhT[:, hb * 4:(hb + 1) * 4, tok], ps[bank][:, :].rearrange("p (k c) -> p k c", k=4), [('ps', bank)], [('xhT', t)])

                    T.dma('sp', xr[0][:], x[row0: row0 + 128, :], writes=[('xr', 0)], key=('xr', 0))
                    T.dma('sp', xr[1][:], x[row0 + 128: row0 + 256, :], writes=[('xr', 1)], key=('xr', 1))
                    d1_mm(0)
                    d1_res(0)
                    for t in range(NT):
                        if t + 1 < NT:
                            d1_mm(t + 1)
                        d1_ln(t)
                        if t + 2 < NT:
                            T.dma('sp', xr[t % 2][:], x[row0 + (t + 2) * 128: row0 + (t + 3) * 128, :], writes=[('xr', t % 2)], key=('xr', t % 2))
                        if t + 1 < NT:
                            d1_res(t + 1)
                        d1_tr(t)
                    T.barrier()
            dump('hT', xhT[:], [('xhT', t) for t in range(NT)])
            if stop == 'hT':
                return True

            with ExitStack() as p2:
                wd_t = sb(p2, "wd_t", [128, NJ, 1024], BF16)
                ln2_t = sb(p2, "ln2_t", [128, 3072], F32)
                T.dma('sp', ln2_t[:], cbc[:, 2048:5120], writes=[('cbcL',)], key=('cbcL2',))
                G2 = ln2_t[:, 0:1024]
                B2 = ln2_t[:, 1024:2048]
                BG = ln2_t[:, 2048:3072]
                wg_t = sb(p2, "wg_t", [128, 8, 1024], BF16)
                wp_t = sb(p2, "wp_t", [128, 2, 1024], BF16)
                actT = sb(p2, "actT", [128, NJ, BLK2], BF16)
                wup = [sb(p2, f"wup{i}", [128, 2, 8, 128], BF16) for i in range(3)]
                rawgu = [sb(p2, f"rawgu{i}", [128, 2, 2 + BLK2], F32) for i in range(2)]
                accg = [sb(p2, f"accg{i}", [128, BLK2], F32) for i in range(2)]
                accu = [sb(p2, f"accu{i}", [128, BLK2], F32) for i in range(2)]
                halo = sb(p2, "halo", [128, 2, NJ, 2], F32)
                hr_ = [sb(p2, f"hr{i}", [128, D], F32) for i in range(2)]
                r2_ = [sb(p2, f"r2{i}", [128, D], F32) for i in range(2)]
                sg_ = [sb(p2, f"sgt{i}", [128, D], F32) for i in range(2)]
                pin_ = [sb(p2, f"pin{i}", [128, 256], F32) for i in range(2)]
                pTb_ = [sb(p2, f"pTb{i}", [128, 2, 128], BF16) for i in range(2)]
                stats_ = [sb(p2, f"stats2{i}", [128, 12], F32) for i in range(2)]
                mv_ = [sb(p2, f"mv2{i}", [128, 2], F32) for i in range(2)]
                rs_ = [sb(p2, f"rs2{i}", [128, 1], F32) for i in range(2)]
                T.issue('pool', lambda e: e.memset(halo[:], 0.0), writes=[('halo', c_) for c_ in range(NJ)])
                def ld2(t_):
                    q_ = t_ % 2
                    T.dma('sp', hr_[q_][:], hscr[t_ * 128:(t_ + 1) * 128, :], reads=[('hscr', t_)], writes=[('hr', q_)], key=('hr', q_))
                    T.dma('sp', pin_[q_][:], p[row0 + t_ * 128:row0 + (t_ + 1) * 128, :], writes=[('pin', q_)], key=('pin', q_))

                NB = S // BLK2
                TPB = BLK2 // 128
                wc = [0]

                def prefetch_wup(upto):
                    while wc[0] < min(upto, NB * NJ):
                        jj = wc[0] % NJ
                        s_ = wc[0] % 3
                        wc[0] += 1
                        T.dma('pool', wup[s_][:, 0, :, :], w_up_v[:, :, jj * 128:(jj + 1) * 128], writes=[('wup', s_, 0)], key=('wup', s_, 0))
                        T.dma('pool', wup[s_][:, 1, :, :], w_up_v[:, :, DFF + jj * 128:DFF + (jj + 1) * 128], writes=[('wup', s_, 1)], key=('wup', s_, 1))
                prefetch_wup(3)
                T.dma('pool', wg_t[:], w_gate_v[:, :, :], writes=[('wg',)], key=('wg',))
                T.dma('pool', wp_t[:], w_proj_v[:, :, :], writes=[('wp',)], key=('wp',))
                T.dma('pool', wd_t[:], w_down_v[:, :, :], writes=[('wd',)], key=('wd',))

                def stage_B2(j_):
                    q_ = j_ % 2
                    kg_, ku_ = ('acc2', 0, q_), ('acc2', 1, q_)
                    act(accg[q_][:], accg[q_][:], AF.Silu, [kg_], [kg_])
                    tt('dve', actT[:, j_, :], accg[q_][:], accu[q_][:], ALU.mult, [kg_, ku_], [('actT', j_)])

                for blk in range(NB):
                    bs = slice(blk * BLK2, (blk + 1) * BLK2)
                    XB = [('xhT', blk * TPB + i) for i in range(TPB)]
                    for j in range(NJ):
                        step = blk * NJ + j
                        prefetch_wup(step + 3)
                        sl = step % 3
                        jp = j % 2
                        rg = rawgu[jp]
                        kr_ = ('raw2', jp)
                        for gu in range(2):
                            bank = jp * 2 + gu
                            for kc in range(8):
                                mm(ps[bank][:, :], wup[sl][:, gu, kc, :], xhT[:, kc, bs], kc == 0, kc == 7, [('wup', sl, gu)] + XB, [('ps', bank)])
                        cp('act', rg[:, :, 0:2], halo[:, :, j, :], [('halo', j)], [kr_ + ('h',)])
                        cp('act', rg[:, :, 2:2 + BLK2], ps_all[:, jp * 1024:(jp + 1) * 1024].rearrange("p (g c) -> p g c", g=2),
                           [('ps', jp * 2), ('ps', jp * 2 + 1)], [kr_])
                        cp('act', halo[:, :, j, :], rg[:, :, BLK2:BLK2 + 2], [kr_], [('halo', j)])
                        acs = (accg[jp], accu[jp])
                        kas = (('acc2', 0, jp), ('acc2', 1, jp))
                        act(acs[0][:], ps[jp * 2][:, :], AF.Identity, [('ps', jp * 2), ('cpp',)], [kas[0]], bias=fcb[:, j:j + 1], scale=fcw[:, j, 2:3])
                        ts('dve', acs[1][:], rg[:, 1, 2:2 + BLK2], fcw[:, NJ + j, 2:3], fcb[:, NJ + j:NJ + j + 1], ALU.mult, ALU.add, [kr_, ('cpp',)], [kas[1]])
                        for tap in (1, 0):
                            for gu in range(2):
                                cidx = gu * NJ + j
                                stt(acs[gu][:], rg[:, gu, tap:tap + BLK2], fcw[:, cidx, tap:tap + 1], acs[gu][:], ALU.mult, ALU.add,
                                    [kr_, kr_ + ('h',), kas[gu], ('cpp',)], [kas[gu]])
                        if j >= 1:
                            stage_B2(j - 1)
                    stage_B2(NJ - 1)
                    AK = [('actT', j) for j in range(NJ)]
                    if stop == 'p2a' and blk == 0:
                        T.barrier()
                        return True
                    for tl in range(TPB):
                        t = blk * TPB + tl
                        tok = slice(t * 128, (t + 1) * 128)
                        ltok = slice(tl * 128, (tl + 1) * 128)
                        grow = row0 + t * 128
                        tp_ = t % 2
                        hr, r2, sg, pin, pTb = hr_[tp_], r2_[tp_], sg_[tp_], pin_[tp_], pTb_[tp_]
                        kh, kr2, kpin, kpt = ('hr', tp_), ('r2', tp_), ('pin', tp_), ('pTb', tp_)
                        if t == 0:
                            ld2(0)
                        if t + 1 < NT:
                            ld2(t + 1)
                        for kc in range(2):
                            tr(ps[6 + tp_][:, kc * 128:(kc + 1) * 128], pin[:, kc * 128:(kc + 1) * 128], [kpin], [('ps', 6 + tp_)], inc=(kc == 1))
                        cp('act', pTb[:], ps[6 + tp_][:, 0:256].rearrange("p (k c) -> p k c", k=2), [('ps', 6 + tp_)], [kpt])
                        for hb in range(2):
                            hs = slice(hb * 512, (hb + 1) * 512)
                            ksg = ('sg', hb, tp_)
                            for kc in range(8):
                                mm(ps[2 + hb][:, :], xhT[:, kc, tok], wg_t[:, kc, hs], kc == 0, kc == 7, [('xhT', t), ('wg',)], [('ps', 2 + hb)])
                            for kc in range(2):
                                mm(ps[4 + hb][:, :], pTb[:, kc, :], wp_t[:, kc, hs], kc == 0, kc == 1, [kpt, ('wp',)], [('ps', 4 + hb)])
                            for j in range(NJ):
                                mm(ps[hb][:, :], actT[:, j, ltok], wd_t[:, j, hs], j == 0, j == NJ - 1, AK + [('wd',)], [('ps', hb)])
                            tt('dve', sg[:, hs], ps[2 + hb][:, :], BG[:, hs], ALU.add, [('ps', 2 + hb), ('cbcL',)], [ksg])
                            act(sg[:, hs], sg[:, hs], AF.Sigmoid, [ksg], [ksg])
                            tt('dve', sg[:, hs], sg[:, hs], ps[4 + hb][:, :], ALU.mult, [ksg, ('ps', 4 + hb)], [ksg])
                            stt(r2[:, hs], hr[:, hs], ALPHA, ps[hb][:, :], ALU.mult, ALU.add, [kh, ('ps', hb)], [kr2])
                            tt('dve', r2[:, hs], r2[:, hs], sg[:, hs], ALU.add, [kr2, ksg], [kr2])
                        ln_tile(r2, G2, B2, (stats_[tp_], mv_[tp_], rs_[tp_]), kr2, r2[:], kr2, sfx=tp_)
                        T.dma('sp', out[grow:grow + 128, :], r2[:], reads=[kr2], writes=[('out', sq, t)], key=('r2o', tp_))
                    if stop == 'p2b' and blk == 0:
                        T.barrier()
                        return True
                T.barrier()
          for sq in range(NSEQ):
            if seq_body(sq):
                break
        T.final_wait()
    nc._trk_log = T.log
    return nc


def _prep_common(inp):
    f = np.float32
    g = lambda k: np.asarray(inp[k], dtype=f)[0]
    cw = g("gdn_conv_w")
    fw = g("ffn_conv_w")
    fb = g("ffn_conv_b")
    cpp = np.zeros((128, 512), f)
    cpp[:, 0:48] = cw.reshape(4, 12, 128).transpose(2, 1, 0).reshape(128, 48)
    cpp[:, 48:180] = fw.reshape(3, 44, 128).transpose(2, 1, 0).reshape(128, 132)
    cpp[:, 180:224] = fb.reshape(44, 128).T
    cpp[:, 224] = g("gdn_norm_g")
    cpp[:, 225:228] = g("mla_q_norm_g").reshape(3, 128).T
    cpp[:, 228:230] = g("mla_kv_norm_g").reshape(2, 128).T
    cbc = np.zeros((128, 5 * 1024 + 128), f)
    for i, k in enumerate(["ln1_g", "ln1_b", "ln2_g", "ln2_b", "ple_b_gate"]):
        cbc[:, i * 1024:(i + 1) * 1024] = g(k)[None, :]
    cbc[:, 5120:5184] = np.tile(g("gdn_a_log"), 16)[None, :]
    cbc[:, 5184:5248] = np.tile(g("gdn_dt_bias"), 16)[None, :]
    j = np.arange(128)[:, None]
    i = np.arange(128)[None, :]
    cmat = np.zeros((128, 1792), f)
    cmat[:, 0:128] = np.eye(128, dtype=f)
    cmat[:, 128:256] = (j <= i).astype(f)
    for q_ in range(4):
        cmat[:, 256 + q_ * 128:384 + q_ * 128] = np.where(j <= i, 0.0, -30000.0).astype(f)
        cmat[:, 768 + q_ * 128:896 + q_ * 128] = np.where(j < i, 0.0, -30000.0).astype(f)
        cmat[:, 1280 + q_ * 128:1408 + q_ * 128] = np.eye(128, dtype=f)
    inv = (np.float32(10000.0) ** (-(np.arange(0, 64, 2, dtype=f)) / np.float32(64))).astype(f)
    ang = (np.arange(S, dtype=f)[:, None] * inv[None, :]).astype(f)
    cos = np.cos(ang.astype(np.float64)).astype(f).T
    sin = np.sin(ang.astype(np.float64)).astype(f).T
    ctab = np.zeros((64, 2 * S), f)
    ctab[0:32, 0:S] = cos
    ctab[32:64, 0:S] = cos
    ctab[0:32, S:] = sin
    ctab[32:64, S:] = sin
    return {
        "w_in": np.ascontiguousarray(g("w_in")), "wq_up": np.ascontiguousarray(g("mla_w_q_up")),
        "wkv_up": np.ascontiguousarray(g("mla_w_kv_up")), "w_out": np.ascontiguousarray(g("w_out")),
        "w_up": np.ascontiguousarray(g("ffn_w_up")), "w_down": np.ascontiguousarray(g("ffn_w_down")),
        "w_gate": np.ascontiguousarray(g("ple_w_gate")), "w_proj": np.ascontiguousarray(g("ple_w_proj")),
        "cpp": cpp, "cbc": cbc, "cmat": cmat, "ctab": ctab,
    }


def kernel(**inputs):
    common = _prep_common(inputs)
    x = np.asarray(inputs["x"], dtype=np.float32)
    p = np.asarray(inputs["p"], dtype=np.float32)[0]
    B = x.shape[0]
    nseq = B // NCORES
    nc = build(nseq)
    in_maps = []
    for c in range(NCORES):
        m = dict(common)
        m["x"] = np.ascontiguousarray(x[c * nseq:(c + 1) * nseq].reshape(nseq * S, D))
        m["p"] = np.ascontiguousarray(p[c * nseq:(c + 1) * nseq].reshape(nseq * S, 256))
        in_maps.append(m)
    res = run_bass_kernel_spmd(nc, in_maps, core_ids=list(range(NCORES)))
    outs = [np.asarray(r["out"]).reshape(nseq, S, D) for r in res.results]
    return np.concatenate(outs, axis=0).astype(np.float32)
```

```python
import numpy as np
from contextlib import ExitStack
import concourse.bass as bass
import concourse.mybir as mybir
from concourse.bass_utils import run_bass_kernel_spmd

F32, BF16 = mybir.dt.float32, mybir.dt.bfloat16
AF = mybir.ActivationFunctionType
ALU = mybir.AluOpType

S = 2048
NT = 16
D = 1024
KC = 8
DFF = 2816
NJ = 22
ALPHA = float(2.0 ** 0.25)
EPS = 1e-6
BLK2 = 512
HS = 1024
NBLK = 2
NTH = 8
EPOCH = 16000
NCORES = 8
TWO_CHAINS = True
OVERLAP_A = False
SEQ_GENS = True


class Trk:
    def __init__(self, nc, es):
        self.nc, self.es = nc, es
        self.engs = {'pe': nc.tensor, 'act': nc.scalar, 'dve': nc.vector, 'pool': nc.gpsimd, 'sp': nc.sync}
        self.cnt = {e: 0 for e in self.engs}
        self.esems = {e: [] for e in self.engs}
        self.seen = {e: {} for e in self.engs}
        self.lastw = {}
        self.rd = {}
        self.dsem = {}
        self.pend = {e: ([], []) for e in self.engs}
        self.latest = {}
        self.log = {e: [] for e in self.engs}

    def newsem(self, name):
        return self.es.enter_context(self.nc.semaphore(name))

    def _wait(self, e, ev):
        sem, val, src = ev
        if src == 'pe' and e == 'pe':
            return
        k = id(sem)
        if self.seen[e].get(k, 0) >= val:
            return
        self.engs[e].wait_ge(sem, val)
        self.log[e].append(('w', id(sem), val))
        self.seen[e][k] = val

    def _deps(self, e, reads, writes):
        for k in reads:
            ev = self.lastw.get(k)
            if ev is not None:
                self._wait(e, ev)
        for k in writes:
            ev = self.lastw.get(k)
            if ev is not None:
                self._wait(e, ev)
            for ev in self.rd.get(k, {}).values():
                self._wait(e, ev)

    def _reg(self, ev, reads, writes):
        sem, val, src = ev
        self.latest[id(sem)] = (sem, val)
        for k in writes:
            self.lastw[k] = ev
            self.rd[k] = {}
        for k in reads:
            self.rd.setdefault(k, {})[id(sem)] = ev

    def issue(self, e, fn, reads=(), writes=(), inc=True):
        writes = list(writes) + [k for k in reads if k[0] == 'ps' and k not in writes]
        reads = [k for k in reads if k[0] != 'ps']
        self._deps(e, reads, writes)
        ins = fn(self.engs[e])
        pr, pw = self.pend[e]
        pr.extend(reads)
        pw.extend(writes)
        if inc:
            n = self.cnt[e]
            ep, off = divmod(n, EPOCH)
            if ep >= len(self.esems[e]):
                self.esems[e].append(self.newsem(f"s_{e}_{ep}"))
            sem = self.esems[e][ep]
            ins.then_inc(sem, 1)
            self.log[e].append(('i', id(sem), 1))
            self.cnt[e] = n + 1
            self._reg((sem, off + 1, e), pr, pw)
            self.pend[e] = ([], [])
        return ins

    def dma(self, q, out, in_, reads=(), writes=(), key=None):
        self._deps(q, reads, writes)
        ins = self.engs[q].dma_start(out=out, in_=in_)
        if key not in self.dsem:
            self.dsem[key] = [self.newsem("d_" + "_".join(str(x) for x in key)), 0]
        d = self.dsem[key]
        d[1] += 16
        ins.then_inc(d[0], 16)
        self.log[q].append(('i', id(d[0]), 16))
        self._reg((d[0], d[1], 'dma'), list(reads), list(writes))
        return ins

    def barrier(self):
        for e in self.engs:
            assert not self.pend[e][0] and not self.pend[e][1], e
        for sem, val in list(self.latest.values()):
            self._wait('sp', (sem, val, 'x'))
        n = self.cnt['sp']
        ep, off = divmod(n, EPOCH)
        if ep >= len(self.esems['sp']):
            self.esems['sp'].append(self.newsem(f"s_sp_{ep}"))
        sem = self.esems['sp'][ep]
        self.engs['sp'].sem_inc(sem, 1)
        self.log['sp'].append(('i', id(sem), 1))
        self.cnt['sp'] = n + 1
        ev = (sem, off + 1, 'sp')
        self.latest[id(sem)] = (sem, off + 1)
        self.seen['sp'][id(sem)] = off + 1
        for e in self.engs:
            if e != 'sp':
                self._wait(e, ev)
        self.lastw.clear()
        self.rd.clear()

    def final_wait(self):
        for sem, val in list(self.latest.values()):
            self._wait('sp', (sem, val, 'x'))


class _Stop(Exception):
    pass


def build(NSEQ, dbg=None, stop=None):
    dbg = dbg or {}
    nc = bass.Bass("TRN2", target_bir_lowering=False)

    def din(name, shape, dt=F32):
        return nc.dram_tensor(name, list(shape), dt, kind="ExternalInput").ap()

    x = din("x", [NSEQ * S, D])
    p = din("p", [NSEQ * S, 256])
    w_in = din("w_in", [D, 2760])
    wq_up = din("wq_up", [384, 768])
    wkv_up = din("wkv_up", [256, 1024])
    w_out = din("w_out", [1024, 1024])
    w_up = din("w_up", [D, 2 * DFF])
    w_down = din("w_down", [DFF, D])
    w_gate = din("w_gate", [D, D])
    w_proj = din("w_proj", [256, D])
    cpp = din("cpp", [128, 512])
    cbc = din("cbc", [128, 5 * 1024 + 128])
    cmat = din("cmat", [128, 14 * 128])
    ctab = din("ctab", [64, 2 * S])
    out = nc.dram_tensor("out", [NSEQ * S, D], F32, kind="ExternalOutput").ap()
    hscr = nc.dram_tensor("hscr", [S, D], F32).ap()
    dbg_t = {k: nc.dram_tensor("dbg_" + k, list(v[0]), v[1], kind="ExternalOutput").ap() for k, v in dbg.items()}

    w_in_v = w_in.rearrange("(kc p) n -> p kc n", p=128)
    wq_v = wq_up.rearrange("(kc p) n -> p kc n", p=128)
    wkv_v = wkv_up.rearrange("(kc p) n -> p kc n", p=128)
    w_out_v = w_out.rearrange("(kc p) n -> p kc n", p=128)
    w_up_v = w_up.rearrange("(kc p) n -> p kc n", p=128)
    w_down_v = w_down.rearrange("(kc p) n -> p kc n", p=128)
    w_gate_v = w_gate.rearrange("(kc p) n -> p kc n", p=128)
    w_proj_v = w_proj.rearrange("(kc p) n -> p kc n", p=128)

    with ExitStack() as es:
        T = Trk(nc, es)

        uid = [0]

        def sb(scope, name, shape, dt):
            uid[0] += 1
            return scope.enter_context(nc.sbuf_tensor(f"{name}_{uid[0]}", list(shape), dt))

        ps_all = es.enter_context(nc.psum_tensor("ps_all", [128, 8 * 512], F32))
        ps = [ps_all[:, b * 512:(b + 1) * 512] for b in range(8)]

        cpp_t = sb(es, "cpp_t", [128, 512], F32)
        cbc_t = sb(es, "cbc_t", [128, 128], F32)
        cmat_t = sb(es, "cmat_t", [128, 1792], F32)
        identb = sb(es, "identb", [128, 128], BF16)
        onesb = sb(es, "onesb", [128, 128], BF16)
        c128b = sb(es, "c128b", [128, 128], BF16)
        c256b = sb(es, "c256b", [128, 128], BF16)
        onesf = sb(es, "onesf", [128, 128], F32)
        w_ab = sb(es, "w_ab", [128, 8, 8], BF16)
        xhT = sb(es, "xhT", [128, 8, S], BF16)

        T.dma('sp', cpp_t[:], cpp[:, :], writes=[('cpp',)], key=('cpp',))
        T.dma('sp', cbc_t[:], cbc[:, 5120:5248], writes=[('cbc',)], key=('cbc',))
        T.dma('sp', cmat_t[:], cmat[:, :], writes=[('cmat',)], key=('cmat',))
        T.dma('pool', w_ab[:], w_in_v[:, :, 2048:2056], writes=[('w_ab',)], key=('w_ab',))
        ident = cmat_t[:, 0:128]
        Umat = cmat_t[:, 128:256]
        NEGM = cmat_t[:, 256:384]
        NEGM4 = cmat_t[:, 256:768].rearrange("p (i c) -> p i c", i=4)
        NEGMS4 = cmat_t[:, 768:1280].rearrange("p (i c) -> p i c", i=4)
        ident4 = cmat_t[:, 1280:1792].rearrange("p (i c) -> p i c", i=4)
        T.issue('dve', lambda e: e.tensor_copy(out=identb[:], in_=ident), reads=[('cmat',)], writes=[('identb',)])
        T.issue('pool', lambda e: e.memset(onesb[:], 1.0), writes=[('onesb',)])
        T.issue('pool', lambda e: e.memset(c128b[:], 1.0 / 128), writes=[('c128b',)])
        T.issue('pool', lambda e: e.memset(c256b[:], 1.0 / 256), writes=[('c256b',)])
        T.issue('pool', lambda e: e.memset(onesf[:], 1.0), writes=[('onesf',)])
        epst = sb(es, "epst", [128, 2], F32)
        T.issue('pool', lambda e: e.memset(epst[:, 0:1], EPS), writes=[('epst',)])
        T.issue('pool', lambda e: e.memset(epst[:, 1:2], 384 * EPS), writes=[('epst',)])
        eps1 = epst[:, 0:1]
        eps384 = epst[:, 1:2]
        gcw = cpp_t[:, 0:48].rearrange("p (c j) -> p c j", j=4)
        fcw = cpp_t[:, 48:180].rearrange("p (c j) -> p c j", j=3)
        fcb = cpp_t[:, 180:224]
        normg = cpp_t[:, 224:225]
        qg = cpp_t[:, 225:228]
        kvg = cpp_t[:, 228:230]
        ALOGB = cbc_t[:, 0:64]
        DTBB = cbc_t[:, 64:128]
        CONST = [('cpp',), ('cbc',), ('cmat',)]

        def act(out_, in_, func, reads, writes, **kw):
            return T.issue('act', lambda e: e.activation(out=out_, in_=in_, func=func, **kw), reads, writes)

        def tt(eng, out_, in0, in1, op, reads, writes):
            return T.issue(eng, lambda e: e.tensor_tensor(out=out_, in0=in0, in1=in1, op=op), reads, writes)

        def ts(eng, out_, in0, s1, s2, op0, op1, reads, writes):
            if s2 is None:
                return T.issue(eng, lambda e: e.tensor_scalar(out=out_, in0=in0, scalar1=s1, scalar2=None, op0=op0), reads, writes)
            return T.issue(eng, lambda e: e.tensor_scalar(out=out_, in0=in0, scalar1=s1, scalar2=s2, op0=op0, op1=op1), reads, writes)

        def stt(out_, in0, sc, in1, op0, op1, reads, writes):
            return T.issue('dve', lambda e: e.scalar_tensor_tensor(out=out_, in0=in0, scalar=sc, in1=in1, op0=op0, op1=op1), reads, writes)

        def cp(eng, out_, in_, reads, writes):
            if eng == 'act':
                return T.issue('act', lambda e: e.copy(out=out_, in_=in_), reads, writes)
            return T.issue(eng, lambda e: e.tensor_copy(out=out_, in_=in_), reads, writes)

        def rsqrt(out_, in_, eps_ap, reads, writes):
            act(out_, in_, AF.Ln, list(reads) + [('epst',)], writes, bias=eps_ap)
            act(out_, out_, AF.Exp, writes, writes, scale=-0.5)

        def amul(out_, in_, m, reads, writes):
            return T.issue('act', lambda e: e.mul(out=out_, in_=in_, mul=m), reads, writes)

        def mm(out_, lhsT, rhs, start, stop, reads, writes, inc=None):
            if inc is None:
                inc = stop
            return T.issue('pe', lambda e: e.matmul(out_, lhsT, rhs, start=start, stop=stop), reads, writes, inc=inc)

        def tr(out_, in_, reads, writes, inc=True):
            return T.issue('pe', lambda e: e.transpose(out_, in_, ident), list(reads) + [('cmat',)], writes, inc=inc)

        def dump(name, src, reads):
            if name in dbg_t:
                T.dma('sp', dbg_t[name], src, reads=reads, writes=[('dbg', name)], key=('dbg', name))

        def ln_tile(r, G, B, scope_tiles, key_r, out_tile, key_out, kc_=('cbcL',), sfx=''):
            stats, mv, rs = scope_tiles
            for hb in range(2):
                T.issue('dve', lambda e: e.bn_stats(out=stats[:, hb * 6:(hb + 1) * 6], in_=r[:, hb * 512:(hb + 1) * 512]),
                        reads=[key_r], writes=[('lnst', sfx)])
            T.issue('dve', lambda e: e.bn_aggr(out=mv[:], in_=stats[:]), reads=[('lnst', sfx)], writes=[('lnmv', sfx)])
            rsqrt(rs[:], mv[:, 1:2], eps1, [('lnmv', sfx)], [('lnrs', sfx)])
            ts('dve', r[:], r[:], mv[:, 0:1], rs[:, 0:1], ALU.subtract, ALU.mult, [key_r, ('lnmv', sfx), ('lnrs', sfx)], [key_r])
            tt('dve', r[:], r[:], G, ALU.mult, [key_r, kc_], [key_r])
            tt('dve', out_tile, r[:], B, ALU.add, [key_r, kc_], [key_out])

        if True:
          def seq_body(sq):
            row0 = sq * S
            with ExitStack() as p1:
                mixT = sb(p1, "mixT", [128, 8, S], BF16)
                with ExitStack() as sc:
                    xin = [sb(sc, f"xin{i}", [128, D], F32) for i in range(2)]
                    for t in range(NT):
                        sl = t % 2
                        T.dma('sp', xin[sl][:], x[row0 + t * 128: row0 + (t + 1) * 128, :], writes=[('xin', sl)], key=('xin', sl))
                        pb = (t % 2) * 2
                        for kc in range(8):
                            bank = pb + kc // 4
                            col = (kc % 4) * 128
                            tr(ps[bank][:, col:col + 128], xin[sl][:, kc * 128:(kc + 1) * 128], [('xin', sl)], [('ps', bank)], inc=(kc % 4 == 3))
                        for hb in range(2):
                            bank = pb + hb
                            cp('act' if hb == 0 else 'dve', xhT[:, hb * 4:(hb + 1) * 4, t * 128:(t + 1) * 128],
                               ps[bank][:, :].rearrange("p (k c) -> p k c", k=4), [('ps', bank)], [('xhT', t)])
                    T.barrier()
                dump('xT', xhT[:], [('xhT', t) for t in range(NT)])
                if stop == 'xT':
                    return True

                with ExitStack() as sc:
                    g_ab = sb(sc, "g_ab", [128, 128], F32)
                    g_beta = sb(sc, "g_beta", [128, 64], F32)
                    g_g = sb(sc, "g_g", [128, 64], F32)
                    g_tmp = sb(sc, "g_tmp", [128, 64], F32)
                    g_eal = sb(sc, "g_eal", [128, 64], F32)
                    g_gc = sb(sc, "g_gc", [128, 64], F32)
                    g_ngc = sb(sc, "g_ngc", [128, 64], F32)
                    g_eg = sb(sc, "g_eg", [128, 64], F32)
                    g_egl = sb(sc, "g_egl", [128, 64], F32)
                    g_egla = sb(sc, "g_egla", [128, 64], F32)
                    raw2 = [sb(sc, f"raw{i}", [128, 3 + HS], F32) for i in range(2)]
                    acc = sb(sc, "acc", [128, HS], F32)
                    sil2 = [sb(sc, f"sil{i}", [128, HS], F32) for i in range(2)]
                    sqb = sb(sc, "sqb", [128, HS], BF16)
                    rstd = [sb(sc, f"rstd{i}", [128, 512], F32) for i in range(2)]
                    halo_g = sb(sc, "halo_g", [128, 12, 3], F32)
                    zero3 = sb(sc, "zero3", [128, 3], F32)
                    wst = [sb(sc, f"wst{i}", [128, 8, 128], BF16) for i in range(3)]
                    hq = sb(sc, "hq", [128, 4, HS], BF16)
                    hk = sb(sc, "hk", [128, 4, HS], BF16)
                    hkg = sb(sc, "hkg", [128, 4, NTH, 128], BF16)
                    hkd = sb(sc, "hkd", [128, 4, NTH, 128], BF16)
                    hv = sb(sc, "hv", [128, 4, NTH, 128], BF16)
                    hz = sb(sc, "hz", [128, 4, HS], BF16)
                    S32 = sb(sc, "S32", [128, 4, 128], F32)
                    Sbf = sb(sc, "Sbf", [128, 4, 128], BF16)

                    def tmpp(name, dt):
                        return sb(sc, name, [128, 4, 128], dt)
                    Ug = tmpp("Ug", F32)
                    EGb = tmpp("EGb", F32)
                    ARG = tmpp("ARG", F32)
                    ARG2 = tmpp("ARG2", F32)
                    DT = tmpp("DT", F32)
                    DTs = tmpp("DTs", F32)
                    Nf = tmpp("Nf", F32)
                    Pb = [tmpp("Pb0_", BF16), tmpp("Pb1_", BF16)]
                    PTb = [tmpp("PTb0_", BF16), tmpp("PTb1_", BF16)]
                    Xb = [tmpp("Xb0_", BF16), tmpp("Xb1_", BF16)]
                    QKD = tmpp("QKD", BF16)
                    nw2T = tmpp("nw2T", BF16)
                    vnew = tmpp("vnew", BF16)
                    qgT = tmpp("qgT", BF16)
                    sqo = tmpp("sqo", BF16)
                    rso = tmpp("rso", F32)
                    o1 = tmpp("o1", F32)

                    T.issue('pool', lambda e: e.memset(zero3[:], 0.0), writes=[('zero3',)])
                    cur_half = [0]

                    for t in range(NT):
                        for kc in range(8):
                            mm(ps[7][:, t * 8:(t + 1) * 8], xhT[:, kc, t * 128:(t + 1) * 128], w_ab[:, kc, :], kc == 0, kc == 7,
                               [('xhT', t), ('w_ab',)], [('ps', 7)], inc=(kc == 7 and t == NT - 1))
                    cp('dve', g_ab[:], ps[7][:, 0:128], [('ps', 7)], [('g_ab',)])
                    abv = g_ab[:].rearrange("p (t c) -> p t c", c=8)
                    v64 = lambda tl: tl[:].rearrange("p (t c) -> p t c", c=4)
                    act(v64(g_beta), abv[:, :, 4:8], AF.Sigmoid, [('g_ab',)], [('g_beta',)])
                    tt('dve', v64(g_tmp), abv[:, :, 0:4], DTBB.rearrange("p (t c) -> p t c", c=4), ALU.add, [('g_ab',), ('cbc',)], [('g_tmp',)])
                    act(g_tmp[:], g_tmp[:], AF.Exp, [('g_tmp',)], [('g_tmp',)])
                    ts('dve', g_tmp[:], g_tmp[:], 1.0, None, ALU.add, None, [('g_tmp',)], [('g_tmp',)])
                    act(g_tmp[:], g_tmp[:], AF.Ln, [('g_tmp',)], [('g_tmp',)])
                    act(g_eal[:], ALOGB, AF.Exp, [('cbc',)], [('g_eal',)])
                    stt(g_g[:], g_tmp[:], -1.0, g_eal[:], ALU.mult, ALU.mult, [('g_tmp',), ('g_eal',)], [('g_g',)])
                    mm(ps[7][:, 128:192], Umat, g_g[:], True, True, [('cmat',), ('g_g',)], [('ps', 7)])
                    mm(ps[7][:, 192:256], onesf[:], g_g[:], True, True, [('onesf',), ('g_g',)], [('ps', 7)])
                    cp('dve', g_gc[:], ps[7][:, 128:192], [('ps', 7)], [('g_gc',)])
                    ts('dve', g_ngc[:], g_gc[:], -1.0, None, ALU.mult, None, [('g_gc',)], [('g_ngc',)])
                    act(g_eg[:], g_gc[:], AF.Exp, [('g_gc',)], [('g_eg',)])
                    tt('dve', g_egl[:], ps[7][:, 192:256], g_gc[:], ALU.subtract, [('ps', 7), ('g_gc',)], [('g_egl',)])
                    act(g_egl[:], g_egl[:], AF.Exp, [('g_egl',)], [('g_egl',)])
                    act(g_egla[:], ps[7][:, 192:256], AF.Exp, [('ps', 7)], [('g_egla',)])
                    GS = [('g_beta',), ('g_gc',), ('g_ngc',), ('g_eg',), ('g_egl',), ('g_egla',), ('g_g',)]

                    wcnt = [0]

                    def load_wchunk(c0):
                        sl = wcnt[0] % 3
                        wcnt[0] += 1
                        T.dma('pool', wst[sl][:], w_in_v[:, :, c0:c0 + 128], writes=[('wst', sl)], key=('wst', sl))
                        return sl

                    def proj_block(sl, tb, bank):
                        gtb = cur_half[0] * NBLK + tb
                        for kc in range(8):
                            mm(ps[bank][:, :], wst[sl][:, kc, :], xhT[:, kc, gtb * 512:(gtb + 1) * 512], kc == 0, kc == 7,
                               [('wst', sl)] + [('xhT', gtb * 4 + i) for i in range(4)], [('ps', bank)])

                    trc = [0]

                    def stage_P(ch, tb):
                        kind, h, cidx, sl, ci = ch
                        par = h
                        rp = ci % 2
                        raw = raw2[rp]
                        bank = 6 + tb % 2
                        cs = slice(tb * 512, (tb + 1) * 512)
                        proj_block(sl, tb, bank)
                        if kind == 'z':
                            act(hz[:, par, cs], ps[bank][:, :], AF.Silu, [('ps', bank)], [('hz', par)])
                            return
                        if tb == 0:
                            if cur_half[0] == 0:
                                cp('act', raw[:, 0:3], zero3[:], [('zero3',)], [('raw', rp, -1)])
                            else:
                                cp('act', raw[:, 0:3], halo_g[:, cidx, :], [('halo_g', cidx)], [('raw', rp, -1)])
                        cp('act', raw[:, 3 + tb * 512: 3 + (tb + 1) * 512], ps[bank][:, :], [('ps', bank)], [('raw', rp, tb)])
                        if tb == NBLK - 1 and cur_half[0] == 0:
                            cp('act', halo_g[:, cidx, :], raw[:, HS:HS + 3], [('raw', rp, tb)], [('halo_g', cidx)])

                    def stage_C(ch, tb):
                        kind, h, cidx, sl, ci = ch
                        if kind == 'z':
                            return
                        rp = ci % 2
                        raw = raw2[rp]
                        cs = slice(tb * 512, (tb + 1) * 512)
                        RK = [('raw', rp, tb), ('raw', rp, tb - 1), ('cpp',)]
                        ka = ('acc', tb)
                        ts('dve', acc[:, cs], raw[:, 3 + tb * 512: 3 + (tb + 1) * 512], gcw[:, cidx, 3:4], None, ALU.mult, None, RK, [ka])
                        for j in (2, 1, 0):
                            stt(acc[:, cs], raw[:, j + tb * 512: j + (tb + 1) * 512], gcw[:, cidx, j:j + 1], acc[:, cs], ALU.mult, ALU.add,
                                RK + [ka], [ka])

                    def stage_S(ch, tb):
                        kind, h, cidx, sl, ci = ch
                        if kind == 'z':
                            return
                        sp_ = ci % 2
                        cs = slice(tb * 512, (tb + 1) * 512)
                        act(sil2[sp_][:, cs], acc[:, cs], AF.Silu, [('acc', tb)], [('sil', sp_, tb)])

                    def stage_N(ch):
                        kind, h, cidx, sl, ci = ch
                        if kind not in ('q', 'k'):
                            return
                        par = h
                        sp_ = ci % 2
                        sl_ = sil2[sp_]
                        for tb in range(NBLK):
                            cs = slice(tb * 512, (tb + 1) * 512)
                            tt('pool', sqb[:, cs], sl_[:, cs], sl_[:, cs], ALU.mult, [('sil', sp_, tb)], [('sqb', tb)])
                            mm(ps[2 + tb][:, :], onesb[:], sqb[:, cs], True, True, [('onesb',), ('sqb', tb)], [('ps', 2 + tb)])
                        for tb in range(NBLK):
                            act(rstd[tb][:], ps[2 + tb][:, :], AF.Ln, [('ps', 2 + tb), ('epst',)], [('rstd', tb)], bias=eps1)
                        for tb in range(NBLK):
                            act(rstd[tb][:], rstd[tb][:], AF.Exp, [('rstd', tb)], [('rstd', tb)], scale=-0.5)
                        for tb in range(NBLK):
                            cs = slice(tb * 512, (tb + 1) * 512)
                            ks = ('sil', sp_, tb)
                            if kind == 'q':
                                stt(hq[:, par, cs], sl_[:, cs], float(128 ** -0.5), rstd[tb][:], ALU.mult, ALU.mult,
                                    [ks, ('rstd', tb)], [('hq', par)])
                            else:
                                tt('dve', sl_[:, cs], sl_[:, cs], rstd[tb][:], ALU.mult, [ks, ('rstd', tb)], [ks])
                                cp('act', hk[:, par, cs], sl_[:, cs], [ks], [('hk', par)])

                    def stage_T(ch, tb):
                        kind, h, cidx, sl, ci = ch
                        if kind not in ('k', 'v'):
                            return
                        par = h
                        sp_ = ci % 2
                        ks = ('sil', sp_, tb)
                        for tl in range(4):
                            n = tb * 4 + tl
                            c = (cur_half[0] * NTH + n) * 4 + h
                            b3 = trc[0] % 2
                            trc[0] += 1
                            tr(ps[b3][:, 0:128], sil2[sp_][:, n * 128:(n + 1) * 128], [ks], [('ps', b3)])
                            if kind == 'k':
                                amul(hkg[:, par, n, :], ps[b3][:, 0:128], g_eg[:, c:c + 1], [('ps', b3), ('g_eg',)], [('hkg', par)])
                                ts('dve', hkd[:, par, n, :], ps[b3][:, 0:128], g_egl[:, c:c + 1], None, ALU.mult, None,
                                   [('ps', b3), ('g_egl',)], [('hkd', par)])
                            else:
                                cp('act' if n % 2 else 'dve', hv[:, par, n, :], ps[b3][:, 0:128], [('ps', b3)], [('hv', par)])

                    def run_A_quad():
                        chs = []
                        for h in range(4):
                            for kind, c0, cidx in (('q', h * 128, h), ('k', 512 + h * 128, 4 + h), ('v', 1024 + h * 128, 8 + h), ('z', 1536 + h * 128, 0)):
                                chs.append([kind, h, cidx, None, len(chs), c0])
                        nch = len(chs)
                        loaded = [0]

                        def ensure_loaded(upto):
                            while loaded[0] <= min(upto, nch - 1):
                                ch_ = chs[loaded[0]]
                                ch_[3] = load_wchunk(ch_[5])
                                loaded[0] += 1
                        nit = NBLK * nch
                        for tau in range(nit + NBLK + 5):
                            if tau < nit:
                                i, tb = divmod(tau, NBLK)
                                if tb == 0:
                                    ensure_loaded(i + 1)
                                stage_P(tuple(chs[i][:5]), tb)
                            if 0 <= tau - 1 < nit:
                                i, tb = divmod(tau - 1, NBLK)
                                stage_C(tuple(chs[i][:5]), tb)
                            if 0 <= tau - 2 < nit:
                                i, tb = divmod(tau - 2, NBLK)
                                stage_S(tuple(chs[i][:5]), tb)
                            tn = tau - (NBLK + 2)
                            if tn >= 0 and tn % NBLK == 0 and tn // NBLK < nch:
                                stage_N(tuple(chs[tn // NBLK][:5]))
                            if 0 <= tau - (NBLK + 3) < nit:
                                i, tb = divmod(tau - (NBLK + 3), NBLK)
                                stage_T(tuple(chs[i][:5]), tb)

                    def H4(b):
                        return ps[b][:, :].rearrange("p (i c) -> p i c", i=4), ('ps', b)

                    def rec_quad(half):
                        if half == 0:
                            T.issue('pool', lambda e: e.memset(S32[:], 0.0), writes=[('S32',)])
                            T.issue('pool', lambda e: e.memset(Sbf[:], 0.0), writes=[('Sbf',)])
                        pGb, kGb = H4(0)
                        pX, kX = H4(6)
                        pKK, kKK = H4(1)
                        pV, kV = H4(1)
                        pQK, kQK = H4(2)
                        pO, kO = H4(2)
                        pNT, kNT = H4(3)
                        pS, kS = H4(3)
                        pP, kP = H4(4)
                        pR, kR = H4(4)
                        pPT, kPT = H4(5)
                        pW, kW = H4(0)
                        for n in range(NTH):
                            tok = slice(n * 128, (n + 1) * 128)
                            gtok = slice((half * NTH + n) * 128, (half * NTH + n + 1) * 128)
                            cc = [(half * NTH + n) * 4 + i for i in range(4)]
                            for i in range(4):
                                amul(Ug[:, i, :], Umat, g_g[:, cc[i]:cc[i] + 1], [('cmat',), ('g_g',)], [('Ug',)])
                            for i in range(4):
                                mm(pGb[:, i, :], onesf[:], Ug[:, i, :], True, True, [('onesf',), ('Ug',)], [kGb], inc=(i == 3))
                            tt('dve', ARG2[:], pGb, NEGMS4, ALU.add, [kGb, ('cmat',)], [('ARG2',)])
                            tt('dve', ARG[:], pGb, NEGM4, ALU.add, [kGb, ('cmat',)], [('ARG',)])
                            act(EGb[:], pGb, AF.Exp, [kGb], [('EGb',)])
                            for i in range(4):
                                mm(pKK[:, i, :], hk[:, i, tok], hk[:, i, tok], True, True, [('hk', i)], [kKK], inc=(i == 3))
                            for i in range(4):
                                mm(pQK[:, i, :], hk[:, i, tok], hq[:, i, tok], True, True, [('hk', i), ('hq', i)], [kQK], inc=(i == 3))
                            for i in range(4):
                                act(DTs[:, i, :], ARG2[:, i, :], AF.Exp, [('ARG2',), ('g_ngc',)], [('DTs',)], bias=g_ngc[:, cc[i]:cc[i] + 1])
                            for i in range(4):
                                act(DT[:, i, :], ARG[:, i, :], AF.Exp, [('ARG',), ('g_ngc',)], [('DT',)], bias=g_ngc[:, cc[i]:cc[i] + 1])
                            for i in range(4):
                                stt(Nf[:, i, :], pKK[:, i, :], g_beta[:, cc[i]:cc[i] + 1], DTs[:, i, :], ALU.mult, ALU.mult,
                                    [kKK, ('g_beta',), ('DTs',)], [('Nf',)])
                            for i in range(4):
                                tr(pNT[:, i, :], Nf[:, i, :], [('Nf',)], [kNT], inc=(i == 3))
                            cur = 0
                            cp('act', Pb[cur][:], Nf[:], [('Nf',)], [('Pb0',)])
                            cp('dve', PTb[cur][:], pNT, [kNT], [('PTb0',)])
                            tt('dve', Xb[cur][:], ident4, Nf[:], ALU.subtract, [('cmat',), ('Nf',)], [('Xb0',)])
                            tt('dve', QKD[:], pQK, DT[:], ALU.mult, [kQK, ('DT',)], [('QKD',)])
                            tt('pool', qgT[:], hq[:, :, tok], EGb[:], ALU.mult, [('hq', i_) for i_ in range(4)] + [('EGb',)], [('qgT',)])
                            xc = 0

                            def x_update(ptb_idx, step_):
                                nonlocal_xc = x_state[0]
                                xn = 1 - nonlocal_xc
                                for i in range(4):
                                    mm(pX[:, i, :], identb[:], Xb[nonlocal_xc][:, i, :], True, False, [('identb',), (f'Xb{nonlocal_xc}',)], [kX], inc=False)
                                    mm(pX[:, i, :], PTb[ptb_idx][:, i, :], Xb[nonlocal_xc][:, i, :], False, True,
                                       [(f'PTb{ptb_idx}',), (f'Xb{nonlocal_xc}',)], [kX], inc=(i == 3))
                                x_state[0] = xn
                                return xn
                            x_state = [0]
                            pend = None
                            for step in range(6):
                                nx = 1 - cur
                                kPc, kPTc = (f'Pb{cur}',), (f'PTb{cur}',)
                                kPn, kPTn = (f'Pb{nx}',), (f'PTb{nx}',)
                                for i in range(4):
                                    mm(pPT[:, i, :], Pb[cur][:, i, :], PTb[cur][:, i, :], True, True, [kPc, kPTc], [kPT], inc=(i == 3))
                                if step < 5:
                                    for i in range(4):
                                        mm(pP[:, i, :], PTb[cur][:, i, :], Pb[cur][:, i, :], True, True, [kPc, kPTc], [kP], inc=(i == 3))
                                if pend is not None:
                                    xn = x_update(pend, step)
                                cp('dve', PTb[nx][:], pPT, [kPT], [kPTn])
                                if step < 5:
                                    cp('act', Pb[nx][:], pP, [kP], [kPn])
                                if pend is not None:
                                    cp('act' if step % 2 else 'dve', Xb[xn][:], pX, [kX], [(f'Xb{xn}',)])
                                pend = nx
                                cur = nx
                            xn = x_update(pend, 6)
                            cp('act', Xb[xn][:], pX, [kX], [(f'Xb{xn}',)])
                            cur = xn
                            kT2 = (f'Xb{cur}',)
                            T2T = Xb[cur]
                            for i in range(4):
                                mm(pW[:, i, :], hkg[:, i, n, :], T2T[:, i, :], True, True, [('hkg', i), kT2], [kW], inc=(i == 3))
                            amul(nw2T[:], pW, -1.0, [kW], [('nw2T',)])
                            for i in range(4):
                                mm(pV[:, i, :], T2T[:, i, :], hv[:, i, n, :], True, False, [kT2, ('hv', i)], [kV], inc=False)
                                mm(pV[:, i, :], nw2T[:, i, :], Sbf[:, i, :], False, True, [('nw2T',), ('Sbf',)], [kV], inc=(i == 3))
                            for i in range(4):
                                ts('dve', vnew[:, i, :], pV[:, i, :], g_beta[:, cc[i]:cc[i] + 1], None, ALU.mult, None, [kV, ('g_beta',)], [('vnew',)])
                            for i in range(4):
                                mm(pS[:, i, :], hkd[:, i, n, :], vnew[:, i, :], True, True, [('hkd', i), ('vnew',)], [kS], inc=(i == 3))
                            for i in range(4):
                                mm(pO[:, i, :], Sbf[:, i, :], qgT[:, i, :], True, False, [('Sbf',), ('qgT',)], [kO], inc=False)
                                mm(pO[:, i, :], vnew[:, i, :], QKD[:, i, :], False, True, [('vnew',), ('QKD',)], [kO], inc=(i == 3))
                            for i in range(4):
                                stt(S32[:, i, :], S32[:, i, :], g_egla[:, cc[i]:cc[i] + 1], pS[:, i, :], ALU.mult, ALU.add,
                                    [('S32',), ('g_egla',), kS], [('S32',)])
                            cp('act', Sbf[:], S32[:], [('S32',)], [('Sbf',)])
                            act(sqo[:], pO, AF.Square, [kO], [('sqo',)])
                            for i in range(4):
                                mm(pR[:, i, :], c128b[:], sqo[:, i, :], True, True, [('c128b',), ('sqo',)], [kR], inc=(i == 3))
                            rsqrt(rso[:], pR, eps1, [kR], [('rso',)])
                            stt(o1[:], pO, normg, rso[:], ALU.mult, ALU.mult, [kO, ('cpp',), ('rso',)], [('o1',)])
                            tt('pool', mixT[:, 0:4, gtok], o1[:], hz[:, :, tok], ALU.mult, [('o1',)] + [('hz', i_) for i_ in range(4)],
                               [('mixT', i_) for i_ in range(4)])

                    def gen_rec_pair(half, pr):
                        ia = 2 * pr
                        ii = (ia, ia + 1)
                        sl_ = slice(ia, ia + 2)
                        B = 4 * pr

                        def HB(b, hf):
                            return ps[b][:, hf * 256:(hf + 1) * 256].rearrange("p (i c) -> p i c", i=2), ('ps', b)
                        pGb, kGb = HB(B, 0)
                        pW, kW = HB(B, 0)
                        pP, kP = HB(B, 1)
                        pR, kR = HB(B, 1)
                        pKK, kKK = HB(B + 1, 0)
                        pV, kV = HB(B + 1, 0)
                        pPT, kPT = HB(B + 1, 1)
                        pQK, kQK = HB(B + 2, 0)
                        pO, kO = HB(B + 2, 0)
                        pX, kX = HB(B + 2, 1)
                        pNT, kNT = HB(B + 3, 0)
                        pS, kS = HB(B + 3, 0)
                        K = lambda nm: (nm, pr)
                        last = ia + 1
                        for n in range(NTH):
                            tok = slice(n * 128, (n + 1) * 128)
                            gtok = slice((half * NTH + n) * 128, (half * NTH + n + 1) * 128)
                            cc = {i: (half * NTH + n) * 4 + i for i in ii}
                            for i in ii:
                                amul(Ug[:, i, :], Umat, g_g[:, cc[i]:cc[i] + 1], [('cmat',), ('g_g',)], [K('Ug')])
                            yield
                            for i in ii:
                                mm(pGb[:, i - ia, :], onesf[:], Ug[:, i, :], True, True, [('onesf',), K('Ug')], [kGb], inc=(i == last))
                            yield
                            tt('dve', ARG2[:, sl_, :], pGb, NEGMS4[:, 0:2, :], ALU.add, [kGb, ('cmat',)], [K('ARG2')])
                            tt('dve', ARG[:, sl_, :], pGb, NEGM4[:, 0:2, :], ALU.add, [kGb, ('cmat',)], [K('ARG')])
                            act(EGb[:, sl_, :], pGb, AF.Exp, [kGb], [K('EGb')])
                            for i in ii:
                                mm(pKK[:, i - ia, :], hk[:, i, tok], hk[:, i, tok], True, True, [('hk', i)], [kKK], inc=(i == last))
                            for i in ii:
                                mm(pQK[:, i - ia, :], hk[:, i, tok], hq[:, i, tok], True, True, [('hk', i), ('hq', i)], [kQK], inc=(i == last))
                            yield
                            for i in ii:
                                act(DTs[:, i, :], ARG2[:, i, :], AF.Exp, [K('ARG2'), ('g_ngc',)], [K('DTs')], bias=g_ngc[:, cc[i]:cc[i] + 1])
                            for i in ii:
                                act(DT[:, i, :], ARG[:, i, :], AF.Exp, [K('ARG'), ('g_ngc',)], [K('DT')], bias=g_ngc[:, cc[i]:cc[i] + 1])
                            yield
                            for i in ii:
                                stt(Nf[:, i, :], pKK[:, i - ia, :], g_beta[:, cc[i]:cc[i] + 1], DTs[:, i, :], ALU.mult, ALU.mult,
                                    [kKK, ('g_beta',), K('DTs')], [K('Nf')])
                            yield
                            for i in ii:
                                tr(pNT[:, i - ia, :], Nf[:, i, :], [K('Nf')], [kNT], inc=(i == last))
                            cur = 0
                            cp('act', Pb[cur][:, sl_, :], Nf[:, sl_, :], [K('Nf')], [K('Pb0')])
                            yield
                            cp('dve', PTb[cur][:, sl_, :], pNT, [kNT], [K('PTb0')])
                            tt('dve', Xb[cur][:, sl_, :], ident4[:, 0:2, :], Nf[:, sl_, :], ALU.subtract, [('cmat',), K('Nf')], [K('Xb0')])
                            tt('dve', QKD[:, sl_, :], pQK, DT[:, sl_, :], ALU.mult, [kQK, K('DT')], [K('QKD')])
                            tt('pool', qgT[:, sl_, :], hq[:, sl_, tok], EGb[:, sl_, :], ALU.mult, [('hq', i_) for i_ in ii] + [K('EGb')], [K('qgT')])
                            yield
                            xs = [0]

                            def x_update(ptb_idx):
                                xc_ = xs[0]
                                xn_ = 1 - xc_
                                for i in ii:
                                    mm(pX[:, i - ia, :], identb[:], Xb[xc_][:, i, :], True, False, [('identb',), K(f'Xb{xc_}')], [kX], inc=False)
                                    mm(pX[:, i - ia, :], PTb[ptb_idx][:, i, :], Xb[xc_][:, i, :], False, True,
                                       [K(f'PTb{ptb_idx}'), K(f'Xb{xc_}')], [kX], inc=(i == last))
                                xs[0] = xn_
                                return xn_
                            pend = None
                            for step in range(6):
                                nx = 1 - cur
                                kPc, kPTc = K(f'Pb{cur}'), K(f'PTb{cur}')
                                kPn, kPTn = K(f'Pb{nx}'), K(f'PTb{nx}')
                                for i in ii:
                                    mm(pPT[:, i - ia, :], Pb[cur][:, i, :], PTb[cur][:, i, :], True, True, [kPc, kPTc], [kPT], inc=(i == last))
                                if step < 5:
                                    for i in ii:
                                        mm(pP[:, i - ia, :], PTb[cur][:, i, :], Pb[cur][:, i, :], True, True, [kPc, kPTc], [kP], inc=(i == last))
                                if pend is not None:
                                    xn = x_update(pend)
                                yield
                                cp('dve', PTb[nx][:, sl_, :], pPT, [kPT], [kPTn])
                                if step < 5:
                                    cp('act', Pb[nx][:, sl_, :], pP, [kP], [kPn])
                                if pend is not None:
                                    cp('act' if step % 2 else 'dve', Xb[xn][:, sl_, :], pX, [kX], [K(f'Xb{xn}')])
                                yield
                                pend = nx
                                cur = nx
                            xn = x_update(pend)
                            yield
                            cp('act', Xb[xn][:, sl_, :], pX, [kX], [K(f'Xb{xn}')])
                            yield
                            cur = xn
                            kT2 = K(f'Xb{cur}')
                            T2T = Xb[cur]
                            for i in ii:
                                mm(pW[:, i - ia, :], hkg[:, i, n, :], T2T[:, i, :], True, True, [('hkg', i), kT2], [kW], inc=(i == last))
                            yield
                            amul(nw2T[:, sl_, :], pW, -1.0, [kW], [K('nw2T')])
                            yield
                            for i in ii:
                                mm(pV[:, i - ia, :], T2T[:, i, :], hv[:, i, n, :], True, False, [kT2, ('hv', i)], [kV], inc=False)
                                mm(pV[:, i - ia, :], nw2T[:, i, :], Sbf[:, i, :], False, True, [K('nw2T'), K('Sbf')], [kV], inc=(i == last))
                            yield
                            for i in ii:
                                ts('dve', vnew[:, i, :], pV[:, i - ia, :], g_beta[:, cc[i]:cc[i] + 1], None, ALU.mult, None, [kV, ('g_beta',)], [K('vnew')])
                            yield
                            for i in ii:
                                mm(pS[:, i - ia, :], hkd[:, i, n, :], vnew[:, i, :], True, True, [('hkd', i), K('vnew')], [kS], inc=(i == last))
                            for i in ii:
                                mm(pO[:, i - ia, :], Sbf[:, i, :], qgT[:, i, :], True, False, [K('Sbf'), K('qgT')], [kO], inc=False)
                                mm(pO[:, i - ia, :], vnew[:, i, :], QKD[:, i, :], False, True, [K('vnew'), K('QKD')], [kO], inc=(i == last))
                            yield
                            for i in ii:
                                stt(S32[:, i, :], S32[:, i, :], g_egla[:, cc[i]:cc[i] + 1], pS[:, i - ia, :], ALU.mult, ALU.add,
                                    [K('S32'), ('g_egla',), kS], [K('S32')])
                            act(sqo[:, sl_, :], pO, AF.Square, [kO], [K('sqo')])
                            yield
                            cp('act', Sbf[:, sl_, :], S32[:, sl_, :], [K('S32')], [K('Sbf')])
                            for i in ii:
                                mm(pR[:, i - ia, :], c128b[:], sqo[:, i, :], True, True, [('c128b',), K('sqo')], [kR], inc=(i == last))
                            yield
                            rsqrt(rso[:, sl_, :], pR, eps1, [kR], [K('rso')])
                            yield
                            stt(o1[:, sl_, :], pO, normg, rso[:, sl_, :], ALU.mult, ALU.mult, [kO, ('cpp',), K('rso')], [K('o1')])
                            yield
                            tt('pool', mixT[:, sl_, gtok], o1[:, sl_, :], hz[:, sl_, tok], ALU.mult, [K('o1')] + [('hz', i_) for i_ in ii],
                               [('mixT', i_) for i_ in ii])
                            yield

                    def rec_two_chains(half):
                        if half == 0:
                            for pr in range(2):
                                T.issue('pool', lambda e: e.memset(S32[:, 2 * pr:2 * pr + 2, :], 0.0), writes=[('S32', pr)])
                                T.issue('pool', lambda e: e.memset(Sbf[:, 2 * pr:2 * pr + 2, :], 0.0), writes=[('Sbf', pr)])
                        gens = [gen_rec_pair(half, 0), gen_rec_pair(half, 1)]
                        while gens:
                            for g_ in list(gens):
                                try:
                                    next(g_)
                                except StopIteration:
                                    gens.remove(g_)

                    for half in range(2):
                        cur_half[0] = half
                        run_A_quad()
                        if TWO_CHAINS:
                            rec_two_chains(half)
                        else:
                            rec_quad(half)
                    T.barrier()
                dump('mixA', mixT[:, 0:4, :], [('mixT', h) for h in range(4)])
                if stop == 'mixA':
                    return True

                wo_t = sb(p1, "wo_t", [128, 8, 1024], BF16)
                with ExitStack() as sc:
                    ctab_t = sb(sc, "ctab_t", [64, 2 * S], F32)
                    T.dma('sp', ctab_t[:], ctab[:, :], writes=[('ctab',)], key=('ctab',))
                    cos2 = ctab_t[:, 0:S]
                    sin2 = ctab_t[:, S:2 * S]
                    wq_t = sb(sc, "wq_t", [128, 3, 768], BF16)
                    wqr_t = sb(sc, "wqr_t", [128, 3, 4, 64], BF16)
                    wkv_t = sb(sc, "wkv_t", [128, 2, 1024], BF16)
                    wkr_t = sb(sc, "wkr_t", [128, 8, 128], BF16)
                    wst = [sb(sc, f"wstm{i}", [128, 8, 128], BF16) for i in range(3)]
                    cqg = sb(sc, "cqg", [128, 3, S], BF16)
                    ckvg = sb(sc, "ckvg", [128, 2, S], BF16)
                    sqr = [sb(sc, f"sqr{i}", [128, 512], BF16) for i in range(2)]
                    rsq = sb(sc, "rsq", [128, S], F32)
                    rskv = sb(sc, "rskv", [128, S], F32)
                    rskvt = sb(sc, "rskvt", [128, NT], F32)
                    krT = sb(sc, "krT", [64, S], BF16)
                    t1 = [sb(sc, f"rt1_{i}", [64, 512], F32) for i in range(2)]
                    t2 = [sb(sc, f"rt2_{i}", [64, 512], F32) for i in range(2)]
                    qn = [sb(sc, f"qn{i}", [128, S], BF16) for i in range(1)]
                    qr = [sb(sc, f"qr{i}", [64, S], BF16) for i in range(1)]
                    kn = [sb(sc, f"kn{i}", [128, S], BF16) for i in range(1)]
                    vh = [sb(sc, f"vh{i}", [128, NT, 128], BF16) for i in range(1)]
                    PT = [sb(sc, f"PTt{i}", [128, 512], BF16) for i in range(3)]
                    den = [sb(sc, f"den{i}", [128, 512], F32) for i in range(2)]

                    wcnt = [0]

                    def load_wchunk2(c0):
                        sl = wcnt[0] % 3
                        wcnt[0] += 1
                        T.dma('pool', wst[sl][:], w_in_v[:, :, c0:c0 + 128], writes=[('wst', sl)], key=('wstm', sl))
                        return sl
                    pre_sl = {0: load_wchunk2(2056), 1: load_wchunk2(2056 + 128)}
                    T.dma('pool', wkr_t[:, :, 0:64], w_in_v[:, :, 2696:2760], writes=[('wkr',)], key=('wkr',))
                    T.dma('pool', wq_t[:], wq_v[:, :, :], writes=[('wq',)], key=('wq',))
                    T.dma('pool', wkv_t[:], wkv_v[:, :, :], writes=[('wkv',)], key=('wkv',))
                    ts('dve', wkr_t[:, :, 64:96], wkr_t[:, :, 32:64], -1.0, None, ALU.mult, None, [('wkr',)], [('wkr2',)])
                    cp('dve', wkr_t[:, :, 96:128], wkr_t[:, :, 0:32], [('wkr',)], [('wkr2',)])
                    for h in range(4):
                        ts('dve', wqr_t[:, :, h, 0:32], wq_t[:, :, h * 192 + 160:h * 192 + 192], -1.0, None, ALU.mult, None, [('wq',)], [('wqr',)])
                        cp('dve', wqr_t[:, :, h, 32:64], wq_t[:, :, h * 192 + 128:h * 192 + 160], [('wq',)], [('wqr',)])

                    XH = lambda tb: [('xhT', tb * 4 + i) for i in range(4)]
                    sqcnt = [0]
                    pend_n = [None]
                    for ci in range(5):
                        isq = ci < 3
                        cc = ci if isq else ci - 3
                        last = cc == (2 if isq else 1)
                        c0 = 2056 + ci * 128
                        if ci + 2 < 5:
                            pre_sl[ci + 2] = load_wchunk2(c0 + 256)
                        if ci == 0:
                            T.dma('pool', wo_t[:], w_out_v[:, :, :], writes=[('wo',)], key=('wo',))
                        sl = pre_sl[ci]
                        nrm = onesb if isq else c256b
                        knrm = ('onesb',) if isq else ('c256b',)
                        for tb in range(4):
                            bank = tb % 2
                            cs = slice(tb * 512, (tb + 1) * 512)
                            for kc in range(8):
                                mm(ps[bank][:, :], wst[sl][:, kc, :], xhT[:, kc, cs], kc == 0, kc == 7,
                                   [('wst', sl)] + XH(tb), [('ps', bank)])
                            if isq:
                                ts('dve', cqg[:, cc, cs], ps[bank][:, :], qg[:, cc:cc + 1], float(384 ** 0.5), ALU.mult, ALU.mult,
                                   [('ps', bank), ('cpp',)], [('cqg',)])
                            else:
                                ts('dve', ckvg[:, cc, cs], ps[bank][:, :], kvg[:, cc:cc + 1], None, ALU.mult, None,
                                   [('ps', bank), ('cpp',)], [('ckvg',)])
                            sqi = sqcnt[0] % 2
                            sqcnt[0] += 1
                            act(sqr[sqi][:], ps[bank][:, :], AF.Square, [('ps', bank)], [('sqr', sqi)])
                            if pend_n[0] is not None:
                                pend_n[0]()

                            def _norm(tb=tb, cs=cs, nrm=nrm, knrm=knrm, sqi=sqi, cc=cc, last=last, isq=isq):
                                mm(ps[2 + tb][:, :], nrm[:], sqr[sqi][:], cc == 0, last, [knrm, ('sqr', sqi)], [('ps', 2 + tb)], inc=True)
                                if last:
                                    if isq:
                                        rsqrt(rsq[:, cs], ps[2 + tb][:, :], eps384, [('ps', 2 + tb)], [('rsq',)])
                                    else:
                                        rsqrt(rskv[:, cs], ps[2 + tb][:, :], eps1, [('ps', 2 + tb)], [('rskv',)])
                            pend_n[0] = _norm
                        if ci == 4:
                            pend_n[0]()
                            pend_n[0] = None
                            for t in range(NT):
                                bank = 6 + t % 2
                                tr(ps[bank][:, 0:128], rskv[:, t * 128:(t + 1) * 128], [('rskv',)], [('ps', bank)])
                                cp('dve', rskvt[:, t:t + 1], ps[bank][:, 0:1], [('ps', bank)], [('rskvt',)])
                    for tb in range(4):
                        cs = slice(tb * 512, (tb + 1) * 512)
                        bA, bB = (5, 6) if tb % 2 == 0 else (0, 1)
                        for kc in range(8):
                            mm(ps[bA][0:64, :], wkr_t[:, kc, 0:64], xhT[:, kc, cs], kc == 0, kc == 7, [('wkr',)] + XH(tb), [('ps', bA)])
                        for kc in range(8):
                            mm(ps[bB][0:64, :], wkr_t[:, kc, 64:128], xhT[:, kc, cs], kc == 0, kc == 7, [('wkr2',)] + XH(tb), [('ps', bB)])
                        tt('dve', t1[tb % 2][:], ps[bA][0:64, :], cos2[:, cs], ALU.mult, [('ps', bA), ('ctab',)], [('t1', tb % 2)])
                        tt('dve', t2[tb % 2][:], ps[bB][0:64, :], sin2[:, cs], ALU.mult, [('ps', bB), ('ctab',)], [('t2', tb % 2)])
                        tt('pool', krT[:, cs], t1[tb % 2][:], t2[tb % 2][:], ALU.add, [('t1', tb % 2), ('t2', tb % 2)], [('krT',)])
                    scale = float(192 ** -0.5)
                    ptc = [0]
                    for h in range(4):
                        par = 0
                        for tb in range(4):
                            cs = slice(tb * 512, (tb + 1) * 512)
                            b0, b1, b5, b6 = (0, 1, 5, 6) if tb % 2 == 0 else (2, 3, 4, 7)
                            for kc in range(3):
                                mm(ps[b0][:, :], wq_t[:, kc, h * 192:h * 192 + 128], cqg[:, kc, cs], kc == 0, kc == 2, [('wq',), ('cqg',)], [('ps', b0)])
                            for kc in range(3):
                                mm(ps[b5][0:64, :], wq_t[:, kc, h * 192 + 128:h * 192 + 192], cqg[:, kc, cs], kc == 0, kc == 2, [('wq',), ('cqg',)], [('ps', b5)])
                            for kc in range(3):
                                mm(ps[b6][0:64, :], wqr_t[:, kc, h, :], cqg[:, kc, cs], kc == 0, kc == 2, [('wqr',), ('cqg',)], [('ps', b6)])
                            for kc in range(2):
                                mm(ps[b1][:, :], wkv_t[:, kc, h * 256:h * 256 + 128], ckvg[:, kc, cs], kc == 0, kc == 1, [('wkv',), ('ckvg',)], [('ps', b1)])
                            tt('dve', qn[par][:, cs], ps[b0][:, :], rsq[:, cs], ALU.mult, [('ps', b0), ('rsq',)], [('qn', par)])
                            tt('dve', t1[tb % 2][:], ps[b5][0:64, :], cos2[:, cs], ALU.mult, [('ps', b5), ('ctab',)], [('t1', tb % 2)])
                            tt('dve', t2[tb % 2][:], ps[b6][0:64, :], sin2[:, cs], ALU.mult, [('ps', b6), ('ctab',)], [('t2', tb % 2)])
                            tt('dve', kn[par][:, cs], ps[b1][:, :], rskv[:, cs], ALU.mult, [('ps', b1), ('rskv',)], [('kn', par)])
                            tt('pool', t1[tb % 2][:], t1[tb % 2][:], t2[tb % 2][:], ALU.add, [('t1', tb % 2), ('t2', tb % 2)], [('t1', tb % 2)])
                            tt('pool', qr[par][:, cs], t1[tb % 2][:], rsq[0:64, cs], ALU.mult, [('t1', tb % 2), ('rsq',)], [('qr', par)])
                        for t in range(NT):
                            bank = 2 + t % 2
                            for kc in range(2):
                                mm(ps[bank][:, 0:128], ckvg[:, kc, t * 128:(t + 1) * 128], wkv_t[:, kc, h * 256 + 128:h * 256 + 256], kc == 0, kc == 1,
                                   [('wkv',), ('ckvg',)], [('ps', bank)])
                            amul(vh[par][:, t, :], ps[bank][:, 0:128], rskvt[:, t:t + 1], [('ps', bank), ('rskvt',)], [('vh', par)])
                        items = [(qb, kt) for qb in range(4) for kt in range(4 * qb + 4)]

                        def att_S(idx):
                            qb, kt = items[idx]
                            r = max(0, kt - 4 * qb)
                            c0 = qb * 512 + r * 128
                            ncol = 512 - r * 128
                            sb_ = ps[idx % 2]
                            ksb = ('ps', idx % 2)
                            mm(sb_[:, 0:ncol], kn[par][:, kt * 128:(kt + 1) * 128], qn[par][:, c0:c0 + ncol], True, False,
                               [('kn', par), ('qn', par)], [ksb], inc=False)
                            mm(sb_[:, 0:ncol], krT[:, kt * 128:(kt + 1) * 128], qr[par][:, c0:c0 + ncol], False, True,
                               [('krT',), ('qr', par)], [ksb])

                        def att_EV(idx):
                            qb, kt = items[idx]
                            nk = 4 * qb + 4
                            r = max(0, kt - 4 * qb)
                            ncol = 512 - r * 128
                            sb_ = ps[idx % 2]
                            ksb = ('ps', idx % 2)
                            pO_, pD_ = ps[4 + (qb % 2) * 2], ps[5 + (qb % 2) * 2]
                            kO, kD = ('ps', 4 + (qb % 2) * 2), ('ps', 5 + (qb % 2) * 2)
                            pi = ptc[0] % 3
                            ptc[0] += 1
                            act(PT[pi][:, 0:ncol], sb_[:, 0:ncol], AF.Exp, [ksb], [('PT', pi)], scale=scale)
                            if kt >= 4 * qb:
                                T.issue('pool', lambda e: e.memset(PT[pi][64:128, 0:64], 0.0), [], [('PT', pi)])
                            mm(pO_[:, r * 128:512], vh[par][:, kt, :], PT[pi][:, 0:ncol], kt == 0, kt == nk - 1, [('vh', par), ('PT', pi)], [kO])
                            mm(pD_[:, r * 128:512], onesb[:], PT[pi][:, 0:ncol], kt == 0, kt == nk - 1, [('onesb',), ('PT', pi)], [kD])
                            if kt == nk - 1:
                                dq = den[qb % 2]
                                T.issue('dve', lambda e: e.reciprocal(out=dq[:], in_=pD_[:, :]), [kD], [('den', qb % 2)])
                                tt('dve', mixT[:, 4 + h, qb * 512:(qb + 1) * 512], pO_[:, :], dq[:], ALU.mult, [kO, ('den', qb % 2)], [('mixT', 4 + h)])

                        att_S(0)
                        for idx in range(len(items)):
                            if idx + 1 < len(items):
                                att_S(idx + 1)
                            att_EV(idx)
                    T.barrier()
                dump('mixB', mixT[:, 4:8, :], [('mixT', 4 + h) for h in range(4)])
                if stop == 'mixB':
                    return True

                with ExitStack() as sc:
                    ln1_t = sb(sc, "ln1_t", [128, 2048], F32)
                    T.dma('sp', ln1_t[:], cbc[:, 0:2048], writes=[('cbcL',)], key=('cbcL',))
                    G1 = ln1_t[:, 0:1024]
                    B1 = ln1_t[:, 1024:2048]
                    xr = [sb(sc, f"xr{i}", [128, D], F32) for i in range(3)]
                    rr = [sb(sc, f"rr{i}", [128, D], F32) for i in range(3)]
                    hh = [sb(sc, f"hh{i}", [128, D], F32) for i in range(2)]
                    stats3 = [sb(sc, f"stats{i}", [128, 12], F32) for i in range(3)]
                    mv3 = [sb(sc, f"mv{i}", [128, 2], F32) for i in range(3)]
                    rs3 = [sb(sc, f"rs{i}", [128, 1], F32) for i in range(3)]

                    def d1_ldx(t):
                        q_ = t % 3
                        T.dma('sp', xr[q_][:], x[row0 + t * 128: row0 + (t + 1) * 128, :], writes=[('xr', q_)], key=('xr', q_))

                    def d1_mm(t):
                        sl = t % 2
                        tok = slice(t * 128, (t + 1) * 128)
                        for hb in range(2):
                            bank = sl * 2 + hb
                            for kc in range(8):
                                mm(ps[bank][:, :], mixT[:, kc, tok], wo_t[:, kc, hb * 512:(hb + 1) * 512], kc == 0, kc == 7,
                                   [('mixT', kc), ('wo',)], [('ps', bank)])

                    def d1_res(t):
                        sl = t % 2
                        q_ = t % 3
                        for hb in range(2):
                            bank = sl * 2 + hb
                            stt(rr[q_][:, hb * 512:(hb + 1) * 512], xr[q_][:, hb * 512:(hb + 1) * 512], ALPHA, ps[bank][:, :], ALU.mult, ALU.add,
                                [('xr', q_), ('ps', bank)], [('rr', q_)])

                    def d1_lna(t):
                        q_ = t % 3
                        r, stats, mv, rs = rr[q_], stats3[q_], mv3[q_], rs3[q_]
                        kr = ('rr', q_)
                        for hb in range(2):
                            T.issue('dve', lambda e: e.bn_stats(out=stats[:, hb * 6:(hb + 1) * 6], in_=r[:, hb * 512:(hb + 1) * 512]),
                                    reads=[kr], writes=[('lnst', q_)])
                        T.issue('dve', lambda e: e.bn_aggr(out=mv[:], in_=stats[:]), reads=[('lnst', q_)], writes=[('lnmv', q_)])
                        rsqrt(rs[:], mv[:, 1:2], eps1, [('lnmv', q_)], [('lnrs', q_)])
                        stt(mv[:, 1:2], mv[:, 0:1], -1.0, rs[:, 0:1], ALU.mult, ALU.mult, [('lnmv', q_), ('lnrs', q_)], [('lnmv', q_)])
                        act(r[:], r[:], AF.Identity, [kr, ('lnmv', q_), ('lnrs', q_)], [kr], bias=mv[:, 1:2], scale=rs[:, 0:1])

                    def d1_lnb(t):
                        q_ = t % 3
                        sl = t % 2
                        r = rr[q_]
                        kr = ('rr', q_)
                        tt('dve', r[:], r[:], G1, ALU.mult, [kr, ('cbcL',)], [kr])
                        tt('dve', hh[sl][:], r[:], B1, ALU.add, [kr, ('cbcL',)], [('hh', sl)])
                        T.dma('sp', hscr[t * 128:(t + 1) * 128, :], hh[sl][:], reads=[('hh', sl)], writes=[('hscr', t)], key=('hh', sl))

                    def d1_tr(t):
                        sl = t % 2
                        tok = slice(t * 128, (t + 1) * 128)
                        for kc in range(8):
                            bank = 4 + sl * 2 + kc // 4
                            col = (kc % 4) * 128
                            tr(ps[bank][:, col:col + 128], hh[sl][:, kc * 128:(kc + 1) * 128], [('hh', sl)], [('ps', bank)], inc=(kc % 4 == 3))
                        for hb in range(2):
                            bank = 4 + sl * 2 + hb
                            cp('act', xhT[:, hb * 4:(hb + 1) * 4, tok], ps[bank][:, :].rearrange("p (k c) -> p k c", k=4), [('ps', bank)], [('xhT', t)])

                    for t_ in range(3):
                        d1_ldx(t_)
                    d1_mm(0)
                    d1_res(0)
                    d1_mm(1)
                    d1_res(1)
                    d1_lna(0)
                    for t in range(NT):
                        if t + 2 < NT:
                            d1_mm(t + 2)
                        if t + 1 < NT:
                            d1_lna(t + 1)
                        d1_lnb(t)
                        if t + 2 < NT:
                            d1_res(t + 2)
                        if t + 3 < NT:
                            d1_ldx(t + 3)
                        d1_tr(t)
                    T.barrier()
            dump('hT', xhT[:], [('xhT', t) for t in range(NT)])
            if stop == 'hT':
                return True

            with ExitStack() as p2:
                wd_t = sb(p2, "wd_t", [128, NJ, 1024], BF16)
                ln2_t = sb(p2, "ln2_t", [128, 3072], F32)
                T.dma('sp', ln2_t[:], cbc[:, 2048:5120], writes=[('cbcL',)], key=('cbcL2',))
                G2 = ln2_t[:, 0:1024]
                B2 = ln2_t[:, 1024:2048]
                BG = ln2_t[:, 2048:3072]
                wg_t = sb(p2, "wg_t", [128, 8, 1024], BF16)
                wp_t = sb(p2, "wp_t", [128, 2, 1024], BF16)
                actT = sb(p2, "actT", [128, NJ, BLK2], BF16)
                wup = [sb(p2, f"wup{i}", [128, 2, 8, 128], BF16) for i in range(3)]
                rawgu = [sb(p2, f"rawgu{i}", [128, 2, 2 + BLK2], F32) for i in range(2)]
                accg = [sb(p2, f"accg{i}", [128, BLK2], F32) for i in range(2)]
                accu = [sb(p2, f"accu{i}", [128, BLK2], F32) for i in range(2)]
                halo = sb(p2, "halo", [128, 2, NJ, 2], F32)
                hr_ = [sb(p2, f"hr{i}", [128, D], F32) for i in range(2)]
                r2_ = [sb(p2, f"r2{i}", [128, D], F32) for i in range(2)]
                sg_ = [sb(p2, f"sgt{i}", [128, D], F32) for i in range(2)]
                pin_ = [sb(p2, f"pin{i}", [128, 256], F32) for i in range(2)]
                pTb_ = [sb(p2, f"pTb{i}", [128, 2, 128], BF16) for i in range(2)]
                stats_ = [sb(p2, f"stats2{i}", [128, 12], F32) for i in range(2)]
                mv_ = [sb(p2, f"mv2{i}", [128, 2], F32) for i in range(2)]
                rs_ = [sb(p2, f"rs2{i}", [128, 1], F32) for i in range(2)]
                T.issue('pool', lambda e: e.memset(halo[:], 0.0), writes=[('halo', c_) for c_ in range(NJ)])
                def ld2(t_):
                    q_ = t_ % 2
                    T.dma('sp', hr_[q_][:], hscr[t_ * 128:(t_ + 1) * 128, :], reads=[('hscr', t_)], writes=[('hr', q_)], key=('hr', q_))
                    T.dma('sp', pin_[q_][:], p[row0 + t_ * 128:row0 + (t_ + 1) * 128, :], writes=[('pin', q_)], key=('pin', q_))

                NB = S // BLK2
                TPB = BLK2 // 128
                wc = [0]

                def prefetch_wup(upto):
                    while wc[0] < min(upto, NB * NJ):
                        jj = wc[0] % NJ
                        s_ = wc[0] % 3
                        wc[0] += 1
                        T.dma('pool', wup[s_][:, 0, :, :], w_up_v[:, :, jj * 128:(jj + 1) * 128], writes=[('wup', s_, 0)], key=('wup', s_, 0))
                        T.dma('pool', wup[s_][:, 1, :, :], w_up_v[:, :, DFF + jj * 128:DFF + (jj + 1) * 128], writes=[('wup', s_, 1)], key=('wup', s_, 1))
                prefetch_wup(3)
                T.dma('pool', wg_t[:], w_gate_v[:, :, :], writes=[('wg',)], key=('wg',))
                T.dma('pool', wp_t[:], w_proj_v[:, :, :], writes=[('wp',)], key=('wp',))
                T.dma('pool', wd_t[:], w_down_v[:, :, :], writes=[('wd',)], key=('wd',))

                def stage_B2(j_):
                    q_ = j_ % 2
                    kg_, ku_ = ('acc2', 0, q_), ('acc2', 1, q_)
                    act(accg[q_][:], accg[q_][:], AF.Silu, [kg_], [kg_])
                    tt('dve', actT[:, j_, :], accg[q_][:], accu[q_][:], ALU.mult, [kg_, ku_], [('actT', j_)])

                for blk in range(NB):
                    bs = slice(blk * BLK2, (blk + 1) * BLK2)
                    XB = [('xhT', blk * TPB + i) for i in range(TPB)]
                    for j in range(NJ):
                        step = blk * NJ + j
                        prefetch_wup(step + 3)
                        sl = step % 3
                        jp = j % 2
                        rg = rawgu[jp]
                        kr_ = ('raw2', jp)
                        for gu in range(2):
                            bank = jp * 2 + gu
                            for kc in range(8):
                                mm(ps[bank][:, :], wup[sl][:, gu, kc, :], xhT[:, kc, bs], kc == 0, kc == 7, [('wup', sl, gu)] + XB, [('ps', bank)])
                        cp('act', rg[:, :, 0:2], halo[:, :, j, :], [('halo', j)], [kr_ + ('h',)])
                        cp('act', rg[:, :, 2:2 + BLK2], ps_all[:, jp * 1024:(jp + 1) * 1024].rearrange("p (g c) -> p g c", g=2),
                           [('ps', jp * 2), ('ps', jp * 2 + 1)], [kr_])
                        cp('act', halo[:, :, j, :], rg[:, :, BLK2:BLK2 + 2], [kr_], [('halo', j)])
                        acs = (accg[jp], accu[jp])
                        kas = (('acc2', 0, jp), ('acc2', 1, jp))
                        act(acs[0][:], ps[jp * 2][:, :], AF.Identity, [('ps', jp * 2), ('cpp',)], [kas[0]], bias=fcb[:, j:j + 1], scale=fcw[:, j, 2:3])
                        ts('dve', acs[1][:], rg[:, 1, 2:2 + BLK2], fcw[:, NJ + j, 2:3], fcb[:, NJ + j:NJ + j + 1], ALU.mult, ALU.add, [kr_, ('cpp',)], [kas[1]])
                        for tap in (1, 0):
                            for gu in range(2):
                                cidx = gu * NJ + j
                                stt(acs[gu][:], rg[:, gu, tap:tap + BLK2], fcw[:, cidx, tap:tap + 1], acs[gu][:], ALU.mult, ALU.add,
                                    [kr_, kr_ + ('h',), kas[gu], ('cpp',)], [kas[gu]])
                        if j >= 1:
                            stage_B2(j - 1)
                    stage_B2(NJ - 1)
                    AK = [('actT', j) for j in range(NJ)]
                    if stop == 'p2a' and blk == 0:
                        T.barrier()
                        return True
                    for tl in range(TPB):
                        t = blk * TPB + tl
                        tok = slice(t * 128, (t + 1) * 128)
                        ltok = slice(tl * 128, (tl + 1) * 128)
                        grow = row0 + t * 128
                        tp_ = t % 2
                        hr, r2, sg, pin, pTb = hr_[tp_], r2_[tp_], sg_[tp_], pin_[tp_], pTb_[tp_]
                        kh, kr2, kpin, kpt = ('hr', tp_), ('r2', tp_), ('pin', tp_), ('pTb', tp_)
                        if t == 0:
                            ld2(0)
                        if t + 1 < NT:
                            ld2(t + 1)
                        for kc in range(2):
                            tr(ps[6 + tp_][:, kc * 128:(kc + 1) * 128], pin[:, kc * 128:(kc + 1) * 128], [kpin], [('ps', 6 + tp_)], inc=(kc == 1))
                        cp('act', pTb[:], ps[6 + tp_][:, 0:256].rearrange("p (k c) -> p k c", k=2), [('ps', 6 + tp_)], [kpt])
                        for hb in range(2):
                            hs = slice(hb * 512, (hb + 1) * 512)
                            ksg = ('sg', hb, tp_)
                            for kc in range(8):
                                mm(ps[2 + hb][:, :], xhT[:, kc, tok], wg_t[:, kc, hs], kc == 0, kc == 7, [('xhT', t), ('wg',)], [('ps', 2 + hb)])
                            for kc in range(2):
                                mm(ps[4 + hb][:, :], pTb[:, kc, :], wp_t[:, kc, hs], kc == 0, kc == 1, [kpt, ('wp',)], [('ps', 4 + hb)])
                            for j in range(NJ):
                                mm(ps[hb][:, :], actT[:, j, ltok], wd_t[:, j, hs], j == 0, j == NJ - 1, AK + [('wd',)], [('ps', hb)])
                            tt('dve', sg[:, hs], ps[2 + hb][:, :], BG[:, hs], ALU.add, [('ps', 2 + hb), ('cbcL',)], [ksg])
                            act(sg[:, hs], sg[:, hs], AF.Sigmoid, [ksg], [ksg])
                            tt('dve', sg[:, hs], sg[:, hs], ps[4 + hb][:, :], ALU.mult, [ksg, ('ps', 4 + hb)], [ksg])
                            stt(r2[:, hs], hr[:, hs], ALPHA, ps[hb][:, :], ALU.mult, ALU.add, [kh, ('ps', hb)], [kr2])
                            tt('dve', r2[:, hs], r2[:, hs], sg[:, hs], ALU.add, [kr2, ksg], [kr2])
                        ln_tile(r2, G2, B2, (stats_[tp_], mv_[tp_], rs_[tp_]), kr2, r2[:], kr2, sfx=tp_)
                        T.dma('sp', out[grow:grow + 128, :], r2[:], reads=[kr2], writes=[('out', sq, t)], key=('r2o', tp_))
                    if stop == 'p2b' and blk == 0:
                        T.barrier()
                        return True
                T.barrier()
          for sq in range(NSEQ):
            if seq_body(sq):
                break
        T.final_wait()
    nc._trk_log = T.log
    return nc


def _prep_common(inp):
    f = np.float32
    g = lambda k: np.asarray(inp[k], dtype=f)[0]
    cw = g("gdn_conv_w")
    fw = g("ffn_conv_w")
    fb = g("ffn_conv_b")
    cpp = np.zeros((128, 512), f)
    cpp[:, 0:48] = cw.reshape(4, 12, 128).transpose(2, 1, 0).reshape(128, 48)
    cpp[:, 48:180] = fw.reshape(3, 44, 128).transpose(2, 1, 0).reshape(128, 132)
    cpp[:, 180:224] = fb.reshape(44, 128).T
    cpp[:, 224] = g("gdn_norm_g")
    cpp[:, 225:228] = g("mla_q_norm_g").reshape(3, 128).T
    cpp[:, 228:230] = g("mla_kv_norm_g").reshape(2, 128).T
    cbc = np.zeros((128, 5 * 1024 + 128), f)
    for i, k in enumerate(["ln1_g", "ln1_b", "ln2_g", "ln2_b", "ple_b_gate"]):
        cbc[:, i * 1024:(i + 1) * 1024] = g(k)[None, :]
    cbc[:, 5120:5184] = np.tile(g("gdn_a_log"), 16)[None, :]
    cbc[:, 5184:5248] = np.tile(g("gdn_dt_bias"), 16)[None, :]
    j = np.arange(128)[:, None]
    i = np.arange(128)[None, :]
    cmat = np.zeros((128, 1792), f)
    cmat[:, 0:128] = np.eye(128, dtype=f)
    cmat[:, 128:256] = (j <= i).astype(f)
    for q_ in range(4):
        cmat[:, 256 + q_ * 128:384 + q_ * 128] = np.where(j <= i, 0.0, -30000.0).astype(f)
        cmat[:, 768 + q_ * 128:896 + q_ * 128] = np.where(j < i, 0.0, -30000.0).astype(f)
        cmat[:, 1280 + q_ * 128:1408 + q_ * 128] = np.eye(128, dtype=f)
    inv = (np.float32(10000.0) ** (-(np.arange(0, 64, 2, dtype=f)) / np.float32(64))).astype(f)
    ang = (np.arange(S, dtype=f)[:, None] * inv[None, :]).astype(f)
    cos = np.cos(ang.astype(np.float64)).astype(f).T
    sin = np.sin(ang.astype(np.float64)).astype(f).T
    ctab = np.zeros((64, 2 * S), f)
    ctab[0:32, 0:S] = cos
    ctab[32:64, 0:S] = cos
    ctab[0:32, S:] = sin
    ctab[32:64, S:] = sin
    return {
        "w_in": np.ascontiguousarray(g("w_in")), "wq_up": np.ascontiguousarray(g("mla_w_q_up")),
        "wkv_up": np.ascontiguousarray(g("mla_w_kv_up")), "w_out": np.ascontiguousarray(g("w_out")),
        "w_up": np.ascontiguousarray(g("ffn_w_up")), "w_down": np.ascontiguousarray(g("ffn_w_down")),
        "w_gate": np.ascontiguousarray(g("ple_w_gate")), "w_proj": np.ascontiguousarray(g("ple_w_proj")),
        "cpp": cpp, "cbc": cbc, "cmat": cmat, "ctab": ctab,
    }


def kernel(**inputs):
    common = _prep_common(inputs)
    x = np.asarray(inputs["x"], dtype=np.float32)
    p = np.asarray(inputs["p"], dtype=np.float32)[0]
    B = x.shape[0]
    nseq = B // NCORES
    nc = build(nseq)
    in_maps = []
    for c in range(NCORES):
        m = dict(common)
        m["x"] = np.ascontiguousarray(x[c * nseq:(c + 1) * nseq].reshape(nseq * S, D))
        m["p"] = np.ascontiguousarray(p[c * nseq:(c + 1) * nseq].reshape(nseq * S, 256))
        in_maps.append(m)
    res = run_bass_kernel_spmd(nc, in_maps, core_ids=list(range(NCORES)))
    outs = [np.asarray(r["out"]).reshape(nseq, S, D) for r in res.results]
    return np.concatenate(outs, axis=0).astype(np.float32)
```

```python
import numpy as np
from contextlib import ExitStack
import concourse.bass as bass
import concourse.mybir as mybir
from concourse.bass_utils import run_bass_kernel_spmd

F32, BF16 = mybir.dt.float32, mybir.dt.bfloat16
AF = mybir.ActivationFunctionType
ALU = mybir.AluOpType

S = 2048
NT = 16
D = 1024
KC = 8
DFF = 2816
NJ = 22
ALPHA = float(2.0 ** 0.25)
EPS = 1e-6
BLK2 = 512
HS = 1024
NBLK = 2
NTH = 8
EPOCH = 16000
NCORES = 8
TWO_CHAINS = True
OVERLAP_A = False
SEQ_GENS = True


class Trk:
    def __init__(self, nc, es):
        self.nc, self.es = nc, es
        self.engs = {'pe': nc.tensor, 'act': nc.scalar, 'dve': nc.vector, 'pool': nc.gpsimd, 'sp': nc.sync}
        self.cnt = {e: 0 for e in self.engs}
        self.esems = {e: [] for e in self.engs}
        self.seen = {e: {} for e in self.engs}
        self.lastw = {}
        self.rd = {}
        self.dsem = {}
        self.pend = {e: ([], []) for e in self.engs}
        self.latest = {}
        self.log = {e: [] for e in self.engs}

    def newsem(self, name):
        return self.es.enter_context(self.nc.semaphore(name))

    def _wait(self, e, ev):
        sem, val, src = ev
        if src == 'pe' and e == 'pe':
            return
        k = id(sem)
        if self.seen[e].get(k, 0) >= val:
            return
        self.engs[e].wait_ge(sem, val)
        self.log[e].append(('w', id(sem), val))
        self.seen[e][k] = val

    def _deps(self, e, reads, writes):
        for k in reads:
            ev = self.lastw.get(k)
            if ev is not None:
                self._wait(e, ev)
        for k in writes:
            ev = self.lastw.get(k)
            if ev is not None:
                self._wait(e, ev)
            for ev in self.rd.get(k, {}).values():
                self._wait(e, ev)

    def _reg(self, ev, reads, writes):
        sem, val, src = ev
        self.latest[id(sem)] = (sem, val)
        for k in writes:
            self.lastw[k] = ev
            self.rd[k] = {}
        for k in reads:
            self.rd.setdefault(k, {})[id(sem)] = ev

    def issue(self, e, fn, reads=(), writes=(), inc=True):
        writes = list(writes) + [k for k in reads if k[0] == 'ps' and k not in writes]
        reads = [k for k in reads if k[0] != 'ps']
        self._deps(e, reads, writes)
        ins = fn(self.engs[e])
        pr, pw = self.pend[e]
        pr.extend(reads)
        pw.extend(writes)
        if inc:
            n = self.cnt[e]
            ep, off = divmod(n, EPOCH)
            if ep >= len(self.esems[e]):
                self.esems[e].append(self.newsem(f"s_{e}_{ep}"))
            sem = self.esems[e][ep]
            ins.then_inc(sem, 1)
            self.log[e].append(('i', id(sem), 1))
            self.cnt[e] = n + 1
            self._reg((sem, off + 1, e), pr, pw)
            self.pend[e] = ([], [])
        return ins

    def dma(self, q, out, in_, reads=(), writes=(), key=None):
        self._deps(q, reads, writes)
        ins = self.engs[q].dma_start(out=out, in_=in_)
        if key not in self.dsem:
            self.dsem[key] = [self.newsem("d_" + "_".join(str(x) for x in key)), 0]
        d = self.dsem[key]
        d[1] += 16
        ins.then_inc(d[0], 16)
        self.log[q].append(('i', id(d[0]), 16))
        self._reg((d[0], d[1], 'dma'), list(reads), list(writes))
        return ins

    def barrier(self):
        for e in self.engs:
            assert not self.pend[e][0] and not self.pend[e][1], e
        for sem, val in list(self.latest.values()):
            self._wait('sp', (sem, val, 'x'))
        n = self.cnt['sp']
        ep, off = divmod(n, EPOCH)
        if ep >= len(self.esems['sp']):
            self.esems['sp'].append(self.newsem(f"s_sp_{ep}"))
        sem = self.esems['sp'][ep]
        self.engs['sp'].sem_inc(sem, 1)
        self.log['sp'].append(('i', id(sem), 1))
        self.cnt['sp'] = n + 1
        ev = (sem, off + 1, 'sp')
        self.latest[id(sem)] = (sem, off + 1)
        self.seen['sp'][id(sem)] = off + 1
        for e in self.engs:
            if e != 'sp':
                self._wait(e, ev)
        self.lastw.clear()
        self.rd.clear()

    def final_wait(self):
        for sem, val in list(self.latest.values()):
            self._wait('sp', (sem, val, 'x'))


class _Stop(Exception):
    pass


def build(NSEQ, dbg=None, stop=None):
    dbg = dbg or {}
    nc = bass.Bass("TRN2", target_bir_lowering=False)

    def din(name, shape, dt=F32):
        return nc.dram_tensor(name, list(shape), dt, kind="ExternalInput").ap()

    x = din("x", [NSEQ * S, D])
    p = din("p", [NSEQ * S, 256])
    w_in = din("w_in", [D, 2760])
    wq_up = din("wq_up", [384, 768])
    wkv_up = din("wkv_up", [256, 1024])
    w_out = din("w_out", [1024, 1024])
    w_up = din("w_up", [D, 2 * DFF])
    w_down = din("w_down", [DFF, D])
    w_gate = din("w_gate", [D, D])
    w_proj = din("w_proj", [256, D])
    cpp = din("cpp", [128, 512])
    cbc = din("cbc", [128, 5 * 1024 + 128])
    cmat = din("cmat", [128, 14 * 128])
    ctab = din("ctab", [64, 2 * S])
    out = nc.dram_tensor("out", [NSEQ * S, D], F32, kind="ExternalOutput").ap()
    hscr = nc.dram_tensor("hscr", [S, D], F32).ap()
    dbg_t = {k: nc.dram_tensor("dbg_" + k, list(v[0]), v[1], kind="ExternalOutput").ap() for k, v in dbg.items()}

    w_in_v = w_in.rearrange("(kc p) n -> p kc n", p=128)
    wq_v = wq_up.rearrange("(kc p) n -> p kc n", p=128)
    wkv_v = wkv_up.rearrange("(kc p) n -> p kc n", p=128)
    w_out_v = w_out.rearrange("(kc p) n -> p kc n", p=128)
    w_up_v = w_up.rearrange("(kc p) n -> p kc n", p=128)
    w_down_v = w_down.rearrange("(kc p) n -> p kc n", p=128)
    w_gate_v = w_gate.rearrange("(kc p) n -> p kc n", p=128)
    w_proj_v = w_proj.rearrange("(kc p) n -> p kc n", p=128)

    with ExitStack() as es:
        T = Trk(nc, es)

        uid = [0]

        def sb(scope, name, shape, dt):
            uid[0] += 1
            return scope.enter_context(nc.sbuf_tensor(f"{name}_{uid[0]}", list(shape), dt))

        ps_all = es.enter_context(nc.psum_tensor("ps_all", [128, 8 * 512], F32))
        ps = [ps_all[:, b * 512:(b + 1) * 512] for b in range(8)]

        cpp_t = sb(es, "cpp_t", [128, 512], F32)
        cbc_t = sb(es, "cbc_t", [128, 128], F32)
        cmat_t = sb(es, "cmat_t", [128, 1792], F32)
        identb = sb(es, "identb", [128, 128], BF16)
        onesb = sb(es, "onesb", [128, 128], BF16)
        c128b = sb(es, "c128b", [128, 128], BF16)
        c256b = sb(es, "c256b", [128, 128], BF16)
        onesf = sb(es, "onesf", [128, 128], F32)
        w_ab = sb(es, "w_ab", [128, 8, 8], BF16)
        xhT = sb(es, "xhT", [128, 8, S], BF16)

        T.dma('sp', cpp_t[:], cpp[:, :], writes=[('cpp',)], key=('cpp',))
        T.dma('sp', cbc_t[:], cbc[:, 5120:5248], writes=[('cbc',)], key=('cbc',))
        T.dma('sp', cmat_t[:], cmat[:, :], writes=[('cmat',)], key=('cmat',))
        T.dma('pool', w_ab[:], w_in_v[:, :, 2048:2056], writes=[('w_ab',)], key=('w_ab',))
        ident = cmat_t[:, 0:128]
        Umat = cmat_t[:, 128:256]
        NEGM = cmat_t[:, 256:384]
        NEGM4 = cmat_t[:, 256:768].rearrange("p (i c) -> p i c", i=4)
        NEGMS4 = cmat_t[:, 768:1280].rearrange("p (i c) -> p i c", i=4)
        ident4 = cmat_t[:, 1280:1792].rearrange("p (i c) -> p i c", i=4)
        T.issue('dve', lambda e: e.tensor_copy(out=identb[:], in_=ident), reads=[('cmat',)], writes=[('identb',)])
        T.issue('pool', lambda e: e.memset(onesb[:], 1.0), writes=[('onesb',)])
        T.issue('pool', lambda e: e.memset(c128b[:], 1.0 / 128), writes=[('c128b',)])
        T.issue('pool', lambda e: e.memset(c256b[:], 1.0 / 256), writes=[('c256b',)])
        T.issue('pool', lambda e: e.memset(onesf[:], 1.0), writes=[('onesf',)])
        epst = sb(es, "epst", [128, 2], F32)
        T.issue('pool', lambda e: e.memset(epst[:, 0:1], EPS), writes=[('epst',)])
        T.issue('pool', lambda e: e.memset(epst[:, 1:2], 384 * EPS), writes=[('epst',)])
        eps1 = epst[:, 0:1]
        eps384 = epst[:, 1:2]
        gcw = cpp_t[:, 0:48].rearrange("p (c j) -> p c j", j=4)
        fcw = cpp_t[:, 48:180].rearrange("p (c j) -> p c j", j=3)
        fcb = cpp_t[:, 180:224]
        normg = cpp_t[:, 224:225]
        qg = cpp_t[:, 225:228]
        kvg = cpp_t[:, 228:230]
        ALOGB = cbc_t[:, 0:64]
        DTBB = cbc_t[:, 64:128]
        CONST = [('cpp',), ('cbc',), ('cmat',)]

        def act(out_, in_, func, reads, writes, **kw):
            return T.issue('act', lambda e: e.activation(out=out_, in_=in_, func=func, **kw), reads, writes)

        def tt(eng, out_, in0, in1, op, reads, writes):
            return T.issue(eng, lambda e: e.tensor_tensor(out=out_, in0=in0, in1=in1, op=op), reads, writes)

        def ts(eng, out_, in0, s1, s2, op0, op1, reads, writes):
            if s2 is None:
                return T.issue(eng, lambda e: e.tensor_scalar(out=out_, in0=in0, scalar1=s1, scalar2=None, op0=op0), reads, writes)
            return T.issue(eng, lambda e: e.tensor_scalar(out=out_, in0=in0, scalar1=s1, scalar2=s2, op0=op0, op1=op1), reads, writes)

        def stt(out_, in0, sc, in1, op0, op1, reads, writes):
            return T.issue('dve', lambda e: e.scalar_tensor_tensor(out=out_, in0=in0, scalar=sc, in1=in1, op0=op0, op1=op1), reads, writes)

        def cp(eng, out_, in_, reads, writes):
            if eng == 'act':
                return T.issue('act', lambda e: e.copy(out=out_, in_=in_), reads, writes)
            return T.issue(eng, lambda e: e.tensor_copy(out=out_, in_=in_), reads, writes)

        def rsqrt(out_, in_, eps_ap, reads, writes):
            act(out_, in_, AF.Ln, list(reads) + [('epst',)], writes, bias=eps_ap)
            act(out_, out_, AF.Exp, writes, writes, scale=-0.5)

        def amul(out_, in_, m, reads, writes):
            return T.issue('act', lambda e: e.mul(out=out_, in_=in_, mul=m), reads, writes)

        def mm(out_, lhsT, rhs, start, stop, reads, writes, inc=None):
            if inc is None:
                inc = stop
            return T.issue('pe', lambda e: e.matmul(out_, lhsT, rhs, start=start, stop=stop), reads, writes, inc=inc)

        def tr(out_, in_, reads, writes, inc=True):
            return T.issue('pe', lambda e: e.transpose(out_, in_, ident), list(reads) + [('cmat',)], writes, inc=inc)

        def dump(name, src, reads):
            if name in dbg_t:
                T.dma('sp', dbg_t[name], src, reads=reads, writes=[('dbg', name)], key=('dbg', name))

        def ln_tile(r, G, B, scope_tiles, key_r, out_tile, key_out, kc_=('cbcL',), sfx=''):
            stats, mv, rs = scope_tiles
            for hb in range(2):
                T.issue('dve', lambda e: e.bn_stats(out=stats[:, hb * 6:(hb + 1) * 6], in_=r[:, hb * 512:(hb + 1) * 512]),
                        reads=[key_r], writes=[('lnst', sfx)])
            T.issue('dve', lambda e: e.bn_aggr(out=mv[:], in_=stats[:]), reads=[('lnst', sfx)], writes=[('lnmv', sfx)])
            rsqrt(rs[:], mv[:, 1:2], eps1, [('lnmv', sfx)], [('lnrs', sfx)])
            ts('dve', r[:], r[:], mv[:, 0:1], rs[:, 0:1], ALU.subtract, ALU.mult, [key_r, ('lnmv', sfx), ('lnrs', sfx)], [key_r])
            tt('dve', r[:], r[:], G, ALU.mult, [key_r, kc_], [key_r])
            tt('dve', out_tile, r[:], B, ALU.add, [key_r, kc_], [key_out])

        if True:
          def seq_body(sq):
            row0 = sq * S
            with ExitStack() as p1:
                mixT = sb(p1, "mixT", [128, 8, S], BF16)
                with ExitStack() as sc:
                    xin = [sb(sc, f"xin{i}", [128, D], F32) for i in range(2)]
                    for t in range(NT):
                        sl = t % 2
                        T.dma('sp', xin[sl][:], x[row0 + t * 128: row0 + (t + 1) * 128, :], writes=[('xin', sl)], key=('xin', sl))
                        pb = (t % 2) * 2
                        for kc in range(8):
                            bank = pb + kc // 4
                            col = (kc % 4) * 128
                            tr(ps[bank][:, col:col + 128], xin[sl][:, kc * 128:(kc + 1) * 128], [('xin', sl)], [('ps', bank)], inc=(kc % 4 == 3))
                        for hb in range(2):
                            bank = pb + hb
                            cp('act' if hb == 0 else 'dve', xhT[:, hb * 4:(hb + 1) * 4, t * 128:(t + 1) * 128],
                               ps[bank][:, :].rearrange("p (k c) -> p k c", k=4), [('ps', bank)], [('xhT', t)])
                    T.barrier()
                dump('xT', xhT[:], [('xhT', t) for t in range(NT)])
                if stop == 'xT':
                    return True

                with ExitStack() as sc:
                    g_ab = sb(sc, "g_ab", [128, 128], F32)
                    g_beta = sb(sc, "g_beta", [128, 64], F32)
                    g_g = sb(sc, "g_g", [128, 64], F32)
                    g_tmp = sb(sc, "g_tmp", [128, 64], F32)
                    g_eal = sb(sc, "g_eal", [128, 64], F32)
                    g_gc = sb(sc, "g_gc", [128, 64], F32)
                    g_ngc = sb(sc, "g_ngc", [128, 64], F32)
                    g_eg = sb(sc, "g_eg", [128, 64], F32)
                    g_egl = sb(sc, "g_egl", [128, 64], F32)
                    g_egla = sb(sc, "g_egla", [128, 64], F32)
                    raw2 = [sb(sc, f"raw{i}", [128, 3 + HS], F32) for i in range(2)]
                    acc = sb(sc, "acc", [128, HS], F32)
                    sil2 = [sb(sc, f"sil{i}", [128, HS], F32) for i in range(2)]
                    sqb = sb(sc, "sqb", [128, HS], BF16)
                    rstd = [sb(sc, f"rstd{i}", [128, 512], F32) for i in range(2)]
                    halo_g = sb(sc, "halo_g", [128, 12, 3], F32)
                    zero3 = sb(sc, "zero3", [128, 3], F32)
                    wst = [sb(sc, f"wst{i}", [128, 8, 128], BF16) for i in range(3)]
                    hq = sb(sc, "hq", [128, 4, HS], BF16)
                    hk = sb(sc, "hk", [128, 4, HS], BF16)
                    hkg = sb(sc, "hkg", [128, 4, NTH, 128], BF16)
                    hkd = sb(sc, "hkd", [128, 4, NTH, 128], BF16)
                    hv = sb(sc, "hv", [128, 4, NTH, 128], BF16)
                    hz = sb(sc, "hz", [128, 4, HS], BF16)
                    S32 = sb(sc, "S32", [128, 4, 128], F32)
                    Sbf = sb(sc, "Sbf", [128, 4, 128], BF16)

                    def tmpp(name, dt):
                        return sb(sc, name, [128, 4, 128], dt)
                    Ug = tmpp("Ug", F32)
                    EGb = tmpp("EGb", F32)
                    ARG = tmpp("ARG", F32)
                    ARG2 = tmpp("ARG2", F32)
                    DT = tmpp("DT", F32)
                    DTs = tmpp("DTs", F32)
                    Nf = tmpp("Nf", F32)
                    Pb = [tmpp("Pb0_", BF16), tmpp("Pb1_", BF16)]
                    PTb = [tmpp("PTb0_", BF16), tmpp("PTb1_", BF16)]
                    Xb = [tmpp("Xb0_", BF16), tmpp("Xb1_", BF16)]
                    QKD = tmpp("QKD", BF16)
                    nw2T = tmpp("nw2T", BF16)
                    vnew = tmpp("vnew", BF16)
                    qgT = tmpp("qgT", BF16)
                    sqo = tmpp("sqo", BF16)
                    rso = tmpp("rso", F32)
                    o1 = tmpp("o1", F32)

                    T.issue('pool', lambda e: e.memset(zero3[:], 0.0), writes=[('zero3',)])
                    cur_half = [0]

                    for t in range(NT):
                        for kc in range(8):
                            mm(ps[7][:, t * 8:(t + 1) * 8], xhT[:, kc, t * 128:(t + 1) * 128], w_ab[:, kc, :], kc == 0, kc == 7,
                               [('xhT', t), ('w_ab',)], [('ps', 7)], inc=(kc == 7 and t == NT - 1))
                    cp('dve', g_ab[:], ps[7][:, 0:128], [('ps', 7)], [('g_ab',)])
                    abv = g_ab[:].rearrange("p (t c) -> p t c", c=8)
                    v64 = lambda tl: tl[:].rearrange("p (t c) -> p t c", c=4)
                    act(v64(g_beta), abv[:, :, 4:8], AF.Sigmoid, [('g_ab',)], [('g_beta',)])
                    tt('dve', v64(g_tmp), abv[:, :, 0:4], DTBB.rearrange("p (t c) -> p t c", c=4), ALU.add, [('g_ab',), ('cbc',)], [('g_tmp',)])
                    act(g_tmp[:], g_tmp[:], AF.Exp, [('g_tmp',)], [('g_tmp',)])
                    ts('dve', g_tmp[:], g_tmp[:], 1.0, None, ALU.add, None, [('g_tmp',)], [('g_tmp',)])
                    act(g_tmp[:], g_tmp[:], AF.Ln, [('g_tmp',)], [('g_tmp',)])
                    act(g_eal[:], ALOGB, AF.Exp, [('cbc',)], [('g_eal',)])
                    stt(g_g[:], g_tmp[:], -1.0, g_eal[:], ALU.mult, ALU.mult, [('g_tmp',), ('g_eal',)], [('g_g',)])
                    mm(ps[7][:, 128:192], Umat, g_g[:], True, True, [('cmat',), ('g_g',)], [('ps', 7)])
                    mm(ps[7][:, 192:256], onesf[:], g_g[:], True, True, [('onesf',), ('g_g',)], [('ps', 7)])
                    cp('dve', g_gc[:], ps[7][:, 128:192], [('ps', 7)], [('g_gc',)])
                    ts('dve', g_ngc[:], g_gc[:], -1.0, None, ALU.mult, None, [('g_gc',)], [('g_ngc',)])
                    act(g_eg[:], g_gc[:], AF.Exp, [('g_gc',)], [('g_eg',)])
                    tt('dve', g_egl[:], ps[7][:, 192:256], g_gc[:], ALU.subtract, [('ps', 7), ('g_gc',)], [('g_egl',)])
                    act(g_egl[:], g_egl[:], AF.Exp, [('g_egl',)], [('g_egl',)])
                    act(g_egla[:], ps[7][:, 192:256], AF.Exp, [('ps', 7)], [('g_egla',)])
                    GS = [('g_beta',), ('g_gc',), ('g_ngc',), ('g_eg',), ('g_egl',), ('g_egla',), ('g_g',)]

                    wcnt = [0]

                    def load_wchunk(c0):
                        sl = wcnt[0] % 3
                        wcnt[0] += 1
                        T.dma('pool', wst[sl][:], w_in_v[:, :, c0:c0 + 128], writes=[('wst', sl)], key=('wst', sl))
                        return sl

                    def proj_block(sl, tb, bank):
                        gtb = cur_half[0] * NBLK + tb
                        for kc in range(8):
                            mm(ps[bank][:, :], wst[sl][:, kc, :], xhT[:, kc, gtb * 512:(gtb + 1) * 512], kc == 0, kc == 7,
                               [('wst', sl)] + [('xhT', gtb * 4 + i) for i in range(4)], [('ps', bank)])

                    trc = [0]

                    def stage_P(ch, tb):
                        kind, h, cidx, sl, ci = ch
                        par = h
                        rp = ci % 2
                        raw = raw2[rp]
                        bank = 6 + tb % 2
                        cs = slice(tb * 512, (tb + 1) * 512)
                        proj_block(sl, tb, bank)
                        if kind == 'z':
                            act(hz[:, par, cs], ps[bank][:, :], AF.Silu, [('ps', bank)], [('hz', par)])
                            return
                        if tb == 0:
                            if cur_half[0] == 0:
                                cp('act', raw[:, 0:3], zero3[:], [('zero3',)], [('raw', rp, -1)])
                            else:
                                cp('act', raw[:, 0:3], halo_g[:, cidx, :], [('halo_g', cidx)], [('raw', rp, -1)])
                        cp('act', raw[:, 3 + tb * 512: 3 + (tb + 1) * 512], ps[bank][:, :], [('ps', bank)], [('raw', rp, tb)])
                        if tb == NBLK - 1 and cur_half[0] == 0:
                            cp('act', halo_g[:, cidx, :], raw[:, HS:HS + 3], [('raw', rp, tb)], [('halo_g', cidx)])

                    def stage_C(ch, tb):
                        kind, h, cidx, sl, ci = ch
                        if kind == 'z':
                            return
                        rp = ci % 2
                        raw = raw2[rp]
                        cs = slice(tb * 512, (tb + 1) * 512)
                        RK = [('raw', rp, tb), ('raw', rp, tb - 1), ('cpp',)]
                        ka = ('acc', tb)
                        ts('dve', acc[:, cs], raw[:, 3 + tb * 512: 3 + (tb + 1) * 512], gcw[:, cidx, 3:4], None, ALU.mult, None, RK, [ka])
                        for j in (2, 1, 0):
                            stt(acc[:, cs], raw[:, j + tb * 512: j + (tb + 1) * 512], gcw[:, cidx, j:j + 1], acc[:, cs], ALU.mult, ALU.add,
                                RK + [ka], [ka])

                    def stage_S(ch, tb):
                        kind, h, cidx, sl, ci = ch
                        if kind == 'z':
                            return
                        sp_ = ci % 2
                        cs = slice(tb * 512, (tb + 1) * 512)
                        act(sil2[sp_][:, cs], acc[:, cs], AF.Silu, [('acc', tb)], [('sil', sp_, tb)])

                    def stage_N(ch):
                        kind, h, cidx, sl, ci = ch
                        if kind not in ('q', 'k'):
                            return
                        par = h
                        sp_ = ci % 2
                        sl_ = sil2[sp_]
                        for tb in range(NBLK):
                            cs = slice(tb * 512, (tb + 1) * 512)
                            tt('pool', sqb[:, cs], sl_[:, cs], sl_[:, cs], ALU.mult, [('sil', sp_, tb)], [('sqb', tb)])
                            mm(ps[2 + tb][:, :], onesb[:], sqb[:, cs], True, True, [('onesb',), ('sqb', tb)], [('ps', 2 + tb)])
                        for tb in range(NBLK):
                            act(rstd[tb][:], ps[2 + tb][:, :], AF.Ln, [('ps', 2 + tb), ('epst',)], [('rstd', tb)], bias=eps1)
                        for tb in range(NBLK):
                            act(rstd[tb][:], rstd[tb][:], AF.Exp, [('rstd', tb)], [('rstd', tb)], scale=-0.5)
                        for tb in range(NBLK):
                            cs = slice(tb * 512, (tb + 1) * 512)
                            ks = ('sil', sp_, tb)
                            if kind == 'q':
                                stt(hq[:, par, cs], sl_[:, cs], float(128 ** -0.5), rstd[tb][:], ALU.mult, ALU.mult,
                                    [ks, ('rstd', tb)], [('hq', par)])
                            else:
                                tt('dve', sl_[:, cs], sl_[:, cs], rstd[tb][:], ALU.mult, [ks, ('rstd', tb)], [ks])
                                cp('act', hk[:, par, cs], sl_[:, cs], [ks], [('hk', par)])

                    def stage_T(ch, tb):
                        kind, h, cidx, sl, ci = ch
                        if kind not in ('k', 'v'):
                            return
                        par = h
                        sp_ = ci % 2
                        ks = ('sil', sp_, tb)
                        for tl in range(4):
                            n = tb * 4 + tl
                            c = (cur_half[0] * NTH + n) * 4 + h
                            b3 = trc[0] % 2
                            trc[0] += 1
                            tr(ps[b3][:, 0:128], sil2[sp_][:, n * 128:(n + 1) * 128], [ks], [('ps', b3)])
                            if kind == 'k':
                                amul(hkg[:, par, n, :], ps[b3][:, 0:128], g_eg[:, c:c + 1], [('ps', b3), ('g_eg',)], [('hkg', par)])
                                ts('dve', hkd[:, par, n, :], ps[b3][:, 0:128], g_egl[:, c:c + 1], None, ALU.mult, None,
                                   [('ps', b3), ('g_egl',)], [('hkd', par)])
                            else:
                                cp('act' if n % 2 else 'dve', hv[:, par, n, :], ps[b3][:, 0:128], [('ps', b3)], [('hv', par)])

                    def run_A_quad():
                        chs = []
                        for h in range(4):
                            for kind, c0, cidx in (('q', h * 128, h), ('k', 512 + h * 128, 4 + h), ('v', 1024 + h * 128, 8 + h), ('z', 1536 + h * 128, 0)):
                                chs.append([kind, h, cidx, None, len(chs), c0])
                        nch = len(chs)
                        loaded = [0]

                        def ensure_loaded(upto):
                            while loaded[0] <= min(upto, nch - 1):
                                ch_ = chs[loaded[0]]
                                ch_[3] = load_wchunk(ch_[5])
                                loaded[0] += 1
                        nit = NBLK * nch
                        for tau in range(nit + NBLK + 5):
                            if tau < nit:
                                i, tb = divmod(tau, NBLK)
                                if tb == 0:
                                    ensure_loaded(i + 1)
                                stage_P(tuple(chs[i][:5]), tb)
                            if 0 <= tau - 1 < nit:
                                i, tb = divmod(tau - 1, NBLK)
                                stage_C(tuple(chs[i][:5]), tb)
                            if 0 <= tau - 2 < nit:
                                i, tb = divmod(tau - 2, NBLK)
                                stage_S(tuple(chs[i][:5]), tb)
                            tn = tau - (NBLK + 2)
                            if tn >= 0 and tn % NBLK == 0 and tn // NBLK < nch:
                                stage_N(tuple(chs[tn // NBLK][:5]))
                            if 0 <= tau - (NBLK + 3) < nit:
                                i, tb = divmod(tau - (NBLK + 3), NBLK)
                                stage_T(tuple(chs[i][:5]), tb)

                    def H4(b):
                        return ps[b][:, :].rearrange("p (i c) -> p i c", i=4), ('ps', b)

                    def rec_quad(half):
                        if half == 0:
                            T.issue('pool', lambda e: e.memset(S32[:], 0.0), writes=[('S32',)])
                            T.issue('pool', lambda e: e.memset(Sbf[:], 0.0), writes=[('Sbf',)])
                        pGb, kGb = H4(0)
                        pX, kX = H4(6)
                        pKK, kKK = H4(1)
                        pV, kV = H4(1)
                        pQK, kQK = H4(2)
                        pO, kO = H4(2)
                        pNT, kNT = H4(3)
                        pS, kS = H4(3)
                        pP, kP = H4(4)
                        pR, kR = H4(4)
                        pPT, kPT = H4(5)
                        pW, kW = H4(0)
                        for n in range(NTH):
                            tok = slice(n * 128, (n + 1) * 128)
                            gtok = slice((half * NTH + n) * 128, (half * NTH + n + 1) * 128)
                            cc = [(half * NTH + n) * 4 + i for i in range(4)]
                            for i in range(4):
                                amul(Ug[:, i, :], Umat, g_g[:, cc[i]:cc[i] + 1], [('cmat',), ('g_g',)], [('Ug',)])
                            for i in range(4):
                                mm(pGb[:, i, :], onesf[:], Ug[:, i, :], True, True, [('onesf',), ('Ug',)], [kGb], inc=(i == 3))
                            tt('dve', ARG2[:], pGb, NEGMS4, ALU.add, [kGb, ('cmat',)], [('ARG2',)])
                            tt('dve', ARG[:], pGb, NEGM4, ALU.add, [kGb, ('cmat',)], [('ARG',)])
                            act(EGb[:], pGb, AF.Exp, [kGb], [('EGb',)])
                            for i in range(4):
                                mm(pKK[:, i, :], hk[:, i, tok], hk[:, i, tok], True, True, [('hk', i)], [kKK], inc=(i == 3))
                            for i in range(4):
                                mm(pQK[:, i, :], hk[:, i, tok], hq[:, i, tok], True, True, [('hk', i), ('hq', i)], [kQK], inc=(i == 3))
                            for i in range(4):
                                act(DTs[:, i, :], ARG2[:, i, :], AF.Exp, [('ARG2',), ('g_ngc',)], [('DTs',)], bias=g_ngc[:, cc[i]:cc[i] + 1])
                            for i in range(4):
                                act(DT[:, i, :], ARG[:, i, :], AF.Exp, [('ARG',), ('g_ngc',)], [('DT',)], bias=g_ngc[:, cc[i]:cc[i] + 1])
                            for i in range(4):
                                stt(Nf[:, i, :], pKK[:, i, :], g_beta[:, cc[i]:cc[i] + 1], DTs[:, i, :], ALU.mult, ALU.mult,
                                    [kKK, ('g_beta',), ('DTs',)], [('Nf',)])
                            for i in range(4):
                                tr(pNT[:, i, :], Nf[:, i, :], [('Nf',)], [kNT], inc=(i == 3))
                            cur = 0
                            cp('act', Pb[cur][:], Nf[:], [('Nf',)], [('Pb0',)])
                            cp('dve', PTb[cur][:], pNT, [kNT], [('PTb0',)])
                            tt('dve', Xb[cur][:], ident4, Nf[:], ALU.subtract, [('cmat',), ('Nf',)], [('Xb0',)])
                            tt('dve', QKD[:], pQK, DT[:], ALU.mult, [kQK, ('DT',)], [('QKD',)])
                            tt('pool', qgT[:], hq[:, :, tok], EGb[:], ALU.mult, [('hq', i_) for i_ in range(4)] + [('EGb',)], [('qgT',)])
                            xc = 0

                            def x_update(ptb_idx, step_):
                                nonlocal_xc = x_state[0]
                                xn = 1 - nonlocal_xc
                                for i in range(4):
                                    mm(pX[:, i, :], identb[:], Xb[nonlocal_xc][:, i, :], True, False, [('identb',), (f'Xb{nonlocal_xc}',)], [kX], inc=False)
                                    mm(pX[:, i, :], PTb[ptb_idx][:, i, :], Xb[nonlocal_xc][:, i, :], False, True,
                                       [(f'PTb{ptb_idx}',), (f'Xb{nonlocal_xc}',)], [kX], inc=(i == 3))
                                x_state[0] = xn
                                return xn
                            x_state = [0]
                            pend = None
                            for step in range(6):
                                nx = 1 - cur
                                kPc, kPTc = (f'Pb{cur}',), (f'PTb{cur}',)
                                kPn, kPTn = (f'Pb{nx}',), (f'PTb{nx}',)
                                for i in range(4):
                                    mm(pPT[:, i, :], Pb[cur][:, i, :], PTb[cur][:, i, :], True, True, [kPc, kPTc], [kPT], inc=(i == 3))
                                if step < 5:
                                    for i in range(4):
                                        mm(pP[:, i, :], PTb[cur][:, i, :], Pb[cur][:, i, :], True, True, [kPc, kPTc], [kP], inc=(i == 3))
                                if pend is not None:
                                    xn = x_update(pend, step)
                                cp('dve', PTb[nx][:], pPT, [kPT], [kPTn])
                                if step < 5:
                                    cp('act', Pb[nx][:], pP, [kP], [kPn])
                                if pend is not None:
                                    cp('act' if step % 2 else 'dve', Xb[xn][:], pX, [kX], [(f'Xb{xn}',)])
                                pend = nx
                                cur = nx
                            xn = x_update(pend, 6)
                            cp('act', Xb[xn][:], pX, [kX], [(f'Xb{xn}',)])
                            cur = xn
                            kT2 = (f'Xb{cur}',)
                            T2T = Xb[cur]
                            for i in range(4):
                                mm(pW[:, i, :], hkg[:, i, n, :], T2T[:, i, :], True, True, [('hkg', i), kT2], [kW], inc=(i == 3))
                            amul(nw2T[:], pW, -1.0, [kW], [('nw2T',)])
                            for i in range(4):
                                mm(pV[:, i, :], T2T[:, i, :], hv[:, i, n, :], True, False, [kT2, ('hv', i)], [kV], inc=False)
                                mm(pV[:, i, :], nw2T[:, i, :], Sbf[:, i, :], False, True, [('nw2T',), ('Sbf',)], [kV], inc=(i == 3))
                            for i in range(4):
                                ts('dve', vnew[:, i, :], pV[:, i, :], g_beta[:, cc[i]:cc[i] + 1], None, ALU.mult, None, [kV, ('g_beta',)], [('vnew',)])
                            for i in range(4):
                                mm(pS[:, i, :], hkd[:, i, n, :], vnew[:, i, :], True, True, [('hkd', i), ('vnew',)], [kS], inc=(i == 3))
                            for i in range(4):
                                mm(pO[:, i, :], Sbf[:, i, :], qgT[:, i, :], True, False, [('Sbf',), ('qgT',)], [kO], inc=False)
                                mm(pO[:, i, :], vnew[:, i, :], QKD[:, i, :], False, True, [('vnew',), ('QKD',)], [kO], inc=(i == 3))
                            for i in range(4):
                                stt(S32[:, i, :], S32[:, i, :], g_egla[:, cc[i]:cc[i] + 1], pS[:, i, :], ALU.mult, ALU.add,
                                    [('S32',), ('g_egla',), kS], [('S32',)])
                            cp('act', Sbf[:], S32[:], [('S32',)], [('Sbf',)])
                            act(sqo[:], pO, AF.Square, [kO], [('sqo',)])
                            for i in range(4):
                                mm(pR[:, i, :], c128b[:], sqo[:, i, :], True, True, [('c128b',), ('sqo',)], [kR], inc=(i == 3))
                            rsqrt(rso[:], pR, eps1, [kR], [('rso',)])
                            stt(o1[:], pO, normg, rso[:], ALU.mult, ALU.mult, [kO, ('cpp',), ('rso',)], [('o1',)])
                            tt('pool', mixT[:, 0:4, gtok], o1[:], hz[:, :, tok], ALU.mult, [('o1',)] + [('hz', i_) for i_ in range(4)],
                               [('mixT', i_) for i_ in range(4)])

                    def gen_rec_pair(half, pr):
                        ia = 2 * pr
                        ii = (ia, ia + 1)
                        sl_ = slice(ia, ia + 2)
                        B = 4 * pr

                        def HB(b, hf):
                            return ps[b][:, hf * 256:(hf + 1) * 256].rearrange("p (i c) -> p i c", i=2), ('ps', b)
                        pGb, kGb = HB(B, 0)
                        pW, kW = HB(B, 0)
                        pP, kP = HB(B, 1)
                        pR, kR = HB(B, 1)
                        pKK, kKK = HB(B + 1, 0)
                        pV, kV = HB(B + 1, 0)
                        pPT, kPT = HB(B + 1, 1)
                        pQK, kQK = HB(B + 2, 0)
                        pO, kO = HB(B + 2, 0)
                        pX, kX = HB(B + 2, 1)
                        pNT, kNT = HB(B + 3, 0)
                        pS, kS = HB(B + 3, 0)
                        K = lambda nm: (nm, pr)
                        last = ia + 1
                        for n in range(NTH):
                            tok = slice(n * 128, (n + 1) * 128)
                            gtok = slice((half * NTH + n) * 128, (half * NTH + n + 1) * 128)
                            cc = {i: (half * NTH + n) * 4 + i for i in ii}
                            for i in ii:
                                amul(Ug[:, i, :], Umat, g_g[:, cc[i]:cc[i] + 1], [('cmat',), ('g_g',)], [K('Ug')])
                            yield
                            for i in ii:
                                mm(pGb[:, i - ia, :], onesf[:], Ug[:, i, :], True, True, [('onesf',), K('Ug')], [kGb], inc=(i == last))
                            yield
                            tt('dve', ARG2[:, sl_, :], pGb, NEGMS4[:, 0:2, :], ALU.add, [kGb, ('cmat',)], [K('ARG2')])
                            tt('dve', ARG[:, sl_, :], pGb, NEGM4[:, 0:2, :], ALU.add, [kGb, ('cmat',)], [K('ARG')])
                            act(EGb[:, sl_, :], pGb, AF.Exp, [kGb], [K('EGb')])
                            for i in ii:
                                mm(pKK[:, i - ia, :], hk[:, i, tok], hk[:, i, tok], True, True, [('hk', i)], [kKK], inc=(i == last))
                            for i in ii:
                                mm(pQK[:, i - ia, :], hk[:, i, tok], hq[:, i, tok], True, True, [('hk', i), ('hq', i)], [kQK], inc=(i == last))
                            yield
                            for i in ii:
                                act(DTs[:, i, :], ARG2[:, i, :], AF.Exp, [K('ARG2'), ('g_ngc',)], [K('DTs')], bias=g_ngc[:, cc[i]:cc[i] + 1])
                            for i in ii:
                                act(DT[:, i, :], ARG[:, i, :], AF.Exp, [K('ARG'), ('g_ngc',)], [K('DT')], bias=g_ngc[:, cc[i]:cc[i] + 1])
                            yield
                            for i in ii:
                                stt(Nf[:, i, :], pKK[:, i - ia, :], g_beta[:, cc[i]:cc[i] + 1], DTs[:, i, :], ALU.mult, ALU.mult,
                                    [kKK, ('g_beta',), K('DTs')], [K('Nf')])
                            yield
                            for i in ii:
                                tr(pNT[:, i - ia, :], Nf[:, i, :], [K('Nf')], [kNT], inc=(i == last))
                            cur = 0
                            cp('act', Pb[cur][:, sl_, :], Nf[:, sl_, :], [K('Nf')], [K('Pb0')])
                            yield
                            cp('dve', PTb[cur][:, sl_, :], pNT, [kNT], [K('PTb0')])
                            tt('dve', Xb[cur][:, sl_, :], ident4[:, 0:2, :], Nf[:, sl_, :], ALU.subtract, [('cmat',), K('Nf')], [K('Xb0')])
                            tt('dve', QKD[:, sl_, :], pQK, DT[:, sl_, :], ALU.mult, [kQK, K('DT')], [K('QKD')])
                            tt('pool', qgT[:, sl_, :], hq[:, sl_, tok], EGb[:, sl_, :], ALU.mult, [('hq', i_) for i_ in ii] + [K('EGb')], [K('qgT')])
                            yield
                            xs = [0]

                            def x_update(ptb_idx):
                                xc_ = xs[0]
                                xn_ = 1 - xc_
                                for i in ii:
                                    mm(pX[:, i - ia, :], identb[:], Xb[xc_][:, i, :], True, False, [('identb',), K(f'Xb{xc_}')], [kX], inc=False)
                                    mm(pX[:, i - ia, :], PTb[ptb_idx][:, i, :], Xb[xc_][:, i, :], False, True,
                                       [K(f'PTb{ptb_idx}'), K(f'Xb{xc_}')], [kX], inc=(i == last))
                                xs[0] = xn_
                                return xn_
                            pend = None
                            for step in range(6):
                                nx = 1 - cur
                                kPc, kPTc = K(f'Pb{cur}'), K(f'PTb{cur}')
                                kPn, kPTn = K(f'Pb{nx}'), K(f'PTb{nx}')
                                for i in ii:
                                    mm(pPT[:, i - ia, :], Pb[cur][:, i, :], PTb[cur][:, i, :], True, True, [kPc, kPTc], [kPT], inc=(i == last))
                                if step < 5:
                                    for i in ii:
                                        mm(pP[:, i - ia, :], PTb[cur][:, i, :], Pb[cur][:, i, :], True, True, [kPc, kPTc], [kP], inc=(i == last))
                                if pend is not None:
                                    xn = x_update(pend)
                                yield
                                cp('dve', PTb[nx][:, sl_, :], pPT, [kPT], [kPTn])
                                if step < 5:
                                    cp('act', Pb[nx][:, sl_, :], pP, [kP], [kPn])
                                if pend is not None:
                                    cp('act' if step % 2 else 'dve', Xb[xn][:, sl_, :], pX, [kX], [K(f'Xb{xn}')])
                                yield
                                pend = nx
                                cur = nx
                            xn = x_update(pend)
                            yield
                            cp('act', Xb[xn][:, sl_, :], pX, [kX], [K(f'Xb{xn}')])
                            yield
                            cur = xn
                            kT2 = K(f'Xb{cur}')
                            T2T = Xb[cur]
                            for i in ii:
                                mm(pW[:, i - ia, :], hkg[:, i, n, :], T2T[:, i, :], True, True, [('hkg', i), kT2], [kW], inc=(i == last))
                            yield
                            amul(nw2T[:, sl_, :], pW, -1.0, [kW], [K('nw2T')])
                            yield
                            for i in ii:
                                mm(pV[:, i - ia, :], T2T[:, i, :], hv[:, i, n, :], True, False, [kT2, ('hv', i)], [kV], inc=False)
                                mm(pV[:, i - ia, :], nw2T[:, i, :], Sbf[:, i, :], False, True, [K('nw2T'), K('Sbf')], [kV], inc=(i == last))
                            yield
                            for i in ii:
                                ts('dve', vnew[:, i, :], pV[:, i - ia, :], g_beta[:, cc[i]:cc[i] + 1], None, ALU.mult, None, [kV, ('g_beta',)], [K('vnew')])
                            yield
                            for i in ii:
                                mm(pS[:, i - ia, :], hkd[:, i, n, :], vnew[:, i, :], True, True, [('hkd', i), K('vnew')], [kS], inc=(i == last))
                            for i in ii:
                                mm(pO[:, i - ia, :], Sbf[:, i, :], qgT[:, i, :], True, False, [K('Sbf'), K('qgT')], [kO], inc=False)
                                mm(pO[:, i - ia, :], vnew[:, i, :], QKD[:, i, :], False, True, [K('vnew'), K('QKD')], [kO], inc=(i == last))
                            yield
                            for i in ii:
                                stt(S32[:, i, :], S32[:, i, :], g_egla[:, cc[i]:cc[i] + 1], pS[:, i - ia, :], ALU.mult, ALU.add,
                                    [K('S32'), ('g_egla',), kS], [K('S32')])
                            act(sqo[:, sl_, :], pO, AF.Square, [kO], [K('sqo')])
                            yield
                            cp('act', Sbf[:, sl_, :], S32[:, sl_, :], [K('S32')], [K('Sbf')])
                            for i in ii:
                                mm(pR[:, i - ia, :], c128b[:], sqo[:, i, :], True, True, [('c128b',), K('sqo')], [kR], inc=(i == last))
                            yield
                            rsqrt(rso[:, sl_, :], pR, eps1, [kR], [K('rso')])
                            yield
                            stt(o1[:, sl_, :], pO, normg, rso[:, sl_, :], ALU.mult, ALU.mult, [kO, ('cpp',), K('rso')], [K('o1')])
                            yield
                            tt('pool', mixT[:, sl_, gtok], o1[:, sl_, :], hz[:, sl_, tok], ALU.mult, [K('o1')] + [('hz', i_) for i_ in ii],
                               [('mixT', i_) for i_ in ii])
                            yield

                    def rec_two_chains(half):
                        if half == 0:
                            for pr in range(2):
                                T.issue('pool', lambda e: e.memset(S32[:, 2 * pr:2 * pr + 2, :], 0.0), writes=[('S32', pr)])
                                T.issue('pool', lambda e: e.memset(Sbf[:, 2 * pr:2 * pr + 2, :], 0.0), writes=[('Sbf', pr)])
                        gens = [gen_rec_pair(half, 0), gen_rec_pair(half, 1)]
                        while gens:
                            for g_ in list(gens):
                                try:
                                    next(g_)
                                except StopIteration:
                                    gens.remove(g_)

                    for half in range(2):
                        cur_half[0] = half
                        run_A_quad()
                        if TWO_CHAINS:
                            rec_two_chains(half)
                        else:
                            rec_quad(half)
                    T.barrier()
                dump('mixA', mixT[:, 0:4, :], [('mixT', h) for h in range(4)])
                if stop == 'mixA':
                    return True

                wo_t = sb(p1, "wo_t", [128, 8, 1024], BF16)
                with ExitStack() as sc:
                    ctab_t = sb(sc, "ctab_t", [64, 2 * S], F32)
                    T.dma('sp', ctab_t[:], ctab[:, :], writes=[('ctab',)], key=('ctab',))
                    cos2 = ctab_t[:, 0:S]
                    sin2 = ctab_t[:, S:2 * S]
                    wq_t = sb(sc, "wq_t", [128, 3, 768], BF16)
                    wqr_t = sb(sc, "wqr_t", [128, 3, 4, 64], BF16)
                    wkv_t = sb(sc, "wkv_t", [128, 2, 1024], BF16)
                    wkr_t = sb(sc, "wkr_t", [128, 8, 128], BF16)
                    wst = [sb(sc, f"wstm{i}", [128, 8, 128], BF16) for i in range(3)]
                    cqg = sb(sc, "cqg", [128, 3, S], BF16)
                    ckvg = sb(sc, "ckvg", [128, 2, S], BF16)
                    sqr = [sb(sc, f"sqr{i}", [128, 512], BF16) for i in range(2)]
                    rsq = sb(sc, "rsq", [128, S], F32)
                    rskv = sb(sc, "rskv", [128, S], F32)
                    rskvt = sb(sc, "rskvt", [128, NT], F32)
                    krT = sb(sc, "krT", [64, S], BF16)
                    t1 = [sb(sc, f"rt1_{i}", [64, 512], F32) for i in range(2)]
                    t2 = [sb(sc, f"rt2_{i}", [64, 512], F32) for i in range(2)]
                    qn = [sb(sc, f"qn{i}", [128, S], BF16) for i in range(1)]
                    qr = [sb(sc, f"qr{i}", [64, S], BF16) for i in range(1)]
                    kn = [sb(sc, f"kn{i}", [128, S], BF16) for i in range(1)]
                    vh = [sb(sc, f"vh{i}", [128, NT, 128], BF16) for i in range(1)]
                    PT = [sb(sc, f"PTt{i}", [128, 512], BF16) for i in range(3)]
                    den = [sb(sc, f"den{i}", [128, 512], F32) for i in range(2)]

                    wcnt = [0]

                    def load_wchunk2(c0):
                        sl = wcnt[0] % 3
                        wcnt[0] += 1
                        T.dma('pool', wst[sl][:], w_in_v[:, :, c0:c0 + 128], writes=[('wst', sl)], key=('wstm', sl))
                        return sl
                    pre_sl = {0: load_wchunk2(2056), 1: load_wchunk2(2056 + 128)}
                    T.dma('pool', wkr_t[:, :, 0:64], w_in_v[:, :, 2696:2760], writes=[('wkr',)], key=('wkr',))
                    T.dma('pool', wq_t[:], wq_v[:, :, :], writes=[('wq',)], key=('wq',))
                    T.dma('pool', wkv_t[:], wkv_v[:, :, :], writes=[('wkv',)], key=('wkv',))
                    ts('dve', wkr_t[:, :, 64:96], wkr_t[:, :, 32:64], -1.0, None, ALU.mult, None, [('wkr',)], [('wkr2',)])
                    cp('dve', wkr_t[:, :, 96:128], wkr_t[:, :, 0:32], [('wkr',)], [('wkr2',)])
                    for h in range(4):
                        ts('dve', wqr_t[:, :, h, 0:32], wq_t[:, :, h * 192 + 160:h * 192 + 192], -1.0, None, ALU.mult, None, [('wq',)], [('wqr',)])
                        cp('dve', wqr_t[:, :, h, 32:64], wq_t[:, :, h * 192 + 128:h * 192 + 160], [('wq',)], [('wqr',)])

                    XH = lambda tb: [('xhT', tb * 4 + i) for i in range(4)]
                    sqcnt = [0]
                    pend_n = [None]
                    for ci in range(5):
                        isq = ci < 3
                        cc = ci if isq else ci - 3
                        last = cc == (2 if isq else 1)
                        c0 = 2056 + ci * 128
                        if ci + 2 < 5:
                            pre_sl[ci + 2] = load_wchunk2(c0 + 256)
                        if ci == 0:
                            T.dma('pool', wo_t[:], w_out_v[:, :, :], writes=[('wo',)], key=('wo',))
                        sl = pre_sl[ci]
                        nrm = onesb if isq else c256b
                        knrm = ('onesb',) if isq else ('c256b',)
                        for tb in range(4):
                            bank = tb % 2
                            cs = slice(tb * 512, (tb + 1) * 512)
                            for kc in range(8):
                                mm(ps[bank][:, :], wst[sl][:, kc, :], xhT[:, kc, cs], kc == 0, kc == 7,
                                   [('wst', sl)] + XH(tb), [('ps', bank)])
                            if isq:
                                ts('dve', cqg[:, cc, cs], ps[bank][:, :], qg[:, cc:cc + 1], float(384 ** 0.5), ALU.mult, ALU.mult,
                                   [('ps', bank), ('cpp',)], [('cqg',)])
                            else:
                                ts('dve', ckvg[:, cc, cs], ps[bank][:, :], kvg[:, cc:cc + 1], None, ALU.mult, None,
                                   [('ps', bank), ('cpp',)], [('ckvg',)])
                            sqi = sqcnt[0] % 2
                            sqcnt[0] += 1
                            act(sqr[sqi][:], ps[bank][:, :], AF.Square, [('ps', bank)], [('sqr', sqi)])
                            if pend_n[0] is not None:
                                pend_n[0]()

                            def _norm(tb=tb, cs=cs, nrm=nrm, knrm=knrm, sqi=sqi, cc=cc, last=last, isq=isq):
                                mm(ps[2 + tb][:, :], nrm[:], sqr[sqi][:], cc == 0, last, [knrm, ('sqr', sqi)], [('ps', 2 + tb)], inc=True)
                                if last:
                                    if isq:
                                        rsqrt(rsq[:, cs], ps[2 + tb][:, :], eps384, [('ps', 2 + tb)], [('rsq',)])
                                    else:
                                        rsqrt(rskv[:, cs], ps[2 + tb][:, :], eps1, [('ps', 2 + tb)], [('rskv',)])
                            pend_n[0] = _norm
                        if ci == 4:
                            pend_n[0]()
                            pend_n[0] = None
                            for t in range(NT):
                                bank = 6 + t % 2
                                tr(ps[bank][:, 0:128], rskv[:, t * 128:(t + 1) * 128], [('rskv',)], [('ps', bank)])
                                cp('dve', rskvt[:, t:t + 1], ps[bank][:, 0:1], [('ps', bank)], [('rskvt',)])
                    for tb in range(4):
                        cs = slice(tb * 512, (tb + 1) * 512)
                        bA, bB = (5, 6) if tb % 2 == 0 else (0, 1)
                        for kc in range(8):
                            mm(ps[bA][0:64, :], wkr_t[:, kc, 0:64], xhT[:, kc, cs], kc == 0, kc == 7, [('wkr',)] + XH(tb), [('ps', bA)])
                        for kc in range(8):
                            mm(ps[bB][0:64, :], wkr_t[:, kc, 64:128], xhT[:, kc, cs], kc == 0, kc == 7, [('wkr2',)] + XH(tb), [('ps', bB)])
                        tt('dve', t1[tb % 2][:], ps[bA][0:64, :], cos2[:, cs], ALU.mult, [('ps', bA), ('ctab',)], [('t1', tb % 2)])
                        tt('dve', t2[tb % 2][:], ps[bB][0:64, :], sin2[:, cs], ALU.mult, [('ps', bB), ('ctab',)], [('t2', tb % 2)])
                        tt('pool', krT[:, cs], t1[tb % 2][:], t2[tb % 2][:], ALU.add, [('t1', tb % 2), ('t2', tb % 2)], [('krT',)])
                    scale = float(192 ** -0.5)
                    ptc = [0]
                    for h in range(4):
                        par = 0
                        for tb in range(4):
                            cs = slice(tb * 512, (tb + 1) * 512)
                            b0, b1, b5, b6 = (0, 1, 5, 6) if tb % 2 == 0 else (2, 3, 4, 7)
                            for kc in range(3):
                                mm(ps[b0][:, :], wq_t[:, kc, h * 192:h * 192 + 128], cqg[:, kc, cs], kc == 0, kc == 2, [('wq',), ('cqg',)], [('ps', b0)])
                            for kc in range(3):
                                mm(ps[b5][0:64, :], wq_t[:, kc, h * 192 + 128:h * 192 + 192], cqg[:, kc, cs], kc == 0, kc == 2, [('wq',), ('cqg',)], [('ps', b5)])
                            for kc in range(3):
                                mm(ps[b6][0:64, :], wqr_t[:, kc, h, :], cqg[:, kc, cs], kc == 0, kc == 2, [('wqr',), ('cqg',)], [('ps', b6)])
                            for kc in range(2):
                                mm(ps[b1][:, :], wkv_t[:, kc, h * 256:h * 256 + 128], ckvg[:, kc, cs], kc == 0, kc == 1, [('wkv',), ('ckvg',)], [('ps', b1)])
                            tt('dve', qn[par][:, cs], ps[b0][:, :], rsq[:, cs], ALU.mult, [('ps', b0), ('rsq',)], [('qn', par)])
                            tt('dve', t1[tb % 2][:], ps[b5][0:64, :], cos2[:, cs], ALU.mult, [('ps', b5), ('ctab',)], [('t1', tb % 2)])
                            tt('dve', t2[tb % 2][:], ps[b6][0:64, :], sin2[:, cs], ALU.mult, [('ps', b6), ('ctab',)], [('t2', tb % 2)])
                            tt('dve', kn[par][:, cs], ps[b1][:, :], rskv[:, cs], ALU.mult, [('ps', b1), ('rskv',)], [('kn', par)])
                            tt('pool', t1[tb % 2][:], t1[tb % 2][:], t2[tb % 2][:], ALU.add, [('t1', tb % 2), ('t2', tb % 2)], [('t1', tb % 2)])
                            tt('pool', qr[par][:, cs], t1[tb % 2][:], rsq[0:64, cs], ALU.mult, [('t1', tb % 2), ('rsq',)], [('qr', par)])
                        for t in range(NT):
                            bank = 2 + t % 2
                            for kc in range(2):
                                mm(ps[bank][:, 0:128], ckvg[:, kc, t * 128:(t + 1) * 128], wkv_t[:, kc, h * 256 + 128:h * 256 + 256], kc == 0, kc == 1,
                                   [('wkv',), ('ckvg',)], [('ps', bank)])
                            amul(vh[par][:, t, :], ps[bank][:, 0:128], rskvt[:, t:t + 1], [('ps', bank), ('rskvt',)], [('vh', par)])
                        items = [(qb, kt) for qb in range(4) for kt in range(4 * qb + 4)]

                        def att_S(idx):
                            qb, kt = items[idx]
                            r = max(0, kt - 4 * qb)
                            c0 = qb * 512 + r * 128
                            ncol = 512 - r * 128
                            sb_ = ps[idx % 2]
                            ksb = ('ps', idx % 2)
                            mm(sb_[:, 0:ncol], kn[par][:, kt * 128:(kt + 1) * 128], qn[par][:, c0:c0 + ncol], True, False,
                               [('kn', par), ('qn', par)], [ksb], inc=False)
                            mm(sb_[:, 0:ncol], krT[:, kt * 128:(kt + 1) * 128], qr[par][:, c0:c0 + ncol], False, True,
                               [('krT',), ('qr', par)], [ksb])

                        def att_EV(idx):
                            qb, kt = items[idx]
                            nk = 4 * qb + 4
                            r = max(0, kt - 4 * qb)
                            ncol = 512 - r * 128
                            sb_ = ps[idx % 2]
                            ksb = ('ps', idx % 2)
                            pO_, pD_ = ps[4 + (qb % 2) * 2], ps[5 + (qb % 2) * 2]
                            kO, kD = ('ps', 4 + (qb % 2) * 2), ('ps', 5 + (qb % 2) * 2)
                            pi = ptc[0] % 3
                            ptc[0] += 1
                            act(PT[pi][:, 0:ncol], sb_[:, 0:ncol], AF.Exp, [ksb], [('PT', pi)], scale=scale)
                            if kt >= 4 * qb:
                                T.issue('pool', lambda e: e.memset(PT[pi][64:128, 0:64], 0.0), [], [('PT', pi)])
                            mm(pO_[:, r * 128:512], vh[par][:, kt, :], PT[pi][:, 0:ncol], kt == 0, kt == nk - 1, [('vh', par), ('PT', pi)], [kO])
                            mm(pD_[:, r * 128:512], onesb[:], PT[pi][:, 0:ncol], kt == 0, kt == nk - 1, [('onesb',), ('PT', pi)], [kD])
                            if kt == nk - 1:
                                dq = den[qb % 2]
                                T.issue('dve', lambda e: e.reciprocal(out=dq[:], in_=pD_[:, :]), [kD], [('den', qb % 2)])
                                tt('dve', mixT[:, 4 + h, qb * 512:(qb + 1) * 512], pO_[:, :], dq[:], ALU.mult, [kO, ('den', qb % 2)], [('mixT', 4 + h)])

                        att_S(0)
                        for idx in range(len(items)):
                            if idx + 1 < len(items):
                                att_S(idx + 1)
                            att_EV(idx)
                    T.barrier()
                dump('mixB', mixT[:, 4:8, :], [('mixT', 4 + h) for h in range(4)])
                if stop == 'mixB':
                    return True

                with ExitStack() as sc:
                    ln1_t = sb(sc, "ln1_t", [128, 2048], F32)
                    T.dma('sp', ln1_t[:], cbc[:, 0:2048], writes=[('cbcL',)], key=('cbcL',))
                    G1 = ln1_t[:, 0:1024]
                    B1 = ln1_t[:, 1024:2048]
                    xr = [sb(sc, f"xr{i}", [128, D], F32) for i in range(3)]
                    rr = [sb(sc, f"rr{i}", [128, D], F32) for i in range(3)]
                    hh = [sb(sc, f"hh{i}", [128, D], F32) for i in range(2)]
                    stats3 = [sb(sc, f"stats{i}", [128, 12], F32) for i in range(3)]
                    mv3 = [sb(sc, f"mv{i}", [128, 2], F32) for i in range(3)]
                    rs3 = [sb(sc, f"rs{i}", [128, 1], F32) for i in range(3)]

                    def d1_ldx(t):
                        q_ = t % 3
                        T.dma('sp', xr[q_][:], x[row0 + t * 128: row0 + (t + 1) * 128, :], writes=[('xr', q_)], key=('xr', q_))

                    def d1_mm(t):
                        sl = t % 2
                        tok = slice(t * 128, (t + 1) * 128)
                        for hb in range(2):
                            bank = sl * 2 + hb
                            for kc in range(8):
                                mm(ps[bank][:, :], mixT[:, kc, tok], wo_t[:, kc, hb * 512:(hb + 1) * 512], kc == 0, kc == 7,
                                   [('mixT', kc), ('wo',)], [('ps', bank)])

                    def d1_res(t):
                        sl = t % 2
                        q_ = t % 3
                        for hb in range(2):
                            bank = sl * 2 + hb
                            stt(rr[q_][:, hb * 512:(hb + 1) * 512], xr[q_][:, hb * 512:(hb + 1) * 512], ALPHA, ps[bank][:, :], ALU.mult, ALU.add,
                                [('xr', q_), ('ps', bank)], [('rr', q_)])

                    def d1_lna(t):
                        q_ = t % 3
                        r, stats, mv, rs = rr[q_], stats3[q_], mv3[q_], rs3[q_]
                        kr = ('rr', q_)
                        for hb in range(2):
                            T.issue('dve', lambda e: e.bn_stats(out=stats[:, hb * 6:(hb + 1) * 6], in_=r[:, hb * 512:(hb + 1) * 512]),
                                    reads=[kr], writes=[('lnst', q_)])
                        T.issue('dve', lambda e: e.bn_aggr(out=mv[:], in_=stats[:]), reads=[('lnst', q_)], writes=[('lnmv', q_)])
                        rsqrt(rs[:], mv[:, 1:2], eps1, [('lnmv', q_)], [('lnrs', q_)])
                        stt(mv[:, 1:2], mv[:, 0:1], -1.0, rs[:, 0:1], ALU.mult, ALU.mult, [('lnmv', q_), ('lnrs', q_)], [('lnmv', q_)])
                        act(r[:], r[:], AF.Identity, [kr, ('lnmv', q_), ('lnrs', q_)], [kr], bias=mv[:, 1:2], scale=rs[:, 0:1])

                    def d1_lnb(t):
                        q_ = t % 3
                        sl = t % 2
                        r = rr[q_]
                        kr = ('rr', q_)
                        tt('dve', r[:], r[:], G1, ALU.mult, [kr, ('cbcL',)], [kr])
                        tt('dve', hh[sl][:], r[:], B1, ALU.add, [kr, ('cbcL',)], [('hh', sl)])
                        T.dma('sp', hscr[t * 128:(t + 1) * 128, :], hh[sl][:], reads=[('hh', sl)], writes=[('hscr', t)], key=('hh', sl))

                    def d1_tr(t):
                        sl = t % 2
                        tok = slice(t * 128, (t + 1) * 128)
                        for kc in range(8):
                            bank = 4 + sl * 2 + kc // 4
                            col = (kc % 4) * 128
                            tr(ps[bank][:, col:col + 128], hh[sl][:, kc * 128:(kc + 1) * 128], [('hh', sl)], [('ps', bank)], inc=(kc % 4 == 3))
                        for hb in range(2):
                            bank = 4 + sl * 2 + hb
                            cp('act', xhT[:, hb * 4:(hb + 1) * 4, tok], ps[bank][:, :].rearrange("p (k c) -> p k c", k=4), [('ps', bank)], [('xhT', t)])

                    for t_ in range(3):
                        d1_ldx(t_)
                    d1_mm(0)
                    d1_res(0)
                    d1_mm(1)
                    d1_res(1)
                    d1_lna(0)
                    for t in range(NT):
                        if t + 2 < NT:
                            d1_mm(t + 2)
                        if t + 1 < NT:
                            d1_lna(t + 1)
                        d1_lnb(t)
                        if t + 2 < NT:
                            d1_res(t + 2)
                        if t + 3 < NT:
                            d1_ldx(t + 3)
                        d1_tr(t)
                    T.barrier()
            dump('hT', xhT[:], [('xhT', t) for t in range(NT)])
            if stop == 'hT':
                return True

            with ExitStack() as p2:
                wd_t = sb(p2, "wd_t", [128, NJ, 1024], BF16)
                ln2_t = sb(p2, "ln2_t", [128, 3072], F32)
                T.dma('sp', ln2_t[:], cbc[:, 2048:5120], writes=[('cbcL',)], key=('cbcL2',))
                G2 = ln2_t[:, 0:1024]
                B2 = ln2_t[:, 1024:2048]
                BG = ln2_t[:, 2048:3072]
                wg_t = sb(p2, "wg_t", [128, 8, 1024], BF16)
                wp_t = sb(p2, "wp_t", [128, 2, 1024], BF16)
                actT = sb(p2, "actT", [128, NJ, BLK2], BF16)
                wup = [sb(p2, f"wup{i}", [128, 2, 8, 128], BF16) for i in range(3)]
                rawgu = [sb(p2, f"rawgu{i}", [128, 2, 2 + BLK2], F32) for i in range(2)]
                accg = [sb(p2, f"accg{i}", [128, BLK2], F32) for i in range(2)]
                accu = [sb(p2, f"accu{i}", [128, BLK2], F32) for i in range(2)]
                halo = sb(p2, "halo", [128, 2, NJ, 2], F32)
                hr_ = [sb(p2, f"hr{i}", [128, D], F32) for i in range(2)]
                r2_ = [sb(p2, f"r2{i}", [128, D], F32) for i in range(2)]
                sg_ = [sb(p2, f"sgt{i}", [128, D], F32) for i in range(2)]
                pin_ = [sb(p2, f"pin{i}", [128, 256], F32) for i in range(2)]
                pTb_ = [sb(p2, f"pTb{i}", [128, 2, 128], BF16) for i in range(2)]
                stats_ = [sb(p2, f"stats2{i}", [128, 12], F32) for i in range(2)]
                mv_ = [sb(p2, f"mv2{i}", [128, 2], F32) for i in range(2)]
                rs_ = [sb(p2, f"rs2{i}", [128, 1], F32) for i in range(2)]
                T.issue('pool', lambda e: e.memset(halo[:], 0.0), writes=[('halo', c_) for c_ in range(NJ)])
                def ld2(t_):
                    q_ = t_ % 2
                    T.dma('sp', hr_[q_][:], hscr[t_ * 128:(t_ + 1) * 128, :], reads=[('hscr', t_)], writes=[('hr', q_)], key=('hr', q_))
                    T.dma('sp', pin_[q_][:], p[row0 + t_ * 128:row0 + (t_ + 1) * 128, :], writes=[('pin', q_)], key=('pin', q_))

                def trp(t_):
                    q_ = t_ % 2
                    for kc in range(2):
                        tr(ps[6 + q_][:, kc * 128:(kc + 1) * 128], pin_[q_][:, kc * 128:(kc + 1) * 128], [('pin', q_)], [('ps', 6 + q_)], inc=(kc == 1))
                    cp('act', pTb_[q_][:], ps[6 + q_][:, 0:256].rearrange("p (k c) -> p k c", k=2), [('ps', 6 + q_)], [('pTb', q_)])

                NB = S // BLK2
                TPB = BLK2 // 128
                wc = [0]

                def prefetch_wup(upto):
                    while wc[0] < min(upto, NB * NJ):
                        jj = wc[0] % NJ
                        s_ = wc[0] % 3
                        wc[0] += 1
                        T.dma('pool', wup[s_][:, 0, :, :], w_up_v[:, :, jj * 128:(jj + 1) * 128], writes=[('wup', s_, 0)], key=('wup', s_, 0))
                        T.dma('pool', wup[s_][:, 1, :, :], w_up_v[:, :, DFF + jj * 128:DFF + (jj + 1) * 128], writes=[('wup', s_, 1)], key=('wup', s_, 1))
                prefetch_wup(3)
                T.dma('pool', wg_t[:], w_gate_v[:, :, :], writes=[('wg',)], key=('wg',))
                T.dma('pool', wp_t[:], w_proj_v[:, :, :], writes=[('wp',)], key=('wp',))
                T.dma('pool', wd_t[:], w_down_v[:, :, :], writes=[('wd',)], key=('wd',))

                def stage_B2(j_):
                    q_ = j_ % 2
                    kg_, ku_ = ('acc2', 0, q_), ('acc2', 1, q_)
                    act(accg[q_][:], accg[q_][:], AF.Silu, [kg_], [kg_])
                    tt('dve', actT[:, j_, :], accg[q_][:], accu[q_][:], ALU.mult, [kg_, ku_], [('actT', j_)])

                for blk in range(NB):
                    bs = slice(blk * BLK2, (blk + 1) * BLK2)
                    XB = [('xhT', blk * TPB + i) for i in range(TPB)]
                    for j in range(NJ):
                        step = blk * NJ + j
                        prefetch_wup(step + 3)
                        sl = step % 3
                        jp = j % 2
                        rg = rawgu[jp]
                        kr_ = ('raw2', jp)
                        for gu in range(2):
                            bank = jp * 2 + gu
                            for kc in range(8):
                                mm(ps[bank][:, :], wup[sl][:, gu, kc, :], xhT[:, kc, bs], kc == 0, kc == 7, [('wup', sl, gu)] + XB, [('ps', bank)])
                        cp('act', rg[:, :, 0:2], halo[:, :, j, :], [('halo', j)], [kr_ + ('h',)])
                        cp('act', rg[:, :, 2:2 + BLK2], ps_all[:, jp * 1024:(jp + 1) * 1024].rearrange("p (g c) -> p g c", g=2),
                           [('ps', jp * 2), ('ps', jp * 2 + 1)], [kr_])
                        cp('act', halo[:, :, j, :], rg[:, :, BLK2:BLK2 + 2], [kr_], [('halo', j)])
                        acs = (accg[jp], accu[jp])
                        kas = (('acc2', 0, jp), ('acc2', 1, jp))
                        act(acs[0][:], ps[jp * 2][:, :], AF.Identity, [('ps', jp * 2), ('cpp',)], [kas[0]], bias=fcb[:, j:j + 1], scale=fcw[:, j, 2:3])
                        ts('dve', acs[1][:], rg[:, 1, 2:2 + BLK2], fcw[:, NJ + j, 2:3], fcb[:, NJ + j:NJ + j + 1], ALU.mult, ALU.add, [kr_, ('cpp',)], [kas[1]])
                        for tap in (1, 0):
                            for gu in range(2):
                                cidx = gu * NJ + j
                                stt(acs[gu][:], rg[:, gu, tap:tap + BLK2], fcw[:, cidx, tap:tap + 1], acs[gu][:], ALU.mult, ALU.add,
                                    [kr_, kr_ + ('h',), kas[gu], ('cpp',)], [kas[gu]])
                        if j >= 1:
                            stage_B2(j - 1)
                    stage_B2(NJ - 1)
                    AK = [('actT', j) for j in range(NJ)]
                    if stop == 'p2a' and blk == 0:
                        T.barrier()
                        return True
                    for tl in range(TPB):
                        t = blk * TPB + tl
                        tok = slice(t * 128, (t + 1) * 128)
                        ltok = slice(tl * 128, (tl + 1) * 128)
                        grow = row0 + t * 128
                        tp_ = t % 2
                        hr, r2, sg, pin, pTb = hr_[tp_], r2_[tp_], sg_[tp_], pin_[tp_], pTb_[tp_]
                        kh, kr2, kpin, kpt = ('hr', tp_), ('r2', tp_), ('pin', tp_), ('pTb', tp_)
                        if t == 0:
                            ld2(0)
                            trp(0)
                        if t + 1 < NT:
                            ld2(t + 1)
                        for hb in range(2):
                            hs = slice(hb * 512, (hb + 1) * 512)
                            ksg = ('sg', hb, tp_)
                            for kc in range(8):
                                mm(ps[2 + hb][:, :], xhT[:, kc, tok], wg_t[:, kc, hs], kc == 0, kc == 7, [('xhT', t), ('wg',)], [('ps', 2 + hb)])
                            for kc in range(2):
                                mm(ps[4 + hb][:, :], pTb[:, kc, :], wp_t[:, kc, hs], kc == 0, kc == 1, [kpt, ('wp',)], [('ps', 4 + hb)])
                            for j in range(NJ):
                                mm(ps[hb][:, :], actT[:, j, ltok], wd_t[:, j, hs], j == 0, j == NJ - 1, AK + [('wd',)], [('ps', hb)])
                            tt('dve', sg[:, hs], ps[2 + hb][:, :], BG[:, hs], ALU.add, [('ps', 2 + hb), ('cbcL',)], [ksg])
                            act(sg[:, hs], sg[:, hs], AF.Sigmoid, [ksg], [ksg])
                            tt('dve', sg[:, hs], sg[:, hs], ps[4 + hb][:, :], ALU.mult, [ksg, ('ps', 4 + hb)], [ksg])
                            stt(r2[:, hs], hr[:, hs], ALPHA, ps[hb][:, :], ALU.mult, ALU.add, [kh, ('ps', hb)], [kr2])
                            tt('dve', r2[:, hs], r2[:, hs], sg[:, hs], ALU.add, [kr2, ksg], [kr2])
                        if t + 1 < NT:
                            trp(t + 1)
                        ln_tile(r2, G2, B2, (stats_[tp_], mv_[tp_], rs_[tp_]), kr2, r2[:], kr2, sfx=tp_)
                        T.dma('sp', out[grow:grow + 128, :], r2[:], reads=[kr2], writes=[('out', sq, t)], key=('r2o', tp_))
                    if stop == 'p2b' and blk == 0:
                        T.barrier()
                        return True
                T.barrier()
          for sq in range(NSEQ):
            if seq_body(sq):
                break
        T.final_wait()
    nc._trk_log = T.log
    return nc


def _prep_common(inp):
    f = np.float32
    g = lambda k: np.asarray(inp[k], dtype=f)[0]
    cw = g("gdn_conv_w")
    fw = g("ffn_conv_w")
    fb = g("ffn_conv_b")
    cpp = np.zeros((128, 512), f)
    cpp[:, 0:48] = cw.reshape(4, 12, 128).transpose(2, 1, 0).reshape(128, 48)
    cpp[:, 48:180] = fw.reshape(3, 44, 128).transpose(2, 1, 0).reshape(128, 132)
    cpp[:, 180:224] = fb.reshape(44, 128).T
    cpp[:, 224] = g("gdn_norm_g")
    cpp[:, 225:228] = g("mla_q_norm_g").reshape(3, 128).T
    cpp[:, 228:230] = g("mla_kv_norm_g").reshape(2, 128).T
    cbc = np.zeros((128, 5 * 1024 + 128), f)
    for i, k in enumerate(["ln1_g", "ln1_b", "ln2_g", "ln2_b", "ple_b_gate"]):
        cbc[:, i * 1024:(i + 1) * 1024] = g(k)[None, :]
    cbc[:, 5120:5184] = np.tile(g("gdn_a_log"), 16)[None, :]
    cbc[:, 5184:5248] = np.tile(g("gdn_dt_bias"), 16)[None, :]
    j = np.arange(128)[:, None]
    i = np.arange(128)[None, :]
    cmat = np.zeros((128, 1792), f)
    cmat[:, 0:128] = np.eye(128, dtype=f)
    cmat[:, 128:256] = (j <= i).astype(f)
    for q_ in range(4):
        cmat[:, 256 + q_ * 128:384 + q_ * 128] = np.where(j <= i, 0.0, -30000.0).astype(f)
        cmat[:, 768 + q_ * 128:896 + q_ * 128] = np.where(j < i, 0.0, -30000.0).astype(f)
        cmat[:, 1280 + q_ * 128:1408 + q_ * 128] = np.eye(128, dtype=f)
    inv = (np.float32(10000.0) ** (-(np.arange(0, 64, 2, dtype=f)) / np.float32(64))).astype(f)
    ang = (np.arange(S, dtype=f)[:, None] * inv[None, :]).astype(f)
    cos = np.cos(ang.astype(np.float64)).astype(f).T
    sin = np.sin(ang.astype(np.float64)).astype(f).T
    ctab = np.zeros((64, 2 * S), f)
    ctab[0:32, 0:S] = cos
    ctab[32:64, 0:S] = cos
    ctab[0:32, S:] = sin
    ctab[32:64, S:] = sin
    return {
        "w_in": np.ascontiguousarray(g("w_in")), "wq_up": np.ascontiguousarray(g("mla_w_q_up")),
        "wkv_up": np.ascontiguousarray(g("mla_w_kv_up")), "w_out": np.ascontiguousarray(g("w_out")),
        "w_up": np.ascontiguousarray(g("ffn_w_up")), "w_down": np.ascontiguousarray(g("ffn_w_down")),
        "w_gate": np.ascontiguousarray(g("ple_w_gate")), "w_proj": np.ascontiguousarray(g("ple_w_proj")),
        "cpp": cpp, "cbc": cbc, "cmat": cmat, "ctab": ctab,
    }


def kernel(**inputs):
    common = _prep_common(inputs)
    x = np.asarray(inputs["x"], dtype=np.float32)
    p = np.asarray(inputs["p"], dtype=np.float32)[0]
    B = x.shape[0]
    nseq = B // NCORES
    nc = build(nseq)
    in_maps = []
    for c in range(NCORES):
        m = dict(common)
        m["x"] = np.ascontiguousarray(x[c * nseq:(c + 1) * nseq].reshape(nseq * S, D))
        m["p"] = np.ascontiguousarray(p[c * nseq:(c + 1) * nseq].reshape(nseq * S, 256))
        in_maps.append(m)
    res = run_bass_kernel_spmd(nc, in_maps, core_ids=list(range(NCORES)))
    outs = [np.asarray(r["out"]).reshape(nseq, S, D) for r in res.results]
    return np.concatenate(outs, axis=0).astype(np.float32)
```

```python
import numpy as np
from contextlib import ExitStack
import concourse.bass as bass
import concourse.mybir as mybir
from concourse.bass_utils import run_bass_kernel_spmd

F32, BF16 = mybir.dt.float32, mybir.dt.bfloat16
AF = mybir.ActivationFunctionType
ALU = mybir.AluOpType

S = 2048
NT = 16
D = 1024
KC = 8
DFF = 2816
NJ = 22
ALPHA = float(2.0 ** 0.25)
EPS = 1e-6
BLK2 = 512
HS = 1024
NBLK = 2
NTH = 8
EPOCH = 16000
NCORES = 8
TWO_CHAINS = True
OVERLAP_A = False
SEQ_GENS = True


class Trk:
    def __init__(self, nc, es):
        self.nc, self.es = nc, es
        self.engs = {'pe': nc.tensor, 'act': nc.scalar, 'dve': nc.vector, 'pool': nc.gpsimd, 'sp': nc.sync}
        self.cnt = {e: 0 for e in self.engs}
        self.esems = {e: [] for e in self.engs}
        self.seen = {e: {} for e in self.engs}
        self.lastw = {}
        self.rd = {}
        self.dsem = {}
        self.pend = {e: ([], []) for e in self.engs}
        self.latest = {}
        self.log = {e: [] for e in self.engs}

    def newsem(self, name):
        return self.es.enter_context(self.nc.semaphore(name))

    def _wait(self, e, ev):
        sem, val, src = ev
        if src == 'pe' and e == 'pe':
            return
        k = id(sem)
        if self.seen[e].get(k, 0) >= val:
            return
        self.engs[e].wait_ge(sem, val)
        self.log[e].append(('w', id(sem), val))
        self.seen[e][k] = val

    def _deps(self, e, reads, writes):
        for k in reads:
            ev = self.lastw.get(k)
            if ev is not None:
                self._wait(e, ev)
        for k in writes:
            ev = self.lastw.get(k)
            if ev is not None:
                self._wait(e, ev)
            for ev in self.rd.get(k, {}).values():
                self._wait(e, ev)

    def _reg(self, ev, reads, writes):
        sem, val, src = ev
        self.latest[id(sem)] = (sem, val)
        for k in writes:
            self.lastw[k] = ev
            self.rd[k] = {}
        for k in reads:
            self.rd.setdefault(k, {})[id(sem)] = ev

    def issue(self, e, fn, reads=(), writes=(), inc=True):
        writes = list(writes) + [k for k in reads if k[0] == 'ps' and k not in writes]
        reads = [k for k in reads if k[0] != 'ps']
        self._deps(e, reads, writes)
        ins = fn(self.engs[e])
        pr, pw = self.pend[e]
        pr.extend(reads)
        pw.extend(writes)
        if inc:
            n = self.cnt[e]
            ep, off = divmod(n, EPOCH)
            if ep >= len(self.esems[e]):
                self.esems[e].append(self.newsem(f"s_{e}_{ep}"))
            sem = self.esems[e][ep]
            ins.then_inc(sem, 1)
            self.log[e].append(('i', id(sem), 1))
            self.cnt[e] = n + 1
            self._reg((sem, off + 1, e), pr, pw)
            self.pend[e] = ([], [])
        return ins

    def dma(self, q, out, in_, reads=(), writes=(), key=None):
        self._deps(q, reads, writes)
        ins = self.engs[q].dma_start(out=out, in_=in_)
        if key not in self.dsem:
            self.dsem[key] = [self.newsem("d_" + "_".join(str(x) for x in key)), 0]
        d = self.dsem[key]
        d[1] += 16
        ins.then_inc(d[0], 16)
        self.log[q].append(('i', id(d[0]), 16))
        self._reg((d[0], d[1], 'dma'), list(reads), list(writes))
        return ins

    def barrier(self):
        for e in self.engs:
            assert not self.pend[e][0] and not self.pend[e][1], e
        for sem, val in list(self.latest.values()):
            self._wait('sp', (sem, val, 'x'))
        n = self.cnt['sp']
        ep, off = divmod(n, EPOCH)
        if ep >= len(self.esems['sp']):
            self.esems['sp'].append(self.newsem(f"s_sp_{ep}"))
        sem = self.esems['sp'][ep]
        self.engs['sp'].sem_inc(sem, 1)
        self.log['sp'].append(('i', id(sem), 1))
        self.cnt['sp'] = n + 1
        ev = (sem, off + 1, 'sp')
        self.latest[id(sem)] = (sem, off + 1)
        self.seen['sp'][id(sem)] = off + 1
        for e in self.engs:
            if e != 'sp':
                self._wait(e, ev)
        self.lastw.clear()
        self.rd.clear()

    def final_wait(self):
        for sem, val in list(self.latest.values()):
            self._wait('sp', (sem, val, 'x'))


class _Stop(Exception):
    pass


def build(NSEQ, dbg=None, stop=None):
    dbg = dbg or {}
    nc = bass.Bass("TRN2", target_bir_lowering=False)

    def din(name, shape, dt=F32):
        return nc.dram_tensor(name, list(shape), dt, kind="ExternalInput").ap()

    x = din("x", [NSEQ * S, D])
    p = din("p", [NSEQ * S, 256])
    w_in = din("w_in", [D, 2760])
    wq_up = din("wq_up", [384, 768])
    wkv_up = din("wkv_up", [256, 1024])
    w_out = din("w_out", [1024, 1024])
    w_up = din("w_up", [D, 2 * DFF])
    w_down = din("w_down", [DFF, D])
    w_gate = din("w_gate", [D, D])
    w_proj = din("w_proj", [256, D])
    cpp = din("cpp", [128, 512])
    cbc = din("cbc", [128, 5 * 1024 + 128])
    cmat = din("cmat", [128, 14 * 128])
    ctab = din("ctab", [64, 2 * S])
    out = nc.dram_tensor("out", [NSEQ * S, D], F32, kind="ExternalOutput").ap()
    hscr = nc.dram_tensor("hscr", [S, D], F32).ap()
    dbg_t = {k: nc.dram_tensor("dbg_" + k, list(v[0]), v[1], kind="ExternalOutput").ap() for k, v in dbg.items()}

    w_in_v = w_in.rearrange("(kc p) n -> p kc n", p=128)
    wq_v = wq_up.rearrange("(kc p) n -> p kc n", p=128)
    wkv_v = wkv_up.rearrange("(kc p) n -> p kc n", p=128)
    w_out_v = w_out.rearrange("(kc p) n -> p kc n", p=128)
    w_up_v = w_up.rearrange("(kc p) n -> p kc n", p=128)
    w_down_v = w_down.rearrange("(kc p) n -> p kc n", p=128)
    w_gate_v = w_gate.rearrange("(kc p) n -> p kc n", p=128)
    w_proj_v = w_proj.rearrange("(kc p) n -> p kc n", p=128)

    with ExitStack() as es:
        T = Trk(nc, es)

        uid = [0]

        def sb(scope, name, shape, dt):
            uid[0] += 1
            return scope.enter_context(nc.sbuf_tensor(f"{name}_{uid[0]}", list(shape), dt))

        ps_all = es.enter_context(nc.psum_tensor("ps_all", [128, 8 * 512], F32))
        ps = [ps_all[:, b * 512:(b + 1) * 512] for b in range(8)]

        cpp_t = sb(es, "cpp_t", [128, 512], F32)
        cbc_t = sb(es, "cbc_t", [128, 128], F32)
        cmat_t = sb(es, "cmat_t", [128, 1792], F32)
        identb = sb(es, "identb", [128, 128], BF16)
        onesb = sb(es, "onesb", [128, 128], BF16)
        c128b = sb(es, "c128b", [128, 128], BF16)
        c256b = sb(es, "c256b", [128, 128], BF16)
        onesf = sb(es, "onesf", [128, 128], F32)
        w_ab = sb(es, "w_ab", [128, 8, 8], BF16)
        xhT = sb(es, "xhT", [128, 8, S], BF16)

        T.dma('sp', cpp_t[:], cpp[:, :], writes=[('cpp',)], key=('cpp',))
        T.dma('sp', cbc_t[:], cbc[:, 5120:5248], writes=[('cbc',)], key=('cbc',))
        T.dma('sp', cmat_t[:], cmat[:, :], writes=[('cmat',)], key=('cmat',))
        T.dma('pool', w_ab[:], w_in_v[:, :, 2048:2056], writes=[('w_ab',)], key=('w_ab',))
        ident = cmat_t[:, 0:128]
        Umat = cmat_t[:, 128:256]
        NEGM = cmat_t[:, 256:384]
        NEGM4 = cmat_t[:, 256:768].rearrange("p (i c) -> p i c", i=4)
        NEGMS4 = cmat_t[:, 768:1280].rearrange("p (i c) -> p i c", i=4)
        ident4 = cmat_t[:, 1280:1792].rearrange("p (i c) -> p i c", i=4)
        T.issue('dve', lambda e: e.tensor_copy(out=identb[:], in_=ident), reads=[('cmat',)], writes=[('identb',)])
        T.issue('pool', lambda e: e.memset(onesb[:], 1.0), writes=[('onesb',)])
        T.issue('pool', lambda e: e.memset(c128b[:], 1.0 / 128), writes=[('c128b',)])
        T.issue('pool', lambda e: e.memset(c256b[:], 1.0 / 256), writes=[('c256b',)])
        T.issue('pool', lambda e: e.memset(onesf[:], 1.0), writes=[('onesf',)])
        epst = sb(es, "epst", [128, 2], F32)
        T.issue('pool', lambda e: e.memset(epst[:, 0:1], EPS), writes=[('epst',)])
        T.issue('pool', lambda e: e.memset(epst[:, 1:2], 384 * EPS), writes=[('epst',)])
        eps1 = epst[:, 0:1]
        eps384 = epst[:, 1:2]
        gcw = cpp_t[:, 0:48].rearrange("p (c j) -> p c j", j=4)
        fcw = cpp_t[:, 48:180].rearrange("p (c j) -> p c j", j=3)
        fcb = cpp_t[:, 180:224]
        normg = cpp_t[:, 224:225]
        qg = cpp_t[:, 225:228]
        kvg = cpp_t[:, 228:230]
        ALOGB = cbc_t[:, 0:64]
        DTBB = cbc_t[:, 64:128]
        CONST = [('cpp',), ('cbc',), ('cmat',)]

        def act(out_, in_, func, reads, writes, **kw):
            return T.issue('act', lambda e: e.activation(out=out_, in_=in_, func=func, **kw), reads, writes)

        def tt(eng, out_, in0, in1, op, reads, writes):
            return T.issue(eng, lambda e: e.tensor_tensor(out=out_, in0=in0, in1=in1, op=op), reads, writes)

        def ts(eng, out_, in0, s1, s2, op0, op1, reads, writes):
            if s2 is None:
                return T.issue(eng, lambda e: e.tensor_scalar(out=out_, in0=in0, scalar1=s1, scalar2=None, op0=op0), reads, writes)
            return T.issue(eng, lambda e: e.tensor_scalar(out=out_, in0=in0, scalar1=s1, scalar2=s2, op0=op0, op1=op1), reads, writes)

        def stt(out_, in0, sc, in1, op0, op1, reads, writes):
            return T.issue('dve', lambda e: e.scalar_tensor_tensor(out=out_, in0=in0, scalar=sc, in1=in1, op0=op0, op1=op1), reads, writes)

        def cp(eng, out_, in_, reads, writes):
            if eng == 'act':
                return T.issue('act', lambda e: e.copy(out=out_, in_=in_), reads, writes)
            return T.issue(eng, lambda e: e.tensor_copy(out=out_, in_=in_), reads, writes)

        def rsqrt(out_, in_, eps_ap, reads, writes):
            act(out_, in_, AF.Ln, list(reads) + [('epst',)], writes, bias=eps_ap)
            act(out_, out_, AF.Exp, writes, writes, scale=-0.5)

        def amul(out_, in_, m, reads, writes):
            return T.issue('act', lambda e: e.mul(out=out_, in_=in_, mul=m), reads, writes)

        def mm(out_, lhsT, rhs, start, stop, reads, writes, inc=None):
            if inc is None:
                inc = stop
            return T.issue('pe', lambda e: e.matmul(out_, lhsT, rhs, start=start, stop=stop), reads, writes, inc=inc)

        def tr(out_, in_, reads, writes, inc=True):
            return T.issue('pe', lambda e: e.transpose(out_, in_, ident), list(reads) + [('cmat',)], writes, inc=inc)

        def dump(name, src, reads):
            if name in dbg_t:
                T.dma('sp', dbg_t[name], src, reads=reads, writes=[('dbg', name)], key=('dbg', name))

        def ln_tile(r, G, B, scope_tiles, key_r, out_tile, key_out, kc_=('cbcL',), sfx=''):
            stats, mv, rs = scope_tiles
            for hb in range(2):
                T.issue('dve', lambda e: e.bn_stats(out=stats[:, hb * 6:(hb + 1) * 6], in_=r[:, hb * 512:(hb + 1) * 512]),
                        reads=[key_r], writes=[('lnst', sfx)])
            T.issue('dve', lambda e: e.bn_aggr(out=mv[:], in_=stats[:]), reads=[('lnst', sfx)], writes=[('lnmv', sfx)])
            rsqrt(rs[:], mv[:, 1:2], eps1, [('lnmv', sfx)], [('lnrs', sfx)])
            ts('dve', r[:], r[:], mv[:, 0:1], rs[:, 0:1], ALU.subtract, ALU.mult, [key_r, ('lnmv', sfx), ('lnrs', sfx)], [key_r])
            tt('dve', r[:], r[:], G, ALU.mult, [key_r, kc_], [key_r])
            tt('dve', out_tile, r[:], B, ALU.add, [key_r, kc_], [key_out])

        if True:
          def seq_body(sq):
            row0 = sq * S
            with ExitStack() as p1:
                mixT = sb(p1, "mixT", [128, 8, S], BF16)
                with ExitStack() as sc:
                    xin = [sb(sc, f"xin{i}", [128, D], F32) for i in range(2)]
                    for t in range(NT):
                        sl = t % 2
                        T.dma('sp', xin[sl][:], x[row0 + t * 128: row0 + (t + 1) * 128, :], writes=[('xin', sl)], key=('xin', sl))
                        pb = (t % 2) * 2
                        for kc in range(8):
                            bank = pb + kc // 4
                            col = (kc % 4) * 128
                            tr(ps[bank][:, col:col + 128], xin[sl][:, kc * 128:(kc + 1) * 128], [('xin', sl)], [('ps', bank)], inc=(kc % 4 == 3))
                        for hb in range(2):
                            bank = pb + hb
                            cp('act' if hb == 0 else 'dve', xhT[:, hb * 4:(hb + 1) * 4, t * 128:(t + 1) * 128],
                               ps[bank][:, :].rearrange("p (k c) -> p k c", k=4), [('ps', bank)], [('xhT', t)])
                    T.barrier()
                dump('xT', xhT[:], [('xhT', t) for t in range(NT)])
                if stop == 'xT':
                    return True

                with ExitStack() as sc:
                    g_ab = sb(sc, "g_ab", [128, 128], F32)
                    g_beta = sb(sc, "g_beta", [128, 64], F32)
                    g_g = sb(sc, "g_g", [128, 64], F32)
                    g_tmp = sb(sc, "g_tmp", [128, 64], F32)
                    g_eal = sb(sc, "g_eal", [128, 64], F32)
                    g_gc = sb(sc, "g_gc", [128, 64], F32)
                    g_ngc = sb(sc, "g_ngc", [128, 64], F32)
                    g_eg = sb(sc, "g_eg", [128, 64], F32)
                    g_egl = sb(sc, "g_egl", [128, 64], F32)
                    g_egla = sb(sc, "g_egla", [128, 64], F32)
                    raw2 = [sb(sc, f"raw{i}", [128, 3 + HS], F32) for i in range(2)]
                    acc = sb(sc, "acc", [128, HS], F32)
                    sil2 = [sb(sc, f"sil{i}", [128, HS], F32) for i in range(2)]
                    sqb = sb(sc, "sqb", [128, HS], BF16)
                    rstd = [sb(sc, f"rstd{i}", [128, 512], F32) for i in range(2)]
                    halo_g = sb(sc, "halo_g", [128, 12, 3], F32)
                    zero3 = sb(sc, "zero3", [128, 3], F32)
                    wst = [sb(sc, f"wst{i}", [128, 8, 128], BF16) for i in range(3)]
                    hq = sb(sc, "hq", [128, 4, HS], BF16)
                    hk = sb(sc, "hk", [128, 4, HS], BF16)
                    hkg = sb(sc, "hkg", [128, 4, NTH, 128], BF16)
                    hkd = sb(sc, "hkd", [128, 4, NTH, 128], BF16)
                    hv = sb(sc, "hv", [128, 4, NTH, 128], BF16)
                    hz = sb(sc, "hz", [128, 4, HS], BF16)
                    S32 = sb(sc, "S32", [128, 4, 128], F32)
                    Sbf = sb(sc, "Sbf", [128, 4, 128], BF16)

                    def tmpp(name, dt):
                        return sb(sc, name, [128, 4, 128], dt)
                    Ug = tmpp("Ug", F32)
                    EGb = tmpp("EGb", F32)
                    ARG = tmpp("ARG", F32)
                    ARG2 = tmpp("ARG2", F32)
                    DT = tmpp("DT", F32)
                    DTs = tmpp("DTs", F32)
                    Nf = tmpp("Nf", F32)
                    Pb = [tmpp("Pb0_", BF16), tmpp("Pb1_", BF16)]
                    PTb = [tmpp("PTb0_", BF16), tmpp("PTb1_", BF16)]
                    Xb = [tmpp("Xb0_", BF16), tmpp("Xb1_", BF16)]
                    QKD = tmpp("QKD", BF16)
                    nw2T = tmpp("nw2T", BF16)
                    vnew = tmpp("vnew", BF16)
                    qgT = tmpp("qgT", BF16)
                    sqo = tmpp("sqo", BF16)
                    rso = tmpp("rso", F32)
                    o1 = tmpp("o1", F32)

                    T.issue('pool', lambda e: e.memset(zero3[:], 0.0), writes=[('zero3',)])
                    cur_half = [0]

                    for t in range(NT):
                        for kc in range(8):
                            mm(ps[7][:, t * 8:(t + 1) * 8], xhT[:, kc, t * 128:(t + 1) * 128], w_ab[:, kc, :], kc == 0, kc == 7,
                               [('xhT', t), ('w_ab',)], [('ps', 7)], inc=(kc == 7 and t == NT - 1))
                    cp('dve', g_ab[:], ps[7][:, 0:128], [('ps', 7)], [('g_ab',)])
                    abv = g_ab[:].rearrange("p (t c) -> p t c", c=8)
                    v64 = lambda tl: tl[:].rearrange("p (t c) -> p t c", c=4)
                    act(v64(g_beta), abv[:, :, 4:8], AF.Sigmoid, [('g_ab',)], [('g_beta',)])
                    tt('dve', v64(g_tmp), abv[:, :, 0:4], DTBB.rearrange("p (t c) -> p t c", c=4), ALU.add, [('g_ab',), ('cbc',)], [('g_tmp',)])
                    act(g_tmp[:], g_tmp[:], AF.Exp, [('g_tmp',)], [('g_tmp',)])
                    ts('dve', g_tmp[:], g_tmp[:], 1.0, None, ALU.add, None, [('g_tmp',)], [('g_tmp',)])
                    act(g_tmp[:], g_tmp[:], AF.Ln, [('g_tmp',)], [('g_tmp',)])
                    act(g_eal[:], ALOGB, AF.Exp, [('cbc',)], [('g_eal',)])
                    stt(g_g[:], g_tmp[:], -1.0, g_eal[:], ALU.mult, ALU.mult, [('g_tmp',), ('g_eal',)], [('g_g',)])
                    mm(ps[7][:, 128:192], Umat, g_g[:], True, True, [('cmat',), ('g_g',)], [('ps', 7)])
                    mm(ps[7][:, 192:256], onesf[:], g_g[:], True, True, [('onesf',), ('g_g',)], [('ps', 7)])
                    cp('dve', g_gc[:], ps[7][:, 128:192], [('ps', 7)], [('g_gc',)])
                    ts('dve', g_ngc[:], g_gc[:], -1.0, None, ALU.mult, None, [('g_gc',)], [('g_ngc',)])
                    act(g_eg[:], g_gc[:], AF.Exp, [('g_gc',)], [('g_eg',)])
                    tt('dve', g_egl[:], ps[7][:, 192:256], g_gc[:], ALU.subtract, [('ps', 7), ('g_gc',)], [('g_egl',)])
                    act(g_egl[:], g_egl[:], AF.Exp, [('g_egl',)], [('g_egl',)])
                    act(g_egla[:], ps[7][:, 192:256], AF.Exp, [('ps', 7)], [('g_egla',)])
                    GS = [('g_beta',), ('g_gc',), ('g_ngc',), ('g_eg',), ('g_egl',), ('g_egla',), ('g_g',)]

                    wcnt = [0]

                    def load_wchunk(c0):
                        sl = wcnt[0] % 3
                        wcnt[0] += 1
                        T.dma('pool', wst[sl][:], w_in_v[:, :, c0:c0 + 128], writes=[('wst', sl)], key=('wst', sl))
                        return sl

                    def proj_block(sl, tb, bank):
                        gtb = cur_half[0] * NBLK + tb
                        for kc in range(8):
                            mm(ps[bank][:, :], wst[sl][:, kc, :], xhT[:, kc, gtb * 512:(gtb + 1) * 512], kc == 0, kc == 7,
                               [('wst', sl)] + [('xhT', gtb * 4 + i) for i in range(4)], [('ps', bank)])

                    trc = [0]

                    def stage_P(ch, tb):
                        kind, h, cidx, sl, ci = ch
                        par = h
                        rp = ci % 2
                        raw = raw2[rp]
                        bank = 6 + tb % 2
                        cs = slice(tb * 512, (tb + 1) * 512)
                        proj_block(sl, tb, bank)
                        if kind == 'z':
                            act(hz[:, par, cs], ps[bank][:, :], AF.Silu, [('ps', bank)], [('hz', par)])
                            return
                        if tb == 0:
                            if cur_half[0] == 0:
                                cp('act', raw[:, 0:3], zero3[:], [('zero3',)], [('raw', rp, -1)])
                            else:
                                cp('act', raw[:, 0:3], halo_g[:, cidx, :], [('halo_g', cidx)], [('raw', rp, -1)])
                        cp('act', raw[:, 3 + tb * 512: 3 + (tb + 1) * 512], ps[bank][:, :], [('ps', bank)], [('raw', rp, tb)])
                        if tb == NBLK - 1 and cur_half[0] == 0:
                            cp('act', halo_g[:, cidx, :], raw[:, HS:HS + 3], [('raw', rp, tb)], [('halo_g', cidx)])

                    def stage_C(ch, tb):
                        kind, h, cidx, sl, ci = ch
                        if kind == 'z':
                            return
                        rp = ci % 2
                        raw = raw2[rp]
                        cs = slice(tb * 512, (tb + 1) * 512)
                        RK = [('raw', rp, tb), ('raw', rp, tb - 1), ('cpp',)]
                        ka = ('acc', tb)
                        ts('dve', acc[:, cs], raw[:, 3 + tb * 512: 3 + (tb + 1) * 512], gcw[:, cidx, 3:4], None, ALU.mult, None, RK, [ka])
                        for j in (2, 1, 0):
                            stt(acc[:, cs], raw[:, j + tb * 512: j + (tb + 1) * 512], gcw[:, cidx, j:j + 1], acc[:, cs], ALU.mult, ALU.add,
                                RK + [ka], [ka])

                    def stage_S(ch, tb):
                        kind, h, cidx, sl, ci = ch
                        if kind == 'z':
                            return
                        sp_ = ci % 2
                        cs = slice(tb * 512, (tb + 1) * 512)
                        act(sil2[sp_][:, cs], acc[:, cs], AF.Silu, [('acc', tb)], [('sil', sp_, tb)])

                    def stage_N(ch):
                        kind, h, cidx, sl, ci = ch
                        if kind not in ('q', 'k'):
                            return
                        par = h
                        sp_ = ci % 2
                        sl_ = sil2[sp_]
                        for tb in range(NBLK):
                            cs = slice(tb * 512, (tb + 1) * 512)
                            tt('pool', sqb[:, cs], sl_[:, cs], sl_[:, cs], ALU.mult, [('sil', sp_, tb)], [('sqb', tb)])
                            mm(ps[2 + tb][:, :], onesb[:], sqb[:, cs], True, True, [('onesb',), ('sqb', tb)], [('ps', 2 + tb)])
                        for tb in range(NBLK):
                            act(rstd[tb][:], ps[2 + tb][:, :], AF.Ln, [('ps', 2 + tb), ('epst',)], [('rstd', tb)], bias=eps1)
                        for tb in range(NBLK):
                            act(rstd[tb][:], rstd[tb][:], AF.Exp, [('rstd', tb)], [('rstd', tb)], scale=-0.5)
                        for tb in range(NBLK):
                            cs = slice(tb * 512, (tb + 1) * 512)
                            ks = ('sil', sp_, tb)
                            if kind == 'q':
                                stt(hq[:, par, cs], sl_[:, cs], float(128 ** -0.5), rstd[tb][:], ALU.mult, ALU.mult,
                                    [ks, ('rstd', tb)], [('hq', par)])
                            else:
                                tt('dve', sl_[:, cs], sl_[:, cs], rstd[tb][:], ALU.mult, [ks, ('rstd', tb)], [ks])
                                cp('act', hk[:, par, cs], sl_[:, cs], [ks], [('hk', par)])

                    def stage_T(ch, tb):
                        kind, h, cidx, sl, ci = ch
                        if kind not in ('k', 'v'):
                            return
                        par = h
                        sp_ = ci % 2
                        ks = ('sil', sp_, tb)
                        for tl in range(4):
                            n = tb * 4 + tl
                            c = (cur_half[0] * NTH + n) * 4 + h
                            b3 = trc[0] % 2
                            trc[0] += 1
                            tr(ps[b3][:, 0:128], sil2[sp_][:, n * 128:(n + 1) * 128], [ks], [('ps', b3)])
                            if kind == 'k':
                                amul(hkg[:, par, n, :], ps[b3][:, 0:128], g_eg[:, c:c + 1], [('ps', b3), ('g_eg',)], [('hkg', par)])
                                ts('dve', hkd[:, par, n, :], ps[b3][:, 0:128], g_egl[:, c:c + 1], None, ALU.mult, None,
                                   [('ps', b3), ('g_egl',)], [('hkd', par)])
                            else:
                                cp('act' if n % 2 else 'dve', hv[:, par, n, :], ps[b3][:, 0:128], [('ps', b3)], [('hv', par)])

                    def run_A_quad():
                        chs = []
                        for h in range(4):
                            for kind, c0, cidx in (('q', h * 128, h), ('k', 512 + h * 128, 4 + h), ('v', 1024 + h * 128, 8 + h), ('z', 1536 + h * 128, 0)):
                                chs.append([kind, h, cidx, None, len(chs), c0])
                        nch = len(chs)
                        loaded = [0]

                        def ensure_loaded(upto):
                            while loaded[0] <= min(upto, nch - 1):
                                ch_ = chs[loaded[0]]
                                ch_[3] = load_wchunk(ch_[5])
                                loaded[0] += 1
                        nit = NBLK * nch
                        for tau in range(nit + NBLK + 5):
                            if tau < nit:
                                i, tb = divmod(tau, NBLK)
                                if tb == 0:
                                    ensure_loaded(i + 1)
                                stage_P(tuple(chs[i][:5]), tb)
                            if 0 <= tau - 1 < nit:
                                i, tb = divmod(tau - 1, NBLK)
                                stage_C(tuple(chs[i][:5]), tb)
                            if 0 <= tau - 2 < nit:
                                i, tb = divmod(tau - 2, NBLK)
                                stage_S(tuple(chs[i][:5]), tb)
                            tn = tau - (NBLK + 2)
                            if tn >= 0 and tn % NBLK == 0 and tn // NBLK < nch:
                                stage_N(tuple(chs[tn // NBLK][:5]))
                            if 0 <= tau - (NBLK + 3) < nit:
                                i, tb = divmod(tau - (NBLK + 3), NBLK)
                                stage_T(tuple(chs[i][:5]), tb)

                    def H4(b):
                        return ps[b][:, :].rearrange("p (i c) -> p i c", i=4), ('ps', b)

                    def rec_quad(half):
                        if half == 0:
                            T.issue('pool', lambda e: e.memset(S32[:], 0.0), writes=[('S32',)])
                            T.issue('pool', lambda e: e.memset(Sbf[:], 0.0), writes=[('Sbf',)])
                        pGb, kGb = H4(0)
                        pX, kX = H4(6)
                        pKK, kKK = H4(1)
                        pV, kV = H4(1)
                        pQK, kQK = H4(2)
                        pO, kO = H4(2)
                        pNT, kNT = H4(3)
                        pS, kS = H4(3)
                        pP, kP = H4(4)
                        pR, kR = H4(4)
                        pPT, kPT = H4(5)
                        pW, kW = H4(0)
                        for n in range(NTH):
                            tok = slice(n * 128, (n + 1) * 128)
                            gtok = slice((half * NTH + n) * 128, (half * NTH + n + 1) * 128)
                            cc = [(half * NTH + n) * 4 + i for i in range(4)]
                            for i in range(4):
                                amul(Ug[:, i, :], Umat, g_g[:, cc[i]:cc[i] + 1], [('cmat',), ('g_g',)], [('Ug',)])
                            for i in range(4):
                                mm(pGb[:, i, :], onesf[:], Ug[:, i, :], True, True, [('onesf',), ('Ug',)], [kGb], inc=(i == 3))
                            tt('dve', ARG2[:], pGb, NEGMS4, ALU.add, [kGb, ('cmat',)], [('ARG2',)])
                            tt('dve', ARG[:], pGb, NEGM4, ALU.add, [kGb, ('cmat',)], [('ARG',)])
                            act(EGb[:], pGb, AF.Exp, [kGb], [('EGb',)])
                            for i in range(4):
                                mm(pKK[:, i, :], hk[:, i, tok], hk[:, i, tok], True, True, [('hk', i)], [kKK], inc=(i == 3))
                            for i in range(4):
                                mm(pQK[:, i, :], hk[:, i, tok], hq[:, i, tok], True, True, [('hk', i), ('hq', i)], [kQK], inc=(i == 3))
                            for i in range(4):
                                act(DTs[:, i, :], ARG2[:, i, :], AF.Exp, [('ARG2',), ('g_ngc',)], [('DTs',)], bias=g_ngc[:, cc[i]:cc[i] + 1])
                            for i in range(4):
                                act(DT[:, i, :], ARG[:, i, :], AF.Exp, [('ARG',), ('g_ngc',)], [('DT',)], bias=g_ngc[:, cc[i]:cc[i] + 1])
                            for i in range(4):
                                stt(Nf[:, i, :], pKK[:, i, :], g_beta[:, cc[i]:cc[i] + 1], DTs[:, i, :], ALU.mult, ALU.mult,
                                    [kKK, ('g_beta',), ('DTs',)], [('Nf',)])
                            for i in range(4):
                                tr(pNT[:, i, :], Nf[:, i, :], [('Nf',)], [kNT], inc=(i == 3))
                            cur = 0
                            cp('act', Pb[cur][:], Nf[:], [('Nf',)], [('Pb0',)])
                            cp('dve', PTb[cur][:], pNT, [kNT], [('PTb0',)])
                            tt('dve', Xb[cur][:], ident4, Nf[:], ALU.subtract, [('cmat',), ('Nf',)], [('Xb0',)])
                            tt('dve', QKD[:], pQK, DT[:], ALU.mult, [kQK, ('DT',)], [('QKD',)])
                            tt('pool', qgT[:], hq[:, :, tok], EGb[:], ALU.mult, [('hq', i_) for i_ in range(4)] + [('EGb',)], [('qgT',)])
                            xc = 0

                            def x_update(ptb_idx, step_):
                                nonlocal_xc = x_state[0]
                                xn = 1 - nonlocal_xc
                                for i in range(4):
                                    mm(pX[:, i, :], identb[:], Xb[nonlocal_xc][:, i, :], True, False, [('identb',), (f'Xb{nonlocal_xc}',)], [kX], inc=False)
                                    mm(pX[:, i, :], PTb[ptb_idx][:, i, :], Xb[nonlocal_xc][:, i, :], False, True,
                                       [(f'PTb{ptb_idx}',), (f'Xb{nonlocal_xc}',)], [kX], inc=(i == 3))
                                x_state[0] = xn
                                return xn
                            x_state = [0]
                            pend = None
                            for step in range(6):
                                nx = 1 - cur
                                kPc, kPTc = (f'Pb{cur}',), (f'PTb{cur}',)
                                kPn, kPTn = (f'Pb{nx}',), (f'PTb{nx}',)
                                for i in range(4):
                                    mm(pPT[:, i, :], Pb[cur][:, i, :], PTb[cur][:, i, :], True, True, [kPc, kPTc], [kPT], inc=(i == 3))
                                if step < 5:
                                    for i in range(4):
                                        mm(pP[:, i, :], PTb[cur][:, i, :], Pb[cur][:, i, :], True, True, [kPc, kPTc], [kP], inc=(i == 3))
                                if pend is not None:
                                    xn = x_update(pend, step)
                                cp('dve', PTb[nx][:], pPT, [kPT], [kPTn])
                                if step < 5:
                                    cp('act', Pb[nx][:], pP, [kP], [kPn])
                                if pend is not None:
                                    cp('act' if step % 2 else 'dve', Xb[xn][:], pX, [kX], [(f'Xb{xn}',)])
                                pend = nx
                                cur = nx
                            xn = x_update(pend, 6)
                            cp('act', Xb[xn][:], pX, [kX], [(f'Xb{xn}',)])
                            cur = xn
                            kT2 = (f'Xb{cur}',)
                            T2T = Xb[cur]
                            for i in range(4):
                                mm(pW[:, i, :], hkg[:, i, n, :], T2T[:, i, :], True, True, [('hkg', i), kT2], [kW], inc=(i == 3))
                            amul(nw2T[:], pW, -1.0, [kW], [('nw2T',)])
                            for i in range(4):
                                mm(pV[:, i, :], T2T[:, i, :], hv[:, i, n, :], True, False, [kT2, ('hv', i)], [kV], inc=False)
                                mm(pV[:, i, :], nw2T[:, i, :], Sbf[:, i, :], False, True, [('nw2T',), ('Sbf',)], [kV], inc=(i == 3))
                            for i in range(4):
                                ts('dve', vnew[:, i, :], pV[:, i, :], g_beta[:, cc[i]:cc[i] + 1], None, ALU.mult, None, [kV, ('g_beta',)], [('vnew',)])
                            for i in range(4):
                                mm(pS[:, i, :], hkd[:, i, n, :], vnew[:, i, :], True, True, [('hkd', i), ('vnew',)], [kS], inc=(i == 3))
                            for i in range(4):
                                mm(pO[:, i, :], Sbf[:, i, :], qgT[:, i, :], True, False, [('Sbf',), ('qgT',)], [kO], inc=False)
                                mm(pO[:, i, :], vnew[:, i, :], QKD[:, i, :], False, True, [('vnew',), ('QKD',)], [kO], inc=(i == 3))
                            for i in range(4):
                                stt(S32[:, i, :], S32[:, i, :], g_egla[:, cc[i]:cc[i] + 1], pS[:, i, :], ALU.mult, ALU.add,
                                    [('S32',), ('g_egla',), kS], [('S32',)])
                            cp('act', Sbf[:], S32[:], [('S32',)], [('Sbf',)])
                            act(sqo[:], pO, AF.Square, [kO], [('sqo',)])
                            for i in range(4):
                                mm(pR[:, i, :], c128b[:], sqo[:, i, :], True, True, [('c128b',), ('sqo',)], [kR], inc=(i == 3))
                            rsqrt(rso[:], pR, eps1, [kR], [('rso',)])
                            stt(o1[:], pO, normg, rso[:], ALU.mult, ALU.mult, [kO, ('cpp',), ('rso',)], [('o1',)])
                            tt('pool', mixT[:, 0:4, gtok], o1[:], hz[:, :, tok], ALU.mult, [('o1',)] + [('hz', i_) for i_ in range(4)],
                               [('mixT', i_) for i_ in range(4)])

                    def gen_rec_pair(half, pr):
                        ia = 2 * pr
                        ii = (ia, ia + 1)
                        sl_ = slice(ia, ia + 2)
                        B = 4 * pr

                        def HB(b, hf):
                            return ps[b][:, hf * 256:(hf + 1) * 256].rearrange("p (i c) -> p i c", i=2), ('ps', b)
                        pGb, kGb = HB(B, 0)
                        pW, kW = HB(B, 0)
                        pP, kP = HB(B, 1)
                        pR, kR = HB(B, 1)
                        pKK, kKK = HB(B + 1, 0)
                        pV, kV = HB(B + 1, 0)
                        pPT, kPT = HB(B + 1, 1)
                        pQK, kQK = HB(B + 2, 0)
                        pO, kO = HB(B + 2, 0)
                        pX, kX = HB(B + 2, 1)
                        pNT, kNT = HB(B + 3, 0)
                        pS, kS = HB(B + 3, 0)
                        K = lambda nm: (nm, pr)
                        last = ia + 1
                        for n in range(NTH):
                            tok = slice(n * 128, (n + 1) * 128)
                            gtok = slice((half * NTH + n) * 128, (half * NTH + n + 1) * 128)
                            cc = {i: (half * NTH + n) * 4 + i for i in ii}
                            for i in ii:
                                amul(Ug[:, i, :], Umat, g_g[:, cc[i]:cc[i] + 1], [('cmat',), ('g_g',)], [K('Ug')])
                            yield
                            for i in ii:
                                mm(pGb[:, i - ia, :], onesf[:], Ug[:, i, :], True, True, [('onesf',), K('Ug')], [kGb], inc=(i == last))
                            yield
                            tt('dve', ARG2[:, sl_, :], pGb, NEGMS4[:, 0:2, :], ALU.add, [kGb, ('cmat',)], [K('ARG2')])
                            tt('dve', ARG[:, sl_, :], pGb, NEGM4[:, 0:2, :], ALU.add, [kGb, ('cmat',)], [K('ARG')])
                            act(EGb[:, sl_, :], pGb, AF.Exp, [kGb], [K('EGb')])
                            for i in ii:
                                mm(pKK[:, i - ia, :], hk[:, i, tok], hk[:, i, tok], True, True, [('hk', i)], [kKK], inc=(i == last))
                            for i in ii:
                                mm(pQK[:, i - ia, :], hk[:, i, tok], hq[:, i, tok], True, True, [('hk', i), ('hq', i)], [kQK], inc=(i == last))
                            yield
                            for i in ii:
                                act(DTs[:, i, :], ARG2[:, i, :], AF.Exp, [K('ARG2'), ('g_ngc',)], [K('DTs')], bias=g_ngc[:, cc[i]:cc[i] + 1])
                            for i in ii:
                                act(DT[:, i, :], ARG[:, i, :], AF.Exp, [K('ARG'), ('g_ngc',)], [K('DT')], bias=g_ngc[:, cc[i]:cc[i] + 1])
                            yield
                            for i in ii:
                                stt(Nf[:, i, :], pKK[:, i - ia, :], g_beta[:, cc[i]:cc[i] + 1], DTs[:, i, :], ALU.mult, ALU.mult,
                                    [kKK, ('g_beta',), K('DTs')], [K('Nf')])
                            yield
                            for i in ii:
                                tr(pNT[:, i - ia, :], Nf[:, i, :], [K('Nf')], [kNT], inc=(i == last))
                            cur = 0
                            cp('act', Pb[cur][:, sl_, :], Nf[:, sl_, :], [K('Nf')], [K('Pb0')])
                            yield
                            cp('dve', PTb[cur][:, sl_, :], pNT, [kNT], [K('PTb0')])
                            tt('dve', Xb[cur][:, sl_, :], ident4[:, 0:2, :], Nf[:, sl_, :], ALU.subtract, [('cmat',), K('Nf')], [K('Xb0')])
                            tt('dve', QKD[:, sl_, :], pQK, DT[:, sl_, :], ALU.mult, [kQK, K('DT')], [K('QKD')])
                            tt('pool', qgT[:, sl_, :], hq[:, sl_, tok], EGb[:, sl_, :], ALU.mult, [('hq', i_) for i_ in ii] + [K('EGb')], [K('qgT')])
                            yield
                            xs = [0]

                            def x_update(ptb_idx):
                                xc_ = xs[0]
                                xn_ = 1 - xc_
                                for i in ii:
                                    mm(pX[:, i - ia, :], identb[:], Xb[xc_][:, i, :], True, False, [('identb',), K(f'Xb{xc_}')], [kX], inc=False)
                                    mm(pX[:, i - ia, :], PTb[ptb_idx][:, i, :], Xb[xc_][:, i, :], False, True,
                                       [K(f'PTb{ptb_idx}'), K(f'Xb{xc_}')], [kX], inc=(i == last))
                                xs[0] = xn_
                                return xn_
                            pend = None
                            for step in range(6):
                                nx = 1 - cur
                                kPc, kPTc = K(f'Pb{cur}'), K(f'PTb{cur}')
                                kPn, kPTn = K(f'Pb{nx}'), K(f'PTb{nx}')
                                for i in ii:
                                    mm(pPT[:, i - ia, :], Pb[cur][:, i, :], PTb[cur][:, i, :], True, True, [kPc, kPTc], [kPT], inc=(i == last))
                                if step < 5:
                                    for i in ii:
                                        mm(pP[:, i - ia, :], PTb[cur][:, i, :], Pb[cur][:, i, :], True, True, [kPc, kPTc], [kP], inc=(i == last))
                                if pend is not None:
                                    xn = x_update(pend)
                                yield
                                cp('dve', PTb[nx][:, sl_, :], pPT, [kPT], [kPTn])
                                if step < 5:
                                    cp('act', Pb[nx][:, sl_, :], pP, [kP], [kPn])
                                if pend is not None:
                                    cp('act' if step % 2 else 'dve', Xb[xn][:, sl_, :], pX, [kX], [K(f'Xb{xn}')])
                                yield
                                pend = nx
                                cur = nx
                            xn = x_update(pend)
                            yield
                            cp('act', Xb[xn][:, sl_, :], pX, [kX], [K(f'Xb{xn}')])
                            yield
                            cur = xn
                            kT2 = K(f'Xb{cur}')
                            T2T = Xb[cur]
                            for i in ii:
                                mm(pW[:, i - ia, :], hkg[:, i, n, :], T2T[:, i, :], True, True, [('hkg', i), kT2], [kW], inc=(i == last))
                            yield
                            amul(nw2T[:, sl_, :], pW, -1.0, [kW], [K('nw2T')])
                            yield
                            for i in ii:
                                mm(pV[:, i - ia, :], T2T[:, i, :], hv[:, i, n, :], True, False, [kT2, ('hv', i)], [kV], inc=False)
                                mm(pV[:, i - ia, :], nw2T[:, i, :], Sbf[:, i, :], False, True, [K('nw2T'), K('Sbf')], [kV], inc=(i == last))
                            yield
                            for i in ii:
                                ts('dve', vnew[:, i, :], pV[:, i - ia, :], g_beta[:, cc[i]:cc[i] + 1], None, ALU.mult, None, [kV, ('g_beta',)], [K('vnew')])
                            yield
                            for i in ii:
                                mm(pS[:, i - ia, :], hkd[:, i, n, :], vnew[:, i, :], True, True, [('hkd', i), K('vnew')], [kS], inc=(i == last))
                            for i in ii:
                                mm(pO[:, i - ia, :], Sbf[:, i, :], qgT[:, i, :], True, False, [K('Sbf'), K('qgT')], [kO], inc=False)
                                mm(pO[:, i - ia, :], vnew[:, i, :], QKD[:, i, :], False, True, [K('vnew'), K('QKD')], [kO], inc=(i == last))
                            yield
                            for i in ii:
                                stt(S32[:, i, :], S32[:, i, :], g_egla[:, cc[i]:cc[i] + 1], pS[:, i - ia, :], ALU.mult, ALU.add,
                                    [K('S32'), ('g_egla',), kS], [K('S32')])
                            act(sqo[:, sl_, :], pO, AF.Square, [kO], [K('sqo')])
                            yield
                            cp('act', Sbf[:, sl_, :], S32[:, sl_, :], [K('S32')], [K('Sbf')])
                            for i in ii:
                                mm(pR[:, i - ia, :], c128b[:], sqo[:, i, :], True, True, [('c128b',), K('sqo')], [kR], inc=(i == last))
                            yield
                            rsqrt(rso[:, sl_, :], pR, eps1, [kR], [K('rso')])
                            yield
                            stt(o1[:, sl_, :], pO, normg, rso[:, sl_, :], ALU.mult, ALU.mult, [kO, ('cpp',), K('rso')], [K('o1')])
                            yield
                            tt('pool', mixT[:, sl_, gtok], o1[:, sl_, :], hz[:, sl_, tok], ALU.mult, [K('o1')] + [('hz', i_) for i_ in ii],
                               [('mixT', i_) for i_ in ii])
                            yield

                    def rec_two_chains(half):
                        if half == 0:
                            for pr in range(2):
                                T.issue('pool', lambda e: e.memset(S32[:, 2 * pr:2 * pr + 2, :], 0.0), writes=[('S32', pr)])
                                T.issue('pool', lambda e: e.memset(Sbf[:, 2 * pr:2 * pr + 2, :], 0.0), writes=[('Sbf', pr)])
                        gens = [gen_rec_pair(half, 0), gen_rec_pair(half, 1)]
                        while gens:
                            for g_ in list(gens):
                                try:
                                    next(g_)
                                except StopIteration:
                                    gens.remove(g_)

                    for half in range(2):
                        cur_half[0] = half
                        run_A_quad()
                        if TWO_CHAINS:
                            rec_two_chains(half)
                        else:
                            rec_quad(half)
                    T.barrier()
                dump('mixA', mixT[:, 0:4, :], [('mixT', h) for h in range(4)])
                if stop == 'mixA':
                    return True

                wo_t = sb(p1, "wo_t", [128, 8, 1024], BF16)
                with ExitStack() as sc:
                    ctab_t = sb(sc, "ctab_t", [64, 2 * S], F32)
                    T.dma('sp', ctab_t[:], ctab[:, :], writes=[('ctab',)], key=('ctab',))
                    cos2 = ctab_t[:, 0:S]
                    sin2 = ctab_t[:, S:2 * S]
                    wq_t = sb(sc, "wq_t", [128, 3, 768], BF16)
                    wqr_t = sb(sc, "wqr_t", [128, 3, 4, 64], BF16)
                    wkv_t = sb(sc, "wkv_t", [128, 2, 1024], BF16)
                    wkr_t = sb(sc, "wkr_t", [128, 8, 128], BF16)
                    wst = [sb(sc, f"wstm{i}", [128, 8, 128], BF16) for i in range(3)]
                    cqg = sb(sc, "cqg", [128, 3, S], BF16)
                    ckvg = sb(sc, "ckvg", [128, 2, S], BF16)
                    sqr = [sb(sc, f"sqr{i}", [128, 512], BF16) for i in range(2)]
                    rsq = sb(sc, "rsq", [128, S], F32)
                    rskv = sb(sc, "rskv", [128, S], F32)
                    krT = sb(sc, "krT", [64, S], BF16)
                    t1 = [sb(sc, f"rt1_{i}", [64, 512], F32) for i in range(2)]
                    t2 = [sb(sc, f"rt2_{i}", [64, 512], F32) for i in range(2)]
                    qn = [sb(sc, f"qn{i}", [128, S], BF16) for i in range(1)]
                    qr = [sb(sc, f"qr{i}", [64, S], BF16) for i in range(1)]
                    kn = [sb(sc, f"kn{i}", [128, S], BF16) for i in range(1)]
                    vh = [sb(sc, f"vh{i}", [128, NT, 128], BF16) for i in range(1)]
                    PT = [sb(sc, f"PTt{i}", [128, 512], BF16) for i in range(3)]
                    den = [sb(sc, f"den{i}", [128, 512], F32) for i in range(2)]

                    wcnt = [0]

                    def load_wchunk2(c0):
                        sl = wcnt[0] % 3
                        wcnt[0] += 1
                        T.dma('pool', wst[sl][:], w_in_v[:, :, c0:c0 + 128], writes=[('wst', sl)], key=('wstm', sl))
                        return sl
                    pre_sl = {0: load_wchunk2(2056), 1: load_wchunk2(2056 + 128)}
                    T.dma('pool', wkr_t[:, :, 0:64], w_in_v[:, :, 2696:2760], writes=[('wkr',)], key=('wkr',))
                    T.dma('pool', wq_t[:], wq_v[:, :, :], writes=[('wq',)], key=('wq',))
                    T.dma('pool', wkv_t[:], wkv_v[:, :, :], writes=[('wkv',)], key=('wkv',))
                    ts('dve', wkr_t[:, :, 64:96], wkr_t[:, :, 32:64], -1.0, None, ALU.mult, None, [('wkr',)], [('wkr2',)])
                    cp('dve', wkr_t[:, :, 96:128], wkr_t[:, :, 0:32], [('wkr',)], [('wkr2',)])
                    for h in range(4):
                        ts('dve', wqr_t[:, :, h, 0:32], wq_t[:, :, h * 192 + 160:h * 192 + 192], -1.0, None, ALU.mult, None, [('wq',)], [('wqr',)])
                        cp('dve', wqr_t[:, :, h, 32:64], wq_t[:, :, h * 192 + 128:h * 192 + 160], [('wq',)], [('wqr',)])

                    XH = lambda tb: [('xhT', tb * 4 + i) for i in range(4)]
                    sqcnt = [0]
                    pend_n = [None]
                    for ci in range(5):
                        isq = ci < 3
                        cc = ci if isq else ci - 3
                        last = cc == (2 if isq else 1)
                        c0 = 2056 + ci * 128
                        if ci + 2 < 5:
                            pre_sl[ci + 2] = load_wchunk2(c0 + 256)
                        if ci == 0:
                            T.dma('pool', wo_t[:], w_out_v[:, :, :], writes=[('wo',)], key=('wo',))
                        sl = pre_sl[ci]
                        nrm = onesb if isq else c256b
                        knrm = ('onesb',) if isq else ('c256b',)
                        for tb in range(4):
                            bank = tb % 2
                            cs = slice(tb * 512, (tb + 1) * 512)
                            for kc in range(8):
                                mm(ps[bank][:, :], wst[sl][:, kc, :], xhT[:, kc, cs], kc == 0, kc == 7,
                                   [('wst', sl)] + XH(tb), [('ps', bank)])
                            if isq:
                                ts('dve', cqg[:, cc, cs], ps[bank][:, :], qg[:, cc:cc + 1], float(384 ** 0.5), ALU.mult, ALU.mult,
                                   [('ps', bank), ('cpp',)], [('cqg',)])
                            else:
                                ts('dve', ckvg[:, cc, cs], ps[bank][:, :], kvg[:, cc:cc + 1], None, ALU.mult, None,
                                   [('ps', bank), ('cpp',)], [('ckvg',)])
                            sqi = sqcnt[0] % 2
                            sqcnt[0] += 1
                            act(sqr[sqi][:], ps[bank][:, :], AF.Square, [('ps', bank)], [('sqr', sqi)])
                            if pend_n[0] is not None:
                                pend_n[0]()

                            def _norm(tb=tb, cs=cs, nrm=nrm, knrm=knrm, sqi=sqi, cc=cc, last=last, isq=isq):
                                mm(ps[2 + tb][:, :], nrm[:], sqr[sqi][:], cc == 0, last, [knrm, ('sqr', sqi)], [('ps', 2 + tb)], inc=True)
                                if last:
                                    if isq:
                                        rsqrt(rsq[:, cs], ps[2 + tb][:, :], eps384, [('ps', 2 + tb)], [('rsq',)])
                                    else:
                                        rsqrt(rskv[:, cs], ps[2 + tb][:, :], eps1, [('ps', 2 + tb)], [('rskv',)])
                                        for c_ in range(2):
                                            tt('dve', ckvg[:, c_, cs], ckvg[:, c_, cs], rskv[:, cs], ALU.mult, [('ckvg',), ('rskv',)], [('ckvg',)])
                            pend_n[0] = _norm
                        if ci == 4:
                            pend_n[0]()
                            pend_n[0] = None
                    for tb in range(4):
                        cs = slice(tb * 512, (tb + 1) * 512)
                        bA, bB = (5, 6) if tb % 2 == 0 else (0, 1)
                        for kc in range(8):
                            mm(ps[bA][0:64, :], wkr_t[:, kc, 0:64], xhT[:, kc, cs], kc == 0, kc == 7, [('wkr',)] + XH(tb), [('ps', bA)])
                        for kc in range(8):
                            mm(ps[bB][0:64, :], wkr_t[:, kc, 64:128], xhT[:, kc, cs], kc == 0, kc == 7, [('wkr2',)] + XH(tb), [('ps', bB)])
                        tt('dve', t1[tb % 2][:], ps[bA][0:64, :], cos2[:, cs], ALU.mult, [('ps', bA), ('ctab',)], [('t1', tb % 2)])
                        tt('dve', t2[tb % 2][:], ps[bB][0:64, :], sin2[:, cs], ALU.mult, [('ps', bB), ('ctab',)], [('t2', tb % 2)])
                        tt('pool', krT[:, cs], t1[tb % 2][:], t2[tb % 2][:], ALU.add, [('t1', tb % 2), ('t2', tb % 2)], [('krT',)])
                    scale = float(192 ** -0.5)
                    ptc = [0]
                    for h in range(4):
                        par = 0
                        for tb in range(4):
                            cs = slice(tb * 512, (tb + 1) * 512)
                            b0, b1, b5, b6 = (0, 1, 5, 6) if tb % 2 == 0 else (2, 3, 4, 7)
                            for kc in range(3):
                                mm(ps[b0][:, :], wq_t[:, kc, h * 192:h * 192 + 128], cqg[:, kc, cs], kc == 0, kc == 2, [('wq',), ('cqg',)], [('ps', b0)])
                            for kc in range(3):
                                mm(ps[b5][0:64, :], wq_t[:, kc, h * 192 + 128:h * 192 + 192], cqg[:, kc, cs], kc == 0, kc == 2, [('wq',), ('cqg',)], [('ps', b5)])
                            for kc in range(3):
                                mm(ps[b6][0:64, :], wqr_t[:, kc, h, :], cqg[:, kc, cs], kc == 0, kc == 2, [('wqr',), ('cqg',)], [('ps', b6)])
                            for kc in range(2):
                                mm(ps[b1][:, :], wkv_t[:, kc, h * 256:h * 256 + 128], ckvg[:, kc, cs], kc == 0, kc == 1, [('wkv',), ('ckvg',)], [('ps', b1)])
                            tt('dve', qn[par][:, cs], ps[b0][:, :], rsq[:, cs], ALU.mult, [('ps', b0), ('rsq',)], [('qn', par)])
                            tt('dve', t1[tb % 2][:], ps[b5][0:64, :], cos2[:, cs], ALU.mult, [('ps', b5), ('ctab',)], [('t1', tb % 2)])
                            tt('dve', t2[tb % 2][:], ps[b6][0:64, :], sin2[:, cs], ALU.mult, [('ps', b6), ('ctab',)], [('t2', tb % 2)])
                            cp('act', kn[par][:, cs], ps[b1][:, :], [('ps', b1)], [('kn', par)])
                            tt('pool', t1[tb % 2][:], t1[tb % 2][:], t2[tb % 2][:], ALU.add, [('t1', tb % 2), ('t2', tb % 2)], [('t1', tb % 2)])
                            tt('pool', qr[par][:, cs], t1[tb % 2][:], rsq[0:64, cs], ALU.mult, [('t1', tb % 2), ('rsq',)], [('qr', par)])
                        for g4 in range(NT // 4):
                            bank = 2 + g4 % 2
                            for tl in range(4):
                                t = g4 * 4 + tl
                                for kc in range(2):
                                    mm(ps[bank][:, tl * 128:(tl + 1) * 128], ckvg[:, kc, t * 128:(t + 1) * 128],
                                       wkv_t[:, kc, h * 256 + 128:h * 256 + 256], kc == 0, kc == 1,
                                       [('wkv',), ('ckvg',)], [('ps', bank)], inc=(kc == 1 and tl == 3))
                            cp('act' if g4 % 2 else 'dve', vh[par][:, g4 * 4:(g4 + 1) * 4, :],
                               ps[bank][:, :].rearrange("p (t c) -> p t c", t=4), [('ps', bank)], [('vh', par)])
                        items = [(qb, kt) for qb in range(4) for kt in range(4 * qb + 4)]

                        def att_S(idx):
                            qb, kt = items[idx]
                            r = max(0, kt - 4 * qb)
                            c0 = qb * 512 + r * 128
                            ncol = 512 - r * 128
                            sb_ = ps[idx % 2]
                            ksb = ('ps', idx % 2)
                            mm(sb_[:, 0:ncol], kn[par][:, kt * 128:(kt + 1) * 128], qn[par][:, c0:c0 + ncol], True, False,
                               [('kn', par), ('qn', par)], [ksb], inc=False)
                            mm(sb_[:, 0:ncol], krT[:, kt * 128:(kt + 1) * 128], qr[par][:, c0:c0 + ncol], False, True,
                               [('krT',), ('qr', par)], [ksb])

                        def att_EV(idx):
                            qb, kt = items[idx]
                            nk = 4 * qb + 4
                            r = max(0, kt - 4 * qb)
                            ncol = 512 - r * 128
                            sb_ = ps[idx % 2]
                            ksb = ('ps', idx % 2)
                            pO_, pD_ = ps[4 + (qb % 2) * 2], ps[5 + (qb % 2) * 2]
                            kO, kD = ('ps', 4 + (qb % 2) * 2), ('ps', 5 + (qb % 2) * 2)
                            pi = ptc[0] % 3
                            ptc[0] += 1
                            act(PT[pi][:, 0:ncol], sb_[:, 0:ncol], AF.Exp, [ksb], [('PT', pi)], scale=scale)
                            if kt >= 4 * qb:
                                T.issue('pool', lambda e: e.memset(PT[pi][64:128, 0:64], 0.0), [], [('PT', pi)])
                            mm(pO_[:, r * 128:512], vh[par][:, kt, :], PT[pi][:, 0:ncol], kt == 0, kt == nk - 1, [('vh', par), ('PT', pi)], [kO])
                            mm(pD_[:, r * 128:512], onesb[:], PT[pi][:, 0:ncol], kt == 0, kt == nk - 1, [('onesb',), ('PT', pi)], [kD])
                            if kt == nk - 1:
                                dq = den[qb % 2]
                                T.issue('dve', lambda e: e.reciprocal(out=dq[:], in_=pD_[:, :]), [kD], [('den', qb % 2)])
                                tt('dve', mixT[:, 4 + h, qb * 512:(qb + 1) * 512], pO_[:, :], dq[:], ALU.mult, [kO, ('den', qb % 2)], [('mixT', 4 + h)])

                        att_S(0)
                        for idx in range(len(items)):
                            if idx + 1 < len(items):
                                att_S(idx + 1)
                            att_EV(idx)
                    T.barrier()
                dump('mixB', mixT[:, 4:8, :], [('mixT', 4 + h) for h in range(4)])
                if stop == 'mixB':
                    return True

                with ExitStack() as sc:
                    ln1_t = sb(sc, "ln1_t", [128, 2048], F32)
                    T.dma('sp', ln1_t[:], cbc[:, 0:2048], writes=[('cbcL',)], key=('cbcL',))
                    G1 = ln1_t[:, 0:1024]
                    B1 = ln1_t[:, 1024:2048]
                    xr = [sb(sc, f"xr{i}", [128, D], F32) for i in range(3)]
                    rr = [sb(sc, f"rr{i}", [128, D], F32) for i in range(3)]
                    hh = [sb(sc, f"hh{i}", [128, D], F32) for i in range(2)]
                    stats3 = [sb(sc, f"stats{i}", [128, 12], F32) for i in range(3)]
                    mv3 = [sb(sc, f"mv{i}", [128, 2], F32) for i in range(3)]
                    rs3 = [sb(sc, f"rs{i}", [128, 1], F32) for i in range(3)]

                    def d1_ldx(t):
                        q_ = t % 3
                        T.dma('sp', xr[q_][:], x[row0 + t * 128: row0 + (t + 1) * 128, :], writes=[('xr', q_)], key=('xr', q_))

                    def d1_mm(t):
                        sl = t % 2
                        tok = slice(t * 128, (t + 1) * 128)
                        for hb in range(2):
                            bank = sl * 2 + hb
                            for kc in range(8):
                                mm(ps[bank][:, :], mixT[:, kc, tok], wo_t[:, kc, hb * 512:(hb + 1) * 512], kc == 0, kc == 7,
                                   [('mixT', kc), ('wo',)], [('ps', bank)])

                    def d1_res(t):
                        sl = t % 2
                        q_ = t % 3
                        for hb in range(2):
                            bank = sl * 2 + hb
                            stt(rr[q_][:, hb * 512:(hb + 1) * 512], xr[q_][:, hb * 512:(hb + 1) * 512], ALPHA, ps[bank][:, :], ALU.mult, ALU.add,
                                [('xr', q_), ('ps', bank)], [('rr', q_)])

                    def d1_lna(t):
                        q_ = t % 3
                        r, stats, mv, rs = rr[q_], stats3[q_], mv3[q_], rs3[q_]
                        kr = ('rr', q_)
                        for hb in range(2):
                            T.issue('dve', lambda e: e.bn_stats(out=stats[:, hb * 6:(hb + 1) * 6], in_=r[:, hb * 512:(hb + 1) * 512]),
                                    reads=[kr], writes=[('lnst', q_)])
                        T.issue('dve', lambda e: e.bn_aggr(out=mv[:], in_=stats[:]), reads=[('lnst', q_)], writes=[('lnmv', q_)])
                        rsqrt(rs[:], mv[:, 1:2], eps1, [('lnmv', q_)], [('lnrs', q_)])
                        stt(mv[:, 1:2], mv[:, 0:1], -1.0, rs[:, 0:1], ALU.mult, ALU.mult, [('lnmv', q_), ('lnrs', q_)], [('lnmv', q_)])
                        act(r[:], r[:], AF.Identity, [kr, ('lnmv', q_), ('lnrs', q_)], [kr], bias=mv[:, 1:2], scale=rs[:, 0:1])

                    def d1_lnb(t):
                        q_ = t % 3
                        sl = t % 2
                        r = rr[q_]
                        kr = ('rr', q_)
                        tt('dve', r[:], r[:], G1, ALU.mult, [kr, ('cbcL',)], [kr])
                        tt('dve', hh[sl][:], r[:], B1, ALU.add, [kr, ('cbcL',)], [('hh', sl)])
                        T.dma('sp', hscr[t * 128:(t + 1) * 128, :], hh[sl][:], reads=[('hh', sl)], writes=[('hscr', t)], key=('hh', sl))

                    def d1_tr(t):
                        sl = t % 2
                        tok = slice(t * 128, (t + 1) * 128)
                        for kc in range(8):
                            bank = 4 + sl * 2 + kc // 4
                            col = (kc % 4) * 128
                            tr(ps[bank][:, col:col + 128], hh[sl][:, kc * 128:(kc + 1) * 128], [('hh', sl)], [('ps', bank)], inc=(kc % 4 == 3))
                        for hb in range(2):
                            bank = 4 + sl * 2 + hb
                            cp('act', xhT[:, hb * 4:(hb + 1) * 4, tok], ps[bank][:, :].rearrange("p (k c) -> p k c", k=4), [('ps', bank)], [('xhT', t)])

                    for t_ in range(3):
                        d1_ldx(t_)
                    d1_mm(0)
                    d1_res(0)
                    d1_mm(1)
                    d1_res(1)
                    d1_lna(0)
                    for t in range(NT):
                        if t + 2 < NT:
                            d1_mm(t + 2)
                        if t + 1 < NT:
                            d1_lna(t + 1)
                        d1_lnb(t)
                        if t + 2 < NT:
                            d1_res(t + 2)
                        if t + 3 < NT:
                            d1_ldx(t + 3)
                        d1_tr(t)
                    T.barrier()
            dump('hT', xhT[:], [('xhT', t) for t in range(NT)])
            if stop == 'hT':
                return True

            with ExitStack() as p2:
                wd_t = sb(p2, "wd_t", [128, NJ, 1024], BF16)
                ln2_t = sb(p2, "ln2_t", [128, 3072], F32)
                T.dma('sp', ln2_t[:], cbc[:, 2048:5120], writes=[('cbcL',)], key=('cbcL2',))
                G2 = ln2_t[:, 0:1024]
                B2 = ln2_t[:, 1024:2048]
                BG = ln2_t[:, 2048:3072]
                wg_t = sb(p2, "wg_t", [128, 8, 1024], BF16)
                wp_t = sb(p2, "wp_t", [128, 2, 1024], BF16)
                actT = sb(p2, "actT", [128, NJ, BLK2], BF16)
                wup = [sb(p2, f"wup{i}", [128, 2, 8, 128], BF16) for i in range(3)]
                rawgu = [sb(p2, f"rawgu{i}", [128, 2, 2 + BLK2], F32) for i in range(2)]
                accg = [sb(p2, f"accg{i}", [128, BLK2], F32) for i in range(2)]
                accu = [sb(p2, f"accu{i}", [128, BLK2], F32) for i in range(2)]
                halo = sb(p2, "halo", [128, 2, NJ, 2], F32)
                hr_ = [sb(p2, f"hr{i}", [128, D], F32) for i in range(2)]
                r2_ = [sb(p2, f"r2{i}", [128, D], F32) for i in range(2)]
                sg_ = [sb(p2, f"sgt{i}", [128, D], F32) for i in range(2)]
                pin_ = [sb(p2, f"pin{i}", [128, 256], F32) for i in range(2)]
                pTb_ = [sb(p2, f"pTb{i}", [128, 2, 128], BF16) for i in range(2)]
                stats_ = [sb(p2, f"stats2{i}", [128, 12], F32) for i in range(2)]
                mv_ = [sb(p2, f"mv2{i}", [128, 2], F32) for i in range(2)]
                rs_ = [sb(p2, f"rs2{i}", [128, 1], F32) for i in range(2)]
                T.issue('pool', lambda e: e.memset(halo[:], 0.0), writes=[('halo', c_) for c_ in range(NJ)])
                def ld2(t_):
                    q_ = t_ % 2
                    T.dma('sp', hr_[q_][:], hscr[t_ * 128:(t_ + 1) * 128, :], reads=[('hscr', t_)], writes=[('hr', q_)], key=('hr', q_))
                    T.dma('sp', pin_[q_][:], p[row0 + t_ * 128:row0 + (t_ + 1) * 128, :], writes=[('pin', q_)], key=('pin', q_))

                def trp(t_):
                    q_ = t_ % 2
                    for kc in range(2):
                        tr(ps[6 + q_][:, kc * 128:(kc + 1) * 128], pin_[q_][:, kc * 128:(kc + 1) * 128], [('pin', q_)], [('ps', 6 + q_)], inc=(kc == 1))
                    cp('act', pTb_[q_][:], ps[6 + q_][:, 0:256].rearrange("p (k c) -> p k c", k=2), [('ps', 6 + q_)], [('pTb', q_)])

                NB = S // BLK2
                TPB = BLK2 // 128
                wc = [0]

                def prefetch_wup(upto):
                    while wc[0] < min(upto, NB * NJ):
                        jj = wc[0] % NJ
                        s_ = wc[0] % 3
                        wc[0] += 1
                        T.dma('pool', wup[s_][:, 0, :, :], w_up_v[:, :, jj * 128:(jj + 1) * 128], writes=[('wup', s_, 0)], key=('wup', s_, 0))
                        T.dma('pool', wup[s_][:, 1, :, :], w_up_v[:, :, DFF + jj * 128:DFF + (jj + 1) * 128], writes=[('wup', s_, 1)], key=('wup', s_, 1))
                prefetch_wup(3)
                T.dma('pool', wg_t[:], w_gate_v[:, :, :], writes=[('wg',)], key=('wg',))
                T.dma('pool', wp_t[:], w_proj_v[:, :, :], writes=[('wp',)], key=('wp',))
                T.dma('pool', wd_t[:], w_down_v[:, :, :], writes=[('wd',)], key=('wd',))

                def stage_B2(j_):
                    q_ = j_ % 2
                    kg_, ku_ = ('acc2', 0, q_), ('acc2', 1, q_)
                    act(accg[q_][:], accg[q_][:], AF.Silu, [kg_], [kg_])
                    tt('dve', actT[:, j_, :], accg[q_][:], accu[q_][:], ALU.mult, [kg_, ku_], [('actT', j_)])

                for blk in range(NB):
                    bs = slice(blk * BLK2, (blk + 1) * BLK2)
                    XB = [('xhT', blk * TPB + i) for i in range(TPB)]
                    for j in range(NJ):
                        step = blk * NJ + j
                        prefetch_wup(step + 3)
                        sl = step % 3
                        jp = j % 2
                        rg = rawgu[jp]
                        kr_ = ('raw2', jp)
                        for gu in range(2):
                            bank = jp * 2 + gu
                            for kc in range(8):
                                mm(ps[bank][:, :], wup[sl][:, gu, kc, :], xhT[:, kc, bs], kc == 0, kc == 7, [('wup', sl, gu)] + XB, [('ps', bank)])
                        cp('act', rg[:, :, 0:2], halo[:, :, j, :], [('halo', j)], [kr_ + ('h',)])
                        cp('act', rg[:, :, 2:2 + BLK2], ps_all[:, jp * 1024:(jp + 1) * 1024].rearrange("p (g c) -> p g c", g=2),
                           [('ps', jp * 2), ('ps', jp * 2 + 1)], [kr_])
                        cp('act', halo[:, :, j, :], rg[:, :, BLK2:BLK2 + 2], [kr_], [('halo', j)])
                        acs = (accg[jp], accu[jp])
                        kas = (('acc2', 0, jp), ('acc2', 1, jp))
                        act(acs[0][:], ps[jp * 2][:, :], AF.Identity, [('ps', jp * 2), ('cpp',)], [kas[0]], bias=fcb[:, j:j + 1], scale=fcw[:, j, 2:3])
                        ts('dve', acs[1][:], rg[:, 1, 2:2 + BLK2], fcw[:, NJ + j, 2:3], fcb[:, NJ + j:NJ + j + 1], ALU.mult, ALU.add, [kr_, ('cpp',)], [kas[1]])
                        for tap in (1, 0):
                            for gu in range(2):
                                cidx = gu * NJ + j
                                stt(acs[gu][:], rg[:, gu, tap:tap + BLK2], fcw[:, cidx, tap:tap + 1], acs[gu][:], ALU.mult, ALU.add,
                                    [kr_, kr_ + ('h',), kas[gu], ('cpp',)], [kas[gu]])
                        if j >= 1:
                            stage_B2(j - 1)
                    stage_B2(NJ - 1)
                    AK = [('actT', j) for j in range(NJ)]
                    if stop == 'p2a' and blk == 0:
                        T.barrier()
                        return True
                    for tl in range(TPB):
                        t = blk * TPB + tl
                        tok = slice(t * 128, (t + 1) * 128)
                        ltok = slice(tl * 128, (tl + 1) * 128)
                        grow = row0 + t * 128
                        tp_ = t % 2
                        hr, r2, sg, pin, pTb = hr_[tp_], r2_[tp_], sg_[tp_], pin_[tp_], pTb_[tp_]
                        kh, kr2, kpin, kpt = ('hr', tp_), ('r2', tp_), ('pin', tp_), ('pTb', tp_)
                        if t == 0:
                            ld2(0)
                            trp(0)
                        if t + 1 < NT:
                            ld2(t + 1)
                        for hb in range(2):
                            hs = slice(hb * 512, (hb + 1) * 512)
                            ksg = ('sg', hb, tp_)
                            for kc in range(8):
                                mm(ps[2 + hb][:, :], xhT[:, kc, tok], wg_t[:, kc, hs], kc == 0, kc == 7, [('xhT', t), ('wg',)], [('ps', 2 + hb)])
                            for kc in range(2):
                                mm(ps[4 + hb][:, :], pTb[:, kc, :], wp_t[:, kc, hs], kc == 0, kc == 1, [kpt, ('wp',)], [('ps', 4 + hb)])
                            for j in range(NJ):
                                mm(ps[hb][:, :], actT[:, j, ltok], wd_t[:, j, hs], j == 0, j == NJ - 1, AK + [('wd',)], [('ps', hb)])
                            tt('dve', sg[:, hs], ps[2 + hb][:, :], BG[:, hs], ALU.add, [('ps', 2 + hb), ('cbcL',)], [ksg])
                            act(sg[:, hs], sg[:, hs], AF.Sigmoid, [ksg], [ksg])
                            tt('dve', sg[:, hs], sg[:, hs], ps[4 + hb][:, :], ALU.mult, [ksg, ('ps', 4 + hb)], [ksg])
                            stt(r2[:, hs], hr[:, hs], ALPHA, ps[hb][:, :], ALU.mult, ALU.add, [kh, ('ps', hb)], [kr2])
                            tt('dve', r2[:, hs], r2[:, hs], sg[:, hs], ALU.add, [kr2, ksg], [kr2])
                        if t + 1 < NT:
                            trp(t + 1)
                        ln_tile(r2, G2, B2, (stats_[tp_], mv_[tp_], rs_[tp_]), kr2, r2[:], kr2, sfx=tp_)
                        T.dma('sp', out[grow:grow + 128, :], r2[:], reads=[kr2], writes=[('out', sq, t)], key=('r2o', tp_))
                    if stop == 'p2b' and blk == 0:
                        T.barrier()
                        return True
                T.barrier()
          for sq in range(NSEQ):
            if seq_body(sq):
                break
        T.final_wait()
    nc._trk_log = T.log
    return nc


def _prep_common(inp):
    f = np.float32
    g = lambda k: np.asarray(inp[k], dtype=f)[0]
    cw = g("gdn_conv_w")
    fw = g("ffn_conv_w")
    fb = g("ffn_conv_b")
    cpp = np.zeros((128, 512), f)
    cpp[:, 0:48] = cw.reshape(4, 12, 128).transpose(2, 1, 0).reshape(128, 48)
    cpp[:, 48:180] = fw.reshape(3, 44, 128).transpose(2, 1, 0).reshape(128, 132)
    cpp[:, 180:224] = fb.reshape(44, 128).T
    cpp[:, 224] = g("gdn_norm_g")
    cpp[:, 225:228] = g("mla_q_norm_g").reshape(3, 128).T
    cpp[:, 228:230] = g("mla_kv_norm_g").reshape(2, 128).T
    cbc = np.zeros((128, 5 * 1024 + 128), f)
    for i, k in enumerate(["ln1_g", "ln1_b", "ln2_g", "ln2_b", "ple_b_gate"]):
        cbc[:, i * 1024:(i + 1) * 1024] = g(k)[None, :]
    cbc[:, 5120:5184] = np.tile(g("gdn_a_log"), 16)[None, :]
    cbc[:, 5184:5248] = np.tile(g("gdn_dt_bias"), 16)[None, :]
    j = np.arange(128)[:, None]
    i = np.arange(128)[None, :]
    cmat = np.zeros((128, 1792), f)
    cmat[:, 0:128] = np.eye(128, dtype=f)
    cmat[:, 128:256] = (j <= i).astype(f)
    for q_ in range(4):
        cmat[:, 256 + q_ * 128:384 + q_ * 128] = np.where(j <= i, 0.0, -30000.0).astype(f)
        cmat[:, 768 + q_ * 128:896 + q_ * 128] = np.where(j < i, 0.0, -30000.0).astype(f)
        cmat[:, 1280 + q_ * 128:1408 + q_ * 128] = np.eye(128, dtype=f)
    inv = (np.float32(10000.0) ** (-(np.arange(0, 64, 2, dtype=f)) / np.float32(64))).astype(f)
    ang = (np.arange(S, dtype=f)[:, None] * inv[None, :]).astype(f)
    cos = np.cos(ang.astype(np.float64)).astype(f).T
    sin = np.sin(ang.astype(np.float64)).astype(f).T
    ctab = np.zeros((64, 2 * S), f)
    ctab[0:32, 0:S] = cos
    ctab[32:64, 0:S] = cos
    ctab[0:32, S:] = sin
    ctab[32:64, S:] = sin
    return {
        "w_in": np.ascontiguousarray(g("w_in")), "wq_up": np.ascontiguousarray(g("mla_w_q_up")),
        "wkv_up": np.ascontiguousarray(g("mla_w_kv_up")), "w_out": np.ascontiguousarray(g("w_out")),
        "w_up": np.ascontiguousarray(g("ffn_w_up")), "w_down": np.ascontiguousarray(g("ffn_w_down")),
        "w_gate": np.ascontiguousarray(g("ple_w_gate")), "w_proj": np.ascontiguousarray(g("ple_w_proj")),
        "cpp": cpp, "cbc": cbc, "cmat": cmat, "ctab": ctab,
    }


def kernel(**inputs):
    common = _prep_common(inputs)
    x = np.asarray(inputs["x"], dtype=np.float32)
    p = np.asarray(inputs["p"], dtype=np.float32)[0]
    B = x.shape[0]
    nseq = B // NCORES
    nc = build(nseq)
    in_maps = []
    for c in range(NCORES):
        m = dict(common)
        m["x"] = np.ascontiguousarray(x[c * nseq:(c + 1) * nseq].reshape(nseq * S, D))
        m["p"] = np.ascontiguousarray(p[c * nseq:(c + 1) * nseq].reshape(nseq * S, 256))
        in_maps.append(m)
    res = run_bass_kernel_spmd(nc, in_maps, core_ids=list(range(NCORES)))
    outs = [np.asarray(r["out"]).reshape(nseq, S, D) for r in res.results]
    return np.concatenate(outs, axis=0).astype(np.float32)
```

```python
import numpy as np
from contextlib import ExitStack
import concourse.bass as bass
import concourse.mybir as mybir
from concourse.bass_utils import run_bass_kernel_spmd

F32, BF16 = mybir.dt.float32, mybir.dt.bfloat16
AF = mybir.ActivationFunctionType
ALU = mybir.AluOpType

S = 2048
NT = 16
D = 1024
KC = 8
DFF = 2816
NJ = 22
ALPHA = float(2.0 ** 0.25)
EPS = 1e-6
BLK2 = 512
HS = 1024
NBLK = 2
NTH = 8
EPOCH = 16000
NCORES = 8
TWO_CHAINS = True
OVERLAP_A = False
SEQ_GENS = True


class Trk:
    def __init__(self, nc, es):
        self.nc, self.es = nc, es
        self.engs = {'pe': nc.tensor, 'act': nc.scalar, 'dve': nc.vector, 'pool': nc.gpsimd, 'sp': nc.sync}
        self.cnt = {e: 0 for e in self.engs}
        self.esems = {e: [] for e in self.engs}
        self.seen = {e: {} for e in self.engs}
        self.lastw = {}
        self.rd = {}
        self.dsem = {}
        self.pend = {e: ([], []) for e in self.engs}
        self.latest = {}
        self.log = {e: [] for e in self.engs}

    def newsem(self, name):
        return self.es.enter_context(self.nc.semaphore(name))

    def _wait(self, e, ev):
        sem, val, src = ev
        if src == 'pe' and e == 'pe':
            return
        k = id(sem)
        if self.seen[e].get(k, 0) >= val:
            return
        self.engs[e].wait_ge(sem, val)
        self.log[e].append(('w', id(sem), val))
        self.seen[e][k] = val

    def _deps(self, e, reads, writes):
        for k in reads:
            ev = self.lastw.get(k)
            if ev is not None:
                self._wait(e, ev)
        for k in writes:
            ev = self.lastw.get(k)
            if ev is not None:
                self._wait(e, ev)
            for ev in self.rd.get(k, {}).values():
                self._wait(e, ev)

    def _reg(self, ev, reads, writes):
        sem, val, src = ev
        self.latest[id(sem)] = (sem, val)
        for k in writes:
            self.lastw[k] = ev
            self.rd[k] = {}
        for k in reads:
            self.rd.setdefault(k, {})[id(sem)] = ev

    def issue(self, e, fn, reads=(), writes=(), inc=True):
        writes = list(writes) + [k for k in reads if k[0] == 'ps' and k not in writes]
        reads = [k for k in reads if k[0] != 'ps']
        self._deps(e, reads, writes)
        ins = fn(self.engs[e])
        pr, pw = self.pend[e]
        pr.extend(reads)
        pw.extend(writes)
        if inc:
            n = self.cnt[e]
            ep, off = divmod(n, EPOCH)
            if ep >= len(self.esems[e]):
                self.esems[e].append(self.newsem(f"s_{e}_{ep}"))
            sem = self.esems[e][ep]
            ins.then_inc(sem, 1)
            self.log[e].append(('i', id(sem), 1))
            self.cnt[e] = n + 1
            self._reg((sem, off + 1, e), pr, pw)
            self.pend[e] = ([], [])
        return ins

    def dma(self, q, out, in_, reads=(), writes=(), key=None):
        self._deps(q, reads, writes)
        ins = self.engs[q].dma_start(out=out, in_=in_)
        if key not in self.dsem:
            self.dsem[key] = [self.newsem("d_" + "_".join(str(x) for x in key)), 0]
        d = self.dsem[key]
        d[1] += 16
        ins.then_inc(d[0], 16)
        self.log[q].append(('i', id(d[0]), 16))
        self._reg((d[0], d[1], 'dma'), list(reads), list(writes))
        return ins

    def barrier(self):
        for e in self.engs:
            assert not self.pend[e][0] and not self.pend[e][1], e
        for sem, val in list(self.latest.values()):
            self._wait('sp', (sem, val, 'x'))
        n = self.cnt['sp']
        ep, off = divmod(n, EPOCH)
        if ep >= len(self.esems['sp']):
            self.esems['sp'].append(self.newsem(f"s_sp_{ep}"))
        sem = self.esems['sp'][ep]
        self.engs['sp'].sem_inc(sem, 1)
        self.log['sp'].append(('i', id(sem), 1))
        self.cnt['sp'] = n + 1
        ev = (sem, off + 1, 'sp')
        self.latest[id(sem)] = (sem, off + 1)
        self.seen['sp'][id(sem)] = off + 1
        for e in self.engs:
            if e != 'sp':
                self._wait(e, ev)
        self.lastw.clear()
        self.rd.clear()

    def final_wait(self):
        for sem, val in list(self.latest.values()):
            self._wait('sp', (sem, val, 'x'))


class _Stop(Exception):
    pass


def build(NSEQ, dbg=None, stop=None):
    dbg = dbg or {}
    nc = bass.Bass("TRN2", target_bir_lowering=False)

    def din(name, shape, dt=F32):
        return nc.dram_tensor(name, list(shape), dt, kind="ExternalInput").ap()

    x = din("x", [NSEQ * S, D])
    p = din("p", [NSEQ * S, 256])
    w_in = din("w_in", [D, 2760])
    wq_up = din("wq_up", [384, 768])
    wkv_up = din("wkv_up", [256, 1024])
    w_out = din("w_out", [1024, 1024])
    w_up = din("w_up", [D, 2 * DFF])
    w_down = din("w_down", [DFF, D])
    w_gate = din("w_gate", [D, D])
    w_proj = din("w_proj", [256, D])
    cpp = din("cpp", [128, 512])
    cbc = din("cbc", [128, 5 * 1024 + 128])
    cmat = din("cmat", [128, 14 * 128])
    ctab = din("ctab", [64, 2 * S])
    out = nc.dram_tensor("out", [NSEQ * S, D], F32, kind="ExternalOutput").ap()
    hscr = nc.dram_tensor("hscr", [S, D], F32).ap()
    dbg_t = {k: nc.dram_tensor("dbg_" + k, list(v[0]), v[1], kind="ExternalOutput").ap() for k, v in dbg.items()}

    w_in_v = w_in.rearrange("(kc p) n -> p kc n", p=128)
    wq_v = wq_up.rearrange("(kc p) n -> p kc n", p=128)
    wkv_v = wkv_up.rearrange("(kc p) n -> p kc n", p=128)
    w_out_v = w_out.rearrange("(kc p) n -> p kc n", p=128)
    w_up_v = w_up.rearrange("(kc p) n -> p kc n", p=128)
    w_down_v = w_down.rearrange("(kc p) n -> p kc n", p=128)
    w_gate_v = w_gate.rearrange("(kc p) n -> p kc n", p=128)
    w_proj_v = w_proj.rearrange("(kc p) n -> p kc n", p=128)

    with ExitStack() as es:
        T = Trk(nc, es)

        uid = [0]

        def sb(scope, name, shape, dt):
            uid[0] += 1
            return scope.enter_context(nc.sbuf_tensor(f"{name}_{uid[0]}", list(shape), dt))

        ps_all = es.enter_context(nc.psum_tensor("ps_all", [128, 8 * 512], F32))
        ps = [ps_all[:, b * 512:(b + 1) * 512] for b in range(8)]

        cpp_t = sb(es, "cpp_t", [128, 512], F32)
        cbc_t = sb(es, "cbc_t", [128, 128], F32)
        cmat_t = sb(es, "cmat_t", [128, 1792], F32)
        identb = sb(es, "identb", [128, 128], BF16)
        onesb = sb(es, "onesb", [128, 128], BF16)
        c128b = sb(es, "c128b", [128, 128], BF16)
        c256b = sb(es, "c256b", [128, 128], BF16)
        onesf = sb(es, "onesf", [128, 128], F32)
        w_ab = sb(es, "w_ab", [128, 8, 8], BF16)
        xhT = sb(es, "xhT", [128, 8, S], BF16)

        T.dma('sp', cpp_t[:], cpp[:, :], writes=[('cpp',)], key=('cpp',))
        T.dma('sp', cbc_t[:], cbc[:, 5120:5248], writes=[('cbc',)], key=('cbc',))
        T.dma('sp', cmat_t[:], cmat[:, :], writes=[('cmat',)], key=('cmat',))
        T.dma('pool', w_ab[:], w_in_v[:, :, 2048:2056], writes=[('w_ab',)], key=('w_ab',))
        ident = cmat_t[:, 0:128]
        Umat = cmat_t[:, 128:256]
        NEGM = cmat_t[:, 256:384]
        NEGM4 = cmat_t[:, 256:768].rearrange("p (i c) -> p i c", i=4)
        NEGMS4 = cmat_t[:, 768:1280].rearrange("p (i c) -> p i c", i=4)
        ident4 = cmat_t[:, 1280:1792].rearrange("p (i c) -> p i c", i=4)
        T.issue('dve', lambda e: e.tensor_copy(out=identb[:], in_=ident), reads=[('cmat',)], writes=[('identb',)])
        T.issue('pool', lambda e: e.memset(onesb[:], 1.0), writes=[('onesb',)])
        T.issue('pool', lambda e: e.memset(c128b[:], 1.0 / 128), writes=[('c128b',)])
        T.issue('pool', lambda e: e.memset(c256b[:], 1.0 / 256), writes=[('c256b',)])
        T.issue('pool', lambda e: e.memset(onesf[:], 1.0), writes=[('onesf',)])
        epst = sb(es, "epst", [128, 2], F32)
        T.issue('pool', lambda e: e.memset(epst[:, 0:1], EPS), writes=[('epst',)])
        T.issue('pool', lambda e: e.memset(epst[:, 1:2], 384 * EPS), writes=[('epst',)])
        eps1 = epst[:, 0:1]
        eps384 = epst[:, 1:2]
        gcw = cpp_t[:, 0:48].rearrange("p (c j) -> p c j", j=4)
        fcw = cpp_t[:, 48:180].rearrange("p (c j) -> p c j", j=3)
        fcb = cpp_t[:, 180:224]
        normg = cpp_t[:, 224:225]
        qg = cpp_t[:, 225:228]
        kvg = cpp_t[:, 228:230]
        ALOGB = cbc_t[:, 0:64]
        DTBB = cbc_t[:, 64:128]
        CONST = [('cpp',), ('cbc',), ('cmat',)]

        def act(out_, in_, func, reads, writes, **kw):
            return T.issue('act', lambda e: e.activation(out=out_, in_=in_, func=func, **kw), reads, writes)

        def tt(eng, out_, in0, in1, op, reads, writes):
            return T.issue(eng, lambda e: e.tensor_tensor(out=out_, in0=in0, in1=in1, op=op), reads, writes)

        def ts(eng, out_, in0, s1, s2, op0, op1, reads, writes):
            if s2 is None:
                return T.issue(eng, lambda e: e.tensor_scalar(out=out_, in0=in0, scalar1=s1, scalar2=None, op0=op0), reads, writes)
            return T.issue(eng, lambda e: e.tensor_scalar(out=out_, in0=in0, scalar1=s1, scalar2=s2, op0=op0, op1=op1), reads, writes)

        def stt(out_, in0, sc, in1, op0, op1, reads, writes):
            return T.issue('dve', lambda e: e.scalar_tensor_tensor(out=out_, in0=in0, scalar=sc, in1=in1, op0=op0, op1=op1), reads, writes)

        def cp(eng, out_, in_, reads, writes):
            if eng == 'act':
                return T.issue('act', lambda e: e.copy(out=out_, in_=in_), reads, writes)
            return T.issue(eng, lambda e: e.tensor_copy(out=out_, in_=in_), reads, writes)

        def rsqrt(out_, in_, eps_ap, reads, writes):
            act(out_, in_, AF.Ln, list(reads) + [('epst',)], writes, bias=eps_ap)
            act(out_, out_, AF.Exp, writes, writes, scale=-0.5)

        def amul(out_, in_, m, reads, writes):
            return T.issue('act', lambda e: e.mul(out=out_, in_=in_, mul=m), reads, writes)

        def mm(out_, lhsT, rhs, start, stop, reads, writes, inc=None):
            if inc is None:
                inc = stop
            return T.issue('pe', lambda e: e.matmul(out_, lhsT, rhs, start=start, stop=stop), reads, writes, inc=inc)

        def tr(out_, in_, reads, writes, inc=True):
            return T.issue('pe', lambda e: e.transpose(out_, in_, ident), list(reads) + [('cmat',)], writes, inc=inc)

        def dump(name, src, reads):
            if name in dbg_t:
                T.dma('sp', dbg_t[name], src, reads=reads, writes=[('dbg', name)], key=('dbg', name))

        def ln_tile(r, G, B, scope_tiles, key_r, out_tile, key_out, kc_=('cbcL',), sfx=''):
            stats, mv, rs = scope_tiles
            for hb in range(2):
                T.issue('dve', lambda e: e.bn_stats(out=stats[:, hb * 6:(hb + 1) * 6], in_=r[:, hb * 512:(hb + 1) * 512]),
                        reads=[key_r], writes=[('lnst', sfx)])
            T.issue('dve', lambda e: e.bn_aggr(out=mv[:], in_=stats[:]), reads=[('lnst', sfx)], writes=[('lnmv', sfx)])
            rsqrt(rs[:], mv[:, 1:2], eps1, [('lnmv', sfx)], [('lnrs', sfx)])
            ts('dve', r[:], r[:], mv[:, 0:1], rs[:, 0:1], ALU.subtract, ALU.mult, [key_r, ('lnmv', sfx), ('lnrs', sfx)], [key_r])
            tt('dve', r[:], r[:], G, ALU.mult, [key_r, kc_], [key_r])
            tt('dve', out_tile, r[:], B, ALU.add, [key_r, kc_], [key_out])

        if True:
          def seq_body(sq):
            row0 = sq * S
            with ExitStack() as p1:
                mixT = sb(p1, "mixT", [128, 8, S], BF16)
                with ExitStack() as sc:
                    xin = [sb(sc, f"xin{i}", [128, D], F32) for i in range(2)]
                    for t in range(NT):
                        sl = t % 2
                        T.dma('sp', xin[sl][:], x[row0 + t * 128: row0 + (t + 1) * 128, :], writes=[('xin', sl)], key=('xin', sl))
                        pb = (t % 2) * 2
                        for kc in range(8):
                            bank = pb + kc // 4
                            col = (kc % 4) * 128
                            tr(ps[bank][:, col:col + 128], xin[sl][:, kc * 128:(kc + 1) * 128], [('xin', sl)], [('ps', bank)], inc=(kc % 4 == 3))
                        for hb in range(2):
                            bank = pb + hb
                            cp('act' if hb == 0 else 'dve', xhT[:, hb * 4:(hb + 1) * 4, t * 128:(t + 1) * 128],
                               ps[bank][:, :].rearrange("p (k c) -> p k c", k=4), [('ps', bank)], [('xhT', t)])
                    T.barrier()
                dump('xT', xhT[:], [('xhT', t) for t in range(NT)])
                if stop == 'xT':
                    return True

                with ExitStack() as sc:
                    g_ab = sb(sc, "g_ab", [128, 128], F32)
                    g_beta = sb(sc, "g_beta", [128, 64], F32)
                    g_g = sb(sc, "g_g", [128, 64], F32)
                    g_tmp = sb(sc, "g_tmp", [128, 64], F32)
                    g_eal = sb(sc, "g_eal", [128, 64], F32)
                    g_gc = sb(sc, "g_gc", [128, 64], F32)
                    g_ngc = sb(sc, "g_ngc", [128, 64], F32)
                    g_eg = sb(sc, "g_eg", [128, 64], F32)
                    g_egl = sb(sc, "g_egl", [128, 64], F32)
                    g_egla = sb(sc, "g_egla", [128, 64], F32)
                    raw2 = [sb(sc, f"raw{i}", [128, 3 + HS], F32) for i in range(2)]
                    acc = sb(sc, "acc", [128, HS], F32)
                    sil2 = [sb(sc, f"sil{i}", [128, HS], F32) for i in range(2)]
                    sqb = sb(sc, "sqb", [128, HS], BF16)
                    rstd = [sb(sc, f"rstd{i}", [128, 512], F32) for i in range(2)]
                    halo_g = sb(sc, "halo_g", [128, 12, 3], F32)
                    zero3 = sb(sc, "zero3", [128, 3], F32)
                    wst = [sb(sc, f"wst{i}", [128, 8, 128], BF16) for i in range(3)]
                    hq = sb(sc, "hq", [128, 4, HS], BF16)
                    hk = sb(sc, "hk", [128, 4, HS], BF16)
                    hkg = sb(sc, "hkg", [128, 4, NTH, 128], BF16)
                    hkd = sb(sc, "hkd", [128, 4, NTH, 128], BF16)
                    hv = sb(sc, "hv", [128, 4, NTH, 128], BF16)
                    hz = sb(sc, "hz", [128, 4, HS], BF16)
                    S32 = sb(sc, "S32", [128, 4, 128], F32)
                    Sbf = sb(sc, "Sbf", [128, 4, 128], BF16)

                    def tmpp(name, dt):
                        return sb(sc, name, [128, 4, 128], dt)
                    Ug = tmpp("Ug", F32)
                    EGb = tmpp("EGb", F32)
                    ARG = tmpp("ARG", F32)
                    ARG2 = tmpp("ARG2", F32)
                    DT = tmpp("DT", F32)
                    DTs = tmpp("DTs", F32)
                    Nf = tmpp("Nf", F32)
                    Pb = [tmpp("Pb0_", BF16), tmpp("Pb1_", BF16)]
                    PTb = [tmpp("PTb0_", BF16), tmpp("PTb1_", BF16)]
                    Xb = [tmpp("Xb0_", BF16), tmpp("Xb1_", BF16)]
                    QKD = tmpp("QKD", BF16)
                    nw2T = tmpp("nw2T", BF16)
                    vnew = tmpp("vnew", BF16)
                    qgT = tmpp("qgT", BF16)
                    sqo = tmpp("sqo", BF16)
                    rso = tmpp("rso", F32)
                    o1 = tmpp("o1", F32)

                    T.issue('pool', lambda e: e.memset(zero3[:], 0.0), writes=[('zero3',)])
                    cur_half = [0]

                    for t in range(NT):
                        for kc in range(8):
                            mm(ps[7][:, t * 8:(t + 1) * 8], xhT[:, kc, t * 128:(t + 1) * 128], w_ab[:, kc, :], kc == 0, kc == 7,
                               [('xhT', t), ('w_ab',)], [('ps', 7)], inc=(kc == 7 and t == NT - 1))
                    cp('dve', g_ab[:], ps[7][:, 0:128], [('ps', 7)], [('g_ab',)])
                    abv = g_ab[:].rearrange("p (t c) -> p t c", c=8)
                    v64 = lambda tl: tl[:].rearrange("p (t c) -> p t c", c=4)
                    act(v64(g_beta), abv[:, :, 4:8], AF.Sigmoid, [('g_ab',)], [('g_beta',)])
                    tt('dve', v64(g_tmp), abv[:, :, 0:4], DTBB.rearrange("p (t c) -> p t c", c=4), ALU.add, [('g_ab',), ('cbc',)], [('g_tmp',)])
                    act(g_tmp[:], g_tmp[:], AF.Exp, [('g_tmp',)], [('g_tmp',)])
                    ts('dve', g_tmp[:], g_tmp[:], 1.0, None, ALU.add, None, [('g_tmp',)], [('g_tmp',)])
                    act(g_tmp[:], g_tmp[:], AF.Ln, [('g_tmp',)], [('g_tmp',)])
                    act(g_eal[:], ALOGB, AF.Exp, [('cbc',)], [('g_eal',)])
                    stt(g_g[:], g_tmp[:], -1.0, g_eal[:], ALU.mult, ALU.mult, [('g_tmp',), ('g_eal',)], [('g_g',)])
                    mm(ps[7][:, 128:192], Umat, g_g[:], True, True, [('cmat',), ('g_g',)], [('ps', 7)])
                    mm(ps[7][:, 192:256], onesf[:], g_g[:], True, True, [('onesf',), ('g_g',)], [('ps', 7)])
                    cp('dve', g_gc[:], ps[7][:, 128:192], [('ps', 7)], [('g_gc',)])
                    ts('dve', g_ngc[:], g_gc[:], -1.0, None, ALU.mult, None, [('g_gc',)], [('g_ngc',)])
                    act(g_eg[:], g_gc[:], AF.Exp, [('g_gc',)], [('g_eg',)])
                    tt('dve', g_egl[:], ps[7][:, 192:256], g_gc[:], ALU.subtract, [('ps', 7), ('g_gc',)], [('g_egl',)])
                    act(g_egl[:], g_egl[:], AF.Exp, [('g_egl',)], [('g_egl',)])
                    act(g_egla[:], ps[7][:, 192:256], AF.Exp, [('ps', 7)], [('g_egla',)])
                    GS = [('g_beta',), ('g_gc',), ('g_ngc',), ('g_eg',), ('g_egl',), ('g_egla',), ('g_g',)]

                    wcnt = [0]

                    def load_wchunk(c0):
                        sl = wcnt[0] % 3
                        wcnt[0] += 1
                        T.dma('pool', wst[sl][:], w_in_v[:, :, c0:c0 + 128], writes=[('wst', sl)], key=('wst', sl))
                        return sl

                    def proj_block(sl, tb, bank):
                        gtb = cur_half[0] * NBLK + tb
                        for kc in range(8):
                            mm(ps[bank][:, :], wst[sl][:, kc, :], xhT[:, kc, gtb * 512:(gtb + 1) * 512], kc == 0, kc == 7,
                               [('wst', sl)] + [('xhT', gtb * 4 + i) for i in range(4)], [('ps', bank)])

                    trc = [0]

                    def stage_P(ch, tb):
                        kind, h, cidx, sl, ci = ch
                        par = h
                        rp = ci % 2
                        raw = raw2[rp]
                        bank = 6 + tb % 2
                        cs = slice(tb * 512, (tb + 1) * 512)
                        proj_block(sl, tb, bank)
                        if kind == 'z':
                            act(hz[:, par, cs], ps[bank][:, :], AF.Silu, [('ps', bank)], [('hz', par)])
                            return
                        if tb == 0:
                            if cur_half[0] == 0:
                                cp('act', raw[:, 0:3], zero3[:], [('zero3',)], [('raw', rp, -1)])
                            else:
                                cp('act', raw[:, 0:3], halo_g[:, cidx, :], [('halo_g', cidx)], [('raw', rp, -1)])
                        cp('act', raw[:, 3 + tb * 512: 3 + (tb + 1) * 512], ps[bank][:, :], [('ps', bank)], [('raw', rp, tb)])
                        if tb == NBLK - 1 and cur_half[0] == 0:
                            cp('act', halo_g[:, cidx, :], raw[:, HS:HS + 3], [('raw', rp, tb)], [('halo_g', cidx)])

                    def stage_C(ch, tb):
                        kind, h, cidx, sl, ci = ch
                        if kind == 'z':
                            return
                        rp = ci % 2
                        raw = raw2[rp]
                        cs = slice(tb * 512, (tb + 1) * 512)
                        RK = [('raw', rp, tb), ('raw', rp, tb - 1), ('cpp',)]
                        ka = ('acc', tb)
                        ts('dve', acc[:, cs], raw[:, 3 + tb * 512: 3 + (tb + 1) * 512], gcw[:, cidx, 3:4], None, ALU.mult, None, RK, [ka])
                        for j in (2, 1, 0):
                            stt(acc[:, cs], raw[:, j + tb * 512: j + (tb + 1) * 512], gcw[:, cidx, j:j + 1], acc[:, cs], ALU.mult, ALU.add,
                                RK + [ka], [ka])

                    def stage_S(ch, tb):
                        kind, h, cidx, sl, ci = ch
                        if kind == 'z':
                            return
                        sp_ = ci % 2
                        cs = slice(tb * 512, (tb + 1) * 512)
                        act(sil2[sp_][:, cs], acc[:, cs], AF.Silu, [('acc', tb)], [('sil', sp_, tb)])

                    def stage_N(ch):
                        kind, h, cidx, sl, ci = ch
                        if kind not in ('q', 'k'):
                            return
                        par = h
                        sp_ = ci % 2
                        sl_ = sil2[sp_]
                        for tb in range(NBLK):
                            cs = slice(tb * 512, (tb + 1) * 512)
                            tt('pool', sqb[:, cs], sl_[:, cs], sl_[:, cs], ALU.mult, [('sil', sp_, tb)], [('sqb', tb)])
                            mm(ps[2 + tb][:, :], onesb[:], sqb[:, cs], True, True, [('onesb',), ('sqb', tb)], [('ps', 2 + tb)])
                        for tb in range(NBLK):
                            act(rstd[tb][:], ps[2 + tb][:, :], AF.Ln, [('ps', 2 + tb), ('epst',)], [('rstd', tb)], bias=eps1)
                        for tb in range(NBLK):
                            act(rstd[tb][:], rstd[tb][:], AF.Exp, [('rstd', tb)], [('rstd', tb)], scale=-0.5)
                        for tb in range(NBLK):
                            cs = slice(tb * 512, (tb + 1) * 512)
                            ks = ('sil', sp_, tb)
                            if kind == 'q':
                                stt(hq[:, par, cs], sl_[:, cs], float(128 ** -0.5), rstd[tb][:], ALU.mult, ALU.mult,
                                    [ks, ('rstd', tb)], [('hq', par)])
                            else:
                                tt('dve', sl_[:, cs], sl_[:, cs], rstd[tb][:], ALU.mult, [ks, ('rstd', tb)], [ks])
                                cp('act', hk[:, par, cs], sl_[:, cs], [ks], [('hk', par)])

                    def stage_T(ch, tb):
                        kind, h, cidx, sl, ci = ch
                        if kind not in ('k', 'v'):
                            return
                        par = h
                        sp_ = ci % 2
                        ks = ('sil', sp_, tb)
                        if kind == 'v':
                            b3 = trc[0] % 2
                            trc[0] += 1
                            for tl in range(4):
                                n = tb * 4 + tl
                                tr(ps[b3][:, tl * 128:(tl + 1) * 128], sil2[sp_][:, n * 128:(n + 1) * 128], [ks], [('ps', b3)], inc=(tl == 3))
                            cp('act' if tb % 2 else 'dve', hv[:, par, tb * 4:(tb + 1) * 4, :],
                               ps[b3][:, :].rearrange("p (t c) -> p t c", t=4), [('ps', b3)], [('hv', par)])
                            return
                        for tl in range(4):
                            n = tb * 4 + tl
                            c = (cur_half[0] * NTH + n) * 4 + h
                            b3 = trc[0] % 2
                            trc[0] += 1
                            tr(ps[b3][:, 0:128], sil2[sp_][:, n * 128:(n + 1) * 128], [ks], [('ps', b3)])
                            if kind == 'k':
                                amul(hkg[:, par, n, :], ps[b3][:, 0:128], g_eg[:, c:c + 1], [('ps', b3), ('g_eg',)], [('hkg', par)])
                                ts('dve', hkd[:, par, n, :], ps[b3][:, 0:128], g_egl[:, c:c + 1], None, ALU.mult, None,
                                   [('ps', b3), ('g_egl',)], [('hkd', par)])
                            else:
                                cp('act' if n % 2 else 'dve', hv[:, par, n, :], ps[b3][:, 0:128], [('ps', b3)], [('hv', par)])

                    def run_A_quad():
                        chs = []
                        for h in range(4):
                            for kind, c0, cidx in (('q', h * 128, h), ('k', 512 + h * 128, 4 + h), ('v', 1024 + h * 128, 8 + h), ('z', 1536 + h * 128, 0)):
                                chs.append([kind, h, cidx, None, len(chs), c0])
                        nch = len(chs)
                        loaded = [0]

                        def ensure_loaded(upto):
                            while loaded[0] <= min(upto, nch - 1):
                                ch_ = chs[loaded[0]]
                                ch_[3] = load_wchunk(ch_[5])
                                loaded[0] += 1
                        nit = NBLK * nch
                        for tau in range(nit + NBLK + 5):
                            if tau < nit:
                                i, tb = divmod(tau, NBLK)
                                if tb == 0:
                                    ensure_loaded(i + 1)
                                stage_P(tuple(chs[i][:5]), tb)
                            if 0 <= tau - 1 < nit:
                                i, tb = divmod(tau - 1, NBLK)
                                stage_C(tuple(chs[i][:5]), tb)
                            if 0 <= tau - 2 < nit:
                                i, tb = divmod(tau - 2, NBLK)
                                stage_S(tuple(chs[i][:5]), tb)
                            tn = tau - (NBLK + 2)
                            if tn >= 0 and tn % NBLK == 0 and tn // NBLK < nch:
                                stage_N(tuple(chs[tn // NBLK][:5]))
                            if 0 <= tau - (NBLK + 3) < nit:
                                i, tb = divmod(tau - (NBLK + 3), NBLK)
                                stage_T(tuple(chs[i][:5]), tb)

                    def H4(b):
                        return ps[b][:, :].rearrange("p (i c) -> p i c", i=4), ('ps', b)

                    def rec_quad(half):
                        if half == 0:
                            T.issue('pool', lambda e: e.memset(S32[:], 0.0), writes=[('S32',)])
                            T.issue('pool', lambda e: e.memset(Sbf[:], 0.0), writes=[('Sbf',)])
                        pGb, kGb = H4(0)
                        pX, kX = H4(6)
                        pKK, kKK = H4(1)
                        pV, kV = H4(1)
                        pQK, kQK = H4(2)
                        pO, kO = H4(2)
                        pNT, kNT = H4(3)
                        pS, kS = H4(3)
                        pP, kP = H4(4)
                        pR, kR = H4(4)
                        pPT, kPT = H4(5)
                        pW, kW = H4(0)
                        for n in range(NTH):
                            tok = slice(n * 128, (n + 1) * 128)
                            gtok = slice((half * NTH + n) * 128, (half * NTH + n + 1) * 128)
                            cc = [(half * NTH + n) * 4 + i for i in range(4)]
                            for i in range(4):
                                amul(Ug[:, i, :], Umat, g_g[:, cc[i]:cc[i] + 1], [('cmat',), ('g_g',)], [('Ug',)])
                            for i in range(4):
                                mm(pGb[:, i, :], onesf[:], Ug[:, i, :], True, True, [('onesf',), ('Ug',)], [kGb], inc=(i == 3))
                            tt('dve', ARG2[:], pGb, NEGMS4, ALU.add, [kGb, ('cmat',)], [('ARG2',)])
                            tt('dve', ARG[:], pGb, NEGM4, ALU.add, [kGb, ('cmat',)], [('ARG',)])
                            act(EGb[:], pGb, AF.Exp, [kGb], [('EGb',)])
                            for i in range(4):
                                mm(pKK[:, i, :], hk[:, i, tok], hk[:, i, tok], True, True, [('hk', i)], [kKK], inc=(i == 3))
                            for i in range(4):
                                mm(pQK[:, i, :], hk[:, i, tok], hq[:, i, tok], True, True, [('hk', i), ('hq', i)], [kQK], inc=(i == 3))
                            for i in range(4):
                                act(DTs[:, i, :], ARG2[:, i, :], AF.Exp, [('ARG2',), ('g_ngc',)], [('DTs',)], bias=g_ngc[:, cc[i]:cc[i] + 1])
                            for i in range(4):
                                act(DT[:, i, :], ARG[:, i, :], AF.Exp, [('ARG',), ('g_ngc',)], [('DT',)], bias=g_ngc[:, cc[i]:cc[i] + 1])
                            for i in range(4):
                                stt(Nf[:, i, :], pKK[:, i, :], g_beta[:, cc[i]:cc[i] + 1], DTs[:, i, :], ALU.mult, ALU.mult,
                                    [kKK, ('g_beta',), ('DTs',)], [('Nf',)])
                            for i in range(4):
                                tr(pNT[:, i, :], Nf[:, i, :], [('Nf',)], [kNT], inc=(i == 3))
                            cur = 0
                            cp('act', Pb[cur][:], Nf[:], [('Nf',)], [('Pb0',)])
                            cp('dve', PTb[cur][:], pNT, [kNT], [('PTb0',)])
                            tt('dve', Xb[cur][:], ident4, Nf[:], ALU.subtract, [('cmat',), ('Nf',)], [('Xb0',)])
                            tt('dve', QKD[:], pQK, DT[:], ALU.mult, [kQK, ('DT',)], [('QKD',)])
                            tt('pool', qgT[:], hq[:, :, tok], EGb[:], ALU.mult, [('hq', i_) for i_ in range(4)] + [('EGb',)], [('qgT',)])
                            xc = 0

                            def x_update(ptb_idx, step_):
                                nonlocal_xc = x_state[0]
                                xn = 1 - nonlocal_xc
                                for i in range(4):
                                    mm(pX[:, i, :], identb[:], Xb[nonlocal_xc][:, i, :], True, False, [('identb',), (f'Xb{nonlocal_xc}',)], [kX], inc=False)
                                    mm(pX[:, i, :], PTb[ptb_idx][:, i, :], Xb[nonlocal_xc][:, i, :], False, True,
                                       [(f'PTb{ptb_idx}',), (f'Xb{nonlocal_xc}',)], [kX], inc=(i == 3))
                                x_state[0] = xn
                                return xn
                            x_state = [0]
                            pend = None
                            for step in range(6):
                                nx = 1 - cur
                                kPc, kPTc = (f'Pb{cur}',), (f'PTb{cur}',)
                                kPn, kPTn = (f'Pb{nx}',), (f'PTb{nx}',)
                                for i in range(4):
                                    mm(pPT[:, i, :], Pb[cur][:, i, :], PTb[cur][:, i, :], True, True, [kPc, kPTc], [kPT], inc=(i == 3))
                                if step < 5:
                                    for i in range(4):
                                        mm(pP[:, i, :], PTb[cur][:, i, :], Pb[cur][:, i, :], True, True, [kPc, kPTc], [kP], inc=(i == 3))
                                if pend is not None:
                                    xn = x_update(pend, step)
                                cp('dve', PTb[nx][:], pPT, [kPT], [kPTn])
                                if step < 5:
                                    cp('act', Pb[nx][:], pP, [kP], [kPn])
                                if pend is not None:
                                    cp('act' if step % 2 else 'dve', Xb[xn][:], pX, [kX], [(f'Xb{xn}',)])
                                pend = nx
                                cur = nx
                            xn = x_update(pend, 6)
                            cp('act', Xb[xn][:], pX, [kX], [(f'Xb{xn}',)])
                            cur = xn
                            kT2 = (f'Xb{cur}',)
                            T2T = Xb[cur]
                            for i in range(4):
                                mm(pW[:, i, :], hkg[:, i, n, :], T2T[:, i, :], True, True, [('hkg', i), kT2], [kW], inc=(i == 3))
                            amul(nw2T[:], pW, -1.0, [kW], [('nw2T',)])
                            for i in range(4):
                                mm(pV[:, i, :], T2T[:, i, :], hv[:, i, n, :], True, False, [kT2, ('hv', i)], [kV], inc=False)
                                mm(pV[:, i, :], nw2T[:, i, :], Sbf[:, i, :], False, True, [('nw2T',), ('Sbf',)], [kV], inc=(i == 3))
                            for i in range(4):
                                ts('dve', vnew[:, i, :], pV[:, i, :], g_beta[:, cc[i]:cc[i] + 1], None, ALU.mult, None, [kV, ('g_beta',)], [('vnew',)])
                            for i in range(4):
                                mm(pS[:, i, :], hkd[:, i, n, :], vnew[:, i, :], True, True, [('hkd', i), ('vnew',)], [kS], inc=(i == 3))
                            for i in range(4):
                                mm(pO[:, i, :], Sbf[:, i, :], qgT[:, i, :], True, False, [('Sbf',), ('qgT',)], [kO], inc=False)
                                mm(pO[:, i, :], vnew[:, i, :], QKD[:, i, :], False, True, [('vnew',), ('QKD',)], [kO], inc=(i == 3))
                            for i in range(4):
                                stt(S32[:, i, :], S32[:, i, :], g_egla[:, cc[i]:cc[i] + 1], pS[:, i, :], ALU.mult, ALU.add,
                                    [('S32',), ('g_egla',), kS], [('S32',)])
                            cp('act', Sbf[:], S32[:], [('S32',)], [('Sbf',)])
                            act(sqo[:], pO, AF.Square, [kO], [('sqo',)])
                            for i in range(4):
                                mm(pR[:, i, :], c128b[:], sqo[:, i, :], True, True, [('c128b',), ('sqo',)], [kR], inc=(i == 3))
                            rsqrt(rso[:], pR, eps1, [kR], [('rso',)])
                            stt(o1[:], pO, normg, rso[:], ALU.mult, ALU.mult, [kO, ('cpp',), ('rso',)], [('o1',)])
                            tt('pool', mixT[:, 0:4, gtok], o1[:], hz[:, :, tok], ALU.mult, [('o1',)] + [('hz', i_) for i_ in range(4)],
                               [('mixT', i_) for i_ in range(4)])

                    def gen_rec_pair(half, pr):
                        ia = 2 * pr
                        ii = (ia, ia + 1)
                        sl_ = slice(ia, ia + 2)
                        B = 4 * pr

                        def HB(b, hf):
                            return ps[b][:, hf * 256:(hf + 1) * 256].rearrange("p (i c) -> p i c", i=2), ('ps', b)
                        pGb, kGb = HB(B, 0)
                        pW, kW = HB(B, 0)
                        pP, kP = HB(B, 1)
                        pR, kR = HB(B, 1)
                        pKK, kKK = HB(B + 1, 0)
                        pV, kV = HB(B + 1, 0)
                        pPT, kPT = HB(B + 1, 1)
                        pQK, kQK = HB(B + 2, 0)
                        pO, kO = HB(B + 2, 0)
                        pX, kX = HB(B + 2, 1)
                        pNT, kNT = HB(B + 3, 0)
                        pS, kS = HB(B + 3, 0)
                        K = lambda nm: (nm, pr)
                        last = ia + 1
                        for n in range(NTH):
                            tok = slice(n * 128, (n + 1) * 128)
                            gtok = slice((half * NTH + n) * 128, (half * NTH + n + 1) * 128)
                            cc = {i: (half * NTH + n) * 4 + i for i in ii}
                            for i in ii:
                                amul(Ug[:, i, :], Umat, g_g[:, cc[i]:cc[i] + 1], [('cmat',), ('g_g',)], [K('Ug')])
                            yield
                            for i in ii:
                                mm(pGb[:, i - ia, :], onesf[:], Ug[:, i, :], True, True, [('onesf',), K('Ug')], [kGb], inc=(i == last))
                            yield
                            tt('dve', ARG2[:, sl_, :], pGb, NEGMS4[:, 0:2, :], ALU.add, [kGb, ('cmat',)], [K('ARG2')])
                            tt('dve', ARG[:, sl_, :], pGb, NEGM4[:, 0:2, :], ALU.add, [kGb, ('cmat',)], [K('ARG')])
                            act(EGb[:, sl_, :], pGb, AF.Exp, [kGb], [K('EGb')])
                            for i in ii:
                                mm(pKK[:, i - ia, :], hk[:, i, tok], hk[:, i, tok], True, True, [('hk', i)], [kKK], inc=(i == last))
                            for i in ii:
                                mm(pQK[:, i - ia, :], hk[:, i, tok], hq[:, i, tok], True, True, [('hk', i), ('hq', i)], [kQK], inc=(i == last))
                            yield
                            for i in ii:
                                act(DTs[:, i, :], ARG2[:, i, :], AF.Exp, [K('ARG2'), ('g_ngc',)], [K('DTs')], bias=g_ngc[:, cc[i]:cc[i] + 1])
                            for i in ii:
                                act(DT[:, i, :], ARG[:, i, :], AF.Exp, [K('ARG'), ('g_ngc',)], [K('DT')], bias=g_ngc[:, cc[i]:cc[i] + 1])
                            yield
                            for i in ii:
                                stt(Nf[:, i, :], pKK[:, i - ia, :], g_beta[:, cc[i]:cc[i] + 1], DTs[:, i, :], ALU.mult, ALU.mult,
                                    [kKK, ('g_beta',), K('DTs')], [K('Nf')])
                            yield
                            for i in ii:
                                tr(pNT[:, i - ia, :], Nf[:, i, :], [K('Nf')], [kNT], inc=(i == last))
                            cur = 0
                            cp('act', Pb[cur][:, sl_, :], Nf[:, sl_, :], [K('Nf')], [K('Pb0')])
                            yield
                            cp('dve', PTb[cur][:, sl_, :], pNT, [kNT], [K('PTb0')])
                            tt('dve', Xb[cur][:, sl_, :], ident4[:, 0:2, :], Nf[:, sl_, :], ALU.subtract, [('cmat',), K('Nf')], [K('Xb0')])
                            tt('dve', QKD[:, sl_, :], pQK, DT[:, sl_, :], ALU.mult, [kQK, K('DT')], [K('QKD')])
                            tt('pool', qgT[:, sl_, :], hq[:, sl_, tok], EGb[:, sl_, :], ALU.mult, [('hq', i_) for i_ in ii] + [K('EGb')], [K('qgT')])
                            yield
                            xs = [0]

                            def x_update(ptb_idx):
                                xc_ = xs[0]
                                xn_ = 1 - xc_
                                for i in ii:
                                    mm(pX[:, i - ia, :], identb[:], Xb[xc_][:, i, :], True, False, [('identb',), K(f'Xb{xc_}')], [kX], inc=False)
                                    mm(pX[:, i - ia, :], PTb[ptb_idx][:, i, :], Xb[xc_][:, i, :], False, True,
                                       [K(f'PTb{ptb_idx}'), K(f'Xb{xc_}')], [kX], inc=(i == last))
                                xs[0] = xn_
                                return xn_
                            pend = None
                            for step in range(6):
                                nx = 1 - cur
                                kPc, kPTc = K(f'Pb{cur}'), K(f'PTb{cur}')
                                kPn, kPTn = K(f'Pb{nx}'), K(f'PTb{nx}')
                                for i in ii:
                                    mm(pPT[:, i - ia, :], Pb[cur][:, i, :], PTb[cur][:, i, :], True, True, [kPc, kPTc], [kPT], inc=(i == last))
                                if step < 5:
                                    for i in ii:
                                        mm(pP[:, i - ia, :], PTb[cur][:, i, :], Pb[cur][:, i, :], True, True, [kPc, kPTc], [kP], inc=(i == last))
                                if pend is not None:
                                    xn = x_update(pend)
                                yield
                                cp('dve', PTb[nx][:, sl_, :], pPT, [kPT], [kPTn])
                                if step < 5:
                                    cp('act', Pb[nx][:, sl_, :], pP, [kP], [kPn])
                                if pend is not None:
                                    cp('act' if step % 2 else 'dve', Xb[xn][:, sl_, :], pX, [kX], [K(f'Xb{xn}')])
                                yield
                                pend = nx
                                cur = nx
                            xn = x_update(pend)
                            yield
                            cp('act', Xb[xn][:, sl_, :], pX, [kX], [K(f'Xb{xn}')])
                            yield
                            cur = xn
                            kT2 = K(f'Xb{cur}')
                            T2T = Xb[cur]
                            for i in ii:
                                mm(pW[:, i - ia, :], hkg[:, i, n, :], T2T[:, i, :], True, True, [('hkg', i), kT2], [kW], inc=(i == last))
                            yield
                            amul(nw2T[:, sl_, :], pW, -1.0, [kW], [K('nw2T')])
                            yield
                            for i in ii:
                                mm(pV[:, i - ia, :], T2T[:, i, :], hv[:, i, n, :], True, False, [kT2, ('hv', i)], [kV], inc=False)
                                mm(pV[:, i - ia, :], nw2T[:, i, :], Sbf[:, i, :], False, True, [K('nw2T'), K('Sbf')], [kV], inc=(i == last))
                            yield
                            for i in ii:
                                ts('dve', vnew[:, i, :], pV[:, i - ia, :], g_beta[:, cc[i]:cc[i] + 1], None, ALU.mult, None, [kV, ('g_beta',)], [K('vnew')])
                            yield
                            for i in ii:
                                mm(pS[:, i - ia, :], hkd[:, i, n, :], vnew[:, i, :], True, True, [('hkd', i), K('vnew')], [kS], inc=(i == last))
                            for i in ii:
                                mm(pO[:, i - ia, :], Sbf[:, i, :], qgT[:, i, :], True, False, [K('Sbf'), K('qgT')], [kO], inc=False)
                                mm(pO[:, i - ia, :], vnew[:, i, :], QKD[:, i, :], False, True, [K('vnew'), K('QKD')], [kO], inc=(i == last))
                            yield
                            for i in ii:
                                stt(S32[:, i, :], S32[:, i, :], g_egla[:, cc[i]:cc[i] + 1], pS[:, i - ia, :], ALU.mult, ALU.add,
                                    [K('S32'), ('g_egla',), kS], [K('S32')])
                            act(sqo[:, sl_, :], pO, AF.Square, [kO], [K('sqo')])
                            yield
                            cp('act', Sbf[:, sl_, :], S32[:, sl_, :], [K('S32')], [K('Sbf')])
                            for i in ii:
                                mm(pR[:, i - ia, :], c128b[:], sqo[:, i, :], True, True, [('c128b',), K('sqo')], [kR], inc=(i == last))
                            yield
                            rsqrt(rso[:, sl_, :], pR, eps1, [kR], [K('rso')])
                            yield
                            stt(o1[:, sl_, :], pO, normg, rso[:, sl_, :], ALU.mult, ALU.mult, [kO, ('cpp',), K('rso')], [K('o1')])
                            yield
                            tt('pool', mixT[:, sl_, gtok], o1[:, sl_, :], hz[:, sl_, tok], ALU.mult, [K('o1')] + [('hz', i_) for i_ in ii],
                               [('mixT', i_) for i_ in ii])
                            yield

                    def rec_two_chains(half):
                        if half == 0:
                            for pr in range(2):
                                T.issue('pool', lambda e: e.memset(S32[:, 2 * pr:2 * pr + 2, :], 0.0), writes=[('S32', pr)])
                                T.issue('pool', lambda e: e.memset(Sbf[:, 2 * pr:2 * pr + 2, :], 0.0), writes=[('Sbf', pr)])
                        gens = [gen_rec_pair(half, 0), gen_rec_pair(half, 1)]
                        while gens:
                            for g_ in list(gens):
                                try:
                                    next(g_)
                                except StopIteration:
                                    gens.remove(g_)

                    for half in range(2):
                        cur_half[0] = half
                        run_A_quad()
                        if TWO_CHAINS:
                            rec_two_chains(half)
                        else:
                            rec_quad(half)
                    T.barrier()
                dump('mixA', mixT[:, 0:4, :], [('mixT', h) for h in range(4)])
                if stop == 'mixA':
                    return True

                wo_t = sb(p1, "wo_t", [128, 8, 1024], BF16)
                with ExitStack() as sc:
                    ctab_t = sb(sc, "ctab_t", [64, 2 * S], F32)
                    T.dma('sp', ctab_t[:], ctab[:, :], writes=[('ctab',)], key=('ctab',))
                    cos2 = ctab_t[:, 0:S]
                    sin2 = ctab_t[:, S:2 * S]
                    wq_t = sb(sc, "wq_t", [128, 3, 768], BF16)
                    wqr_t = sb(sc, "wqr_t", [128, 3, 4, 64], BF16)
                    wkv_t = sb(sc, "wkv_t", [128, 2, 1024], BF16)
                    wkr_t = sb(sc, "wkr_t", [128, 8, 128], BF16)
                    wst = [sb(sc, f"wstm{i}", [128, 8, 128], BF16) for i in range(3)]
                    cqg = sb(sc, "cqg", [128, 3, S], BF16)
                    ckvg = sb(sc, "ckvg", [128, 2, S], BF16)
                    sqr = [sb(sc, f"sqr{i}", [128, 512], BF16) for i in range(2)]
                    rsq = sb(sc, "rsq", [128, S], F32)
                    rskv = sb(sc, "rskv", [128, S], F32)
                    krT = sb(sc, "krT", [64, S], BF16)
                    t1 = [sb(sc, f"rt1_{i}", [64, 512], F32) for i in range(2)]
                    t2 = [sb(sc, f"rt2_{i}", [64, 512], F32) for i in range(2)]
                    qn = [sb(sc, f"qn{i}", [128, S], BF16) for i in range(1)]
                    qr = [sb(sc, f"qr{i}", [64, S], BF16) for i in range(1)]
                    kn = [sb(sc, f"kn{i}", [128, S], BF16) for i in range(1)]
                    vh = [sb(sc, f"vh{i}", [128, NT, 128], BF16) for i in range(1)]
                    PT = [sb(sc, f"PTt{i}", [128, 512], BF16) for i in range(3)]
                    den = [sb(sc, f"den{i}", [128, 512], F32) for i in range(2)]

                    wcnt = [0]

                    def load_wchunk2(c0):
                        sl = wcnt[0] % 3
                        wcnt[0] += 1
                        T.dma('pool', wst[sl][:], w_in_v[:, :, c0:c0 + 128], writes=[('wst', sl)], key=('wstm', sl))
                        return sl
                    pre_sl = {0: load_wchunk2(2056), 1: load_wchunk2(2056 + 128)}
                    T.dma('pool', wkr_t[:, :, 0:64], w_in_v[:, :, 2696:2760], writes=[('wkr',)], key=('wkr',))
                    T.dma('pool', wq_t[:], wq_v[:, :, :], writes=[('wq',)], key=('wq',))
                    T.dma('pool', wkv_t[:], wkv_v[:, :, :], writes=[('wkv',)], key=('wkv',))
                    ts('dve', wkr_t[:, :, 64:96], wkr_t[:, :, 32:64], -1.0, None, ALU.mult, None, [('wkr',)], [('wkr2',)])
                    cp('dve', wkr_t[:, :, 96:128], wkr_t[:, :, 0:32], [('wkr',)], [('wkr2',)])
                    for h in range(4):
                        ts('dve', wqr_t[:, :, h, 0:32], wq_t[:, :, h * 192 + 160:h * 192 + 192], -1.0, None, ALU.mult, None, [('wq',)], [('wqr',)])
                        cp('dve', wqr_t[:, :, h, 32:64], wq_t[:, :, h * 192 + 128:h * 192 + 160], [('wq',)], [('wqr',)])

                    XH = lambda tb: [('xhT', tb * 4 + i) for i in range(4)]
                    sqcnt = [0]
                    pend_n = [None]
                    for ci in range(5):
                        isq = ci < 3
                        cc = ci if isq else ci - 3
                        last = cc == (2 if isq else 1)
                        c0 = 2056 + ci * 128
                        if ci + 2 < 5:
                            pre_sl[ci + 2] = load_wchunk2(c0 + 256)
                        if ci == 0:
                            T.dma('pool', wo_t[:], w_out_v[:, :, :], writes=[('wo',)], key=('wo',))
                        sl = pre_sl[ci]
                        nrm = onesb if isq else c256b
                        knrm = ('onesb',) if isq else ('c256b',)
                        for tb in range(4):
                            bank = tb % 2
                            cs = slice(tb * 512, (tb + 1) * 512)
                            for kc in range(8):
                                mm(ps[bank][:, :], wst[sl][:, kc, :], xhT[:, kc, cs], kc == 0, kc == 7,
                                   [('wst', sl)] + XH(tb), [('ps', bank)])
                            if isq:
                                ts('dve', cqg[:, cc, cs], ps[bank][:, :], qg[:, cc:cc + 1], float(384 ** 0.5), ALU.mult, ALU.mult,
                                   [('ps', bank), ('cpp',)], [('cqg',)])
                            else:
                                ts('dve', ckvg[:, cc, cs], ps[bank][:, :], kvg[:, cc:cc + 1], None, ALU.mult, None,
                                   [('ps', bank), ('cpp',)], [('ckvg',)])
                            sqi = sqcnt[0] % 2
                            sqcnt[0] += 1
                            act(sqr[sqi][:], ps[bank][:, :], AF.Square, [('ps', bank)], [('sqr', sqi)])
                            if pend_n[0] is not None:
                                pend_n[0]()

                            def _norm(tb=tb, cs=cs, nrm=nrm, knrm=knrm, sqi=sqi, cc=cc, last=last, isq=isq):
                                mm(ps[2 + tb][:, :], nrm[:], sqr[sqi][:], cc == 0, last, [knrm, ('sqr', sqi)], [('ps', 2 + tb)], inc=True)
                                if last:
                                    if isq:
                                        rsqrt(rsq[:, cs], ps[2 + tb][:, :], eps384, [('ps', 2 + tb)], [('rsq',)])
                                    else:
                                        rsqrt(rskv[:, cs], ps[2 + tb][:, :], eps1, [('ps', 2 + tb)], [('rskv',)])
                                        for c_ in range(2):
                                            tt('dve', ckvg[:, c_, cs], ckvg[:, c_, cs], rskv[:, cs], ALU.mult, [('ckvg',), ('rskv',)], [('ckvg',)])
                            pend_n[0] = _norm
                        if ci == 4:
                            pend_n[0]()
                            pend_n[0] = None
                    for tb in range(4):
                        cs = slice(tb * 512, (tb + 1) * 512)
                        bA, bB = (5, 6) if tb % 2 == 0 else (0, 1)
                        for kc in range(8):
                            mm(ps[bA][0:64, :], wkr_t[:, kc, 0:64], xhT[:, kc, cs], kc == 0, kc == 7, [('wkr',)] + XH(tb), [('ps', bA)])
                        for kc in range(8):
                            mm(ps[bB][0:64, :], wkr_t[:, kc, 64:128], xhT[:, kc, cs], kc == 0, kc == 7, [('wkr2',)] + XH(tb), [('ps', bB)])
                        tt('dve', t1[tb % 2][:], ps[bA][0:64, :], cos2[:, cs], ALU.mult, [('ps', bA), ('ctab',)], [('t1', tb % 2)])
                        tt('dve', t2[tb % 2][:], ps[bB][0:64, :], sin2[:, cs], ALU.mult, [('ps', bB), ('ctab',)], [('t2', tb % 2)])
                        tt('pool', krT[:, cs], t1[tb % 2][:], t2[tb % 2][:], ALU.add, [('t1', tb % 2), ('t2', tb % 2)], [('krT',)])
                    scale = float(192 ** -0.5)
                    ptc = [0]
                    for h in range(4):
                        par = 0
                        for tb in range(4):
                            cs = slice(tb * 512, (tb + 1) * 512)
                            b0, b1, b5, b6 = (0, 1, 5, 6) if tb % 2 == 0 else (2, 3, 4, 7)
                            for kc in range(3):
                                mm(ps[b0][:, :], wq_t[:, kc, h * 192:h * 192 + 128], cqg[:, kc, cs], kc == 0, kc == 2, [('wq',), ('cqg',)], [('ps', b0)])
                            for kc in range(3):
                                mm(ps[b5][0:64, :], wq_t[:, kc, h * 192 + 128:h * 192 + 192], cqg[:, kc, cs], kc == 0, kc == 2, [('wq',), ('cqg',)], [('ps', b5)])
                            for kc in range(3):
                                mm(ps[b6][0:64, :], wqr_t[:, kc, h, :], cqg[:, kc, cs], kc == 0, kc == 2, [('wqr',), ('cqg',)], [('ps', b6)])
                            for kc in range(2):
                                mm(ps[b1][:, :], wkv_t[:, kc, h * 256:h * 256 + 128], ckvg[:, kc, cs], kc == 0, kc == 1, [('wkv',), ('ckvg',)], [('ps', b1)])
                            tt('dve', qn[par][:, cs], ps[b0][:, :], rsq[:, cs], ALU.mult, [('ps', b0), ('rsq',)], [('qn', par)])
                            tt('dve', t1[tb % 2][:], ps[b5][0:64, :], cos2[:, cs], ALU.mult, [('ps', b5), ('ctab',)], [('t1', tb % 2)])
                            tt('dve', t2[tb % 2][:], ps[b6][0:64, :], sin2[:, cs], ALU.mult, [('ps', b6), ('ctab',)], [('t2', tb % 2)])
                            cp('act', kn[par][:, cs], ps[b1][:, :], [('ps', b1)], [('kn', par)])
                            tt('pool', t1[tb % 2][:], t1[tb % 2][:], t2[tb % 2][:], ALU.add, [('t1', tb % 2), ('t2', tb % 2)], [('t1', tb % 2)])
                            tt('pool', qr[par][:, cs], t1[tb % 2][:], rsq[0:64, cs], ALU.mult, [('t1', tb % 2), ('rsq',)], [('qr', par)])
                        for g4 in range(NT // 4):
                            bank = 2 + g4 % 2
                            for tl in range(4):
                                t = g4 * 4 + tl
                                for kc in range(2):
                                    mm(ps[bank][:, tl * 128:(tl + 1) * 128], ckvg[:, kc, t * 128:(t + 1) * 128],
                                       wkv_t[:, kc, h * 256 + 128:h * 256 + 256], kc == 0, kc == 1,
                                       [('wkv',), ('ckvg',)], [('ps', bank)], inc=(kc == 1 and tl == 3))
                            cp('act' if g4 % 2 else 'dve', vh[par][:, g4 * 4:(g4 + 1) * 4, :],
                               ps[bank][:, :].rearrange("p (t c) -> p t c", t=4), [('ps', bank)], [('vh', par)])
                        items = [(qb, kt) for qb in range(4) for kt in range(4 * qb + 4)]

                        def att_S(idx):
                            qb, kt = items[idx]
                            r = max(0, kt - 4 * qb)
                            c0 = qb * 512 + r * 128
                            ncol = 512 - r * 128
                            sb_ = ps[idx % 2]
                            ksb = ('ps', idx % 2)
                            mm(sb_[:, 0:ncol], kn[par][:, kt * 128:(kt + 1) * 128], qn[par][:, c0:c0 + ncol], True, False,
                               [('kn', par), ('qn', par)], [ksb], inc=False)
                            mm(sb_[:, 0:ncol], krT[:, kt * 128:(kt + 1) * 128], qr[par][:, c0:c0 + ncol], False, True,
                               [('krT',), ('qr', par)], [ksb])

                        def att_EV(idx):
                            qb, kt = items[idx]
                            nk = 4 * qb + 4
                            r = max(0, kt - 4 * qb)
                            ncol = 512 - r * 128
                            sb_ = ps[idx % 2]
                            ksb = ('ps', idx % 2)
                            pO_, pD_ = ps[4 + (qb % 2) * 2], ps[5 + (qb % 2) * 2]
                            kO, kD = ('ps', 4 + (qb % 2) * 2), ('ps', 5 + (qb % 2) * 2)
                            pi = ptc[0] % 3
                            ptc[0] += 1
                            act(PT[pi][:, 0:ncol], sb_[:, 0:ncol], AF.Exp, [ksb], [('PT', pi)], scale=scale)
                            if kt >= 4 * qb:
                                T.issue('pool', lambda e: e.memset(PT[pi][64:128, 0:64], 0.0), [], [('PT', pi)])
                            mm(pO_[:, r * 128:512], vh[par][:, kt, :], PT[pi][:, 0:ncol], kt == 0, kt == nk - 1, [('vh', par), ('PT', pi)], [kO])
                            mm(pD_[:, r * 128:512], onesb[:], PT[pi][:, 0:ncol], kt == 0, kt == nk - 1, [('onesb',), ('PT', pi)], [kD])
                            if kt == nk - 1:
                                dq = den[qb % 2]
                                T.issue('dve', lambda e: e.reciprocal(out=dq[:], in_=pD_[:, :]), [kD], [('den', qb % 2)])
                                tt('dve', mixT[:, 4 + h, qb * 512:(qb + 1) * 512], pO_[:, :], dq[:], ALU.mult, [kO, ('den', qb % 2)], [('mixT', 4 + h)])

                        att_S(0)
                        for idx in range(len(items)):
                            if idx + 1 < len(items):
                                att_S(idx + 1)
                            att_EV(idx)
                    T.barrier()
                dump('mixB', mixT[:, 4:8, :], [('mixT', 4 + h) for h in range(4)])
                if stop == 'mixB':
                    return True

                with ExitStack() as sc:
                    ln1_t = sb(sc, "ln1_t", [128, 2048], F32)
                    T.dma('sp', ln1_t[:], cbc[:, 0:2048], writes=[('cbcL',)], key=('cbcL',))
                    G1 = ln1_t[:, 0:1024]
                    B1 = ln1_t[:, 1024:2048]
                    xr = [sb(sc, f"xr{i}", [128, D], F32) for i in range(3)]
                    rr = [sb(sc, f"rr{i}", [128, D], F32) for i in range(3)]
                    hh = [sb(sc, f"hh{i}", [128, D], F32) for i in range(2)]
                    stats3 = [sb(sc, f"stats{i}", [128, 12], F32) for i in range(3)]
                    mv3 = [sb(sc, f"mv{i}", [128, 2], F32) for i in range(3)]
                    rs3 = [sb(sc, f"rs{i}", [128, 1], F32) for i in range(3)]

                    def d1_ldx(t):
                        q_ = t % 3
                        T.dma('sp', xr[q_][:], x[row0 + t * 128: row0 + (t + 1) * 128, :], writes=[('xr', q_)], key=('xr', q_))

                    def d1_mm(t):
                        sl = t % 2
                        tok = slice(t * 128, (t + 1) * 128)
                        for hb in range(2):
                            bank = sl * 2 + hb
                            for kc in range(8):
                                mm(ps[bank][:, :], mixT[:, kc, tok], wo_t[:, kc, hb * 512:(hb + 1) * 512], kc == 0, kc == 7,
                                   [('mixT', kc), ('wo',)], [('ps', bank)])

                    def d1_res(t):
                        sl = t % 2
                        q_ = t % 3
                        for hb in range(2):
                            bank = sl * 2 + hb
                            stt(rr[q_][:, hb * 512:(hb + 1) * 512], xr[q_][:, hb * 512:(hb + 1) * 512], ALPHA, ps[bank][:, :], ALU.mult, ALU.add,
                                [('xr', q_), ('ps', bank)], [('rr', q_)])

                    def d1_lna(t):
                        q_ = t % 3
                        r, stats, mv, rs = rr[q_], stats3[q_], mv3[q_], rs3[q_]
                        kr = ('rr', q_)
                        for hb in range(2):
                            T.issue('dve', lambda e: e.bn_stats(out=stats[:, hb * 6:(hb + 1) * 6], in_=r[:, hb * 512:(hb + 1) * 512]),
                                    reads=[kr], writes=[('lnst', q_)])
                        T.issue('dve', lambda e: e.bn_aggr(out=mv[:], in_=stats[:]), reads=[('lnst', q_)], writes=[('lnmv', q_)])
                        rsqrt(rs[:], mv[:, 1:2], eps1, [('lnmv', q_)], [('lnrs', q_)])
                        stt(mv[:, 1:2], mv[:, 0:1], -1.0, rs[:, 0:1], ALU.mult, ALU.mult, [('lnmv', q_), ('lnrs', q_)], [('lnmv', q_)])
                        act(r[:], r[:], AF.Identity, [kr, ('lnmv', q_), ('lnrs', q_)], [kr], bias=mv[:, 1:2], scale=rs[:, 0:1])

                    def d1_lnb(t):
                        q_ = t % 3
                        sl = t % 2
                        r = rr[q_]
                        kr = ('rr', q_)
                        tt('dve', r[:], r[:], G1, ALU.mult, [kr, ('cbcL',)], [kr])
                        tt('dve', hh[sl][:], r[:], B1, ALU.add, [kr, ('cbcL',)], [('hh', sl)])
                        T.dma('sp', hscr[t * 128:(t + 1) * 128, :], hh[sl][:], reads=[('hh', sl)], writes=[('hscr', t)], key=('hh', sl))

                    def d1_tr(t):
                        sl = t % 2
                        tok = slice(t * 128, (t + 1) * 128)
                        for kc in range(8):
                            bank = 4 + sl * 2 + kc // 4
                            col = (kc % 4) * 128
                            tr(ps[bank][:, col:col + 128], hh[sl][:, kc * 128:(kc + 1) * 128], [('hh', sl)], [('ps', bank)], inc=(kc % 4 == 3))
                        for hb in range(2):
                            bank = 4 + sl * 2 + hb
                            cp('act', xhT[:, hb * 4:(hb + 1) * 4, tok], ps[bank][:, :].rearrange("p (k c) -> p k c", k=4), [('ps', bank)], [('xhT', t)])

                    for t_ in range(3):
                        d1_ldx(t_)
                    d1_mm(0)
                    d1_res(0)
                    d1_mm(1)
                    d1_res(1)
                    d1_lna(0)
                    for t in range(NT):
                        if t + 2 < NT:
                            d1_mm(t + 2)
                        if t + 1 < NT:
                            d1_lna(t + 1)
                        d1_lnb(t)
                        if t + 2 < NT:
                            d1_res(t + 2)
                        if t + 3 < NT:
                            d1_ldx(t + 3)
                        d1_tr(t)
                    T.barrier()
            dump('hT', xhT[:], [('xhT', t) for t in range(NT)])
            if stop == 'hT':
                return True

            with ExitStack() as p2:
                wd_t = sb(p2, "wd_t", [128, NJ, 1024], BF16)
                ln2_t = sb(p2, "ln2_t", [128, 3072], F32)
                T.dma('sp', ln2_t[:], cbc[:, 2048:5120], writes=[('cbcL',)], key=('cbcL2',))
                G2 = ln2_t[:, 0:1024]
                B2 = ln2_t[:, 1024:2048]
                BG = ln2_t[:, 2048:3072]
                wg_t = sb(p2, "wg_t", [128, 8, 1024], BF16)
                wp_t = sb(p2, "wp_t", [128, 2, 1024], BF16)
                actT = sb(p2, "actT", [128, NJ, BLK2], BF16)
                wup = [sb(p2, f"wup{i}", [128, 2, 8, 128], BF16) for i in range(3)]
                rawgu = [sb(p2, f"rawgu{i}", [128, 2, 2 + BLK2], F32) for i in range(2)]
                accg = [sb(p2, f"accg{i}", [128, BLK2], F32) for i in range(2)]
                accu = [sb(p2, f"accu{i}", [128, BLK2], F32) for i in range(2)]
                halo = sb(p2, "halo", [128, 2, NJ, 2], F32)
                hr_ = [sb(p2, f"hr{i}", [128, D], F32) for i in range(2)]
                r2_ = [sb(p2, f"r2{i}", [128, D], F32) for i in range(2)]
                sg_ = [sb(p2, f"sgt{i}", [128, D], F32) for i in range(2)]
                pin_ = [sb(p2, f"pin{i}", [128, 256], F32) for i in range(2)]
                pTb_ = [sb(p2, f"pTb{i}", [128, 2, 128], BF16) for i in range(2)]
                stats_ = [sb(p2, f"stats2{i}", [128, 12], F32) for i in range(2)]
                mv_ = [sb(p2, f"mv2{i}", [128, 2], F32) for i in range(2)]
                rs_ = [sb(p2, f"rs2{i}", [128, 1], F32) for i in range(2)]
                T.issue('pool', lambda e: e.memset(halo[:], 0.0), writes=[('halo', c_) for c_ in range(NJ)])
                def ld2(t_):
                    q_ = t_ % 2
                    T.dma('sp', hr_[q_][:], hscr[t_ * 128:(t_ + 1) * 128, :], reads=[('hscr', t_)], writes=[('hr', q_)], key=('hr', q_))
                    T.dma('sp', pin_[q_][:], p[row0 + t_ * 128:row0 + (t_ + 1) * 128, :], writes=[('pin', q_)], key=('pin', q_))

                def trp(t_):
                    q_ = t_ % 2
                    for kc in range(2):
                        tr(ps[6 + q_][:, kc * 128:(kc + 1) * 128], pin_[q_][:, kc * 128:(kc + 1) * 128], [('pin', q_)], [('ps', 6 + q_)], inc=(kc == 1))
                    cp('act', pTb_[q_][:], ps[6 + q_][:, 0:256].rearrange("p (k c) -> p k c", k=2), [('ps', 6 + q_)], [('pTb', q_)])

                NB = S // BLK2
                TPB = BLK2 // 128
                wc = [0]

                def prefetch_wup(upto):
                    while wc[0] < min(upto, NB * NJ):
                        jj = wc[0] % NJ
                        s_ = wc[0] % 3
                        wc[0] += 1
                        T.dma('pool', wup[s_][:, 0, :, :], w_up_v[:, :, jj * 128:(jj + 1) * 128], writes=[('wup', s_, 0)], key=('wup', s_, 0))
                        T.dma('pool', wup[s_][:, 1, :, :], w_up_v[:, :, DFF + jj * 128:DFF + (jj + 1) * 128], writes=[('wup', s_, 1)], key=('wup', s_, 1))
                prefetch_wup(3)
                T.dma('pool', wg_t[:], w_gate_v[:, :, :], writes=[('wg',)], key=('wg',))
                T.dma('pool', wp_t[:], w_proj_v[:, :, :], writes=[('wp',)], key=('wp',))
                T.dma('pool', wd_t[:], w_down_v[:, :, :], writes=[('wd',)], key=('wd',))

                def stage_B2(j_):
                    q_ = j_ % 2
                    kg_, ku_ = ('acc2', 0, q_), ('acc2', 1, q_)
                    act(accg[q_][:], accg[q_][:], AF.Silu, [kg_], [kg_])
                    tt('dve', actT[:, j_, :], accg[q_][:], accu[q_][:], ALU.mult, [kg_, ku_], [('actT', j_)])

                for blk in range(NB):
                    bs = slice(blk * BLK2, (blk + 1) * BLK2)
                    XB = [('xhT', blk * TPB + i) for i in range(TPB)]
                    for j in range(NJ):
                        step = blk * NJ + j
                        prefetch_wup(step + 3)
                        sl = step % 3
                        jp = j % 2
                        rg = rawgu[jp]
                        kr_ = ('raw2', jp)
                        for gu in range(2):
                            bank = jp * 2 + gu
                            for kc in range(8):
                                mm(ps[bank][:, :], wup[sl][:, gu, kc, :], xhT[:, kc, bs], kc == 0, kc == 7, [('wup', sl, gu)] + XB, [('ps', bank)])
                        cp('act', rg[:, :, 0:2], halo[:, :, j, :], [('halo', j)], [kr_ + ('h',)])
                        cp('act', rg[:, :, 2:2 + BLK2], ps_all[:, jp * 1024:(jp + 1) * 1024].rearrange("p (g c) -> p g c", g=2),
                           [('ps', jp * 2), ('ps', jp * 2 + 1)], [kr_])
                        cp('act', halo[:, :, j, :], rg[:, :, BLK2:BLK2 + 2], [kr_], [('halo', j)])
                        acs = (accg[jp], accu[jp])
                        kas = (('acc2', 0, jp), ('acc2', 1, jp))
                        act(acs[0][:], ps[jp * 2][:, :], AF.Identity, [('ps', jp * 2), ('cpp',)], [kas[0]], bias=fcb[:, j:j + 1], scale=fcw[:, j, 2:3])
                        ts('dve', acs[1][:], rg[:, 1, 2:2 + BLK2], fcw[:, NJ + j, 2:3], fcb[:, NJ + j:NJ + j + 1], ALU.mult, ALU.add, [kr_, ('cpp',)], [kas[1]])
                        for tap in (1, 0):
                            for gu in range(2):
                                cidx = gu * NJ + j
                                stt(acs[gu][:], rg[:, gu, tap:tap + BLK2], fcw[:, cidx, tap:tap + 1], acs[gu][:], ALU.mult, ALU.add,
                                    [kr_, kr_ + ('h',), kas[gu], ('cpp',)], [kas[gu]])
                        if j >= 1:
                            stage_B2(j - 1)
                    stage_B2(NJ - 1)
                    AK = [('actT', j) for j in range(NJ)]
                    if stop == 'p2a' and blk == 0:
                        T.barrier()
                        return True
                    for tl in range(TPB):
                        t = blk * TPB + tl
                        tok = slice(t * 128, (t + 1) * 128)
                        ltok = slice(tl * 128, (tl + 1) * 128)
                        grow = row0 + t * 128
                        tp_ = t % 2
                        hr, r2, sg, pin, pTb = hr_[tp_], r2_[tp_], sg_[tp_], pin_[tp_], pTb_[tp_]
                        kh, kr2, kpin, kpt = ('hr', tp_), ('r2', tp_), ('pin', tp_), ('pTb', tp_)
                        if t == 0:
                            ld2(0)
                            trp(0)
                        if t + 1 < NT:
                            ld2(t + 1)
                        for hb in range(2):
                            hs = slice(hb * 512, (hb + 1) * 512)
                            ksg = ('sg', hb, tp_)
                            for kc in range(8):
                                mm(ps[2 + hb][:, :], xhT[:, kc, tok], wg_t[:, kc, hs], kc == 0, kc == 7, [('xhT', t), ('wg',)], [('ps', 2 + hb)])
                            for kc in range(2):
                                mm(ps[4 + hb][:, :], pTb[:, kc, :], wp_t[:, kc, hs], kc == 0, kc == 1, [kpt, ('wp',)], [('ps', 4 + hb)])
                            for j in range(NJ):
                                mm(ps[hb][:, :], actT[:, j, ltok], wd_t[:, j, hs], j == 0, j == NJ - 1, AK + [('wd',)], [('ps', hb)])
                            tt('dve', sg[:, hs], ps[2 + hb][:, :], BG[:, hs], ALU.add, [('ps', 2 + hb), ('cbcL',)], [ksg])
                            act(sg[:, hs], sg[:, hs], AF.Sigmoid, [ksg], [ksg])
                            tt('dve', sg[:, hs], sg[:, hs], ps[4 + hb][:, :], ALU.mult, [ksg, ('ps', 4 + hb)], [ksg])
                            stt(r2[:, hs], hr[:, hs], ALPHA, ps[hb][:, :], ALU.mult, ALU.add, [kh, ('ps', hb)], [kr2])
                            tt('dve', r2[:, hs], r2[:, hs], sg[:, hs], ALU.add, [kr2, ksg], [kr2])
                        if t + 1 < NT:
                            trp(t + 1)
                        ln_tile(r2, G2, B2, (stats_[tp_], mv_[tp_], rs_[tp_]), kr2, r2[:], kr2, sfx=tp_)
                        T.dma('sp', out[grow:grow + 128, :], r2[:], reads=[kr2], writes=[('out', sq, t)], key=('r2o', tp_))
                    if stop == 'p2b' and blk == 0:
                        T.barrier()
                        return True
                T.barrier()
          for sq in range(NSEQ):
            if seq_body(sq):
                break
        T.final_wait()
    nc._trk_log = T.log
    return nc


def _prep_common(inp):
    f = np.float32
    g = lambda k: np.asarray(inp[k], dtype=f)[0]
    cw = g("gdn_conv_w")
    fw = g("ffn_conv_w")
    fb = g("ffn_conv_b")
    cpp = np.zeros((128, 512), f)
    cpp[:, 0:48] = cw.reshape(4, 12, 128).transpose(2, 1, 0).reshape(128, 48)
    cpp[:, 48:180] = fw.reshape(3, 44, 128).transpose(2, 1, 0).reshape(128, 132)
    cpp[:, 180:224] = fb.reshape(44, 128).T
    cpp[:, 224] = g("gdn_norm_g")
    cpp[:, 225:228] = g("mla_q_norm_g").reshape(3, 128).T
    cpp[:, 228:230] = g("mla_kv_norm_g").reshape(2, 128).T
    cbc = np.zeros((128, 5 * 1024 + 128), f)
    for i, k in enumerate(["ln1_g", "ln1_b", "ln2_g", "ln2_b", "ple_b_gate"]):
        cbc[:, i * 1024:(i + 1) * 1024] = g(k)[None, :]
    cbc[:, 5120:5184] = np.tile(g("gdn_a_log"), 16)[None, :]
    cbc[:, 5184:5248] = np.tile(g("gdn_dt_bias"), 16)[None, :]
    j = np.arange(128)[:, None]
    i = np.arange(128)[None, :]
    cmat = np.zeros((128, 1792), f)
    cmat[:, 0:128] = np.eye(128, dtype=f)
    cmat[:, 128:256] = (j <= i).astype(f)
    for q_ in range(4):
        cmat[:, 256 + q_ * 128:384 + q_ * 128] = np.where(j <= i, 0.0, -30000.0).astype(f)
        cmat[:, 768 + q_ * 128:896 + q_ * 128] = np.where(j < i, 0.0, -30000.0).astype(f)
        cmat[:, 1280 + q_ * 128:1408 + q_ * 128] = np.eye(128, dtype=f)
    inv = (np.float32(10000.0) ** (-(np.arange(0, 64, 2, dtype=f)) / np.float32(64))).astype(f)
    ang = (np.arange(S, dtype=f)[:, None] * inv[None, :]).astype(f)
    cos = np.cos(ang.astype(np.float64)).astype(f).T
    sin = np.sin(ang.astype(np.float64)).astype(f).T
    ctab = np.zeros((64, 2 * S), f)
    ctab[0:32, 0:S] = cos
    ctab[32:64, 0:S] = cos
    ctab[0:32, S:] = sin
    ctab[32:64, S:] = sin
    return {
        "w_in": np.ascontiguousarray(g("w_in")), "wq_up": np.ascontiguousarray(g("mla_w_q_up")),
        "wkv_up": np.ascontiguousarray(g("mla_w_kv_up")), "w_out": np.ascontiguousarray(g("w_out")),
        "w_up": np.ascontiguousarray(g("ffn_w_up")), "w_down": np.ascontiguousarray(g("ffn_w_down")),
        "w_gate": np.ascontiguousarray(g("ple_w_gate")), "w_proj": np.ascontiguousarray(g("ple_w_proj")),
        "cpp": cpp, "cbc": cbc, "cmat": cmat, "ctab": ctab,
    }


def kernel(**inputs):
    common = _prep_common(inputs)
    x = np.asarray(inputs["x"], dtype=np.float32)
    p = np.asarray(inputs["p"], dtype=np.float32)[0]
    B = x.shape[0]
    nseq = B // NCORES
    nc = build(nseq)
    in_maps = []
    for c in range(NCORES):
        m = dict(common)
        m["x"] = np.ascontiguousarray(x[c * nseq:(c + 1) * nseq].reshape(nseq * S, D))
        m["p"] = np.ascontiguousarray(p[c * nseq:(c + 1) * nseq].reshape(nseq * S, 256))
        in_maps.append(m)
    res = run_bass_kernel_spmd(nc, in_maps, core_ids=list(range(NCORES)))
    outs = [np.asarray(r["out"]).reshape(nseq, S, D) for r in res.results]
    return np.concatenate(outs, axis=0).astype(np.float32)
```

```python
import numpy as np
from contextlib import ExitStack
import concourse.bass as bass
import concourse.mybir as mybir
from concourse.bass_utils import run_bass_kernel_spmd

F32, BF16 = mybir.dt.float32, mybir.dt.bfloat16
AF = mybir.ActivationFunctionType
ALU = mybir.AluOpType

S = 2048
NT = 16
D = 1024
KC = 8
DFF = 2816
NJ = 22
ALPHA = float(2.0 ** 0.25)
EPS = 1e-6
BLK2 = 512
HS = 1024
NBLK = 2
NTH = 8
EPOCH = 16000
NCORES = 8
TWO_CHAINS = True
OVERLAP_A = False
SEQ_GENS = True


class Trk:
    def __init__(self, nc, es):
        self.nc, self.es = nc, es
        self.engs = {'pe': nc.tensor, 'act': nc.scalar, 'dve': nc.vector, 'pool': nc.gpsimd, 'sp': nc.sync}
        self.cnt = {e: 0 for e in self.engs}
        self.esems = {e: [] for e in self.engs}
        self.seen = {e: {} for e in self.engs}
        self.lastw = {}
        self.rd = {}
        self.dsem = {}
        self.pend = {e: ([], []) for e in self.engs}
        self.latest = {}
        self.log = {e: [] for e in self.engs}

    def newsem(self, name):
        return self.es.enter_context(self.nc.semaphore(name))

    def _wait(self, e, ev):
        sem, val, src = ev
        if src == 'pe' and e == 'pe':
            return
        k = id(sem)
        if self.seen[e].get(k, 0) >= val:
            return
        self.engs[e].wait_ge(sem, val)
        self.log[e].append(('w', id(sem), val))
        self.seen[e][k] = val

    def _deps(self, e, reads, writes):
        for k in reads:
            ev = self.lastw.get(k)
            if ev is not None:
                self._wait(e, ev)
        for k in writes:
            ev = self.lastw.get(k)
            if ev is not None:
                self._wait(e, ev)
            for ev in self.rd.get(k, {}).values():
                self._wait(e, ev)

    def _reg(self, ev, reads, writes):
        sem, val, src = ev
        self.latest[id(sem)] = (sem, val)
        for k in writes:
            self.lastw[k] = ev
            self.rd[k] = {}
        for k in reads:
            self.rd.setdefault(k, {})[id(sem)] = ev

    def issue(self, e, fn, reads=(), writes=(), inc=True):
        writes = list(writes) + [k for k in reads if k[0] == 'ps' and k not in writes]
        reads = [k for k in reads if k[0] != 'ps']
        self._deps(e, reads, writes)
        ins = fn(self.engs[e])
        pr, pw = self.pend[e]
        pr.extend(reads)
        pw.extend(writes)
        if inc:
            n = self.cnt[e]
            ep, off = divmod(n, EPOCH)
            if ep >= len(self.esems[e]):
                self.esems[e].append(self.newsem(f"s_{e}_{ep}"))
            sem = self.esems[e][ep]
            ins.then_inc(sem, 1)
            self.log[e].append(('i', id(sem), 1))
            self.cnt[e] = n + 1
            self._reg((sem, off + 1, e), pr, pw)
            self.pend[e] = ([], [])
        return ins

    def dma(self, q, out, in_, reads=(), writes=(), key=None):
        self._deps(q, reads, writes)
        ins = self.engs[q].dma_start(out=out, in_=in_)
        if key not in self.dsem:
            self.dsem[key] = [self.newsem("d_" + "_".join(str(x) for x in key)), 0]
        d = self.dsem[key]
        d[1] += 16
        ins.then_inc(d[0], 16)
        self.log[q].append(('i', id(d[0]), 16))
        self._reg((d[0], d[1], 'dma'), list(reads), list(writes))
        return ins

    def barrier(self):
        for e in self.engs:
            assert not self.pend[e][0] and not self.pend[e][1], e
        for sem, val in list(self.latest.values()):
            self._wait('sp', (sem, val, 'x'))
        n = self.cnt['sp']
        ep, off = divmod(n, EPOCH)
        if ep >= len(self.esems['sp']):
            self.esems['sp'].append(self.newsem(f"s_sp_{ep}"))
        sem = self.esems['sp'][ep]
        self.engs['sp'].sem_inc(sem, 1)
        self.log['sp'].append(('i', id(sem), 1))
        self.cnt['sp'] = n + 1
        ev = (sem, off + 1, 'sp')
        self.latest[id(sem)] = (sem, off + 1)
        self.seen['sp'][id(sem)] = off + 1
        for e in self.engs:
            if e != 'sp':
                self._wait(e, ev)
        self.lastw.clear()
        self.rd.clear()

    def final_wait(self):
        for sem, val in list(self.latest.values()):
            self._wait('sp', (sem, val, 'x'))


class _Stop(Exception):
    pass


def build(NSEQ, dbg=None, stop=None):
    dbg = dbg or {}
    nc = bass.Bass("TRN2", target_bir_lowering=False)

    def din(name, shape, dt=F32):
        return nc.dram_tensor(name, list(shape), dt, kind="ExternalInput").ap()

    x = din("x", [NSEQ * S, D])
    p = din("p", [NSEQ * S, 256])
    w_in = din("w_in", [D, 2760])
    wq_up = din("wq_up", [384, 768])
    wkv_up = din("wkv_up", [256, 1024])
    w_out = din("w_out", [1024, 1024])
    w_up = din("w_up", [D, 2 * DFF])
    w_down = din("w_down", [DFF, D])
    w_gate = din("w_gate", [D, D])
    w_proj = din("w_proj", [256, D])
    cpp = din("cpp", [128, 512])
    cbc = din("cbc", [128, 5 * 1024 + 128])
    cmat = din("cmat", [128, 14 * 128])
    ctab = din("ctab", [64, 2 * S])
    out = nc.dram_tensor("out", [NSEQ * S, D], F32, kind="ExternalOutput").ap()
    hscr = nc.dram_tensor("hscr", [S, D], F32).ap()
    dbg_t = {k: nc.dram_tensor("dbg_" + k, list(v[0]), v[1], kind="ExternalOutput").ap() for k, v in dbg.items()}

    w_in_v = w_in.rearrange("(kc p) n -> p kc n", p=128)
    wq_v = wq_up.rearrange("(kc p) n -> p kc n", p=128)
    wkv_v = wkv_up.rearrange("(kc p) n -> p kc n", p=128)
    w_out_v = w_out.rearrange("(kc p) n -> p kc n", p=128)
    w_up_v = w_up.rearrange("(kc p) n -> p kc n", p=128)
    w_down_v = w_down.rearrange("(kc p) n -> p kc n", p=128)
    w_gate_v = w_gate.rearrange("(kc p) n -> p kc n", p=128)
    w_proj_v = w_proj.rearrange("(kc p) n -> p kc n", p=128)

    with ExitStack() as es:
        T = Trk(nc, es)

        uid = [0]

        def sb(scope, name, shape, dt):
            uid[0] += 1
            return scope.enter_context(nc.sbuf_tensor(f"{name}_{uid[0]}", list(shape), dt))

        ps_all = es.enter_context(nc.psum_tensor("ps_all", [128, 8 * 512], F32))
        ps = [ps_all[:, b * 512:(b + 1) * 512] for b in range(8)]

        cpp_t = sb(es, "cpp_t", [128, 512], F32)
        cbc_t = sb(es, "cbc_t", [128, 128], F32)
        cmat_t = sb(es, "cmat_t", [128, 1792], F32)
        identb = sb(es, "identb", [128, 128], BF16)
        onesb = sb(es, "onesb", [128, 128], BF16)
        c128b = sb(es, "c128b", [128, 128], BF16)
        c256b = sb(es, "c256b", [128, 128], BF16)
        onesf = sb(es, "onesf", [128, 128], F32)
        w_ab = sb(es, "w_ab", [128, 8, 8], BF16)
        xhT = sb(es, "xhT", [128, 8, S], BF16)

        T.dma('sp', cpp_t[:], cpp[:, :], writes=[('cpp',)], key=('cpp',))
        T.dma('sp', cbc_t[:], cbc[:, 5120:5248], writes=[('cbc',)], key=('cbc',))
        T.dma('sp', cmat_t[:], cmat[:, :], writes=[('cmat',)], key=('cmat',))
        T.dma('pool', w_ab[:], w_in_v[:, :, 2048:2056], writes=[('w_ab',)], key=('w_ab',))
        ident = cmat_t[:, 0:128]
        Umat = cmat_t[:, 128:256]
        NEGM = cmat_t[:, 256:384]
        NEGM4 = cmat_t[:, 256:768].rearrange("p (i c) -> p i c", i=4)
        NEGMS4 = cmat_t[:, 768:1280].rearrange("p (i c) -> p i c", i=4)
        ident4 = cmat_t[:, 1280:1792].rearrange("p (i c) -> p i c", i=4)
        T.issue('dve', lambda e: e.tensor_copy(out=identb[:], in_=ident), reads=[('cmat',)], writes=[('identb',)])
        T.issue('pool', lambda e: e.memset(onesb[:], 1.0), writes=[('onesb',)])
        T.issue('pool', lambda e: e.memset(c128b[:], 1.0 / 128), writes=[('c128b',)])
        T.issue('pool', lambda e: e.memset(c256b[:], 1.0 / 256), writes=[('c256b',)])
        T.issue('pool', lambda e: e.memset(onesf[:], 1.0), writes=[('onesf',)])
        epst = sb(es, "epst", [128, 2], F32)
        T.issue('pool', lambda e: e.memset(epst[:, 0:1], EPS), writes=[('epst',)])
        T.issue('pool', lambda e: e.memset(epst[:, 1:2], 384 * EPS), writes=[('epst',)])
        eps1 = epst[:, 0:1]
        eps384 = epst[:, 1:2]
        gcw = cpp_t[:, 0:48].rearrange("p (c j) -> p c j", j=4)
        fcw = cpp_t[:, 48:180].rearrange("p (c j) -> p c j", j=3)
        fcb = cpp_t[:, 180:224]
        normg = cpp_t[:, 224:225]
        qg = cpp_t[:, 225:228]
        kvg = cpp_t[:, 228:230]
        ALOGB = cbc_t[:, 0:64]
        DTBB = cbc_t[:, 64:128]
        CONST = [('cpp',), ('cbc',), ('cmat',)]

        def act(out_, in_, func, reads, writes, **kw):
            return T.issue('act', lambda e: e.activation(out=out_, in_=in_, func=func, **kw), reads, writes)

        def tt(eng, out_, in0, in1, op, reads, writes):
            return T.issue(eng, lambda e: e.tensor_tensor(out=out_, in0=in0, in1=in1, op=op), reads, writes)

        def ts(eng, out_, in0, s1, s2, op0, op1, reads, writes):
            if s2 is None:
                return T.issue(eng, lambda e: e.tensor_scalar(out=out_, in0=in0, scalar1=s1, scalar2=None, op0=op0), reads, writes)
            return T.issue(eng, lambda e: e.tensor_scalar(out=out_, in0=in0, scalar1=s1, scalar2=s2, op0=op0, op1=op1), reads, writes)

        def stt(out_, in0, sc, in1, op0, op1, reads, writes):
            return T.issue('dve', lambda e: e.scalar_tensor_tensor(out=out_, in0=in0, scalar=sc, in1=in1, op0=op0, op1=op1), reads, writes)

        def cp(eng, out_, in_, reads, writes):
            if eng == 'act':
                return T.issue('act', lambda e: e.copy(out=out_, in_=in_), reads, writes)
            return T.issue(eng, lambda e: e.tensor_copy(out=out_, in_=in_), reads, writes)

        def rsqrt(out_, in_, eps_ap, reads, writes):
            act(out_, in_, AF.Ln, list(reads) + [('epst',)], writes, bias=eps_ap)
            act(out_, out_, AF.Exp, writes, writes, scale=-0.5)

        def amul(out_, in_, m, reads, writes):
            return T.issue('act', lambda e: e.mul(out=out_, in_=in_, mul=m), reads, writes)

        def mm(out_, lhsT, rhs, start, stop, reads, writes, inc=None):
            if inc is None:
                inc = stop
            return T.issue('pe', lambda e: e.matmul(out_, lhsT, rhs, start=start, stop=stop), reads, writes, inc=inc)

        def tr(out_, in_, reads, writes, inc=True):
            return T.issue('pe', lambda e: e.transpose(out_, in_, ident), list(reads) + [('cmat',)], writes, inc=inc)

        def dump(name, src, reads):
            if name in dbg_t:
                T.dma('sp', dbg_t[name], src, reads=reads, writes=[('dbg', name)], key=('dbg', name))

        def ln_tile(r, G, B, scope_tiles, key_r, out_tile, key_out, kc_=('cbcL',), sfx=''):
            stats, mv, rs = scope_tiles
            for hb in range(2):
                T.issue('dve', lambda e: e.bn_stats(out=stats[:, hb * 6:(hb + 1) * 6], in_=r[:, hb * 512:(hb + 1) * 512]),
                        reads=[key_r], writes=[('lnst', sfx)])
            T.issue('dve', lambda e: e.bn_aggr(out=mv[:], in_=stats[:]), reads=[('lnst', sfx)], writes=[('lnmv', sfx)])
            rsqrt(rs[:], mv[:, 1:2], eps1, [('lnmv', sfx)], [('lnrs', sfx)])
            ts('dve', r[:], r[:], mv[:, 0:1], rs[:, 0:1], ALU.subtract, ALU.mult, [key_r, ('lnmv', sfx), ('lnrs', sfx)], [key_r])
            tt('dve', r[:], r[:], G, ALU.mult, [key_r, kc_], [key_r])
            tt('dve', out_tile, r[:], B, ALU.add, [key_r, kc_], [key_out])

        if True:
          def seq_body(sq):
            row0 = sq * S
            with ExitStack() as p1:
                mixT = sb(p1, "mixT", [128, 8, S], BF16)
                with ExitStack() as sc:
                    xin = [sb(sc, f"xin{i}", [128, D], F32) for i in range(2)]
                    for t in range(NT):
                        sl = t % 2
                        T.dma('sp', xin[sl][:], x[row0 + t * 128: row0 + (t + 1) * 128, :], writes=[('xin', sl)], key=('xin', sl))
                        pb = (t % 2) * 2
                        for kc in range(8):
                            bank = pb + kc // 4
                            col = (kc % 4) * 128
                            tr(ps[bank][:, col:col + 128], xin[sl][:, kc * 128:(kc + 1) * 128], [('xin', sl)], [('ps', bank)], inc=(kc % 4 == 3))
                        for hb in range(2):
                            bank = pb + hb
                            cp('act' if hb == 0 else 'dve', xhT[:, hb * 4:(hb + 1) * 4, t * 128:(t + 1) * 128],
                               ps[bank][:, :].rearrange("p (k c) -> p k c", k=4), [('ps', bank)], [('xhT', t)])
                    T.barrier()
                dump('xT', xhT[:], [('xhT', t) for t in range(NT)])
                if stop == 'xT':
                    return True

                with ExitStack() as sc:
                    g_ab = sb(sc, "g_ab", [128, 128], F32)
                    g_beta = sb(sc, "g_beta", [128, 64], F32)
                    g_g = sb(sc, "g_g", [128, 64], F32)
                    g_tmp = sb(sc, "g_tmp", [128, 64], F32)
                    g_eal = sb(sc, "g_eal", [128, 64], F32)
                    g_gc = sb(sc, "g_gc", [128, 64], F32)
                    g_ngc = sb(sc, "g_ngc", [128, 64], F32)
                    g_eg = sb(sc, "g_eg", [128, 64], F32)
                    g_egl = sb(sc, "g_egl", [128, 64], F32)
                    g_egla = sb(sc, "g_egla", [128, 64], F32)
                    raw2 = [sb(sc, f"raw{i}", [128, 3 + HS], F32) for i in range(2)]
                    acc = sb(sc, "acc", [128, HS], F32)
                    sil2 = [sb(sc, f"sil{i}", [128, HS], F32) for i in range(2)]
                    sqb = sb(sc, "sqb", [128, HS], BF16)
                    rstd = [sb(sc, f"rstd{i}", [128, 512], F32) for i in range(2)]
                    halo_g = sb(sc, "halo_g", [128, 12, 3], F32)
                    zero3 = sb(sc, "zero3", [128, 3], F32)
                    wst = [sb(sc, f"wst{i}", [128, 8, 128], BF16) for i in range(3)]
                    hq = sb(sc, "hq", [128, 4, HS], BF16)
                    hk = sb(sc, "hk", [128, 4, HS], BF16)
                    hkg = sb(sc, "hkg", [128, 4, NTH, 128], BF16)
                    hkd = sb(sc, "hkd", [128, 4, NTH, 128], BF16)
                    hv = sb(sc, "hv", [128, 4, NTH, 128], BF16)
                    hz = sb(sc, "hz", [128, 4, HS], BF16)
                    S32 = sb(sc, "S32", [128, 4, 128], F32)
                    Sbf = sb(sc, "Sbf", [128, 4, 128], BF16)

                    def tmpp(name, dt):
                        return sb(sc, name, [128, 4, 128], dt)
                    Ug = tmpp("Ug", F32)
                    EGb = tmpp("EGb", F32)
                    ARG = tmpp("ARG", F32)
                    ARG2 = tmpp("ARG2", F32)
                    DT = tmpp("DT", F32)
                    DTs = tmpp("DTs", F32)
                    Nf = tmpp("Nf", F32)
                    Pb = [tmpp("Pb0_", BF16), tmpp("Pb1_", BF16)]
                    PTb = [tmpp("PTb0_", BF16), tmpp("PTb1_", BF16)]
                    Xb = [tmpp("Xb0_", BF16), tmpp("Xb1_", BF16)]
                    QKD = tmpp("QKD", BF16)
                    nw2T = tmpp("nw2T", BF16)
                    vnew = tmpp("vnew", BF16)
                    qgT = tmpp("qgT", BF16)
                    sqo = tmpp("sqo", BF16)
                    rso = tmpp("rso", F32)
                    o1 = tmpp("o1", F32)

                    T.issue('pool', lambda e: e.memset(zero3[:], 0.0), writes=[('zero3',)])
                    cur_half = [0]

                    for t in range(NT):
                        for kc in range(8):
                            mm(ps[7][:, t * 8:(t + 1) * 8], xhT[:, kc, t * 128:(t + 1) * 128], w_ab[:, kc, :], kc == 0, kc == 7,
                               [('xhT', t), ('w_ab',)], [('ps', 7)], inc=(kc == 7 and t == NT - 1))
                    cp('dve', g_ab[:], ps[7][:, 0:128], [('ps', 7)], [('g_ab',)])
                    abv = g_ab[:].rearrange("p (t c) -> p t c", c=8)
                    v64 = lambda tl: tl[:].rearrange("p (t c) -> p t c", c=4)
                    act(v64(g_beta), abv[:, :, 4:8], AF.Sigmoid, [('g_ab',)], [('g_beta',)])
                    tt('dve', v64(g_tmp), abv[:, :, 0:4], DTBB.rearrange("p (t c) -> p t c", c=4), ALU.add, [('g_ab',), ('cbc',)], [('g_tmp',)])
                    act(g_tmp[:], g_tmp[:], AF.Exp, [('g_tmp',)], [('g_tmp',)])
                    ts('dve', g_tmp[:], g_tmp[:], 1.0, None, ALU.add, None, [('g_tmp',)], [('g_tmp',)])
                    act(g_tmp[:], g_tmp[:], AF.Ln, [('g_tmp',)], [('g_tmp',)])
                    act(g_eal[:], ALOGB, AF.Exp, [('cbc',)], [('g_eal',)])
                    stt(g_g[:], g_tmp[:], -1.0, g_eal[:], ALU.mult, ALU.mult, [('g_tmp',), ('g_eal',)], [('g_g',)])
                    mm(ps[7][:, 128:192], Umat, g_g[:], True, True, [('cmat',), ('g_g',)], [('ps', 7)])
                    mm(ps[7][:, 192:256], onesf[:], g_g[:], True, True, [('onesf',), ('g_g',)], [('ps', 7)])
                    cp('dve', g_gc[:], ps[7][:, 128:192], [('ps', 7)], [('g_gc',)])
                    ts('dve', g_ngc[:], g_gc[:], -1.0, None, ALU.mult, None, [('g_gc',)], [('g_ngc',)])
                    act(g_eg[:], g_gc[:], AF.Exp, [('g_gc',)], [('g_eg',)])
                    tt('dve', g_egl[:], ps[7][:, 192:256], g_gc[:], ALU.subtract, [('ps', 7), ('g_gc',)], [('g_egl',)])
                    act(g_egl[:], g_egl[:], AF.Exp, [('g_egl',)], [('g_egl',)])
                    act(g_egla[:], ps[7][:, 192:256], AF.Exp, [('ps', 7)], [('g_egla',)])
                    GS = [('g_beta',), ('g_gc',), ('g_ngc',), ('g_eg',), ('g_egl',), ('g_egla',), ('g_g',)]

                    wcnt = [0]

                    def load_wchunk(c0):
                        sl = wcnt[0] % 3
                        wcnt[0] += 1
                        T.dma('pool', wst[sl][:], w_in_v[:, :, c0:c0 + 128], writes=[('wst', sl)], key=('wst', sl))
                        return sl

                    def proj_block(sl, tb, bank):
                        gtb = cur_half[0] * NBLK + tb
                        for kc in range(8):
                            mm(ps[bank][:, :], wst[sl][:, kc, :], xhT[:, kc, gtb * 512:(gtb + 1) * 512], kc == 0, kc == 7,
                               [('wst', sl)] + [('xhT', gtb * 4 + i) for i in range(4)], [('ps', bank)])

                    trc = [0]

                    def stage_P(ch, tb):
                        kind, h, cidx, sl, ci = ch
                        par = h
                        rp = ci % 2
                        raw = raw2[rp]
                        bank = 6 + tb % 2
                        cs = slice(tb * 512, (tb + 1) * 512)
                        proj_block(sl, tb, bank)
                        if kind == 'z':
                            act(hz[:, par, cs], ps[bank][:, :], AF.Silu, [('ps', bank)], [('hz', par)])
                            return
                        if tb == 0:
                            if cur_half[0] == 0:
                                cp('act', raw[:, 0:3], zero3[:], [('zero3',)], [('raw', rp, -1)])
                            else:
                                cp('act', raw[:, 0:3], halo_g[:, cidx, :], [('halo_g', cidx)], [('raw', rp, -1)])
                        cp('act', raw[:, 3 + tb * 512: 3 + (tb + 1) * 512], ps[bank][:, :], [('ps', bank)], [('raw', rp, tb)])
                        if tb == NBLK - 1 and cur_half[0] == 0:
                            cp('act', halo_g[:, cidx, :], raw[:, HS:HS + 3], [('raw', rp, tb)], [('halo_g', cidx)])

                    def stage_C(ch, tb):
                        kind, h, cidx, sl, ci = ch
                        if kind == 'z':
                            return
                        rp = ci % 2
                        raw = raw2[rp]
                        cs = slice(tb * 512, (tb + 1) * 512)
                        RK = [('raw', rp, tb), ('raw', rp, tb - 1), ('cpp',)]
                        ka = ('acc', tb)
                        ts('dve', acc[:, cs], raw[:, 3 + tb * 512: 3 + (tb + 1) * 512], gcw[:, cidx, 3:4], None, ALU.mult, None, RK, [ka])
                        for j in (2, 1, 0):
                            stt(acc[:, cs], raw[:, j + tb * 512: j + (tb + 1) * 512], gcw[:, cidx, j:j + 1], acc[:, cs], ALU.mult, ALU.add,
                                RK + [ka], [ka])

                    def stage_S(ch, tb):
                        kind, h, cidx, sl, ci = ch
                        if kind == 'z':
                            return
                        sp_ = ci % 2
                        cs = slice(tb * 512, (tb + 1) * 512)
                        act(sil2[sp_][:, cs], acc[:, cs], AF.Silu, [('acc', tb)], [('sil', sp_, tb)])

                    def stage_N(ch):
                        kind, h, cidx, sl, ci = ch
                        if kind not in ('q', 'k'):
                            return
                        par = h
                        sp_ = ci % 2
                        sl_ = sil2[sp_]
                        for tb in range(NBLK):
                            cs = slice(tb * 512, (tb + 1) * 512)
                            tt('pool', sqb[:, cs], sl_[:, cs], sl_[:, cs], ALU.mult, [('sil', sp_, tb)], [('sqb', tb)])
                            mm(ps[2 + tb][:, :], onesb[:], sqb[:, cs], True, True, [('onesb',), ('sqb', tb)], [('ps', 2 + tb)])
                        for tb in range(NBLK):
                            act(rstd[tb][:], ps[2 + tb][:, :], AF.Ln, [('ps', 2 + tb), ('epst',)], [('rstd', tb)], bias=eps1)
                        for tb in range(NBLK):
                            act(rstd[tb][:], rstd[tb][:], AF.Exp, [('rstd', tb)], [('rstd', tb)], scale=-0.5)
                        for tb in range(NBLK):
                            cs = slice(tb * 512, (tb + 1) * 512)
                            ks = ('sil', sp_, tb)
                            if kind == 'q':
                                stt(hq[:, par, cs], sl_[:, cs], float(128 ** -0.5), rstd[tb][:], ALU.mult, ALU.mult,
                                    [ks, ('rstd', tb)], [('hq', par)])
                            else:
                                tt('dve', sl_[:, cs], sl_[:, cs], rstd[tb][:], ALU.mult, [ks, ('rstd', tb)], [ks])
                                cp('act', hk[:, par, cs], sl_[:, cs], [ks], [('hk', par)])

                    def stage_T(ch, tb):
                        kind, h, cidx, sl, ci = ch
                        if kind not in ('k', 'v'):
                            return
                        par = h
                        sp_ = ci % 2
                        ks = ('sil', sp_, tb)
                        if kind == 'v':
                            b3 = trc[0] % 2
                            trc[0] += 1
                            for tl in range(4):
                                n = tb * 4 + tl
                                tr(ps[b3][:, tl * 128:(tl + 1) * 128], sil2[sp_][:, n * 128:(n + 1) * 128], [ks], [('ps', b3)], inc=(tl == 3))
                            cp('act' if tb % 2 else 'dve', hv[:, par, tb * 4:(tb + 1) * 4, :],
                               ps[b3][:, :].rearrange("p (t c) -> p t c", t=4), [('ps', b3)], [('hv', par)])
                            return
                        bx = trc[0] % 2
                        by = 4 + trc[0] % 2
                        trc[0] += 1
                        for tl in range(4):
                            n = tb * 4 + tl
                            tr(ps[bx][:, tl * 128:(tl + 1) * 128], sil2[sp_][:, n * 128:(n + 1) * 128], [ks], [('ps', bx)], inc=False)
                            tr(ps[by][:, tl * 128:(tl + 1) * 128], sil2[sp_][:, n * 128:(n + 1) * 128], [ks], [('ps', by)], inc=(tl == 3))
                        for tl in range(4):
                            n = tb * 4 + tl
                            c = (cur_half[0] * NTH + n) * 4 + h
                            amul(hkg[:, par, n, :], ps[bx][:, tl * 128:(tl + 1) * 128], g_eg[:, c:c + 1], [('ps', bx), ('g_eg',)], [('hkg', par)])
                        for tl in range(4):
                            n = tb * 4 + tl
                            c = (cur_half[0] * NTH + n) * 4 + h
                            ts('dve', hkd[:, par, n, :], ps[by][:, tl * 128:(tl + 1) * 128], g_egl[:, c:c + 1], None, ALU.mult, None,
                               [('ps', by), ('g_egl',)], [('hkd', par)])

                    def run_A_quad():
                        chs = []
                        for h in range(4):
                            for kind, c0, cidx in (('q', h * 128, h), ('k', 512 + h * 128, 4 + h), ('v', 1024 + h * 128, 8 + h), ('z', 1536 + h * 128, 0)):
                                chs.append([kind, h, cidx, None, len(chs), c0])
                        nch = len(chs)
                        loaded = [0]

                        def ensure_loaded(upto):
                            while loaded[0] <= min(upto, nch - 1):
                                ch_ = chs[loaded[0]]
                                ch_[3] = load_wchunk(ch_[5])
                                loaded[0] += 1
                        nit = NBLK * nch
                        for tau in range(nit + NBLK + 5):
                            if tau < nit:
                                i, tb = divmod(tau, NBLK)
                                if tb == 0:
                                    ensure_loaded(i + 1)
                                stage_P(tuple(chs[i][:5]), tb)
                            if 0 <= tau - 1 < nit:
                                i, tb = divmod(tau - 1, NBLK)
                                stage_C(tuple(chs[i][:5]), tb)
                            if 0 <= tau - 2 < nit:
                                i, tb = divmod(tau - 2, NBLK)
                                stage_S(tuple(chs[i][:5]), tb)
                            tn = tau - (NBLK + 2)
                            if tn >= 0 and tn % NBLK == 0 and tn // NBLK < nch:
                                stage_N(tuple(chs[tn // NBLK][:5]))
                            if 0 <= tau - (NBLK + 3) < nit:
                                i, tb = divmod(tau - (NBLK + 3), NBLK)
                                stage_T(tuple(chs[i][:5]), tb)

                    def H4(b):
                        return ps[b][:, :].rearrange("p (i c) -> p i c", i=4), ('ps', b)

                    def rec_quad(half):
                        if half == 0:
                            T.issue('pool', lambda e: e.memset(S32[:], 0.0), writes=[('S32',)])
                            T.issue('pool', lambda e: e.memset(Sbf[:], 0.0), writes=[('Sbf',)])
                        pGb, kGb = H4(0)
                        pX, kX = H4(6)
                        pKK, kKK = H4(1)
                        pV, kV = H4(1)
                        pQK, kQK = H4(2)
                        pO, kO = H4(2)
                        pNT, kNT = H4(3)
                        pS, kS = H4(3)
                        pP, kP = H4(4)
                        pR, kR = H4(4)
                        pPT, kPT = H4(5)
                        pW, kW = H4(0)
                        for n in range(NTH):
                            tok = slice(n * 128, (n + 1) * 128)
                            gtok = slice((half * NTH + n) * 128, (half * NTH + n + 1) * 128)
                            cc = [(half * NTH + n) * 4 + i for i in range(4)]
                            for i in range(4):
                                amul(Ug[:, i, :], Umat, g_g[:, cc[i]:cc[i] + 1], [('cmat',), ('g_g',)], [('Ug',)])
                            for i in range(4):
                                mm(pGb[:, i, :], onesf[:], Ug[:, i, :], True, True, [('onesf',), ('Ug',)], [kGb], inc=(i == 3))
                            tt('dve', ARG2[:], pGb, NEGMS4, ALU.add, [kGb, ('cmat',)], [('ARG2',)])
                            tt('dve', ARG[:], pGb, NEGM4, ALU.add, [kGb, ('cmat',)], [('ARG',)])
                            act(EGb[:], pGb, AF.Exp, [kGb], [('EGb',)])
                            for i in range(4):
                                mm(pKK[:, i, :], hk[:, i, tok], hk[:, i, tok], True, True, [('hk', i)], [kKK], inc=(i == 3))
                            for i in range(4):
                                mm(pQK[:, i, :], hk[:, i, tok], hq[:, i, tok], True, True, [('hk', i), ('hq', i)], [kQK], inc=(i == 3))
                            for i in range(4):
                                act(DTs[:, i, :], ARG2[:, i, :], AF.Exp, [('ARG2',), ('g_ngc',)], [('DTs',)], bias=g_ngc[:, cc[i]:cc[i] + 1])
                            for i in range(4):
                                act(DT[:, i, :], ARG[:, i, :], AF.Exp, [('ARG',), ('g_ngc',)], [('DT',)], bias=g_ngc[:, cc[i]:cc[i] + 1])
                            for i in range(4):
                                stt(Nf[:, i, :], pKK[:, i, :], g_beta[:, cc[i]:cc[i] + 1], DTs[:, i, :], ALU.mult, ALU.mult,
                                    [kKK, ('g_beta',), ('DTs',)], [('Nf',)])
                            for i in range(4):
                                tr(pNT[:, i, :], Nf[:, i, :], [('Nf',)], [kNT], inc=(i == 3))
                            cur = 0
                            cp('act', Pb[cur][:], Nf[:], [('Nf',)], [('Pb0',)])
                            cp('dve', PTb[cur][:], pNT, [kNT], [('PTb0',)])
                            tt('dve', Xb[cur][:], ident4, Nf[:], ALU.subtract, [('cmat',), ('Nf',)], [('Xb0',)])
                            tt('dve', QKD[:], pQK, DT[:], ALU.mult, [kQK, ('DT',)], [('QKD',)])
                            tt('pool', qgT[:], hq[:, :, tok], EGb[:], ALU.mult, [('hq', i_) for i_ in range(4)] + [('EGb',)], [('qgT',)])
                            xc = 0

                            def x_update(ptb_idx, step_):
                                nonlocal_xc = x_state[0]
                                xn = 1 - nonlocal_xc
                                for i in range(4):
                                    mm(pX[:, i, :], identb[:], Xb[nonlocal_xc][:, i, :], True, False, [('identb',), (f'Xb{nonlocal_xc}',)], [kX], inc=False)
                                    mm(pX[:, i, :], PTb[ptb_idx][:, i, :], Xb[nonlocal_xc][:, i, :], False, True,
                                       [(f'PTb{ptb_idx}',), (f'Xb{nonlocal_xc}',)], [kX], inc=(i == 3))
                                x_state[0] = xn
                                return xn
                            x_state = [0]
                            pend = None
                            for step in range(6):
                                nx = 1 - cur
                                kPc, kPTc = (f'Pb{cur}',), (f'PTb{cur}',)
                                kPn, kPTn = (f'Pb{nx}',), (f'PTb{nx}',)
                                for i in range(4):
                                    mm(pPT[:, i, :], Pb[cur][:, i, :], PTb[cur][:, i, :], True, True, [kPc, kPTc], [kPT], inc=(i == 3))
                                if step < 5:
                                    for i in range(4):
                                        mm(pP[:, i, :], PTb[cur][:, i, :], Pb[cur][:, i, :], True, True, [kPc, kPTc], [kP], inc=(i == 3))
                                if pend is not None:
                                    xn = x_update(pend, step)
                                cp('dve', PTb[nx][:], pPT, [kPT], [kPTn])
                                if step < 5:
                                    cp('act', Pb[nx][:], pP, [kP], [kPn])
                                if pend is not None:
                                    cp('act' if step % 2 else 'dve', Xb[xn][:], pX, [kX], [(f'Xb{xn}',)])
                                pend = nx
                                cur = nx
                            xn = x_update(pend, 6)
                            cp('act', Xb[xn][:], pX, [kX], [(f'Xb{xn}',)])
                            cur = xn
                            kT2 = (f'Xb{cur}',)
                            T2T = Xb[cur]
                            for i in range(4):
                                mm(pW[:, i, :], hkg[:, i, n, :], T2T[:, i, :], True, True, [('hkg', i), kT2], [kW], inc=(i == 3))
                            amul(nw2T[:], pW, -1.0, [kW], [('nw2T',)])
                            for i in range(4):
                                mm(pV[:, i, :], T2T[:, i, :], hv[:, i, n, :], True, False, [kT2, ('hv', i)], [kV], inc=False)
                                mm(pV[:, i, :], nw2T[:, i, :], Sbf[:, i, :], False, True, [('nw2T',), ('Sbf',)], [kV], inc=(i == 3))
                            for i in range(4):
                                ts('dve', vnew[:, i, :], pV[:, i, :], g_beta[:, cc[i]:cc[i] + 1], None, ALU.mult, None, [kV, ('g_beta',)], [('vnew',)])
                            for i in range(4):
                                mm(pS[:, i, :], hkd[:, i, n, :], vnew[:, i, :], True, True, [('hkd', i), ('vnew',)], [kS], inc=(i == 3))
                            for i in range(4):
                                mm(pO[:, i, :], Sbf[:, i, :], qgT[:, i, :], True, False, [('Sbf',), ('qgT',)], [kO], inc=False)
                                mm(pO[:, i, :], vnew[:, i, :], QKD[:, i, :], False, True, [('vnew',), ('QKD',)], [kO], inc=(i == 3))
                            for i in range(4):
                                stt(S32[:, i, :], S32[:, i, :], g_egla[:, cc[i]:cc[i] + 1], pS[:, i, :], ALU.mult, ALU.add,
                                    [('S32',), ('g_egla',), kS], [('S32',)])
                            cp('act', Sbf[:], S32[:], [('S32',)], [('Sbf',)])
                            act(sqo[:], pO, AF.Square, [kO], [('sqo',)])
                            for i in range(4):
                                mm(pR[:, i, :], c128b[:], sqo[:, i, :], True, True, [('c128b',), ('sqo',)], [kR], inc=(i == 3))
                            rsqrt(rso[:], pR, eps1, [kR], [('rso',)])
                            stt(o1[:], pO, normg, rso[:], ALU.mult, ALU.mult, [kO, ('cpp',), ('rso',)], [('o1',)])
                            tt('pool', mixT[:, 0:4, gtok], o1[:], hz[:, :, tok], ALU.mult, [('o1',)] + [('hz', i_) for i_ in range(4)],
                               [('mixT', i_) for i_ in range(4)])

                    def gen_rec_pair(half, pr):
                        ia = 2 * pr
                        ii = (ia, ia + 1)
                        sl_ = slice(ia, ia + 2)
                        B = 4 * pr

                        def HB(b, hf):
                            return ps[b][:, hf * 256:(hf + 1) * 256].rearrange("p (i c) -> p i c", i=2), ('ps', b)
                        pGb, kGb = HB(B, 0)
                        pW, kW = HB(B, 0)
                        pP, kP = HB(B, 1)
                        pR, kR = HB(B, 1)
                        pKK, kKK = HB(B + 1, 0)
                        pV, kV = HB(B + 1, 0)
                        pPT, kPT = HB(B + 1, 1)
                        pQK, kQK = HB(B + 2, 0)
                        pO, kO = HB(B + 2, 0)
                        pX, kX = HB(B + 2, 1)
                        pNT, kNT = HB(B + 3, 0)
                        pS, kS = HB(B + 3, 0)
                        K = lambda nm: (nm, pr)
                        last = ia + 1
                        for n in range(NTH):
                            tok = slice(n * 128, (n + 1) * 128)
                            gtok = slice((half * NTH + n) * 128, (half * NTH + n + 1) * 128)
                            cc = {i: (half * NTH + n) * 4 + i for i in ii}
                            for i in ii:
                                amul(Ug[:, i, :], Umat, g_g[:, cc[i]:cc[i] + 1], [('cmat',), ('g_g',)], [K('Ug')])
                            yield
                            for i in ii:
                                mm(pGb[:, i - ia, :], onesf[:], Ug[:, i, :], True, True, [('onesf',), K('Ug')], [kGb], inc=(i == last))
                            yield
                            tt('dve', ARG2[:, sl_, :], pGb, NEGMS4[:, 0:2, :], ALU.add, [kGb, ('cmat',)], [K('ARG2')])
                            tt('dve', ARG[:, sl_, :], pGb, NEGM4[:, 0:2, :], ALU.add, [kGb, ('cmat',)], [K('ARG')])
                            act(EGb[:, sl_, :], pGb, AF.Exp, [kGb], [K('EGb')])
                            for i in ii:
                                mm(pKK[:, i - ia, :], hk[:, i, tok], hk[:, i, tok], True, True, [('hk', i)], [kKK], inc=(i == last))
                            for i in ii:
                                mm(pQK[:, i - ia, :], hk[:, i, tok], hq[:, i, tok], True, True, [('hk', i), ('hq', i)], [kQK], inc=(i == last))
                            yield
                            for i in ii:
                                act(DTs[:, i, :], ARG2[:, i, :], AF.Exp, [K('ARG2'), ('g_ngc',)], [K('DTs')], bias=g_ngc[:, cc[i]:cc[i] + 1])
                            for i in ii:
                                act(DT[:, i, :], ARG[:, i, :], AF.Exp, [K('ARG'), ('g_ngc',)], [K('DT')], bias=g_ngc[:, cc[i]:cc[i] + 1])
                            yield
                            for i in ii:
                                stt(Nf[:, i, :], pKK[:, i - ia, :], g_beta[:, cc[i]:cc[i] + 1], DTs[:, i, :], ALU.mult, ALU.mult,
                                    [kKK, ('g_beta',), K('DTs')], [K('Nf')])
                            yield
                            for i in ii:
                                tr(pNT[:, i - ia, :], Nf[:, i, :], [K('Nf')], [kNT], inc=(i == last))
                            cur = 0
                            cp('act', Pb[cur][:, sl_, :], Nf[:, sl_, :], [K('Nf')], [K('Pb0')])
                            yield
                            cp('dve', PTb[cur][:, sl_, :], pNT, [kNT], [K('PTb0')])
                            tt('dve', Xb[cur][:, sl_, :], ident4[:, 0:2, :], Nf[:, sl_, :], ALU.subtract, [('cmat',), K('Nf')], [K('Xb0')])
                            tt('dve', QKD[:, sl_, :], pQK, DT[:, sl_, :], ALU.mult, [kQK, K('DT')], [K('QKD')])
                            tt('pool', qgT[:, sl_, :], hq[:, sl_, tok], EGb[:, sl_, :], ALU.mult, [('hq', i_) for i_ in ii] + [K('EGb')], [K('qgT')])
                            yield
                            xs = [0]

                            def x_update(ptb_idx):
                                xc_ = xs[0]
                                xn_ = 1 - xc_
                                for i in ii:
                                    mm(pX[:, i - ia, :], identb[:], Xb[xc_][:, i, :], True, False, [('identb',), K(f'Xb{xc_}')], [kX], inc=False)
                                    mm(pX[:, i - ia, :], PTb[ptb_idx][:, i, :], Xb[xc_][:, i, :], False, True,
                                       [K(f'PTb{ptb_idx}'), K(f'Xb{xc_}')], [kX], inc=(i == last))
                                xs[0] = xn_
                                return xn_
                            pend = None
                            for step in range(6):
                                nx = 1 - cur
                                kPc, kPTc = K(f'Pb{cur}'), K(f'PTb{cur}')
                                kPn, kPTn = K(f'Pb{nx}'), K(f'PTb{nx}')
                                for i in ii:
                                    mm(pPT[:, i - ia, :], Pb[cur][:, i, :], PTb[cur][:, i, :], True, True, [kPc, kPTc], [kPT], inc=(i == last))
                                if step < 5:
                                    for i in ii:
                                        mm(pP[:, i - ia, :], PTb[cur][:, i, :], Pb[cur][:, i, :], True, True, [kPc, kPTc], [kP], inc=(i == last))
                                if pend is not None:
                                    xn = x_update(pend)
                                yield
                                cp('dve', PTb[nx][:, sl_, :], pPT, [kPT], [kPTn])
                                if step < 5:
                                    cp('act', Pb[nx][:, sl_, :], pP, [kP], [kPn])
                                if pend is not None:
                                    cp('act' if step % 2 else 'dve', Xb[xn][:, sl_, :], pX, [kX], [K(f'Xb{xn}')])
                                yield
                                pend = nx
                                cur = nx
                            xn = x_update(pend)
                            yield
                            cp('act', Xb[xn][:, sl_, :], pX, [kX], [K(f'Xb{xn}')])
                            yield
                            cur = xn
                            kT2 = K(f'Xb{cur}')
                            T2T = Xb[cur]
                            for i in ii:
                                mm(pW[:, i - ia, :], hkg[:, i, n, :], T2T[:, i, :], True, True, [('hkg', i), kT2], [kW], inc=(i == last))
                            yield
                            amul(nw2T[:, sl_, :], pW, -1.0, [kW], [K('nw2T')])
                            yield
                            for i in ii:
                                mm(pV[:, i - ia, :], T2T[:, i, :], hv[:, i, n, :], True, False, [kT2, ('hv', i)], [kV], inc=False)
                                mm(pV[:, i - ia, :], nw2T[:, i, :], Sbf[:, i, :], False, True, [K('nw2T'), K('Sbf')], [kV], inc=(i == last))
                            yield
                            for i in ii:
                                ts('dve', vnew[:, i, :], pV[:, i - ia, :], g_beta[:, cc[i]:cc[i] + 1], None, ALU.mult, None, [kV, ('g_beta',)], [K('vnew')])
                            yield
                            for i in ii:
                                mm(pS[:, i - ia, :], hkd[:, i, n, :], vnew[:, i, :], True, True, [('hkd', i), K('vnew')], [kS], inc=(i == last))
                            for i in ii:
                                mm(pO[:, i - ia, :], Sbf[:, i, :], qgT[:, i, :], True, False, [K('Sbf'), K('qgT')], [kO], inc=False)
                                mm(pO[:, i - ia, :], vnew[:, i, :], QKD[:, i, :], False, True, [K('vnew'), K('QKD')], [kO], inc=(i == last))
                            yield
                            for i in ii:
                                stt(S32[:, i, :], S32[:, i, :], g_egla[:, cc[i]:cc[i] + 1], pS[:, i - ia, :], ALU.mult, ALU.add,
                                    [K('S32'), ('g_egla',), kS], [K('S32')])
                            act(sqo[:, sl_, :], pO, AF.Square, [kO], [K('sqo')])
                            yield
                            cp('act', Sbf[:, sl_, :], S32[:, sl_, :], [K('S32')], [K('Sbf')])
                            for i in ii:
                                mm(pR[:, i - ia, :], c128b[:], sqo[:, i, :], True, True, [('c128b',), K('sqo')], [kR], inc=(i == last))
                            yield
                            rsqrt(rso[:, sl_, :], pR, eps1, [kR], [K('rso')])
                            yield
                            stt(o1[:, sl_, :], pO, normg, rso[:, sl_, :], ALU.mult, ALU.mult, [kO, ('cpp',), K('rso')], [K('o1')])
                            yield
                            tt('pool', mixT[:, sl_, gtok], o1[:, sl_, :], hz[:, sl_, tok], ALU.mult, [K('o1')] + [('hz', i_) for i_ in ii],
                               [('mixT', i_) for i_ in ii])
                            yield

                    def rec_two_chains(half):
                        if half == 0:
                            for pr in range(2):
                                T.issue('pool', lambda e: e.memset(S32[:, 2 * pr:2 * pr + 2, :], 0.0), writes=[('S32', pr)])
                                T.issue('pool', lambda e: e.memset(Sbf[:, 2 * pr:2 * pr + 2, :], 0.0), writes=[('Sbf', pr)])
                        gens = [gen_rec_pair(half, 0), gen_rec_pair(half, 1)]
                        while gens:
                            for g_ in list(gens):
                                try:
                                    next(g_)
                                except StopIteration:
                                    gens.remove(g_)

                    for half in range(2):
                        cur_half[0] = half
                        run_A_quad()
                        if TWO_CHAINS:
                            rec_two_chains(half)
                        else:
                            rec_quad(half)
                    T.barrier()
                dump('mixA', mixT[:, 0:4, :], [('mixT', h) for h in range(4)])
                if stop == 'mixA':
                    return True

                wo_t = sb(p1, "wo_t", [128, 8, 1024], BF16)
                with ExitStack() as sc:
                    ctab_t = sb(sc, "ctab_t", [64, 2 * S], F32)
                    T.dma('sp', ctab_t[:], ctab[:, :], writes=[('ctab',)], key=('ctab',))
                    cos2 = ctab_t[:, 0:S]
                    sin2 = ctab_t[:, S:2 * S]
                    wq_t = sb(sc, "wq_t", [128, 3, 768], BF16)
                    wqr_t = sb(sc, "wqr_t", [128, 3, 4, 64], BF16)
                    wkv_t = sb(sc, "wkv_t", [128, 2, 1024], BF16)
                    wkr_t = sb(sc, "wkr_t", [128, 8, 128], BF16)
                    wst = [sb(sc, f"wstm{i}", [128, 8, 128], BF16) for i in range(3)]
                    cqg = sb(sc, "cqg", [128, 3, S], BF16)
                    ckvg = sb(sc, "ckvg", [128, 2, S], BF16)
                    sqr = [sb(sc, f"sqr{i}", [128, 512], BF16) for i in range(2)]
                    rsq = sb(sc, "rsq", [128, S], F32)
                    rskv = sb(sc, "rskv", [128, S], F32)
                    krT = sb(sc, "krT", [64, S], BF16)
                    t1 = [sb(sc, f"rt1_{i}", [64, 512], F32) for i in range(2)]
                    t2 = [sb(sc, f"rt2_{i}", [64, 512], F32) for i in range(2)]
                    qn = [sb(sc, f"qn{i}", [128, S], BF16) for i in range(1)]
                    qr = [sb(sc, f"qr{i}", [64, S], BF16) for i in range(1)]
                    kn = [sb(sc, f"kn{i}", [128, S], BF16) for i in range(1)]
                    vh = [sb(sc, f"vh{i}", [128, NT, 128], BF16) for i in range(1)]
                    PT = [sb(sc, f"PTt{i}", [128, 512], BF16) for i in range(3)]
                    den = [sb(sc, f"den{i}", [128, 512], F32) for i in range(2)]

                    wcnt = [0]

                    def load_wchunk2(c0):
                        sl = wcnt[0] % 3
                        wcnt[0] += 1
                        T.dma('pool', wst[sl][:], w_in_v[:, :, c0:c0 + 128], writes=[('wst', sl)], key=('wstm', sl))
                        return sl
                    pre_sl = {0: load_wchunk2(2056), 1: load_wchunk2(2056 + 128)}
                    T.dma('pool', wkr_t[:, :, 0:64], w_in_v[:, :, 2696:2760], writes=[('wkr',)], key=('wkr',))
                    T.dma('pool', wq_t[:], wq_v[:, :, :], writes=[('wq',)], key=('wq',))
                    T.dma('pool', wkv_t[:], wkv_v[:, :, :], writes=[('wkv',)], key=('wkv',))
                    ts('dve', wkr_t[:, :, 64:96], wkr_t[:, :, 32:64], -1.0, None, ALU.mult, None, [('wkr',)], [('wkr2',)])
                    cp('dve', wkr_t[:, :, 96:128], wkr_t[:, :, 0:32], [('wkr',)], [('wkr2',)])
                    for h in range(4):
                        ts('dve', wqr_t[:, :, h, 0:32], wq_t[:, :, h * 192 + 160:h * 192 + 192], -1.0, None, ALU.mult, None, [('wq',)], [('wqr',)])
                        cp('dve', wqr_t[:, :, h, 32:64], wq_t[:, :, h * 192 + 128:h * 192 + 160], [('wq',)], [('wqr',)])

                    XH = lambda tb: [('xhT', tb * 4 + i) for i in range(4)]
                    sqcnt = [0]
                    pend_n = [None]
                    for ci in range(5):
                        isq = ci < 3
                        cc = ci if isq else ci - 3
                        last = cc == (2 if isq else 1)
                        c0 = 2056 + ci * 128
                        if ci + 2 < 5:
                            pre_sl[ci + 2] = load_wchunk2(c0 + 256)
                        if ci == 0:
                            T.dma('pool', wo_t[:], w_out_v[:, :, :], writes=[('wo',)], key=('wo',))
                        sl = pre_sl[ci]
                        nrm = onesb if isq else c256b
                        knrm = ('onesb',) if isq else ('c256b',)
                        for tb in range(4):
                            bank = tb % 2
                            cs = slice(tb * 512, (tb + 1) * 512)
                            for kc in range(8):
                                mm(ps[bank][:, :], wst[sl][:, kc, :], xhT[:, kc, cs], kc == 0, kc == 7,
                                   [('wst', sl)] + XH(tb), [('ps', bank)])
                            if isq:
                                ts('dve', cqg[:, cc, cs], ps[bank][:, :], qg[:, cc:cc + 1], float(384 ** 0.5), ALU.mult, ALU.mult,
                                   [('ps', bank), ('cpp',)], [('cqg',)])
                            else:
                                ts('dve', ckvg[:, cc, cs], ps[bank][:, :], kvg[:, cc:cc + 1], None, ALU.mult, None,
                                   [('ps', bank), ('cpp',)], [('ckvg',)])
                            sqi = sqcnt[0] % 2
                            sqcnt[0] += 1
                            act(sqr[sqi][:], ps[bank][:, :], AF.Square, [('ps', bank)], [('sqr', sqi)])
                            if pend_n[0] is not None:
                                pend_n[0]()

                            def _norm(tb=tb, cs=cs, nrm=nrm, knrm=knrm, sqi=sqi, cc=cc, last=last, isq=isq):
                                mm(ps[2 + tb][:, :], nrm[:], sqr[sqi][:], cc == 0, last, [knrm, ('sqr', sqi)], [('ps', 2 + tb)], inc=True)
                                if last:
                                    if isq:
                                        rsqrt(rsq[:, cs], ps[2 + tb][:, :], eps384, [('ps', 2 + tb)], [('rsq',)])
                                    else:
                                        rsqrt(rskv[:, cs], ps[2 + tb][:, :], eps1, [('ps', 2 + tb)], [('rskv',)])
                                        for c_ in range(2):
                                            tt('dve', ckvg[:, c_, cs], ckvg[:, c_, cs], rskv[:, cs], ALU.mult, [('ckvg',), ('rskv',)], [('ckvg',)])
                            pend_n[0] = _norm
                        if ci == 4:
                            pend_n[0]()
                            pend_n[0] = None
                    for tb in range(4):
                        cs = slice(tb * 512, (tb + 1) * 512)
                        bA, bB = (5, 6) if tb % 2 == 0 else (0, 1)
                        for kc in range(8):
                            mm(ps[bA][0:64, :], wkr_t[:, kc, 0:64], xhT[:, kc, cs], kc == 0, kc == 7, [('wkr',)] + XH(tb), [('ps', bA)])
                        for kc in range(8):
                            mm(ps[bB][0:64, :], wkr_t[:, kc, 64:128], xhT[:, kc, cs], kc == 0, kc == 7, [('wkr2',)] + XH(tb), [('ps', bB)])
                        tt('dve', t1[tb % 2][:], ps[bA][0:64, :], cos2[:, cs], ALU.mult, [('ps', bA), ('ctab',)], [('t1', tb % 2)])
                        tt('dve', t2[tb % 2][:], ps[bB][0:64, :], sin2[:, cs], ALU.mult, [('ps', bB), ('ctab',)], [('t2', tb % 2)])
                        tt('pool', krT[:, cs], t1[tb % 2][:], t2[tb % 2][:], ALU.add, [('t1', tb % 2), ('t2', tb % 2)], [('krT',)])
                    scale = float(192 ** -0.5)
                    ptc = [0]
                    for h in range(4):
                        par = 0
                        for tb in range(4):
                            cs = slice(tb * 512, (tb + 1) * 512)
                            b0, b1, b5, b6 = (0, 1, 5, 6) if tb % 2 == 0 else (2, 3, 4, 7)
                            for kc in range(3):
                                mm(ps[b0][:, :], wq_t[:, kc, h * 192:h * 192 + 128], cqg[:, kc, cs], kc == 0, kc == 2, [('wq',), ('cqg',)], [('ps', b0)])
                            for kc in range(3):
                                mm(ps[b5][0:64, :], wq_t[:, kc, h * 192 + 128:h * 192 + 192], cqg[:, kc, cs], kc == 0, kc == 2, [('wq',), ('cqg',)], [('ps', b5)])
                            for kc in range(3):
                                mm(ps[b6][0:64, :], wqr_t[:, kc, h, :], cqg[:, kc, cs], kc == 0, kc == 2, [('wqr',), ('cqg',)], [('ps', b6)])
                            for kc in range(2):
                                mm(ps[b1][:, :], wkv_t[:, kc, h * 256:h * 256 + 128], ckvg[:, kc, cs], kc == 0, kc == 1, [('wkv',), ('ckvg',)], [('ps', b1)])
                            tt('dve', qn[par][:, cs], ps[b0][:, :], rsq[:, cs], ALU.mult, [('ps', b0), ('rsq',)], [('qn', par)])
                            tt('dve', t1[tb % 2][:], ps[b5][0:64, :], cos2[:, cs], ALU.mult, [('ps', b5), ('ctab',)], [('t1', tb % 2)])
                            tt('dve', t2[tb % 2][:], ps[b6][0:64, :], sin2[:, cs], ALU.mult, [('ps', b6), ('ctab',)], [('t2', tb % 2)])
                            cp('act', kn[par][:, cs], ps[b1][:, :], [('ps', b1)], [('kn', par)])
                            tt('pool', t1[tb % 2][:], t1[tb % 2][:], t2[tb % 2][:], ALU.add, [('t1', tb % 2), ('t2', tb % 2)], [('t1', tb % 2)])
                            tt('pool', qr[par][:, cs], t1[tb % 2][:], rsq[0:64, cs], ALU.mult, [('t1', tb % 2), ('rsq',)], [('qr', par)])
                        for g4 in range(NT // 4):
                            bank = 2 + g4 % 2
                            for tl in range(4):
                                t = g4 * 4 + tl
                                for kc in range(2):
                                    mm(ps[bank][:, tl * 128:(tl + 1) * 128], ckvg[:, kc, t * 128:(t + 1) * 128],
                                       wkv_t[:, kc, h * 256 + 128:h * 256 + 256], kc == 0, kc == 1,
                                       [('wkv',), ('ckvg',)], [('ps', bank)], inc=(kc == 1 and tl == 3))
                            cp('act' if g4 % 2 else 'dve', vh[par][:, g4 * 4:(g4 + 1) * 4, :],
                               ps[bank][:, :].rearrange("p (t c) -> p t c", t=4), [('ps', bank)], [('vh', par)])
                        items = [(qb, kt) for qb in range(4) for kt in range(4 * qb + 4)]

                        def att_S(idx):
                            qb, kt = items[idx]
                            r = max(0, kt - 4 * qb)
                            c0 = qb * 512 + r * 128
                            ncol = 512 - r * 128
                            sb_ = ps[idx % 2]
                            ksb = ('ps', idx % 2)
                            mm(sb_[:, 0:ncol], kn[par][:, kt * 128:(kt + 1) * 128], qn[par][:, c0:c0 + ncol], True, False,
                               [('kn', par), ('qn', par)], [ksb], inc=False)
                            mm(sb_[:, 0:ncol], krT[:, kt * 128:(kt + 1) * 128], qr[par][:, c0:c0 + ncol], False, True,
                               [('krT',), ('qr', par)], [ksb])

                        def att_EV(idx):
                            qb, kt = items[idx]
                            nk = 4 * qb + 4
                            r = max(0, kt - 4 * qb)
                            ncol = 512 - r * 128
                            sb_ = ps[idx % 2]
                            ksb = ('ps', idx % 2)
                            pO_, pD_ = ps[4 + (qb % 2) * 2], ps[5 + (qb % 2) * 2]
                            kO, kD = ('ps', 4 + (qb % 2) * 2), ('ps', 5 + (qb % 2) * 2)
                            pi = ptc[0] % 3
                            ptc[0] += 1
                            act(PT[pi][:, 0:ncol], sb_[:, 0:ncol], AF.Exp, [ksb], [('PT', pi)], scale=scale)
                            if kt >= 4 * qb:
                                T.issue('pool', lambda e: e.memset(PT[pi][64:128, 0:64], 0.0), [], [('PT', pi)])
                            mm(pO_[:, r * 128:512], vh[par][:, kt, :], PT[pi][:, 0:ncol], kt == 0, kt == nk - 1, [('vh', par), ('PT', pi)], [kO])
                            mm(pD_[:, r * 128:512], onesb[:], PT[pi][:, 0:ncol], kt == 0, kt == nk - 1, [('onesb',), ('PT', pi)], [kD])
                            if kt == nk - 1:
                                dq = den[qb % 2]
                                T.issue('dve', lambda e: e.reciprocal(out=dq[:], in_=pD_[:, :]), [kD], [('den', qb % 2)])
                                tt('dve', mixT[:, 4 + h, qb * 512:(qb + 1) * 512], pO_[:, :], dq[:], ALU.mult, [kO, ('den', qb % 2)], [('mixT', 4 + h)])

                        att_S(0)
                        for idx in range(len(items)):
                            if idx + 1 < len(items):
                                att_S(idx + 1)
                            att_EV(idx)
                    T.barrier()
                dump('mixB', mixT[:, 4:8, :], [('mixT', 4 + h) for h in range(4)])
                if stop == 'mixB':
                    return True

                with ExitStack() as sc:
                    ln1_t = sb(sc, "ln1_t", [128, 2048], F32)
                    T.dma('sp', ln1_t[:], cbc[:, 0:2048], writes=[('cbcL',)], key=('cbcL',))
                    G1 = ln1_t[:, 0:1024]
                    B1 = ln1_t[:, 1024:2048]
                    xr = [sb(sc, f"xr{i}", [128, D], F32) for i in range(3)]
                    rr = [sb(sc, f"rr{i}", [128, D], F32) for i in range(3)]
                    hh = [sb(sc, f"hh{i}", [128, D], F32) for i in range(2)]
                    stats3 = [sb(sc, f"stats{i}", [128, 12], F32) for i in range(3)]
                    mv3 = [sb(sc, f"mv{i}", [128, 2], F32) for i in range(3)]
                    rs3 = [sb(sc, f"rs{i}", [128, 1], F32) for i in range(3)]

                    def d1_ldx(t):
                        q_ = t % 3
                        T.dma('sp', xr[q_][:], x[row0 + t * 128: row0 + (t + 1) * 128, :], writes=[('xr', q_)], key=('xr', q_))

                    def d1_mm(t):
                        sl = t % 2
                        tok = slice(t * 128, (t + 1) * 128)
                        for hb in range(2):
                            bank = sl * 2 + hb
                            for kc in range(8):
                                mm(ps[bank][:, :], mixT[:, kc, tok], wo_t[:, kc, hb * 512:(hb + 1) * 512], kc == 0, kc == 7,
                                   [('mixT', kc), ('wo',)], [('ps', bank)])

                    def d1_res(t):
                        sl = t % 2
                        q_ = t % 3
                        for hb in range(2):
                            bank = sl * 2 + hb
                            stt(rr[q_][:, hb * 512:(hb + 1) * 512], xr[q_][:, hb * 512:(hb + 1) * 512], ALPHA, ps[bank][:, :], ALU.mult, ALU.add,
                                [('xr', q_), ('ps', bank)], [('rr', q_)])

                    def d1_lna(t):
                        q_ = t % 3
                        r, stats, mv, rs = rr[q_], stats3[q_], mv3[q_], rs3[q_]
                        kr = ('rr', q_)
                        for hb in range(2):
                            T.issue('dve', lambda e: e.bn_stats(out=stats[:, hb * 6:(hb + 1) * 6], in_=r[:, hb * 512:(hb + 1) * 512]),
                                    reads=[kr], writes=[('lnst', q_)])
                        T.issue('dve', lambda e: e.bn_aggr(out=mv[:], in_=stats[:]), reads=[('lnst', q_)], writes=[('lnmv', q_)])
                        rsqrt(rs[:], mv[:, 1:2], eps1, [('lnmv', q_)], [('lnrs', q_)])
                        stt(mv[:, 1:2], mv[:, 0:1], -1.0, rs[:, 0:1], ALU.mult, ALU.mult, [('lnmv', q_), ('lnrs', q_)], [('lnmv', q_)])
                        act(r[:], r[:], AF.Identity, [kr, ('lnmv', q_), ('lnrs', q_)], [kr], bias=mv[:, 1:2], scale=rs[:, 0:1])

                    def d1_lnb(t):
                        q_ = t % 3
                        sl = t % 2
                        r = rr[q_]
                        kr = ('rr', q_)
                        tt('dve', r[:], r[:], G1, ALU.mult, [kr, ('cbcL',)], [kr])
                        tt('dve', hh[sl][:], r[:], B1, ALU.add, [kr, ('cbcL',)], [('hh', sl)])
                        T.dma('sp', hscr[t * 128:(t + 1) * 128, :], hh[sl][:], reads=[('hh', sl)], writes=[('hscr', t)], key=('hh', sl))

                    def d1_tr(t):
                        sl = t % 2
                        tok = slice(t * 128, (t + 1) * 128)
                        for kc in range(8):
                            bank = 4 + sl * 2 + kc // 4
                            col = (kc % 4) * 128
                            tr(ps[bank][:, col:col + 128], hh[sl][:, kc * 128:(kc + 1) * 128], [('hh', sl)], [('ps', bank)], inc=(kc % 4 == 3))
                        for hb in range(2):
                            bank = 4 + sl * 2 + hb
                            cp('act', xhT[:, hb * 4:(hb + 1) * 4, tok], ps[bank][:, :].rearrange("p (k c) -> p k c", k=4), [('ps', bank)], [('xhT', t)])

                    for t_ in range(3):
                        d1_ldx(t_)
                    d1_mm(0)
                    d1_res(0)
                    d1_mm(1)
                    d1_res(1)
                    d1_lna(0)
                    for t in range(NT):
                        if t + 2 < NT:
                            d1_mm(t + 2)
                        if t + 1 < NT:
                            d1_lna(t + 1)
                        d1_lnb(t)
                        if t + 2 < NT:
                            d1_res(t + 2)
                        if t + 3 < NT:
                            d1_ldx(t + 3)
                        d1_tr(t)
                    T.barrier()
            dump('hT', xhT[:], [('xhT', t) for t in range(NT)])
            if stop == 'hT':
                return True

            with ExitStack() as p2:
                wd_t = sb(p2, "wd_t", [128, NJ, 1024], BF16)
                ln2_t = sb(p2, "ln2_t", [128, 3072], F32)
                T.dma('sp', ln2_t[:], cbc[:, 2048:5120], writes=[('cbcL',)], key=('cbcL2',))
                G2 = ln2_t[:, 0:1024]
                B2 = ln2_t[:, 1024:2048]
                BG = ln2_t[:, 2048:3072]
                wg_t = sb(p2, "wg_t", [128, 8, 1024], BF16)
                wp_t = sb(p2, "wp_t", [128, 2, 1024], BF16)
                actT = sb(p2, "actT", [128, NJ, BLK2], BF16)
                wup = [sb(p2, f"wup{i}", [128, 2, 8, 128], BF16) for i in range(3)]
                rawgu = [sb(p2, f"rawgu{i}", [128, 2, 2 + BLK2], F32) for i in range(2)]
                accg = [sb(p2, f"accg{i}", [128, BLK2], F32) for i in range(2)]
                accu = [sb(p2, f"accu{i}", [128, BLK2], F32) for i in range(2)]
                halo = sb(p2, "halo", [128, 2, NJ, 2], F32)
                hr_ = [sb(p2, f"hr{i}", [128, D], F32) for i in range(2)]
                r2_ = [sb(p2, f"r2{i}", [128, D], F32) for i in range(2)]
                sg_ = [sb(p2, f"sgt{i}", [128, D], F32) for i in range(2)]
                pin_ = [sb(p2, f"pin{i}", [128, 256], F32) for i in range(2)]
                pTb_ = [sb(p2, f"pTb{i}", [128, 2, 128], BF16) for i in range(2)]
                stats_ = [sb(p2, f"stats2{i}", [128, 12], F32) for i in range(2)]
                mv_ = [sb(p2, f"mv2{i}", [128, 2], F32) for i in range(2)]
                rs_ = [sb(p2, f"rs2{i}", [128, 1], F32) for i in range(2)]
                T.issue('pool', lambda e: e.memset(halo[:], 0.0), writes=[('halo', c_) for c_ in range(NJ)])
                def ld2(t_):
                    q_ = t_ % 2
                    T.dma('sp', hr_[q_][:], hscr[t_ * 128:(t_ + 1) * 128, :], reads=[('hscr', t_)], writes=[('hr', q_)], key=('hr', q_))
                    T.dma('sp', pin_[q_][:], p[row0 + t_ * 128:row0 + (t_ + 1) * 128, :], writes=[('pin', q_)], key=('pin', q_))

                def trp(t_):
                    q_ = t_ % 2
                    for kc in range(2):
                        tr(ps[6 + q_][:, kc * 128:(kc + 1) * 128], pin_[q_][:, kc * 128:(kc + 1) * 128], [('pin', q_)], [('ps', 6 + q_)], inc=(kc == 1))
                    cp('act', pTb_[q_][:], ps[6 + q_][:, 0:256].rearrange("p (k c) -> p k c", k=2), [('ps', 6 + q_)], [('pTb', q_)])

                NB = S // BLK2
                TPB = BLK2 // 128
                wc = [0]

                def prefetch_wup(upto):
                    while wc[0] < min(upto, NB * NJ):
                        jj = wc[0] % NJ
                        s_ = wc[0] % 3
                        wc[0] += 1
                        T.dma('pool', wup[s_][:, 0, :, :], w_up_v[:, :, jj * 128:(jj + 1) * 128], writes=[('wup', s_, 0)], key=('wup', s_, 0))
                        T.dma('pool', wup[s_][:, 1, :, :], w_up_v[:, :, DFF + jj * 128:DFF + (jj + 1) * 128], writes=[('wup', s_, 1)], key=('wup', s_, 1))
                prefetch_wup(3)
                T.dma('pool', wg_t[:], w_gate_v[:, :, :], writes=[('wg',)], key=('wg',))
                T.dma('pool', wp_t[:], w_proj_v[:, :, :], writes=[('wp',)], key=('wp',))
                T.dma('pool', wd_t[:], w_down_v[:, :, :], writes=[('wd',)], key=('wd',))

                def stage_B2(j_):
                    q_ = j_ % 2
                    kg_, ku_ = ('acc2', 0, q_), ('acc2', 1, q_)
                    act(accg[q_][:], accg[q_][:], AF.Silu, [kg_], [kg_])
                    tt('dve', actT[:, j_, :], accg[q_][:], accu[q_][:], ALU.mult, [kg_, ku_], [('actT', j_)])

                for blk in range(NB):
                    bs = slice(blk * BLK2, (blk + 1) * BLK2)
                    XB = [('xhT', blk * TPB + i) for i in range(TPB)]
                    for j in range(NJ):
                        step = blk * NJ + j
                        prefetch_wup(step + 3)
                        sl = step % 3
                        jp = j % 2
                        rg = rawgu[jp]
                        kr_ = ('raw2', jp)
                        for gu in range(2):
                            bank = jp * 2 + gu
                            for kc in range(8):
                                mm(ps[bank][:, :], wup[sl][:, gu, kc, :], xhT[:, kc, bs], kc == 0, kc == 7, [('wup', sl, gu)] + XB, [('ps', bank)])
                        cp('act', rg[:, :, 0:2], halo[:, :, j, :], [('halo', j)], [kr_ + ('h',)])
                        cp('act', rg[:, :, 2:2 + BLK2], ps_all[:, jp * 1024:(jp + 1) * 1024].rearrange("p (g c) -> p g c", g=2),
                           [('ps', jp * 2), ('ps', jp * 2 + 1)], [kr_])
                        cp('act', halo[:, :, j, :], rg[:, :, BLK2:BLK2 + 2], [kr_], [('halo', j)])
                        acs = (accg[jp], accu[jp])
                        kas = (('acc2', 0, jp), ('acc2', 1, jp))
                        act(acs[0][:], ps[jp * 2][:, :], AF.Identity, [('ps', jp * 2), ('cpp',)], [kas[0]], bias=fcb[:, j:j + 1], scale=fcw[:, j, 2:3])
                        ts('dve', acs[1][:], rg[:, 1, 2:2 + BLK2], fcw[:, NJ + j, 2:3], fcb[:, NJ + j:NJ + j + 1], ALU.mult, ALU.add, [kr_, ('cpp',)], [kas[1]])
                        for tap in (1, 0):
                            for gu in range(2):
                                cidx = gu * NJ + j
                                stt(acs[gu][:], rg[:, gu, tap:tap + BLK2], fcw[:, cidx, tap:tap + 1], acs[gu][:], ALU.mult, ALU.add,
                                    [kr_, kr_ + ('h',), kas[gu], ('cpp',)], [kas[gu]])
                        if j >= 1:
                            stage_B2(j - 1)
                    stage_B2(NJ - 1)
                    AK = [('actT', j) for j in range(NJ)]
                    if stop == 'p2a' and blk == 0:
                        T.barrier()
                        return True
                    for tl in range(TPB):
                        t = blk * TPB + tl
                        tok = slice(t * 128, (t + 1) * 128)
                        ltok = slice(tl * 128, (tl + 1) * 128)
                        grow = row0 + t * 128
                        tp_ = t % 2
                        hr, r2, sg, pin, pTb = hr_[tp_], r2_[tp_], sg_[tp_], pin_[tp_], pTb_[tp_]
                        kh, kr2, kpin, kpt = ('hr', tp_), ('r2', tp_), ('pin', tp_), ('pTb', tp_)
                        if t == 0:
                            ld2(0)
                            trp(0)
                        if t + 1 < NT:
                            ld2(t + 1)
                        for hb in range(2):
                            hs = slice(hb * 512, (hb + 1) * 512)
                            ksg = ('sg', hb, tp_)
                            for kc in range(8):
                                mm(ps[2 + hb][:, :], xhT[:, kc, tok], wg_t[:, kc, hs], kc == 0, kc == 7, [('xhT', t), ('wg',)], [('ps', 2 + hb)])
                            for kc in range(2):
                                mm(ps[4 + hb][:, :], pTb[:, kc, :], wp_t[:, kc, hs], kc == 0, kc == 1, [kpt, ('wp',)], [('ps', 4 + hb)])
                            for j in range(NJ):
                                mm(ps[hb][:, :], actT[:, j, ltok], wd_t[:, j, hs], j == 0, j == NJ - 1, AK + [('wd',)], [('ps', hb)])
                            tt('dve', sg[:, hs], ps[2 + hb][:, :], BG[:, hs], ALU.add, [('ps', 2 + hb), ('cbcL',)], [ksg])
                            act(sg[:, hs], sg[:, hs], AF.Sigmoid, [ksg], [ksg])
                            tt('dve', sg[:, hs], sg[:, hs], ps[4 + hb][:, :], ALU.mult, [ksg, ('ps', 4 + hb)], [ksg])
                            stt(r2[:, hs], hr[:, hs], ALPHA, ps[hb][:, :], ALU.mult, ALU.add, [kh, ('ps', hb)], [kr2])
                            tt('dve', r2[:, hs], r2[:, hs], sg[:, hs], ALU.add, [kr2, ksg], [kr2])
                        if t + 1 < NT:
                            trp(t + 1)
                        ln_tile(r2, G2, B2, (stats_[tp_], mv_[tp_], rs_[tp_]), kr2, r2[:], kr2, sfx=tp_)
                        T.dma('sp', out[grow:grow + 128, :], r2[:], reads=[kr2], writes=[('out', sq, t)], key=('r2o', tp_))
                    if stop == 'p2b' and blk == 0:
                        T.barrier()
                        return True
                T.barrier()
          for sq in range(NSEQ):
            if seq_body(sq):
                break
        T.final_wait()
    nc._trk_log = T.log
    return nc


def _prep_common(inp):
    f = np.float32
    g = lambda k: np.asarray(inp[k], dtype=f)[0]
    cw = g("gdn_conv_w")
    fw = g("ffn_conv_w")
    fb = g("ffn_conv_b")
    cpp = np.zeros((128, 512), f)
    cpp[:, 0:48] = cw.reshape(4, 12, 128).transpose(2, 1, 0).reshape(128, 48)
    cpp[:, 48:180] = fw.reshape(3, 44, 128).transpose(2, 1, 0).reshape(128, 132)
    cpp[:, 180:224] = fb.reshape(44, 128).T
    cpp[:, 224] = g("gdn_norm_g")
    cpp[:, 225:228] = g("mla_q_norm_g").reshape(3, 128).T
    cpp[:, 228:230] = g("mla_kv_norm_g").reshape(2, 128).T
    cbc = np.zeros((128, 5 * 1024 + 128), f)
    for i, k in enumerate(["ln1_g", "ln1_b", "ln2_g", "ln2_b", "ple_b_gate"]):
        cbc[:, i * 1024:(i + 1) * 1024] = g(k)[None, :]
    cbc[:, 5120:5184] = np.tile(g("gdn_a_log"), 16)[None, :]
    cbc[:, 5184:5248] = np.tile(g("gdn_dt_bias"), 16)[None, :]
    j = np.arange(128)[:, None]
    i = np.arange(128)[None, :]
    cmat = np.zeros((128, 1792), f)
    cmat[:, 0:128] = np.eye(128, dtype=f)
    cmat[:, 128:256] = (j <= i).astype(f)
    for q_ in range(4):
        cmat[:, 256 + q_ * 128:384 + q_ * 128] = np.where(j <= i, 0.0, -30000.0).astype(f)
        cmat[:, 768 + q_ * 128:896 + q_ * 128] = np.where(j < i, 0.0, -30000.0).astype(f)
        cmat[:, 1280 + q_ * 128:1408 + q_ * 128] = np.eye(128, dtype=f)
    inv = (np.float32(10000.0) ** (-(np.arange(0, 64, 2, dtype=f)) / np.float32(64))).astype(f)
    ang = (np.arange(S, dtype=f)[:, None] * inv[None, :]).astype(f)
    cos = np.cos(ang.astype(np.float64)).astype(f).T
    sin = np.sin(ang.astype(np.float64)).astype(f).T
    ctab = np.zeros((64, 2 * S), f)
    ctab[0:32, 0:S] = cos
    ctab[32:64, 0:S] = cos
    ctab[0:32, S:] = sin
    ctab[32:64, S:] = sin
    return {
        "w_in": np.ascontiguousarray(g("w_in")), "wq_up": np.ascontiguousarray(g("mla_w_q_up")),
        "wkv_up": np.ascontiguousarray(g("mla_w_kv_up")), "w_out": np.ascontiguousarray(g("w_out")),
        "w_up": np.ascontiguousarray(g("ffn_w_up")), "w_down": np.ascontiguousarray(g("ffn_w_down")),
        "w_gate": np.ascontiguousarray(g("ple_w_gate")), "w_proj": np.ascontiguousarray(g("ple_w_proj")),
        "cpp": cpp, "cbc": cbc, "cmat": cmat, "ctab": ctab,
    }


def kernel(**inputs):
    common = _prep_common(inputs)
    x = np.asarray(inputs["x"], dtype=np.float32)
    p = np.asarray(inputs["p"], dtype=np.float32)[0]
    B = x.shape[0]
    nseq = B // NCORES
    nc = build(nseq)
    in_maps = []
    for c in range(NCORES):
        m = dict(common)
        m["x"] = np.ascontiguousarray(x[c * nseq:(c + 1) * nseq].reshape(nseq * S, D))
        m["p"] = np.ascontiguousarray(p[c * nseq:(c + 1) * nseq].reshape(nseq * S, 256))
        in_maps.append(m)
    res = run_bass_kernel_spmd(nc, in_maps, core_ids=list(range(NCORES)))
    outs = [np.asarray(r["out"]).reshape(nseq, S, D) for r in res.results]
    return np.concatenate(outs, axis=0).astype(np.float32)
```

```python
import numpy as np
from contextlib import ExitStack
import concourse.bass as bass
import concourse.mybir as mybir
from concourse.bass_utils import run_bass_kernel_spmd

F32, BF16 = mybir.dt.float32, mybir.dt.bfloat16
AF = mybir.ActivationFunctionType
ALU = mybir.AluOpType

S = 2048
NT = 16
D = 1024
KC = 8
DFF = 2816
NJ = 22
ALPHA = float(2.0 ** 0.25)
EPS = 1e-6
BLK2 = 512
HS = 1024
NBLK = 2
NTH = 8
EPOCH = 16000
NCORES = 8
TWO_CHAINS = True
OVERLAP_A = False
SEQ_GENS = True


class Trk:
    def __init__(self, nc, es):
        self.nc, self.es = nc, es
        self.engs = {'pe': nc.tensor, 'act': nc.scalar, 'dve': nc.vector, 'pool': nc.gpsimd, 'sp': nc.sync}
        self.cnt = {e: 0 for e in self.engs}
        self.esems = {e: [] for e in self.engs}
        self.seen = {e: {} for e in self.engs}
        self.lastw = {}
        self.rd = {}
        self.dsem = {}
        self.pend = {e: ([], []) for e in self.engs}
        self.latest = {}
        self.log = {e: [] for e in self.engs}

    def newsem(self, name):
        return self.es.enter_context(self.nc.semaphore(name))

    def _wait(self, e, ev):
        sem, val, src = ev
        if src == 'pe' and e == 'pe':
            return
        k = id(sem)
        if self.seen[e].get(k, 0) >= val:
            return
        self.engs[e].wait_ge(sem, val)
        self.log[e].append(('w', id(sem), val))
        self.seen[e][k] = val

    def _deps(self, e, reads, writes):
        for k in reads:
            ev = self.lastw.get(k)
            if ev is not None:
                self._wait(e, ev)
        for k in writes:
            ev = self.lastw.get(k)
            if ev is not None:
                self._wait(e, ev)
            for ev in self.rd.get(k, {}).values():
                self._wait(e, ev)

    def _reg(self, ev, reads, writes):
        sem, val, src = ev
        self.latest[id(sem)] = (sem, val)
        for k in writes:
            self.lastw[k] = ev
            self.rd[k] = {}
        for k in reads:
            self.rd.setdefault(k, {})[id(sem)] = ev

    def issue(self, e, fn, reads=(), writes=(), inc=True):
        writes = list(writes) + [k for k in reads if k[0] == 'ps' and k not in writes]
        reads = [k for k in reads if k[0] != 'ps']
        self._deps(e, reads, writes)
        ins = fn(self.engs[e])
        pr, pw = self.pend[e]
        pr.extend(reads)
        pw.extend(writes)
        if inc:
            n = self.cnt[e]
            ep, off = divmod(n, EPOCH)
            if ep >= len(self.esems[e]):
                self.esems[e].append(self.newsem(f"s_{e}_{ep}"))
            sem = self.esems[e][ep]
            ins.then_inc(sem, 1)
            self.log[e].append(('i', id(sem), 1))
            self.cnt[e] = n + 1
            self._reg((sem, off + 1, e), pr, pw)
            self.pend[e] = ([], [])
        return ins

    def dma(self, q, out, in_, reads=(), writes=(), key=None):
        self._deps(q, reads, writes)
        ins = self.engs[q].dma_start(out=out, in_=in_)
        if key not in self.dsem:
            self.dsem[key] = [self.newsem("d_" + "_".join(str(x) for x in key)), 0]
        d = self.dsem[key]
        d[1] += 16
        ins.then_inc(d[0], 16)
        self.log[q].append(('i', id(d[0]), 16))
        self._reg((d[0], d[1], 'dma'), list(reads), list(writes))
        return ins

    def barrier(self):
        for e in self.engs:
            assert not self.pend[e][0] and not self.pend[e][1], e
        for sem, val in list(self.latest.values()):
            self._wait('sp', (sem, val, 'x'))
        n = self.cnt['sp']
        ep, off = divmod(n, EPOCH)
        if ep >= len(self.esems['sp']):
            self.esems['sp'].append(self.newsem(f"s_sp_{ep}"))
        sem = self.esems['sp'][ep]
        self.engs['sp'].sem_inc(sem, 1)
        self.log['sp'].append(('i', id(sem), 1))
        self.cnt['sp'] = n + 1
        ev = (sem, off + 1, 'sp')
        self.latest[id(sem)] = (sem, off + 1)
        self.seen['sp'][id(sem)] = off + 1
        for e in self.engs:
            if e != 'sp':
                self._wait(e, ev)
        self.lastw.clear()
        self.rd.clear()

    def final_wait(self):
        for sem, val in list(self.latest.values()):
            self._wait('sp', (sem, val, 'x'))


class _Stop(Exception):
    pass


def build(NSEQ, dbg=None, stop=None):
    dbg = dbg or {}
    nc = bass.Bass("TRN2", target_bir_lowering=False)

    def din(name, shape, dt=F32):
        return nc.dram_tensor(name, list(shape), dt, kind="ExternalInput").ap()

    x = din("x", [NSEQ * S, D])
    p = din("p", [NSEQ * S, 256])
    w_in = din("w_in", [D, 2760])
    wq_up = din("wq_up", [384, 768])
    wkv_up = din("wkv_up", [256, 1024])
    w_out = din("w_out", [1024, 1024])
    w_up = din("w_up", [D, 2 * DFF])
    w_down = din("w_down", [DFF, D])
    w_gate = din("w_gate", [D, D])
    w_proj = din("w_proj", [256, D])
    cpp = din("cpp", [128, 512])
    cbc = din("cbc", [128, 5 * 1024 + 128])
    cmat = din("cmat", [128, 14 * 128])
    ctab = din("ctab", [64, 2 * S])
    out = nc.dram_tensor("out", [NSEQ * S, D], F32, kind="ExternalOutput").ap()
    hscr = nc.dram_tensor("hscr", [S, D], F32).ap()
    dbg_t = {k: nc.dram_tensor("dbg_" + k, list(v[0]), v[1], kind="ExternalOutput").ap() for k, v in dbg.items()}

    w_in_v = w_in.rearrange("(kc p) n -> p kc n", p=128)
    wq_v = wq_up.rearrange("(kc p) n -> p kc n", p=128)
    wkv_v = wkv_up.rearrange("(kc p) n -> p kc n", p=128)
    w_out_v = w_out.rearrange("(kc p) n -> p kc n", p=128)
    w_up_v = w_up.rearrange("(kc p) n -> p kc n", p=128)
    w_down_v = w_down.rearrange("(kc p) n -> p kc n", p=128)
    w_gate_v = w_gate.rearrange("(kc p) n -> p kc n", p=128)
    w_proj_v = w_proj.rearrange("(kc p) n -> p kc n", p=128)

    with ExitStack() as es:
        T = Trk(nc, es)

        uid = [0]

        def sb(scope, name, shape, dt):
            uid[0] += 1
            return scope.enter_context(nc.sbuf_tensor(f"{name}_{uid[0]}", list(shape), dt))

        ps_all = es.enter_context(nc.psum_tensor("ps_all", [128, 8 * 512], F32))
        ps = [ps_all[:, b * 512:(b + 1) * 512] for b in range(8)]

        cpp_t = sb(es, "cpp_t", [128, 512], F32)
        cbc_t = sb(es, "cbc_t", [128, 128], F32)
        cmat_t = sb(es, "cmat_t", [128, 1792], F32)
        identb = sb(es, "identb", [128, 128], BF16)
        onesb = sb(es, "onesb", [128, 128], BF16)
        c128b = sb(es, "c128b", [128, 128], BF16)
        c256b = sb(es, "c256b", [128, 128], BF16)
        onesf = sb(es, "onesf", [128, 128], F32)
        w_ab = sb(es, "w_ab", [128, 8, 8], BF16)
        xhT = sb(es, "xhT", [128, 8, S], BF16)

        T.dma('sp', cpp_t[:], cpp[:, :], writes=[('cpp',)], key=('cpp',))
        T.dma('sp', cbc_t[:], cbc[:, 5120:5248], writes=[('cbc',)], key=('cbc',))
        T.dma('sp', cmat_t[:], cmat[:, :], writes=[('cmat',)], key=('cmat',))
        T.dma('pool', w_ab[:], w_in_v[:, :, 2048:2056], writes=[('w_ab',)], key=('w_ab',))
        ident = cmat_t[:, 0:128]
        Umat = cmat_t[:, 128:256]
        NEGM = cmat_t[:, 256:384]
        NEGM4 = cmat_t[:, 256:768].rearrange("p (i c) -> p i c", i=4)
        NEGMS4 = cmat_t[:, 768:1280].rearrange("p (i c) -> p i c", i=4)
        ident4 = cmat_t[:, 1280:1792].rearrange("p (i c) -> p i c", i=4)
        T.issue('dve', lambda e: e.tensor_copy(out=identb[:], in_=ident), reads=[('cmat',)], writes=[('identb',)])
        T.issue('pool', lambda e: e.memset(onesb[:], 1.0), writes=[('onesb',)])
        T.issue('pool', lambda e: e.memset(c128b[:], 1.0 / 128), writes=[('c128b',)])
        T.issue('pool', lambda e: e.memset(c256b[:], 1.0 / 256), writes=[('c256b',)])
        T.issue('pool', lambda e: e.memset(onesf[:], 1.0), writes=[('onesf',)])
        epst = sb(es, "epst", [128, 2], F32)
        T.issue('pool', lambda e: e.memset(epst[:, 0:1], EPS), writes=[('epst',)])
        T.issue('pool', lambda e: e.memset(epst[:, 1:2], 384 * EPS), writes=[('epst',)])
        eps1 = epst[:, 0:1]
        eps384 = epst[:, 1:2]
        gcw = cpp_t[:, 0:48].rearrange("p (c j) -> p c j", j=4)
        fcw = cpp_t[:, 48:180].rearrange("p (c j) -> p c j", j=3)
        fcb = cpp_t[:, 180:224]
        normg = cpp_t[:, 224:225]
        qg = cpp_t[:, 225:228]
        kvg = cpp_t[:, 228:230]
        ALOGB = cbc_t[:, 0:64]
        DTBB = cbc_t[:, 64:128]
        CONST = [('cpp',), ('cbc',), ('cmat',)]

        def act(out_, in_, func, reads, writes, **kw):
            return T.issue('act', lambda e: e.activation(out=out_, in_=in_, func=func, **kw), reads, writes)

        def tt(eng, out_, in0, in1, op, reads, writes):
            return T.issue(eng, lambda e: e.tensor_tensor(out=out_, in0=in0, in1=in1, op=op), reads, writes)

        def ts(eng, out_, in0, s1, s2, op0, op1, reads, writes):
            if s2 is None:
                return T.issue(eng, lambda e: e.tensor_scalar(out=out_, in0=in0, scalar1=s1, scalar2=None, op0=op0), reads, writes)
            return T.issue(eng, lambda e: e.tensor_scalar(out=out_, in0=in0, scalar1=s1, scalar2=s2, op0=op0, op1=op1), reads, writes)

        def stt(out_, in0, sc, in1, op0, op1, reads, writes):
            return T.issue('dve', lambda e: e.scalar_tensor_tensor(out=out_, in0=in0, scalar=sc, in1=in1, op0=op0, op1=op1), reads, writes)

        def cp(eng, out_, in_, reads, writes):
            if eng == 'act':
                return T.issue('act', lambda e: e.copy(out=out_, in_=in_), reads, writes)
            return T.issue(eng, lambda e: e.tensor_copy(out=out_, in_=in_), reads, writes)

        def rsqrt(out_, in_, eps_ap, reads, writes):
            act(out_, in_, AF.Ln, list(reads) + [('epst',)], writes, bias=eps_ap)
            act(out_, out_, AF.Exp, writes, writes, scale=-0.5)

        def amul(out_, in_, m, reads, writes):
            return T.issue('act', lambda e: e.mul(out=out_, in_=in_, mul=m), reads, writes)

        def mm(out_, lhsT, rhs, start, stop, reads, writes, inc=None):
            if inc is None:
                inc = stop
            return T.issue('pe', lambda e: e.matmul(out_, lhsT, rhs, start=start, stop=stop), reads, writes, inc=inc)

        def tr(out_, in_, reads, writes, inc=True):
            return T.issue('pe', lambda e: e.transpose(out_, in_, ident), list(reads) + [('cmat',)], writes, inc=inc)

        def dump(name, src, reads):
            if name in dbg_t:
                T.dma('sp', dbg_t[name], src, reads=reads, writes=[('dbg', name)], key=('dbg', name))

        def ln_tile(r, G, B, scope_tiles, key_r, out_tile, key_out, kc_=('cbcL',), sfx=''):
            stats, mv, rs = scope_tiles
            for hb in range(2):
                T.issue('dve', lambda e: e.bn_stats(out=stats[:, hb * 6:(hb + 1) * 6], in_=r[:, hb * 512:(hb + 1) * 512]),
                        reads=[key_r], writes=[('lnst', sfx)])
            T.issue('dve', lambda e: e.bn_aggr(out=mv[:], in_=stats[:]), reads=[('lnst', sfx)], writes=[('lnmv', sfx)])
            rsqrt(rs[:], mv[:, 1:2], eps1, [('lnmv', sfx)], [('lnrs', sfx)])
            ts('dve', r[:], r[:], mv[:, 0:1], rs[:, 0:1], ALU.subtract, ALU.mult, [key_r, ('lnmv', sfx), ('lnrs', sfx)], [key_r])
            tt('dve', r[:], r[:], G, ALU.mult, [key_r, kc_], [key_r])
            tt('dve', out_tile, r[:], B, ALU.add, [key_r, kc_], [key_out])

        if True:
          def seq_body(sq):
            row0 = sq * S
            with ExitStack() as p1:
                mixT = sb(p1, "mixT", [128, 8, S], BF16)
                with ExitStack() as sc:
                    xin = [sb(sc, f"xin{i}", [128, D], F32) for i in range(2)]
                    for t in range(NT):
                        sl = t % 2
                        T.dma('sp', xin[sl][:], x[row0 + t * 128: row0 + (t + 1) * 128, :], writes=[('xin', sl)], key=('xin', sl))
                        pb = (t % 2) * 2
                        for kc in range(8):
                            bank = pb + kc // 4
                            col = (kc % 4) * 128
                            tr(ps[bank][:, col:col + 128], xin[sl][:, kc * 128:(kc + 1) * 128], [('xin', sl)], [('ps', bank)], inc=(kc % 4 == 3))
                        for hb in range(2):
                            bank = pb + hb
                            cp('act' if hb == 0 else 'dve', xhT[:, hb * 4:(hb + 1) * 4, t * 128:(t + 1) * 128],
                               ps[bank][:, :].rearrange("p (k c) -> p k c", k=4), [('ps', bank)], [('xhT', t)])
                    T.barrier()
                dump('xT', xhT[:], [('xhT', t) for t in range(NT)])
                if stop == 'xT':
                    return True

                with ExitStack() as sc:
                    g_ab = sb(sc, "g_ab", [128, 128], F32)
                    g_beta = sb(sc, "g_beta", [128, 64], F32)
                    g_g = sb(sc, "g_g", [128, 64], F32)
                    g_tmp = sb(sc, "g_tmp", [128, 64], F32)
                    g_eal = sb(sc, "g_eal", [128, 64], F32)
                    g_gc = sb(sc, "g_gc", [128, 64], F32)
                    g_ngc = sb(sc, "g_ngc", [128, 64], F32)
                    g_eg = sb(sc, "g_eg", [128, 64], F32)
                    g_egl = sb(sc, "g_egl", [128, 64], F32)
                    g_egla = sb(sc, "g_egla", [128, 64], F32)
                    raw2 = [sb(sc, f"raw{i}", [128, 3 + HS], F32) for i in range(2)]
                    acc = sb(sc, "acc", [128, HS], F32)
                    sil2 = [sb(sc, f"sil{i}", [128, HS], F32) for i in range(2)]
                    sqb = sb(sc, "sqb", [128, HS], BF16)
                    rstd = [sb(sc, f"rstd{i}", [128, 512], F32) for i in range(2)]
                    halo_g = sb(sc, "halo_g", [128, 12, 3], F32)
                    zero3 = sb(sc, "zero3", [128, 3], F32)
                    wst = [sb(sc, f"wst{i}", [128, 8, 128], BF16) for i in range(3)]
                    hq = sb(sc, "hq", [128, 4, HS], BF16)
                    hk = sb(sc, "hk", [128, 4, HS], BF16)
                    hkg = sb(sc, "hkg", [128, 4, NTH, 128], BF16)
                    hkd = sb(sc, "hkd", [128, 4, NTH, 128], BF16)
                    hv = sb(sc, "hv", [128, 4, NTH, 128], BF16)
                    hz = sb(sc, "hz", [128, 4, HS], BF16)
                    S32 = sb(sc, "S32", [128, 4, 128], F32)
                    Sbf = sb(sc, "Sbf", [128, 4, 128], BF16)

                    def tmpp(name, dt):
                        return sb(sc, name, [128, 4, 128], dt)
                    Ug = tmpp("Ug", F32)
                    EGb = tmpp("EGb", F32)
                    ARG = tmpp("ARG", F32)
                    ARG2 = tmpp("ARG2", F32)
                    DT = tmpp("DT", F32)
                    DTs = tmpp("DTs", F32)
                    Nf = tmpp("Nf", F32)
                    Pb = [tmpp("Pb0_", BF16), tmpp("Pb1_", BF16)]
                    PTb = [tmpp("PTb0_", BF16), tmpp("PTb1_", BF16)]
                    Xb = [tmpp("Xb0_", BF16), tmpp("Xb1_", BF16)]
                    QKD = tmpp("QKD", BF16)
                    nw2T = tmpp("nw2T", BF16)
                    vnew = tmpp("vnew", BF16)
                    qgT = tmpp("qgT", BF16)
                    sqo = tmpp("sqo", BF16)
                    rso = tmpp("rso", F32)
                    o1 = tmpp("o1", F32)

                    T.issue('pool', lambda e: e.memset(zero3[:], 0.0), writes=[('zero3',)])
                    cur_half = [0]

                    for t in range(NT):
                        for kc in range(8):
                            mm(ps[7][:, t * 8:(t + 1) * 8], xhT[:, kc, t * 128:(t + 1) * 128], w_ab[:, kc, :], kc == 0, kc == 7,
                               [('xhT', t), ('w_ab',)], [('ps', 7)], inc=(kc == 7 and t == NT - 1))
                    cp('dve', g_ab[:], ps[7][:, 0:128], [('ps', 7)], [('g_ab',)])
                    abv = g_ab[:].rearrange("p (t c) -> p t c", c=8)
                    v64 = lambda tl: tl[:].rearrange("p (t c) -> p t c", c=4)
                    act(v64(g_beta), abv[:, :, 4:8], AF.Sigmoid, [('g_ab',)], [('g_beta',)])
                    tt('dve', v64(g_tmp), abv[:, :, 0:4], DTBB.rearrange("p (t c) -> p t c", c=4), ALU.add, [('g_ab',), ('cbc',)], [('g_tmp',)])
                    act(g_tmp[:], g_tmp[:], AF.Exp, [('g_tmp',)], [('g_tmp',)])
                    ts('dve', g_tmp[:], g_tmp[:], 1.0, None, ALU.add, None, [('g_tmp',)], [('g_tmp',)])
                    act(g_tmp[:], g_tmp[:], AF.Ln, [('g_tmp',)], [('g_tmp',)])
                    act(g_eal[:], ALOGB, AF.Exp, [('cbc',)], [('g_eal',)])
                    stt(g_g[:], g_tmp[:], -1.0, g_eal[:], ALU.mult, ALU.mult, [('g_tmp',), ('g_eal',)], [('g_g',)])
                    mm(ps[7][:, 128:192], Umat, g_g[:], True, True, [('cmat',), ('g_g',)], [('ps', 7)])
                    mm(ps[7][:, 192:256], onesf[:], g_g[:], True, True, [('onesf',), ('g_g',)], [('ps', 7)])
                    cp('dve', g_gc[:], ps[7][:, 128:192], [('ps', 7)], [('g_gc',)])
                    ts('dve', g_ngc[:], g_gc[:], -1.0, None, ALU.mult, None, [('g_gc',)], [('g_ngc',)])
                    act(g_eg[:], g_gc[:], AF.Exp, [('g_gc',)], [('g_eg',)])
                    tt('dve', g_egl[:], ps[7][:, 192:256], g_gc[:], ALU.subtract, [('ps', 7), ('g_gc',)], [('g_egl',)])
                    act(g_egl[:], g_egl[:], AF.Exp, [('g_egl',)], [('g_egl',)])
                    act(g_egla[:], ps[7][:, 192:256], AF.Exp, [('ps', 7)], [('g_egla',)])
                    GS = [('g_beta',), ('g_gc',), ('g_ngc',), ('g_eg',), ('g_egl',), ('g_egla',), ('g_g',)]

                    wcnt = [0]

                    def load_wchunk(c0):
                        sl = wcnt[0] % 3
                        wcnt[0] += 1
                        T.dma('pool', wst[sl][:], w_in_v[:, :, c0:c0 + 128], writes=[('wst', sl)], key=('wst', sl))
                        return sl

                    def proj_block(sl, tb, bank):
                        gtb = cur_half[0] * NBLK + tb
                        for kc in range(8):
                            mm(ps[bank][:, :], wst[sl][:, kc, :], xhT[:, kc, gtb * 512:(gtb + 1) * 512], kc == 0, kc == 7,
                               [('wst', sl)] + [('xhT', gtb * 4 + i) for i in range(4)], [('ps', bank)])

                    trc = [0]

                    def stage_P(ch, tb):
                        kind, h, cidx, sl, ci = ch
                        par = h
                        rp = ci % 2
                        raw = raw2[rp]
                        bank = 6 + tb % 2
                        cs = slice(tb * 512, (tb + 1) * 512)
                        proj_block(sl, tb, bank)
                        if kind == 'z':
                            act(hz[:, par, cs], ps[bank][:, :], AF.Silu, [('ps', bank)], [('hz', par)])
                            return
                        if tb == 0:
                            if cur_half[0] == 0:
                                cp('act', raw[:, 0:3], zero3[:], [('zero3',)], [('raw', rp, -1)])
                            else:
                                cp('act', raw[:, 0:3], halo_g[:, cidx, :], [('halo_g', cidx)], [('raw', rp, -1)])
                        cp('act', raw[:, 3 + tb * 512: 3 + (tb + 1) * 512], ps[bank][:, :], [('ps', bank)], [('raw', rp, tb)])
                        if tb == NBLK - 1 and cur_half[0] == 0:
                            cp('act', halo_g[:, cidx, :], raw[:, HS:HS + 3], [('raw', rp, tb)], [('halo_g', cidx)])

                    def stage_C(ch, tb):
                        kind, h, cidx, sl, ci = ch
                        if kind == 'z':
                            return
                        rp = ci % 2
                        raw = raw2[rp]
                        cs = slice(tb * 512, (tb + 1) * 512)
                        RK = [('raw', rp, tb), ('raw', rp, tb - 1), ('cpp',)]
                        ka = ('acc', tb)
                        ts('dve', acc[:, cs], raw[:, 3 + tb * 512: 3 + (tb + 1) * 512], gcw[:, cidx, 3:4], None, ALU.mult, None, RK, [ka])
                        for j in (2, 1, 0):
                            stt(acc[:, cs], raw[:, j + tb * 512: j + (tb + 1) * 512], gcw[:, cidx, j:j + 1], acc[:, cs], ALU.mult, ALU.add,
                                RK + [ka], [ka])

                    def stage_S(ch, tb):
                        kind, h, cidx, sl, ci = ch
                        if kind == 'z':
                            return
                        sp_ = ci % 2
                        cs = slice(tb * 512, (tb + 1) * 512)
                        act(sil2[sp_][:, cs], acc[:, cs], AF.Silu, [('acc', tb)], [('sil', sp_, tb)])

                    def stage_N(ch):
                        kind, h, cidx, sl, ci = ch
                        if kind not in ('q', 'k'):
                            return
                        par = h
                        sp_ = ci % 2
                        sl_ = sil2[sp_]
                        for tb in range(NBLK):
                            cs = slice(tb * 512, (tb + 1) * 512)
                            tt('pool', sqb[:, cs], sl_[:, cs], sl_[:, cs], ALU.mult, [('sil', sp_, tb)], [('sqb', tb)])
                            mm(ps[2 + tb][:, :], onesb[:], sqb[:, cs], True, True, [('onesb',), ('sqb', tb)], [('ps', 2 + tb)])
                        for tb in range(NBLK):
                            act(rstd[tb][:], ps[2 + tb][:, :], AF.Ln, [('ps', 2 + tb), ('epst',)], [('rstd', tb)], bias=eps1)
                        for tb in range(NBLK):
                            act(rstd[tb][:], rstd[tb][:], AF.Exp, [('rstd', tb)], [('rstd', tb)], scale=-0.5)
                        for tb in range(NBLK):
                            cs = slice(tb * 512, (tb + 1) * 512)
                            ks = ('sil', sp_, tb)
                            if kind == 'q':
                                stt(hq[:, par, cs], sl_[:, cs], float(128 ** -0.5), rstd[tb][:], ALU.mult, ALU.mult,
                                    [ks, ('rstd', tb)], [('hq', par)])
                            else:
                                tt('dve', sl_[:, cs], sl_[:, cs], rstd[tb][:], ALU.mult, [ks, ('rstd', tb)], [ks])
                                cp('act', hk[:, par, cs], sl_[:, cs], [ks], [('hk', par)])

                    def stage_T(ch, tb):
                        kind, h, cidx, sl, ci = ch
                        if kind not in ('k', 'v'):
                            return
                        par = h
                        sp_ = ci % 2
                        ks = ('sil', sp_, tb)
                        if kind == 'v':
                            b3 = trc[0] % 2
                            trc[0] += 1
                            for tl in range(4):
                                n = tb * 4 + tl
                                tr(ps[b3][:, tl * 128:(tl + 1) * 128], sil2[sp_][:, n * 128:(n + 1) * 128], [ks], [('ps', b3)], inc=(tl == 3))
                            cp('act' if tb % 2 else 'dve', hv[:, par, tb * 4:(tb + 1) * 4, :],
                               ps[b3][:, :].rearrange("p (t c) -> p t c", t=4), [('ps', b3)], [('hv', par)])
                            return
                        bx = trc[0] % 2
                        by = 4 + trc[0] % 2
                        trc[0] += 1
                        for tl in range(4):
                            n = tb * 4 + tl
                            tr(ps[bx][:, tl * 128:(tl + 1) * 128], sil2[sp_][:, n * 128:(n + 1) * 128], [ks], [('ps', bx)], inc=False)
                            tr(ps[by][:, tl * 128:(tl + 1) * 128], sil2[sp_][:, n * 128:(n + 1) * 128], [ks], [('ps', by)], inc=(tl == 3))
                        for tl in range(4):
                            n = tb * 4 + tl
                            c = (cur_half[0] * NTH + n) * 4 + h
                            amul(hkg[:, par, n, :], ps[bx][:, tl * 128:(tl + 1) * 128], g_eg[:, c:c + 1], [('ps', bx), ('g_eg',)], [('hkg', par)])
                        for tl in range(4):
                            n = tb * 4 + tl
                            c = (cur_half[0] * NTH + n) * 4 + h
                            ts('dve', hkd[:, par, n, :], ps[by][:, tl * 128:(tl + 1) * 128], g_egl[:, c:c + 1], None, ALU.mult, None,
                               [('ps', by), ('g_egl',)], [('hkd', par)])

                    def run_A_quad():
                        chs = []
                        for h in range(4):
                            for kind, c0, cidx in (('q', h * 128, h), ('k', 512 + h * 128, 4 + h), ('v', 1024 + h * 128, 8 + h), ('z', 1536 + h * 128, 0)):
                                chs.append([kind, h, cidx, None, len(chs), c0])
                        nch = len(chs)
                        loaded = [0]

                        def ensure_loaded(upto):
                            while loaded[0] <= min(upto, nch - 1):
                                ch_ = chs[loaded[0]]
                                ch_[3] = load_wchunk(ch_[5])
                                loaded[0] += 1
                        nit = NBLK * nch
                        for tau in range(nit + NBLK + 5):
                            if tau < nit:
                                i, tb = divmod(tau, NBLK)
                                if tb == 0:
                                    ensure_loaded(i + 1)
                                stage_P(tuple(chs[i][:5]), tb)
                            if 0 <= tau - 1 < nit:
                                i, tb = divmod(tau - 1, NBLK)
                                stage_C(tuple(chs[i][:5]), tb)
                            if 0 <= tau - 2 < nit:
                                i, tb = divmod(tau - 2, NBLK)
                                stage_S(tuple(chs[i][:5]), tb)
                            tn = tau - (NBLK + 2)
                            if tn >= 0 and tn % NBLK == 0 and tn // NBLK < nch:
                                stage_N(tuple(chs[tn // NBLK][:5]))
                            if 0 <= tau - (NBLK + 3) < nit:
                                i, tb = divmod(tau - (NBLK + 3), NBLK)
                                stage_T(tuple(chs[i][:5]), tb)

                    def H4(b):
                        return ps[b][:, :].rearrange("p (i c) -> p i c", i=4), ('ps', b)

                    def rec_quad(half):
                        if half == 0:
                            T.issue('pool', lambda e: e.memset(S32[:], 0.0), writes=[('S32',)])
                            T.issue('pool', lambda e: e.memset(Sbf[:], 0.0), writes=[('Sbf',)])
                        pGb, kGb = H4(0)
                        pX, kX = H4(6)
                        pKK, kKK = H4(1)
                        pV, kV = H4(1)
                        pQK, kQK = H4(2)
                        pO, kO = H4(2)
                        pNT, kNT = H4(3)
                        pS, kS = H4(3)
                        pP, kP = H4(4)
                        pR, kR = H4(4)
                        pPT, kPT = H4(5)
                        pW, kW = H4(0)
                        for n in range(NTH):
                            tok = slice(n * 128, (n + 1) * 128)
                            gtok = slice((half * NTH + n) * 128, (half * NTH + n + 1) * 128)
                            cc = [(half * NTH + n) * 4 + i for i in range(4)]
                            for i in range(4):
                                amul(Ug[:, i, :], Umat, g_g[:, cc[i]:cc[i] + 1], [('cmat',), ('g_g',)], [('Ug',)])
                            for i in range(4):
                                mm(pGb[:, i, :], onesf[:], Ug[:, i, :], True, True, [('onesf',), ('Ug',)], [kGb], inc=(i == 3))
                            tt('dve', ARG2[:], pGb, NEGMS4, ALU.add, [kGb, ('cmat',)], [('ARG2',)])
                            tt('dve', ARG[:], pGb, NEGM4, ALU.add, [kGb, ('cmat',)], [('ARG',)])
                            act(EGb[:], pGb, AF.Exp, [kGb], [('EGb',)])
                            for i in range(4):
                                mm(pKK[:, i, :], hk[:, i, tok], hk[:, i, tok], True, True, [('hk', i)], [kKK], inc=(i == 3))
                            for i in range(4):
                                mm(pQK[:, i, :], hk[:, i, tok], hq[:, i, tok], True, True, [('hk', i), ('hq', i)], [kQK], inc=(i == 3))
                            for i in range(4):
                                act(DTs[:, i, :], ARG2[:, i, :], AF.Exp, [('ARG2',), ('g_ngc',)], [('DTs',)], bias=g_ngc[:, cc[i]:cc[i] + 1])
                            for i in range(4):
                                act(DT[:, i, :], ARG[:, i, :], AF.Exp, [('ARG',), ('g_ngc',)], [('DT',)], bias=g_ngc[:, cc[i]:cc[i] + 1])
                            for i in range(4):
                                stt(Nf[:, i, :], pKK[:, i, :], g_beta[:, cc[i]:cc[i] + 1], DTs[:, i, :], ALU.mult, ALU.mult,
                                    [kKK, ('g_beta',), ('DTs',)], [('Nf',)])
                            for i in range(4):
                                tr(pNT[:, i, :], Nf[:, i, :], [('Nf',)], [kNT], inc=(i == 3))
                            cur = 0
                            cp('act', Pb[cur][:], Nf[:], [('Nf',)], [('Pb0',)])
                            cp('dve', PTb[cur][:], pNT, [kNT], [('PTb0',)])
                            tt('dve', Xb[cur][:], ident4, Nf[:], ALU.subtract, [('cmat',), ('Nf',)], [('Xb0',)])
                            tt('dve', QKD[:], pQK, DT[:], ALU.mult, [kQK, ('DT',)], [('QKD',)])
                            tt('pool', qgT[:], hq[:, :, tok], EGb[:], ALU.mult, [('hq', i_) for i_ in range(4)] + [('EGb',)], [('qgT',)])
                            xc = 0

                            def x_update(ptb_idx, step_):
                                nonlocal_xc = x_state[0]
                                xn = 1 - nonlocal_xc
                                for i in range(4):
                                    mm(pX[:, i, :], identb[:], Xb[nonlocal_xc][:, i, :], True, False, [('identb',), (f'Xb{nonlocal_xc}',)], [kX], inc=False)
                                    mm(pX[:, i, :], PTb[ptb_idx][:, i, :], Xb[nonlocal_xc][:, i, :], False, True,
                                       [(f'PTb{ptb_idx}',), (f'Xb{nonlocal_xc}',)], [kX], inc=(i == 3))
                                x_state[0] = xn
                                return xn
                            x_state = [0]
                            pend = None
                            for step in range(6):
                                nx = 1 - cur
                                kPc, kPTc = (f'Pb{cur}',), (f'PTb{cur}',)
                                kPn, kPTn = (f'Pb{nx}',), (f'PTb{nx}',)
                                for i in range(4):
                                    mm(pPT[:, i, :], Pb[cur][:, i, :], PTb[cur][:, i, :], True, True, [kPc, kPTc], [kPT], inc=(i == 3))
                                if step < 5:
                                    for i in range(4):
                                        mm(pP[:, i, :], PTb[cur][:, i, :], Pb[cur][:, i, :], True, True, [kPc, kPTc], [kP], inc=(i == 3))
                                if pend is not None:
                                    xn = x_update(pend, step)
                                cp('dve', PTb[nx][:], pPT, [kPT], [kPTn])
                                if step < 5:
                                    cp('act', Pb[nx][:], pP, [kP], [kPn])
                                if pend is not None:
                                    cp('act' if step % 2 else 'dve', Xb[xn][:], pX, [kX], [(f'Xb{xn}',)])
                                pend = nx
                                cur = nx
                            xn = x_update(pend, 6)
                            cp('act', Xb[xn][:], pX, [kX], [(f'Xb{xn}',)])
                            cur = xn
                            kT2 = (f'Xb{cur}',)
                            T2T = Xb[cur]
                            for i in range(4):
                                mm(pW[:, i, :], hkg[:, i, n, :], T2T[:, i, :], True, True, [('hkg', i), kT2], [kW], inc=(i == 3))
                            amul(nw2T[:], pW, -1.0, [kW], [('nw2T',)])
                            for i in range(4):
                                mm(pV[:, i, :], T2T[:, i, :], hv[:, i, n, :], True, False, [kT2, ('hv', i)], [kV], inc=False)
                                mm(pV[:, i, :], nw2T[:, i, :], Sbf[:, i, :], False, True, [('nw2T',), ('Sbf',)], [kV], inc=(i == 3))
                            for i in range(4):
                                ts('dve', vnew[:, i, :], pV[:, i, :], g_beta[:, cc[i]:cc[i] + 1], None, ALU.mult, None, [kV, ('g_beta',)], [('vnew',)])
                            for i in range(4):
                                mm(pS[:, i, :], hkd[:, i, n, :], vnew[:, i, :], True, True, [('hkd', i), ('vnew',)], [kS], inc=(i == 3))
                            for i in range(4):
                                mm(pO[:, i, :], Sbf[:, i, :], qgT[:, i, :], True, False, [('Sbf',), ('qgT',)], [kO], inc=False)
                                mm(pO[:, i, :], vnew[:, i, :], QKD[:, i, :], False, True, [('vnew',), ('QKD',)], [kO], inc=(i == 3))
                            for i in range(4):
                                stt(S32[:, i, :], S32[:, i, :], g_egla[:, cc[i]:cc[i] + 1], pS[:, i, :], ALU.mult, ALU.add,
                                    [('S32',), ('g_egla',), kS], [('S32',)])
                            cp('act', Sbf[:], S32[:], [('S32',)], [('Sbf',)])
                            act(sqo[:], pO, AF.Square, [kO], [('sqo',)])
                            for i in range(4):
                                mm(pR[:, i, :], c128b[:], sqo[:, i, :], True, True, [('c128b',), ('sqo',)], [kR], inc=(i == 3))
                            rsqrt(rso[:], pR, eps1, [kR], [('rso',)])
                            stt(o1[:], pO, normg, rso[:], ALU.mult, ALU.mult, [kO, ('cpp',), ('rso',)], [('o1',)])
                            tt('pool', mixT[:, 0:4, gtok], o1[:], hz[:, :, tok], ALU.mult, [('o1',)] + [('hz', i_) for i_ in range(4)],
                               [('mixT', i_) for i_ in range(4)])

                    def gen_rec_pair(half, pr):
                        ia = 2 * pr
                        ii = (ia, ia + 1)
                        sl_ = slice(ia, ia + 2)
                        B = 4 * pr

                        def HB(b, hf):
                            return ps[b][:, hf * 256:(hf + 1) * 256].rearrange("p (i c) -> p i c", i=2), ('ps', b)
                        pGb, kGb = HB(B, 0)
                        pW, kW = HB(B, 0)
                        pP, kP = HB(B, 1)
                        pR, kR = HB(B, 1)
                        pKK, kKK = HB(B + 1, 0)
                        pV, kV = HB(B + 1, 0)
                        pPT, kPT = HB(B + 1, 1)
                        pQK, kQK = HB(B + 2, 0)
                        pO, kO = HB(B + 2, 0)
                        pX, kX = HB(B + 2, 1)
                        pNT, kNT = HB(B + 3, 0)
                        pS, kS = HB(B + 3, 0)
                        K = lambda nm: (nm, pr)
                        last = ia + 1
                        for n in range(NTH):
                            tok = slice(n * 128, (n + 1) * 128)
                            gtok = slice((half * NTH + n) * 128, (half * NTH + n + 1) * 128)
                            cc = {i: (half * NTH + n) * 4 + i for i in ii}
                            for i in ii:
                                amul(Ug[:, i, :], Umat, g_g[:, cc[i]:cc[i] + 1], [('cmat',), ('g_g',)], [K('Ug')])
                            yield
                            for i in ii:
                                mm(pGb[:, i - ia, :], onesf[:], Ug[:, i, :], True, True, [('onesf',), K('Ug')], [kGb], inc=(i == last))
                            yield
                            tt('dve', ARG2[:, sl_, :], pGb, NEGMS4[:, 0:2, :], ALU.add, [kGb, ('cmat',)], [K('ARG2')])
                            tt('dve', ARG[:, sl_, :], pGb, NEGM4[:, 0:2, :], ALU.add, [kGb, ('cmat',)], [K('ARG')])
                            for i in ii:
                                mm(pKK[:, i - ia, :], hk[:, i, tok], hk[:, i, tok], True, True, [('hk', i)], [kKK], inc=(i == last))
                            for i in ii:
                                mm(pQK[:, i - ia, :], hk[:, i, tok], hq[:, i, tok], True, True, [('hk', i), ('hq', i)], [kQK], inc=(i == last))
                            yield
                            for i in ii:
                                act(DTs[:, i, :], ARG2[:, i, :], AF.Exp, [K('ARG2'), ('g_ngc',)], [K('DTs')], bias=g_ngc[:, cc[i]:cc[i] + 1])
                            for i in ii:
                                act(DT[:, i, :], ARG[:, i, :], AF.Exp, [K('ARG'), ('g_ngc',)], [K('DT')], bias=g_ngc[:, cc[i]:cc[i] + 1])
                            act(EGb[:, sl_, :], pGb, AF.Exp, [kGb], [K('EGb')])
                            yield
                            for i in ii:
                                stt(Nf[:, i, :], pKK[:, i - ia, :], g_beta[:, cc[i]:cc[i] + 1], DTs[:, i, :], ALU.mult, ALU.mult,
                                    [kKK, ('g_beta',), K('DTs')], [K('Nf')])
                            yield
                            for i in ii:
                                tr(pNT[:, i - ia, :], Nf[:, i, :], [K('Nf')], [kNT], inc=(i == last))
                            cur = 0
                            cp('act', Pb[cur][:, sl_, :], Nf[:, sl_, :], [K('Nf')], [K('Pb0')])
                            yield
                            cp('dve', PTb[cur][:, sl_, :], pNT, [kNT], [K('PTb0')])
                            tt('dve', Xb[cur][:, sl_, :], ident4[:, 0:2, :], Nf[:, sl_, :], ALU.subtract, [('cmat',), K('Nf')], [K('Xb0')])
                            tt('dve', QKD[:, sl_, :], pQK, DT[:, sl_, :], ALU.mult, [kQK, K('DT')], [K('QKD')])
                            tt('pool', qgT[:, sl_, :], hq[:, sl_, tok], EGb[:, sl_, :], ALU.mult, [('hq', i_) for i_ in ii] + [K('EGb')], [K('qgT')])
                            yield
                            xs = [0]

                            def x_update(ptb_idx):
                                xc_ = xs[0]
                                xn_ = 1 - xc_
                                for i in ii:
                                    mm(pX[:, i - ia, :], identb[:], Xb[xc_][:, i, :], True, False, [('identb',), K(f'Xb{xc_}')], [kX], inc=False)
                                    mm(pX[:, i - ia, :], PTb[ptb_idx][:, i, :], Xb[xc_][:, i, :], False, True,
                                       [K(f'PTb{ptb_idx}'), K(f'Xb{xc_}')], [kX], inc=(i == last))
                                xs[0] = xn_
                                return xn_
                            pend = None
                            for step in range(6):
                                nx = 1 - cur
                                kPc, kPTc = K(f'Pb{cur}'), K(f'PTb{cur}')
                                kPn, kPTn = K(f'Pb{nx}'), K(f'PTb{nx}')
                                for i in ii:
                                    mm(pPT[:, i - ia, :], Pb[cur][:, i, :], PTb[cur][:, i, :], True, True, [kPc, kPTc], [kPT], inc=(i == last))
                                if step < 5:
                                    for i in ii:
                                        mm(pP[:, i - ia, :], PTb[cur][:, i, :], Pb[cur][:, i, :], True, True, [kPc, kPTc], [kP], inc=(i == last))
                                if pend is not None:
                                    xn = x_update(pend)
                                yield
                                cp('dve', PTb[nx][:, sl_, :], pPT, [kPT], [kPTn])
                                if step < 5:
                                    cp('act', Pb[nx][:, sl_, :], pP, [kP], [kPn])
                                if pend is not None:
                                    cp('act' if step % 2 else 'dve', Xb[xn][:, sl_, :], pX, [kX], [K(f'Xb{xn}')])
                                yield
                                pend = nx
                                cur = nx
                            xn = x_update(pend)
                            yield
                            cp('act', Xb[xn][:, sl_, :], pX, [kX], [K(f'Xb{xn}')])
                            yield
                            cur = xn
                            kT2 = K(f'Xb{cur}')
                            T2T = Xb[cur]
                            for i in ii:
                                mm(pW[:, i - ia, :], hkg[:, i, n, :], T2T[:, i, :], True, True, [('hkg', i), kT2], [kW], inc=(i == last))
                            yield
                            amul(nw2T[:, sl_, :], pW, -1.0, [kW], [K('nw2T')])
                            yield
                            for i in ii:
                                mm(pV[:, i - ia, :], T2T[:, i, :], hv[:, i, n, :], True, False, [kT2, ('hv', i)], [kV], inc=False)
                                mm(pV[:, i - ia, :], nw2T[:, i, :], Sbf[:, i, :], False, True, [K('nw2T'), K('Sbf')], [kV], inc=(i == last))
                            yield
                            for i in ii:
                                ts('dve', vnew[:, i, :], pV[:, i - ia, :], g_beta[:, cc[i]:cc[i] + 1], None, ALU.mult, None, [kV, ('g_beta',)], [K('vnew')])
                            yield
                            for i in ii:
                                mm(pS[:, i - ia, :], hkd[:, i, n, :], vnew[:, i, :], True, True, [('hkd', i), K('vnew')], [kS], inc=(i == last))
                            for i in ii:
                                mm(pO[:, i - ia, :], Sbf[:, i, :], qgT[:, i, :], True, False, [K('Sbf'), K('qgT')], [kO], inc=False)
                                mm(pO[:, i - ia, :], vnew[:, i, :], QKD[:, i, :], False, True, [K('vnew'), K('QKD')], [kO], inc=(i == last))
                            yield
                            for i in ii:
                                stt(S32[:, i, :], S32[:, i, :], g_egla[:, cc[i]:cc[i] + 1], pS[:, i - ia, :], ALU.mult, ALU.add,
                                    [K('S32'), ('g_egla',), kS], [K('S32')])
                            act(sqo[:, sl_, :], pO, AF.Square, [kO], [K('sqo')])
                            yield
                            cp('act', Sbf[:, sl_, :], S32[:, sl_, :], [K('S32')], [K('Sbf')])
                            for i in ii:
                                mm(pR[:, i - ia, :], c128b[:], sqo[:, i, :], True, True, [('c128b',), K('sqo')], [kR], inc=(i == last))
                            yield
                            rsqrt(rso[:, sl_, :], pR, eps1, [kR], [K('rso')])
                            yield
                            stt(o1[:, sl_, :], pO, normg, rso[:, sl_, :], ALU.mult, ALU.mult, [kO, ('cpp',), K('rso')], [K('o1')])
                            yield
                            tt('pool', mixT[:, sl_, gtok], o1[:, sl_, :], hz[:, sl_, tok], ALU.mult, [K('o1')] + [('hz', i_) for i_ in ii],
                               [('mixT', i_) for i_ in ii])
                            yield

                    def rec_two_chains(half):
                        if half == 0:
                            for pr in range(2):
                                T.issue('pool', lambda e: e.memset(S32[:, 2 * pr:2 * pr + 2, :], 0.0), writes=[('S32', pr)])
                                T.issue('pool', lambda e: e.memset(Sbf[:, 2 * pr:2 * pr + 2, :], 0.0), writes=[('Sbf', pr)])
                        gens = [gen_rec_pair(half, 0), gen_rec_pair(half, 1)]
                        while gens:
                            for g_ in list(gens):
                                try:
                                    next(g_)
                                except StopIteration:
                                    gens.remove(g_)

                    for half in range(2):
                        cur_half[0] = half
                        run_A_quad()
                        if TWO_CHAINS:
                            rec_two_chains(half)
                        else:
                            rec_quad(half)
                    T.barrier()
                dump('mixA', mixT[:, 0:4, :], [('mixT', h) for h in range(4)])
                if stop == 'mixA':
                    return True

                wo_t = sb(p1, "wo_t", [128, 8, 1024], BF16)
                with ExitStack() as sc:
                    ctab_t = sb(sc, "ctab_t", [64, 2 * S], F32)
                    T.dma('sp', ctab_t[:], ctab[:, :], writes=[('ctab',)], key=('ctab',))
                    cos2 = ctab_t[:, 0:S]
                    sin2 = ctab_t[:, S:2 * S]
                    wq_t = sb(sc, "wq_t", [128, 3, 768], BF16)
                    wqr_t = sb(sc, "wqr_t", [128, 3, 4, 64], BF16)
                    wkv_t = sb(sc, "wkv_t", [128, 2, 1024], BF16)
                    wkr_t = sb(sc, "wkr_t", [128, 8, 128], BF16)
                    wst = [sb(sc, f"wstm{i}", [128, 8, 128], BF16) for i in range(3)]
                    cqg = sb(sc, "cqg", [128, 3, S], BF16)
                    ckvg = sb(sc, "ckvg", [128, 2, S], BF16)
                    sqr = [sb(sc, f"sqr{i}", [128, 512], BF16) for i in range(2)]
                    rsq = sb(sc, "rsq", [128, S], F32)
                    rskv = sb(sc, "rskv", [128, S], F32)
                    krT = sb(sc, "krT", [64, S], BF16)
                    t1 = [sb(sc, f"rt1_{i}", [64, 512], F32) for i in range(2)]
                    t2 = [sb(sc, f"rt2_{i}", [64, 512], F32) for i in range(2)]
                    qn = [sb(sc, f"qn{i}", [128, S], BF16) for i in range(1)]
                    qr = [sb(sc, f"qr{i}", [64, S], BF16) for i in range(1)]
                    kn = [sb(sc, f"kn{i}", [128, S], BF16) for i in range(1)]
                    vh = [sb(sc, f"vh{i}", [128, NT, 128], BF16) for i in range(1)]
                    PT = [sb(sc, f"PTt{i}", [128, 512], BF16) for i in range(3)]
                    den = [sb(sc, f"den{i}", [128, 512], F32) for i in range(2)]

                    wcnt = [0]

                    def load_wchunk2(c0):
                        sl = wcnt[0] % 3
                        wcnt[0] += 1
                        T.dma('pool', wst[sl][:], w_in_v[:, :, c0:c0 + 128], writes=[('wst', sl)], key=('wstm', sl))
                        return sl
                    pre_sl = {0: load_wchunk2(2056), 1: load_wchunk2(2056 + 128)}
                    T.dma('pool', wkr_t[:, :, 0:64], w_in_v[:, :, 2696:2760], writes=[('wkr',)], key=('wkr',))
                    T.dma('pool', wq_t[:], wq_v[:, :, :], writes=[('wq',)], key=('wq',))
                    T.dma('pool', wkv_t[:], wkv_v[:, :, :], writes=[('wkv',)], key=('wkv',))
                    ts('dve', wkr_t[:, :, 64:96], wkr_t[:, :, 32:64], -1.0, None, ALU.mult, None, [('wkr',)], [('wkr2',)])
                    cp('dve', wkr_t[:, :, 96:128], wkr_t[:, :, 0:32], [('wkr',)], [('wkr2',)])
                    for h in range(4):
                        ts('dve', wqr_t[:, :, h, 0:32], wq_t[:, :, h * 192 + 160:h * 192 + 192], -1.0, None, ALU.mult, None, [('wq',)], [('wqr',)])
                        cp('dve', wqr_t[:, :, h, 32:64], wq_t[:, :, h * 192 + 128:h * 192 + 160], [('wq',)], [('wqr',)])

                    XH = lambda tb: [('xhT', tb * 4 + i) for i in range(4)]
                    sqcnt = [0]
                    pend_n = [None]
                    for ci in range(5):
                        isq = ci < 3
                        cc = ci if isq else ci - 3
                        last = cc == (2 if isq else 1)
                        c0 = 2056 + ci * 128
                        if ci + 2 < 5:
                            pre_sl[ci + 2] = load_wchunk2(c0 + 256)
                        if ci == 0:
                            T.dma('pool', wo_t[:], w_out_v[:, :, :], writes=[('wo',)], key=('wo',))
                        sl = pre_sl[ci]
                        nrm = onesb if isq else c256b
                        knrm = ('onesb',) if isq else ('c256b',)
                        for tb in range(4):
                            bank = tb % 2
                            cs = slice(tb * 512, (tb + 1) * 512)
                            for kc in range(8):
                                mm(ps[bank][:, :], wst[sl][:, kc, :], xhT[:, kc, cs], kc == 0, kc == 7,
                                   [('wst', sl)] + XH(tb), [('ps', bank)])
                            if isq:
                                ts('dve', cqg[:, cc, cs], ps[bank][:, :], qg[:, cc:cc + 1], float(384 ** 0.5), ALU.mult, ALU.mult,
                                   [('ps', bank), ('cpp',)], [('cqg',)])
                            else:
                                ts('dve', ckvg[:, cc, cs], ps[bank][:, :], kvg[:, cc:cc + 1], None, ALU.mult, None,
                                   [('ps', bank), ('cpp',)], [('ckvg',)])
                            sqi = sqcnt[0] % 2
                            sqcnt[0] += 1
                            act(sqr[sqi][:], ps[bank][:, :], AF.Square, [('ps', bank)], [('sqr', sqi)])
                            if pend_n[0] is not None:
                                pend_n[0]()

                            def _norm(tb=tb, cs=cs, nrm=nrm, knrm=knrm, sqi=sqi, cc=cc, last=last, isq=isq):
                                mm(ps[2 + tb][:, :], nrm[:], sqr[sqi][:], cc == 0, last, [knrm, ('sqr', sqi)], [('ps', 2 + tb)], inc=True)
                                if last:
                                    if isq:
                                        rsqrt(rsq[:, cs], ps[2 + tb][:, :], eps384, [('ps', 2 + tb)], [('rsq',)])
                                    else:
                                        rsqrt(rskv[:, cs], ps[2 + tb][:, :], eps1, [('ps', 2 + tb)], [('rskv',)])
                                        for c_ in range(2):
                                            tt('dve', ckvg[:, c_, cs], ckvg[:, c_, cs], rskv[:, cs], ALU.mult, [('ckvg',), ('rskv',)], [('ckvg',)])
                            pend_n[0] = _norm
                        if ci == 4:
                            pend_n[0]()
                            pend_n[0] = None
                    for tb in range(4):
                        cs = slice(tb * 512, (tb + 1) * 512)
                        bA, bB = (5, 6) if tb % 2 == 0 else (0, 1)
                        for kc in range(8):
                            mm(ps[bA][0:64, :], wkr_t[:, kc, 0:64], xhT[:, kc, cs], kc == 0, kc == 7, [('wkr',)] + XH(tb), [('ps', bA)])
                        for kc in range(8):
                            mm(ps[bB][0:64, :], wkr_t[:, kc, 64:128], xhT[:, kc, cs], kc == 0, kc == 7, [('wkr2',)] + XH(tb), [('ps', bB)])
                        tt('dve', t1[tb % 2][:], ps[bA][0:64, :], cos2[:, cs], ALU.mult, [('ps', bA), ('ctab',)], [('t1', tb % 2)])
                        tt('dve', t2[tb % 2][:], ps[bB][0:64, :], sin2[:, cs], ALU.mult, [('ps', bB), ('ctab',)], [('t2', tb % 2)])
                        tt('pool', krT[:, cs], t1[tb % 2][:], t2[tb % 2][:], ALU.add, [('t1', tb % 2), ('t2', tb % 2)], [('krT',)])
                    scale = float(192 ** -0.5)
                    ptc = [0]
                    for h in range(4):
                        par = 0
                        for tb in range(4):
                            cs = slice(tb * 512, (tb + 1) * 512)
                            b0, b1, b5, b6 = (0, 1, 5, 6) if tb % 2 == 0 else (2, 3, 4, 7)
                            for kc in range(3):
                                mm(ps[b0][:, :], wq_t[:, kc, h * 192:h * 192 + 128], cqg[:, kc, cs], kc == 0, kc == 2, [('wq',), ('cqg',)], [('ps', b0)])
                            for kc in range(3):
                                mm(ps[b5][0:64, :], wq_t[:, kc, h * 192 + 128:h * 192 + 192], cqg[:, kc, cs], kc == 0, kc == 2, [('wq',), ('cqg',)], [('ps', b5)])
                            for kc in range(3):
                                mm(ps[b6][0:64, :], wqr_t[:, kc, h, :], cqg[:, kc, cs], kc == 0, kc == 2, [('wqr',), ('cqg',)], [('ps', b6)])
                            for kc in range(2):
                                mm(ps[b1][:, :], wkv_t[:, kc, h * 256:h * 256 + 128], ckvg[:, kc, cs], kc == 0, kc == 1, [('wkv',), ('ckvg',)], [('ps', b1)])
                            tt('dve', qn[par][:, cs], ps[b0][:, :], rsq[:, cs], ALU.mult, [('ps', b0), ('rsq',)], [('qn', par)])
                            tt('dve', t1[tb % 2][:], ps[b5][0:64, :], cos2[:, cs], ALU.mult, [('ps', b5), ('ctab',)], [('t1', tb % 2)])
                            tt('dve', t2[tb % 2][:], ps[b6][0:64, :], sin2[:, cs], ALU.mult, [('ps', b6), ('ctab',)], [('t2', tb % 2)])
                            cp('act', kn[par][:, cs], ps[b1][:, :], [('ps', b1)], [('kn', par)])
                            tt('pool', t1[tb % 2][:], t1[tb % 2][:], t2[tb % 2][:], ALU.add, [('t1', tb % 2), ('t2', tb % 2)], [('t1', tb % 2)])
                            tt('pool', qr[par][:, cs], t1[tb % 2][:], rsq[0:64, cs], ALU.mult, [('t1', tb % 2), ('rsq',)], [('qr', par)])
                        for g4 in range(NT // 4):
                            bank = 2 + g4 % 2
                            for tl in range(4):
                                t = g4 * 4 + tl
                                for kc in range(2):
                                    mm(ps[bank][:, tl * 128:(tl + 1) * 128], ckvg[:, kc, t * 128:(t + 1) * 128],
                                       wkv_t[:, kc, h * 256 + 128:h * 256 + 256], kc == 0, kc == 1,
                                       [('wkv',), ('ckvg',)], [('ps', bank)], inc=(kc == 1 and tl == 3))
                            cp('act' if g4 % 2 else 'dve', vh[par][:, g4 * 4:(g4 + 1) * 4, :],
                               ps[bank][:, :].rearrange("p (t c) -> p t c", t=4), [('ps', bank)], [('vh', par)])
                        items = [(qb, kt) for qb in range(4) for kt in range(4 * qb + 4)]

                        def att_S(idx):
                            qb, kt = items[idx]
                            r = max(0, kt - 4 * qb)
                            c0 = qb * 512 + r * 128
                            ncol = 512 - r * 128
                            sb_ = ps[idx % 2]
                            ksb = ('ps', idx % 2)
                            mm(sb_[:, 0:ncol], kn[par][:, kt * 128:(kt + 1) * 128], qn[par][:, c0:c0 + ncol], True, False,
                               [('kn', par), ('qn', par)], [ksb], inc=False)
                            mm(sb_[:, 0:ncol], krT[:, kt * 128:(kt + 1) * 128], qr[par][:, c0:c0 + ncol], False, True,
                               [('krT',), ('qr', par)], [ksb])

                        def att_EV(idx):
                            qb, kt = items[idx]
                            nk = 4 * qb + 4
                            r = max(0, kt - 4 * qb)
                            ncol = 512 - r * 128
                            sb_ = ps[idx % 2]
                            ksb = ('ps', idx % 2)
                            pO_, pD_ = ps[4 + (qb % 2) * 2], ps[5 + (qb % 2) * 2]
                            kO, kD = ('ps', 4 + (qb % 2) * 2), ('ps', 5 + (qb % 2) * 2)
                            pi = ptc[0] % 3
                            ptc[0] += 1
                            act(PT[pi][:, 0:ncol], sb_[:, 0:ncol], AF.Exp, [ksb], [('PT', pi)], scale=scale)
                            if kt >= 4 * qb:
                                T.issue('pool', lambda e: e.memset(PT[pi][64:128, 0:64], 0.0), [], [('PT', pi)])
                            mm(pO_[:, r * 128:512], vh[par][:, kt, :], PT[pi][:, 0:ncol], kt == 0, kt == nk - 1, [('vh', par), ('PT', pi)], [kO])
                            mm(pD_[:, r * 128:512], onesb[:], PT[pi][:, 0:ncol], kt == 0, kt == nk - 1, [('onesb',), ('PT', pi)], [kD])
                            if kt == nk - 1:
                                dq = den[qb % 2]
                                T.issue('dve', lambda e: e.reciprocal(out=dq[:], in_=pD_[:, :]), [kD], [('den', qb % 2)])
                                tt('dve', mixT[:, 4 + h, qb * 512:(qb + 1) * 512], pO_[:, :], dq[:], ALU.mult, [kO, ('den', qb % 2)], [('mixT', 4 + h)])

                        att_S(0)
                        for idx in range(len(items)):
                            if idx + 1 < len(items):
                                att_S(idx + 1)
                            att_EV(idx)
                    T.barrier()
                dump('mixB', mixT[:, 4:8, :], [('mixT', 4 + h) for h in range(4)])
                if stop == 'mixB':
                    return True

                with ExitStack() as sc:
                    ln1_t = sb(sc, "ln1_t", [128, 2048], F32)
                    T.dma('sp', ln1_t[:], cbc[:, 0:2048], writes=[('cbcL',)], key=('cbcL',))
                    G1 = ln1_t[:, 0:1024]
                    B1 = ln1_t[:, 1024:2048]
                    xr = [sb(sc, f"xr{i}", [128, D], F32) for i in range(3)]
                    rr = [sb(sc, f"rr{i}", [128, D], F32) for i in range(3)]
                    hh = [sb(sc, f"hh{i}", [128, D], F32) for i in range(2)]
                    stats3 = [sb(sc, f"stats{i}", [128, 12], F32) for i in range(3)]
                    mv3 = [sb(sc, f"mv{i}", [128, 2], F32) for i in range(3)]
                    rs3 = [sb(sc, f"rs{i}", [128, 1], F32) for i in range(3)]

                    def d1_ldx(t):
                        q_ = t % 3
                        T.dma('sp', xr[q_][:], x[row0 + t * 128: row0 + (t + 1) * 128, :], writes=[('xr', q_)], key=('xr', q_))

                    def d1_mm(t):
                        sl = t % 2
                        tok = slice(t * 128, (t + 1) * 128)
                        for hb in range(2):
                            bank = sl * 2 + hb
                            for kc in range(8):
                                mm(ps[bank][:, :], mixT[:, kc, tok], wo_t[:, kc, hb * 512:(hb + 1) * 512], kc == 0, kc == 7,
                                   [('mixT', kc), ('wo',)], [('ps', bank)])

                    def d1_res(t):
                        sl = t % 2
                        q_ = t % 3
                        for hb in range(2):
                            bank = sl * 2 + hb
                            stt(rr[q_][:, hb * 512:(hb + 1) * 512], xr[q_][:, hb * 512:(hb + 1) * 512], ALPHA, ps[bank][:, :], ALU.mult, ALU.add,
                                [('xr', q_), ('ps', bank)], [('rr', q_)])

                    def d1_lna(t):
                        q_ = t % 3
                        r, stats, mv, rs = rr[q_], stats3[q_], mv3[q_], rs3[q_]
                        kr = ('rr', q_)
                        for hb in range(2):
                            T.issue('dve', lambda e: e.bn_stats(out=stats[:, hb * 6:(hb + 1) * 6], in_=r[:, hb * 512:(hb + 1) * 512]),
                                    reads=[kr], writes=[('lnst', q_)])
                        T.issue('dve', lambda e: e.bn_aggr(out=mv[:], in_=stats[:]), reads=[('lnst', q_)], writes=[('lnmv', q_)])
                        rsqrt(rs[:], mv[:, 1:2], eps1, [('lnmv', q_)], [('lnrs', q_)])
                        stt(mv[:, 1:2], mv[:, 0:1], -1.0, rs[:, 0:1], ALU.mult, ALU.mult, [('lnmv', q_), ('lnrs', q_)], [('lnmv', q_)])
                        act(r[:], r[:], AF.Identity, [kr, ('lnmv', q_), ('lnrs', q_)], [kr], bias=mv[:, 1:2], scale=rs[:, 0:1])

                    def d1_lnb(t):
                        q_ = t % 3
                        sl = t % 2
                        r = rr[q_]
                        kr = ('rr', q_)
                        tt('dve', r[:], r[:], G1, ALU.mult, [kr, ('cbcL',)], [kr])
                        tt('dve', hh[sl][:], r[:], B1, ALU.add, [kr, ('cbcL',)], [('hh', sl)])
                        T.dma('sp', hscr[t * 128:(t + 1) * 128, :], hh[sl][:], reads=[('hh', sl)], writes=[('hscr', t)], key=('hh', sl))

                    def d1_tr(t):
                        sl = t % 2
                        tok = slice(t * 128, (t + 1) * 128)
                        for kc in range(8):
                            bank = 4 + sl * 2 + kc // 4
                            col = (kc % 4) * 128
                            tr(ps[bank][:, col:col + 128], hh[sl][:, kc * 128:(kc + 1) * 128], [('hh', sl)], [('ps', bank)], inc=(kc % 4 == 3))
                        for hb in range(2):
                            bank = 4 + sl * 2 + hb
                            cp('act', xhT[:, hb * 4:(hb + 1) * 4, tok], ps[bank][:, :].rearrange("p (k c) -> p k c", k=4), [('ps', bank)], [('xhT', t)])

                    for t_ in range(3):
                        d1_ldx(t_)
                    d1_mm(0)
                    d1_res(0)
                    d1_mm(1)
                    d1_res(1)
                    d1_lna(0)
                    for t in range(NT):
                        if t + 2 < NT:
                            d1_mm(t + 2)
                        if t + 1 < NT:
                            d1_lna(t + 1)
                        d1_lnb(t)
                        if t + 2 < NT:
                            d1_res(t + 2)
                        if t + 3 < NT:
                            d1_ldx(t + 3)
                        d1_tr(t)
                    T.barrier()
            dump('hT', xhT[:], [('xhT', t) for t in range(NT)])
            if stop == 'hT':
                return True

            with ExitStack() as p2:
                wd_t = sb(p2, "wd_t", [128, NJ, 1024], BF16)
                ln2_t = sb(p2, "ln2_t", [128, 3072], F32)
                T.dma('sp', ln2_t[:], cbc[:, 2048:5120], writes=[('cbcL',)], key=('cbcL2',))
                G2 = ln2_t[:, 0:1024]
                B2 = ln2_t[:, 1024:2048]
                BG = ln2_t[:, 2048:3072]
                wg_t = sb(p2, "wg_t", [128, 8, 1024], BF16)
                wp_t = sb(p2, "wp_t", [128, 2, 1024], BF16)
                actT = sb(p2, "actT", [128, NJ, BLK2], BF16)
                wup = [sb(p2, f"wup{i}", [128, 2, 8, 128], BF16) for i in range(3)]
                rawgu = [sb(p2, f"rawgu{i}", [128, 2, 2 + BLK2], F32) for i in range(2)]
                accg = [sb(p2, f"accg{i}", [128, BLK2], F32) for i in range(2)]
                accu = [sb(p2, f"accu{i}", [128, BLK2], F32) for i in range(2)]
                halo = sb(p2, "halo", [128, 2, NJ, 2], F32)
                hr_ = [sb(p2, f"hr{i}", [128, D], F32) for i in range(2)]
                r2_ = [sb(p2, f"r2{i}", [128, D], F32) for i in range(2)]
                sg_ = [sb(p2, f"sgt{i}", [128, D], F32) for i in range(2)]
                pin_ = [sb(p2, f"pin{i}", [128, 256], F32) for i in range(2)]
                pTb_ = [sb(p2, f"pTb{i}", [128, 2, 128], BF16) for i in range(2)]
                stats_ = [sb(p2, f"stats2{i}", [128, 12], F32) for i in range(2)]
                mv_ = [sb(p2, f"mv2{i}", [128, 2], F32) for i in range(2)]
                rs_ = [sb(p2, f"rs2{i}", [128, 1], F32) for i in range(2)]
                T.issue('pool', lambda e: e.memset(halo[:], 0.0), writes=[('halo', c_) for c_ in range(NJ)])
                def ld2(t_):
                    q_ = t_ % 2
                    T.dma('sp', hr_[q_][:], hscr[t_ * 128:(t_ + 1) * 128, :], reads=[('hscr', t_)], writes=[('hr', q_)], key=('hr', q_))
                    T.dma('sp', pin_[q_][:], p[row0 + t_ * 128:row0 + (t_ + 1) * 128, :], writes=[('pin', q_)], key=('pin', q_))

                def trp(t_):
                    q_ = t_ % 2
                    for kc in range(2):
                        tr(ps[6 + q_][:, kc * 128:(kc + 1) * 128], pin_[q_][:, kc * 128:(kc + 1) * 128], [('pin', q_)], [('ps', 6 + q_)], inc=(kc == 1))
                    cp('act', pTb_[q_][:], ps[6 + q_][:, 0:256].rearrange("p (k c) -> p k c", k=2), [('ps', 6 + q_)], [('pTb', q_)])

                NB = S // BLK2
                TPB = BLK2 // 128
                wc = [0]

                def prefetch_wup(upto):
                    while wc[0] < min(upto, NB * NJ):
                        jj = wc[0] % NJ
                        s_ = wc[0] % 3
                        wc[0] += 1
                        T.dma('pool', wup[s_][:, 0, :, :], w_up_v[:, :, jj * 128:(jj + 1) * 128], writes=[('wup', s_, 0)], key=('wup', s_, 0))
                        T.dma('pool', wup[s_][:, 1, :, :], w_up_v[:, :, DFF + jj * 128:DFF + (jj + 1) * 128], writes=[('wup', s_, 1)], key=('wup', s_, 1))
                prefetch_wup(3)
                T.dma('pool', wg_t[:], w_gate_v[:, :, :], writes=[('wg',)], key=('wg',))
                T.dma('pool', wp_t[:], w_proj_v[:, :, :], writes=[('wp',)], key=('wp',))
                T.dma('pool', wd_t[:], w_down_v[:, :, :], writes=[('wd',)], key=('wd',))

                def stage_B2(j_):
                    q_ = j_ % 2
                    kg_, ku_ = ('acc2', 0, q_), ('acc2', 1, q_)
                    act(accg[q_][:], accg[q_][:], AF.Silu, [kg_], [kg_])
                    tt('dve', actT[:, j_, :], accg[q_][:], accu[q_][:], ALU.mult, [kg_, ku_], [('actT', j_)])

                for blk in range(NB):
                    bs = slice(blk * BLK2, (blk + 1) * BLK2)
                    XB = [('xhT', blk * TPB + i) for i in range(TPB)]
                    for j in range(NJ):
                        step = blk * NJ + j
                        prefetch_wup(step + 3)
                        sl = step % 3
                        jp = j % 2
                        rg = rawgu[jp]
                        kr_ = ('raw2', jp)
                        for gu in range(2):
                            bank = jp * 2 + gu
                            for kc in range(8):
                                mm(ps[bank][:, :], wup[sl][:, gu, kc, :], xhT[:, kc, bs], kc == 0, kc == 7, [('wup', sl, gu)] + XB, [('ps', bank)])
                        cp('act', rg[:, :, 0:2], halo[:, :, j, :], [('halo', j)], [kr_ + ('h',)])
                        cp('act', rg[:, :, 2:2 + BLK2], ps_all[:, jp * 1024:(jp + 1) * 1024].rearrange("p (g c) -> p g c", g=2),
                           [('ps', jp * 2), ('ps', jp * 2 + 1)], [kr_])
                        cp('act', halo[:, :, j, :], rg[:, :, BLK2:BLK2 + 2], [kr_], [('halo', j)])
                        acs = (accg[jp], accu[jp])
                        kas = (('acc2', 0, jp), ('acc2', 1, jp))
                        act(acs[0][:], ps[jp * 2][:, :], AF.Identity, [('ps', jp * 2), ('cpp',)], [kas[0]], bias=fcb[:, j:j + 1], scale=fcw[:, j, 2:3])
                        ts('dve', acs[1][:], rg[:, 1, 2:2 + BLK2], fcw[:, NJ + j, 2:3], fcb[:, NJ + j:NJ + j + 1], ALU.mult, ALU.add, [kr_, ('cpp',)], [kas[1]])
                        for tap in (1, 0):
                            for gu in range(2):
                                cidx = gu * NJ + j
                                stt(acs[gu][:], rg[:, gu, tap:tap + BLK2], fcw[:, cidx, tap:tap + 1], acs[gu][:], ALU.mult, ALU.add,
                                    [kr_, kr_ + ('h',), kas[gu], ('cpp',)], [kas[gu]])
                        if j >= 1:
                            stage_B2(j - 1)
                    stage_B2(NJ - 1)
                    AK = [('actT', j) for j in range(NJ)]
                    if stop == 'p2a' and blk == 0:
                        T.barrier()
                        return True
                    for tl in range(TPB):
                        t = blk * TPB + tl
                        tok = slice(t * 128, (t + 1) * 128)
                        ltok = slice(tl * 128, (tl + 1) * 128)
                        grow = row0 + t * 128
                        tp_ = t % 2
                        hr, r2, sg, pin, pTb = hr_[tp_], r2_[tp_], sg_[tp_], pin_[tp_], pTb_[tp_]
                        kh, kr2, kpin, kpt = ('hr', tp_), ('r2', tp_), ('pin', tp_), ('pTb', tp_)
                        if t == 0:
                            ld2(0)
                            trp(0)
                        if t + 1 < NT:
                            ld2(t + 1)
                        for hb in range(2):
                            hs = slice(hb * 512, (hb + 1) * 512)
                            ksg = ('sg', hb, tp_)
                            for kc in range(8):
                                mm(ps[2 + hb][:, :], xhT[:, kc, tok], wg_t[:, kc, hs], kc == 0, kc == 7, [('xhT', t), ('wg',)], [('ps', 2 + hb)])
                            for kc in range(2):
                                mm(ps[4 + hb][:, :], pTb[:, kc, :], wp_t[:, kc, hs], kc == 0, kc == 1, [kpt, ('wp',)], [('ps', 4 + hb)])
                            for j in range(NJ):
                                mm(ps[hb][:, :], actT[:, j, ltok], wd_t[:, j, hs], j == 0, j == NJ - 1, AK + [('wd',)], [('ps', hb)])
                            tt('dve', sg[:, hs], ps[2 + hb][:, :], BG[:, hs], ALU.add, [('ps', 2 + hb), ('cbcL',)], [ksg])
                            act(sg[:, hs], sg[:, hs], AF.Sigmoid, [ksg], [ksg])
                            tt('dve', sg[:, hs], sg[:, hs], ps[4 + hb][:, :], ALU.mult, [ksg, ('ps', 4 + hb)], [ksg])
                            stt(r2[:, hs], hr[:, hs], ALPHA, ps[hb][:, :], ALU.mult, ALU.add, [kh, ('ps', hb)], [kr2])
                            tt('dve', r2[:, hs], r2[:, hs], sg[:, hs], ALU.add, [kr2, ksg], [kr2])
                        if t + 1 < NT:
                            trp(t + 1)
                        ln_tile(r2, G2, B2, (stats_[tp_], mv_[tp_], rs_[tp_]), kr2, r2[:], kr2, sfx=tp_)
                        T.dma('sp', out[grow:grow + 128, :], r2[:], reads=[kr2], writes=[('out', sq, t)], key=('r2o', tp_))
                    if stop == 'p2b' and blk == 0:
                        T.barrier()
                        return True
                T.barrier()
          for sq in range(NSEQ):
            if seq_body(sq):
                break
        T.final_wait()
    nc._trk_log = T.log
    return nc


def _prep_common(inp):
    f = np.float32
    g = lambda k: np.asarray(inp[k], dtype=f)[0]
    cw = g("gdn_conv_w")
    fw = g("ffn_conv_w")
    fb = g("ffn_conv_b")
    cpp = np.zeros((128, 512), f)
    cpp[:, 0:48] = cw.reshape(4, 12, 128).transpose(2, 1, 0).reshape(128, 48)
    cpp[:, 48:180] = fw.reshape(3, 44, 128).transpose(2, 1, 0).reshape(128, 132)
    cpp[:, 180:224] = fb.reshape(44, 128).T
    cpp[:, 224] = g("gdn_norm_g")
    cpp[:, 225:228] = g("mla_q_norm_g").reshape(3, 128).T
    cpp[:, 228:230] = g("mla_kv_norm_g").reshape(2, 128).T
    cbc = np.zeros((128, 5 * 1024 + 128), f)
    for i, k in enumerate(["ln1_g", "ln1_b", "ln2_g", "ln2_b", "ple_b_gate"]):
        cbc[:, i * 1024:(i + 1) * 1024] = g(k)[None, :]
    cbc[:, 5120:5184] = np.tile(g("gdn_a_log"), 16)[None, :]
    cbc[:, 5184:5248] = np.tile(g("gdn_dt_bias"), 16)[None, :]
    j = np.arange(128)[:, None]
    i = np.arange(128)[None, :]
    cmat = np.zeros((128, 1792), f)
    cmat[:, 0:128] = np.eye(128, dtype=f)
    cmat[:, 128:256] = (j <= i).astype(f)
    for q_ in range(4):
        cmat[:, 256 + q_ * 128:384 + q_ * 128] = np.where(j <= i, 0.0, -30000.0).astype(f)
        cmat[:, 768 + q_ * 128:896 + q_ * 128] = np.where(j < i, 0.0, -30000.0).astype(f)
        cmat[:, 1280 + q_ * 128:1408 + q_ * 128] = np.eye(128, dtype=f)
    inv = (np.float32(10000.0) ** (-(np.arange(0, 64, 2, dtype=f)) / np.float32(64))).astype(f)
    ang = (np.arange(S, dtype=f)[:, None] * inv[None, :]).astype(f)
    cos = np.cos(ang.astype(np.float64)).astype(f).T
    sin = np.sin(ang.astype(np.float64)).astype(f).T
    ctab = np.zeros((64, 2 * S), f)
    ctab[0:32, 0:S] = cos
    ctab[32:64, 0:S] = cos
    ctab[0:32, S:] = sin
    ctab[32:64, S:] = sin
    return {
        "w_in": np.ascontiguousarray(g("w_in")), "wq_up": np.ascontiguousarray(g("mla_w_q_up")),
        "wkv_up": np.ascontiguousarray(g("mla_w_kv_up")), "w_out": np.ascontiguousarray(g("w_out")),
        "w_up": np.ascontiguousarray(g("ffn_w_up")), "w_down": np.ascontiguousarray(g("ffn_w_down")),
        "w_gate": np.ascontiguousarray(g("ple_w_gate")), "w_proj": np.ascontiguousarray(g("ple_w_proj")),
        "cpp": cpp, "cbc": cbc, "cmat": cmat, "ctab": ctab,
    }


def kernel(**inputs):
    common = _prep_common(inputs)
    x = np.asarray(inputs["x"], dtype=np.float32)
    p = np.asarray(inputs["p"], dtype=np.float32)[0]
    B = x.shape[0]
    nseq = B // NCORES
    nc = build(nseq)
    in_maps = []
    for c in range(NCORES):
        m = dict(common)
        m["x"] = np.ascontiguousarray(x[c * nseq:(c + 1) * nseq].reshape(nseq * S, D))
        m["p"] = np.ascontiguousarray(p[c * nseq:(c + 1) * nseq].reshape(nseq * S, 256))
        in_maps.append(m)
    res = run_bass_kernel_spmd(nc, in_maps, core_ids=list(range(NCORES)))
    outs = [np.asarray(r["out"]).reshape(nseq, S, D) for r in res.results]
    return np.concatenate(outs, axis=0).astype(np.float32)
```
